# Optimizing a Trainium2 kernel written in Bass

```python
import math
import jax, jax.numpy as jnp
from jax import lax
import numpy as np

D_MODEL = 1024
BATCH = 32
SEQ = 2048
DEPTH = 1
DEC_BATCH = 128
DEC_SEQ = 4
PAST_LEN = 8192
PAGE_SIZE = 128

D_FF = 2816
GDN_HEADS = 8
GDN_HEAD_DIM = 64
GDN_WIDTH = GDN_HEADS * GDN_HEAD_DIM
GDN_CONV = 4
GDN_CHUNK = 64
NSA_Q_HEADS = 8
NSA_KV_HEADS = 2
NSA_HEAD_DIM = 64
NSA_GROUP = NSA_Q_HEADS // NSA_KV_HEADS
NSA_WIDTH = NSA_Q_HEADS * NSA_HEAD_DIM
KV_WIDTH = NSA_KV_HEADS * NSA_HEAD_DIM
CMP_BLOCK = 32
SEL_BLOCK = 64
CMP_PER_SEL = SEL_BLOCK // CMP_BLOCK
SEL_TOPK = 16
WINDOW = 512
SEL_QBLK = 16
WIN_QBLK = 128
IN_SPLITS = (3 * GDN_WIDTH, GDN_WIDTH, GDN_HEADS, GDN_HEADS, NSA_WIDTH,
             2 * KV_WIDTH, 2 * KV_WIDTH, 2 * KV_WIDTH, 3 * NSA_Q_HEADS, 2 * D_MODEL)
D_IN = sum(IN_SPLITS)
DN_ALPHA = (2.0 * DEPTH) ** 0.25
DN_BETA = (8.0 * DEPTH) ** -0.25
NEG = -1e30
BIG = 1e4

kernel_name = 'hybrid_gdn_nsa_macaron_step'


def _split_in(p):
    offs = [0]
    for w in IN_SPLITS:
        offs.append(offs[-1] + w)
    return [p[..., offs[i]:offs[i + 1]] for i in range(len(IN_SPLITS))]


def _layer_norm(x, g, b, eps=1e-5):
    xf = x.astype(jnp.float32)
    mu = xf.mean(-1, keepdims=True)
    var = jnp.mean(jnp.square(xf - mu), -1, keepdims=True)
    return ((xf - mu) * lax.rsqrt(var + eps)).astype(x.dtype) * g + b


def _l2norm(x, eps=1e-6):
    return x * lax.rsqrt(jnp.sum(x * x, -1, keepdims=True) + eps)


def _masked_softmax(s, mask, axis=-1):
    s = jnp.where(mask, s.astype(jnp.float32), NEG)
    p = jax.nn.softmax(s, axis=axis)
    return jnp.where(mask, p, 0.0)


def _swiglu(h, w_gu, w_down):
    gt, up = jnp.split(h @ w_gu, 2, axis=-1)
    return (jax.nn.silu(gt) * up) @ w_down


def _causal_conv(x, buf, w):
    L = x.shape[1]
    xc = jnp.concatenate([buf, x], axis=1)
    y = sum(xc[:, j:j + L] * w[j] for j in range(GDN_CONV))
    return jax.nn.silu(y), xc[:, -(GDN_CONV - 1):]


def _gated_delta(q, k, v, g, beta, s0):
    B, L, H, _ = q.shape
    C = min(GDN_CHUNK, L)
    Lp = -(-L // C) * C
    pad = Lp - L

    def prep(t):
        t = jnp.pad(t, [(0, 0), (0, pad)] + [(0, 0)] * (t.ndim - 2))
        t = t.reshape((B, Lp // C, C) + t.shape[2:])
        return jnp.moveaxis(t, 3, 2)

    qc, kc, vc, gc, bc = [prep(t) for t in (q, k, v, g, beta)]
    G = jnp.cumsum(gc, axis=-1)
    ix = jnp.arange(C)
    incl = ix[:, None] >= ix[None, :]
    strict = ix[:, None] > ix[None, :]
    decay = jnp.exp(jnp.where(incl, G[..., :, None] - G[..., None, :], NEG))
    kb = kc * bc[..., None]
    a_kk = jnp.einsum('bnhid,bnhjd->bnhij', kb, kc) * jnp.where(strict, decay, 0.0)
    eye = jnp.eye(C, dtype=q.dtype)
    t_inv = lax.linalg.triangular_solve(eye + a_kk, jnp.broadcast_to(eye, a_kk.shape),
                                        left_side=True, lower=True)
    u = t_inv @ (vc * bc[..., None])
    w = t_inv @ (kb * jnp.exp(G)[..., None])
    a_qk = jnp.einsum('bnhid,bnhjd->bnhij', qc, kc) * decay
    q_dec = qc * jnp.exp(G)[..., None]
    k_dec = kc * jnp.exp(G[..., -1:] - G)[..., None]
    g_last = jnp.exp(G[..., -1])

    def step(S, xs):
        u_c, w_c, aqk_c, qd_c, kd_c, gl_c = xs
        v_new = u_c - w_c @ S
        o = qd_c @ S + aqk_c @ v_new
        S = S * gl_c[..., None, None] + jnp.einsum('bhcd,bhce->bhde', kd_c, v_new)
        return S, o

    xs = tuple(jnp.moveaxis(t, 1, 0) for t in (u, w, a_qk, q_dec, k_dec, g_last))
    s_fin, o = lax.scan(step, s0, xs)
    o = jnp.moveaxis(jnp.moveaxis(o, 0, 1), 2, 3).reshape(B, Lp, H, -1)[:, :L]
    return o, s_fin


def _gdn_mixer(qkv, z, a, b, conv_buf, s0, conv_w, a_log, dt_bias, norm_w):
    B, L, _ = qkv.shape
    qkv_c, new_buf = _causal_conv(qkv, conv_buf, conv_w)
    qkv_c = qkv_c.astype(jnp.float32)
    q, k, v = [t.reshape(B, L, GDN_HEADS, GDN_HEAD_DIM) for t in jnp.split(qkv_c, 3, axis=-1)]
    q = _l2norm(q) * GDN_HEAD_DIM ** -0.5
    k = _l2norm(k)
    g = -jnp.exp(a_log.astype(jnp.float32)) * jax.nn.softplus(a.astype(jnp.float32) + dt_bias.astype(jnp.float32))
    beta = jax.nn.sigmoid(b.astype(jnp.float32))
    o, s_fin = _gated_delta(q, k, v, g, beta, s0.astype(jnp.float32))
    o = o * lax.rsqrt(jnp.mean(o * o, -1, keepdims=True) + 1e-6)
    o = o * norm_w.astype(jnp.float32) * jax.nn.silu(z.reshape(B, L, GDN_HEADS, GDN_HEAD_DIM).astype(jnp.float32))
    return o.reshape(B, L, GDN_WIDTH).astype(qkv.dtype), new_buf, s_fin.astype(qkv.dtype)


def _nsa_compressed(qg, kv_full, w_cmp, q_pos):
    B, T = kv_full.shape[:2]
    n_cmp = T // CMP_BLOCK
    blocks = kv_full.reshape(B, n_cmp, CMP_BLOCK, 2, NSA_KV_HEADS, NSA_HEAD_DIM)
    kvb = jnp.einsum('bnjshd,sjh->bnshd', blocks, w_cmp)
    s = jnp.einsum('blhgd,bnhd->bhgln', qg, kvb[:, :, 0]) * NSA_HEAD_DIM ** -0.5
    mask = ((jnp.arange(n_cmp) + 1) * CMP_BLOCK - 1)[None, :] <= q_pos[:, None]
    p = _masked_softmax(s, mask)
    o = jnp.einsum('bhgln,bnhd->blhgd', p.astype(qg.dtype), kvb[:, :, 1])
    imp = p.sum(2).reshape(B, NSA_KV_HEADS, -1, n_cmp // CMP_PER_SEL, CMP_PER_SEL).sum(-1)
    return o, imp


def _nsa_select(imp, q_pos):
    n_sel = imp.shape[-1]
    blk = jnp.arange(n_sel)[None, :]
    cur = (q_pos // SEL_BLOCK)[:, None]
    valid = blk * SEL_BLOCK <= q_pos[:, None]
    forced = (blk == 0) | (blk == cur) | (blk == cur - 1)
    score = jnp.where(valid, jnp.where(forced, BIG, imp), -1.0)
    top_v, top_i = lax.top_k(jnp.moveaxis(score, 1, 2), min(SEL_TOPK, n_sel))
    return top_i, top_v >= 0.0


def _nsa_sel_attend(qg, kv_g, top_i, sel_ok, q_pos):
    key_pos = top_i[..., None] * SEL_BLOCK + jnp.arange(SEL_BLOCK)
    mask = sel_ok[..., None] & (key_pos <= q_pos[None, :, None, None, None])
    s = jnp.einsum('blhgd,blhkjd->blhgkj', qg, kv_g[..., 0, :]) * NSA_HEAD_DIM ** -0.5
    p = _masked_softmax(s, mask[:, :, :, None], axis=(-2, -1))
    return jnp.einsum('blhgkj,blhkjd->blhgd', p.astype(qg.dtype), kv_g[..., 1, :])


def _nsa_win_attend(qg, kv, q_pos, k_pos):
    s = jnp.einsum('blhgd,bshd->bhgls', qg, kv[:, :, 0]) * NSA_HEAD_DIM ** -0.5
    diff = q_pos[:, None] - k_pos[None, :]
    mask = (diff >= 0) & (diff < WINDOW) & (k_pos[None, :] >= 0)
    p = _masked_softmax(s, mask)
    return jnp.einsum('bhgls,bshd->blhgd', p.astype(qg.dtype), kv[:, :, 1])


def _nsa_prompt(qg, kv_cmp, kv_sel, kv_win, w_cmp):
    B, S = qg.shape[:2]
    pos = jnp.arange(S)
    o_cmp, imp = _nsa_compressed(qg, kv_cmp, w_cmp, pos)
    top_i, sel_ok = _nsa_select(imp, pos)
    sel_blocks = jnp.moveaxis(kv_sel.reshape(B, S // SEL_BLOCK, SEL_BLOCK, 2, NSA_KV_HEADS, NSA_HEAD_DIM), 4, 1)
    b_ix = jnp.arange(B)[:, None, None, None]
    h_ix = jnp.arange(NSA_KV_HEADS)[None, None, :, None]
    nqb = S // SEL_QBLK

    def to_blocks(t):
        return jnp.moveaxis(t.reshape((B, nqb, SEL_QBLK) + t.shape[2:]), 1, 0)

    def sel_block(xs):
        q_b, i_b, ok_b, pos_b = xs
        kv_g = sel_blocks[b_ix, h_ix, i_b]
        return _nsa_sel_attend(q_b, kv_g, i_b, ok_b, pos_b)

    o_sel = lax.map(sel_block, (to_blocks(qg), to_blocks(top_i), to_blocks(sel_ok), pos.reshape(nqb, SEL_QBLK)))
    o_sel = jnp.moveaxis(o_sel, 0, 1).reshape(qg.shape)
    kv_pad = jnp.pad(kv_win, ((0, 0), (WINDOW, 0), (0, 0), (0, 0), (0, 0)))

    def win_block(i):
        start = i * WIN_QBLK
        q_b = lax.dynamic_slice_in_dim(qg, start, WIN_QBLK, axis=1)
        kv_b = lax.dynamic_slice_in_dim(kv_pad, start, WIN_QBLK + WINDOW, axis=1)
        q_pos = start + jnp.arange(WIN_QBLK)
        k_pos = start - WINDOW + jnp.arange(WIN_QBLK + WINDOW)
        return _nsa_win_attend(q_b, kv_b, q_pos, k_pos)

    o_win = lax.map(win_block, jnp.arange(S // WIN_QBLK))
    o_win = jnp.moveaxis(o_win, 0, 1).reshape(qg.shape)
    return o_cmp, o_sel, o_win, kv_win[:, S - min(WINDOW, S):]


def _nsa_sample(qg, kv_cmp, kv_sel, kv_win, w_cmp, cache_cmp, cache_sel, cache_win, page_table):
    B, L = qg.shape[:2]
    n_pages = page_table.shape[1]
    past = n_pages * PAGE_SIZE
    pos = past + jnp.arange(L)
    t_pad = -(-(past + L) // SEL_BLOCK) * SEL_BLOCK

    def pad_rows(t):
        return jnp.pad(t, ((0, 0), (0, t_pad - past - L), (0, 0), (0, 0), (0, 0)))

    cmp_past = cache_cmp[page_table].reshape((B, past) + cache_cmp.shape[2:])
    o_cmp, imp = _nsa_compressed(qg, pad_rows(jnp.concatenate([cmp_past, kv_cmp], axis=1)), w_cmp, pos)
    top_i, sel_ok = _nsa_select(imp, pos)
    sub = PAGE_SIZE // SEL_BLOCK
    pool = cache_sel.reshape((cache_sel.shape[0], sub, SEL_BLOCK) + cache_sel.shape[2:])
    n_past_blk = past // SEL_BLOCK
    b_ix = jnp.arange(B)[:, None, None, None]
    h_ix = jnp.arange(NSA_KV_HEADS)[None, None, :, None]
    phys = page_table[b_ix, jnp.minimum(top_i // sub, n_pages - 1)]
    kv_past = pool[phys, top_i % sub, :, :, h_ix]
    tail = pad_rows(kv_sel).reshape(B, -1, SEL_BLOCK, 2, NSA_KV_HEADS, NSA_HEAD_DIM)
    kv_tail = tail[b_ix, jnp.clip(top_i - n_past_blk, 0, tail.shape[1] - 1), :, :, h_ix]
    kv_g = jnp.where((top_i >= n_past_blk)[..., None, None, None], kv_tail, kv_past)
    o_sel = _nsa_sel_attend(qg, kv_g, top_i, sel_ok, pos)
    wb = cache_win.shape[1]
    kv_all = jnp.concatenate([cache_win, kv_win], axis=1)
    o_win = _nsa_win_attend(qg, kv_all, pos, past - wb + jnp.arange(wb + L))
    return o_cmp, o_sel, o_win, kv_all[:, L:]


def _mix(h, mw, past):
    w_in, conv_w, a_log, dt_bias, norm_w, w_cmp, w_br_gdn, w_br_nsa, w_out = mw
    B, L, _ = h.shape
    qkv_g, z_g, a_g, b_g, q_n, kv_c, kv_s, kv_w, gate_n, merge_logits = _split_in(h @ w_in)

    def rows(t):
        return t.reshape(B, L, 2, NSA_KV_HEADS, NSA_HEAD_DIM)

    qg = q_n.reshape(B, L, NSA_KV_HEADS, NSA_GROUP, NSA_HEAD_DIM)
    kv_c, kv_s, kv_w = rows(kv_c), rows(kv_s), rows(kv_w)
    if past is None:
        conv_buf = jnp.zeros((B, GDN_CONV - 1, 3 * GDN_WIDTH), h.dtype)
        s0 = jnp.zeros((B, GDN_HEADS, GDN_HEAD_DIM, GDN_HEAD_DIM), jnp.float32)
        o_c, o_s, o_w, win_buf = _nsa_prompt(qg, kv_c, kv_s, kv_w, w_cmp)
    else:
        s0, conv_buf, cache_cmp, cache_sel, cache_win, page_table = past
        o_c, o_s, o_w, win_buf = _nsa_sample(qg, kv_c, kv_s, kv_w, w_cmp, cache_cmp, cache_sel, cache_win, page_table)
    o_gdn, conv_new, s_new = _gdn_mixer(qkv_g, z_g, a_g, b_g, conv_buf, s0, conv_w, a_log, dt_bias, norm_w)
    gn = jax.nn.sigmoid(gate_n).reshape(B, L, 3, NSA_KV_HEADS, NSA_GROUP, 1)
    o_nsa = (gn[:, :, 0] * o_c + gn[:, :, 1] * o_s + gn[:, :, 2] * o_w).reshape(B, L, NSA_WIDTH)
    m_gdn, m_nsa = jnp.split(jax.nn.sigmoid(merge_logits), 2, axis=-1)
    y = (m_gdn * (o_gdn @ w_br_gdn) + m_nsa * (o_nsa @ w_br_nsa)) @ w_out
    return y, (s_new, conv_new, kv_c, kv_s, win_buf)


def _layer(x, c, mw, past, ln_g, ln_b, w_ada, b_ada, w_ff1_gu, w_ff1_dn, w_ff2_gu, w_ff2_dn):
    mod = (jax.nn.silu(c) @ w_ada + b_ada).reshape(c.shape[0], 3, 3, 1, D_MODEL)

    def modulate(t, i):
        return t * (1.0 + mod[:, i, 1]) + mod[:, i, 0]

    x = _layer_norm(DN_ALPHA * x + 0.5 * mod[:, 0, 2] * _swiglu(modulate(x, 0), w_ff1_gu, w_ff1_dn), ln_g[0], ln_b[0])
    y, st = _mix(modulate(x, 1), mw, past)
    x = _layer_norm(DN_ALPHA * x + mod[:, 1, 2] * y, ln_g[1], ln_b[1])
    x = _layer_norm(DN_ALPHA * x + 0.5 * mod[:, 2, 2] * _swiglu(modulate(x, 2), w_ff2_gu, w_ff2_dn), ln_g[2], ln_b[2])
    return x, st


def setup_inputs(seed: int = 0) -> dict:
    key = jax.random.key(seed)
    ks = iter(jax.random.split(key, 40))

    def nrm(shape, s=1.0):
        return jax.random.normal(next(ks), shape, jnp.float32) * s

    n_pages = PAST_LEN // PAGE_SIZE
    used = DEC_BATCH * n_pages
    n_phys = used + max(1, used // 4)
    win_len = min(WINDOW, PAST_LEN)
    page_table = jax.random.permutation(next(ks), n_phys)[:used].astype(jnp.int32).reshape(DEC_BATCH, n_pages)
    Ld = DEPTH
    dt = jnp.exp(jax.random.uniform(next(ks), (Ld, GDN_HEADS), jnp.float32, math.log(1e-3), math.log(1e-1)))
    return {
        'x_prompt': nrm((BATCH, SEQ, D_MODEL)),
        'x_sample': nrm((DEC_BATCH, DEC_SEQ, D_MODEL)),
        'c_prompt': nrm((BATCH, D_MODEL)),
        'c_sample': nrm((DEC_BATCH, D_MODEL)),
        'state_gdn': nrm((Ld, DEC_BATCH, GDN_HEADS, GDN_HEAD_DIM, GDN_HEAD_DIM), 0.1),
        'state_gdn_conv': nrm((Ld, DEC_BATCH, GDN_CONV - 1, 3 * GDN_WIDTH)),
        'cache_cmp_kv': nrm((Ld, n_phys, PAGE_SIZE, 2, NSA_KV_HEADS, NSA_HEAD_DIM)),
        'cache_sel_kv': nrm((Ld, n_phys, PAGE_SIZE, 2, NSA_KV_HEADS, NSA_HEAD_DIM)),
        'cache_win_kv': nrm((Ld, DEC_BATCH, win_len, 2, NSA_KV_HEADS, NSA_HEAD_DIM)),
        'page_table': page_table,
        'ln_g': 1.0 + nrm((Ld, 3, D_MODEL), 0.02),
        'ln_b': nrm((Ld, 3, D_MODEL), 0.02),
        'w_ada': nrm((Ld, D_MODEL, 9 * D_MODEL), 0.5 * D_MODEL ** -0.5),
        'b_ada': nrm((Ld, 9 * D_MODEL), 0.02),
        'w_ff1_gu': nrm((Ld, D_MODEL, 2 * D_FF), D_MODEL ** -0.5),
        'w_ff1_dn': nrm((Ld, D_FF, D_MODEL), DN_BETA * D_FF ** -0.5),
        'w_ff2_gu': nrm((Ld, D_MODEL, 2 * D_FF), D_MODEL ** -0.5),
        'w_ff2_dn': nrm((Ld, D_FF, D_MODEL), DN_BETA * D_FF ** -0.5),
        'w_in': nrm((Ld, D_MODEL, D_IN), D_MODEL ** -0.5),
        'gdn_conv_w': nrm((Ld, GDN_CONV, 3 * GDN_WIDTH), 0.5),
        'gdn_a_log': jnp.log(jax.random.uniform(next(ks), (Ld, GDN_HEADS), jnp.float32, 1.0, 16.0)),
        'gdn_dt_bias': dt + jnp.log(-jnp.expm1(-dt)),
        'gdn_norm_w': 1.0 + nrm((Ld, GDN_HEAD_DIM), 0.02),
        'nsa_w_cmp': (1.0 + nrm((Ld, 2, CMP_BLOCK, NSA_KV_HEADS), 0.1)) / CMP_BLOCK,
        'w_br_gdn': nrm((Ld, GDN_WIDTH, D_MODEL), GDN_WIDTH ** -0.5),
        'w_br_nsa': nrm((Ld, NSA_WIDTH, D_MODEL), NSA_WIDTH ** -0.5),
        'w_out': nrm((Ld, D_MODEL, D_MODEL), DN_BETA * D_MODEL ** -0.5),
    }


def reference(x_prompt, x_sample, c_prompt, c_sample, state_gdn, state_gdn_conv, cache_cmp_kv, cache_sel_kv,
              cache_win_kv, page_table, ln_g, ln_b, w_ada, b_ada, w_ff1_gu, w_ff1_dn, w_ff2_gu, w_ff2_dn,
              w_in, gdn_conv_w, gdn_a_log, gdn_dt_bias, gdn_norm_w, nsa_w_cmp, w_br_gdn, w_br_nsa, w_out):
    y_p, y_s = x_prompt, x_sample
    p_st, s_st = [], []
    for l in range(DEPTH):
        mw = (w_in[l], gdn_conv_w[l], gdn_a_log[l], gdn_dt_bias[l], gdn_norm_w[l], nsa_w_cmp[l],
              w_br_gdn[l], w_br_nsa[l], w_out[l])
        fw = (ln_g[l], ln_b[l], w_ada[l], b_ada[l], w_ff1_gu[l], w_ff1_dn[l], w_ff2_gu[l], w_ff2_dn[l])
        y_p, st_p = _layer(y_p, c_prompt, mw, None, *fw)
        past = (state_gdn[l], state_gdn_conv[l], cache_cmp_kv[l], cache_sel_kv[l], cache_win_kv[l], page_table)
        y_s, st_s = _layer(y_s, c_sample, mw, past, *fw)
        p_st.append(st_p)
        s_st.append(st_s)
    p_gdn, p_conv, p_cmp_kv, p_sel_kv, p_win_kv = [jnp.stack(t) for t in zip(*p_st)]
    s_gdn, s_conv, s_cmp_kv, s_sel_kv, s_win_kv = [jnp.stack(t) for t in zip(*s_st)]
    return (y_p, y_s, p_gdn, p_conv, p_cmp_kv, p_sel_kv, p_win_kv, s_gdn, s_conv, s_cmp_kv, s_sel_kv, s_win_kv)
```

```python
import contextlib
import numpy as np
import concourse.bass as bass
import concourse.mybir as mybir
from concourse.bass_utils import run_bass_kernel_spmd

F32 = mybir.dt.float32
BF16 = mybir.dt.bfloat16
I32 = mybir.dt.int32
AF = mybir.ActivationFunctionType
ALU = mybir.AluOpType
AX = mybir.AxisListType

D = 1024
DFF = 2816
DIN = 5416
NCORES = 8
ALPHA = 2.0 ** 0.25
LS = 4

FULL_CFG = dict(NP=4, SEQ=2048, NS=16, NPAGES=64, NPHYS=10240)


class Tok:
    __slots__ = ("w", "r", "name")

    def __init__(self, name=""):
        self.w = None
        self.r = {}
        self.name = name


class Ev:
    __slots__ = ("dim", "val")

    def __init__(self, dim, val):
        self.dim = dim
        self.val = val


class KB:
    def __init__(self, nc, es, kq=None):
        self.nc = nc
        self.eng = {"pe": nc.tensor, "act": nc.scalar, "dve": nc.vector, "pool": nc.gpsimd, "sp": nc.sync}
        self.sem = {e: es.enter_context(nc.semaphore("s_" + e)) for e in self.eng}
        self.cnt = {e: 0 for e in self.eng}
        self.waited = {e: {} for e in self.eng}
        kq = kq or {"sp": 24, "act": 8, "pool": 16}
        self.dq = {q: [es.enter_context(nc.semaphore("d_%s%d" % (q, i))) for i in range(k)] for q, k in kq.items()}
        self.dqn = {q: 0 for q in kq}
        self.pe_pending = []
        self.nops = 0

    def semof(self, dim):
        if isinstance(dim, str):
            return self.sem[dim]
        return self.dq[dim[0]][dim[1]]

    def _collect(self, e, R, W):
        deps = {}

        def add(ev):
            if ev is None:
                return
            if e == "pe" and ev.dim == "pe":
                return
            if ev.val is None:
                raise RuntimeError("dependency on unsignaled PE op")
            if deps.get(ev.dim, 0) < ev.val:
                deps[ev.dim] = ev.val

        for t in R:
            add(t.w)
        for t in W:
            add(t.w)
            for r in t.r.values():
                add(r)
        return deps

    def _waits(self, e, deps):
        wd = self.waited[e]
        for dim, val in deps.items():
            if wd.get(dim, 0) < val:
                self.eng[e].wait_ge(self.semof(dim), val)
                wd[dim] = val

    def _record(self, ev, R, W):
        for t in R:
            t.r[ev.dim] = ev
        for t in W:
            t.w = ev
            t.r = {}

    def op(self, e, fn, R=(), W=(), sig=True):
        self._waits(e, self._collect(e, R, W))
        ins = fn(self.eng[e])
        self.nops += 1
        if sig:
            self.cnt[e] += 1
            ins.then_inc(self.sem[e], 1)
            ev = Ev(e, self.cnt[e])
            if e == "pe":
                for p in self.pe_pending:
                    p.val = self.cnt[e]
                self.pe_pending = []
        else:
            assert e == "pe"
            ev = Ev("pe", None)
            self.pe_pending.append(ev)
        self._record(ev, R, W)
        return ev

    def dma(self, q, out, in_, R=(), W=(), **kw):
        slots = self.dq[q]
        i = self.dqn[q]
        self.dqn[q] += 1
        k = i % len(slots)
        val = 16 * (i // len(slots) + 1)
        deps = self._collect(q, R, W)
        dim = (q, k)
        if val > 16:
            deps[dim] = max(deps.get(dim, 0), val - 16)
        self._waits(q, deps)
        ins = self.eng[q].dma_start(out=out, in_=in_, **kw)
        ins.then_inc(slots[k], 16)
        self.nops += 1
        ev = Ev(dim, val)
        self._record(ev, R, W)
        return ev

    def dma_fn(self, q, fn, R=(), W=()):
        slots = self.dq[q]
        i = self.dqn[q]
        self.dqn[q] += 1
        k = i % len(slots)
        val = 16 * (i // len(slots) + 1)
        deps = self._collect(q, R, W)
        dim = (q, k)
        if val > 16:
            deps[dim] = max(deps.get(dim, 0), val - 16)
        self._waits(q, deps)
        ins = fn(self.eng[q])
        ins.then_inc(slots[k], 16)
        self.nops += 1
        ev = Ev(dim, val)
        self._record(ev, R, W)
        return ev

    def _all_targets(self):
        targets = {}
        for e in self.eng:
            if self.cnt[e] > 0:
                targets[e] = self.cnt[e]
        for q, slots in self.dq.items():
            n = self.dqn[q]
            for k in range(len(slots)):
                uses = (n - k + len(slots) - 1) // len(slots) if n > k else 0
                if uses > 0:
                    targets[(q, k)] = 16 * uses
        return targets

    def barrier(self):
        assert not self.pe_pending
        targets = self._all_targets()
        for e in self.eng:
            self._waits(e, targets)

    def finish(self):
        assert not self.pe_pending
        self._waits("sp", self._all_targets())

    def mm(self, out, lhsT, rhs, start, stop, R=(), W=(), sig=None):
        if sig is None:
            sig = stop
        return self.op("pe", lambda e: e.matmul(out, lhsT, rhs, start=start, stop=stop), R, W, sig)

    def tr(self, out, in_, ident, R=(), W=(), sig=True):
        return self.op("pe", lambda e: e.transpose(out, in_, ident), R, W, sig)

    def act(self, out, in_, func, R=(), W=(), bias=None, scale=None, **kw):
        kws = dict(kw)
        if bias is not None:
            kws["bias"] = bias
        if scale is not None:
            kws["scale"] = scale
        return self.op("act", lambda e: e.activation(out=out, in_=in_, func=func, **kws), R, W)

    def tt(self, out, in0, in1, op, R=(), W=(), eng="dve"):
        return self.op(eng, lambda e: e.tensor_tensor(out=out, in0=in0, in1=in1, op=op), R, W)

    def ts(self, out, in0, s1, s2, op0, op1=None, R=(), W=(), eng="dve"):
        if op1 is None:
            return self.op(eng, lambda e: e.tensor_scalar(out=out, in0=in0, scalar1=s1, scalar2=None, op0=op0), R, W)
        return self.op(eng, lambda e: e.tensor_scalar(out=out, in0=in0, scalar1=s1, scalar2=s2, op0=op0, op1=op1), R, W)

    def stt(self, out, in0, scalar, in1, op0, op1, R=(), W=(), eng="dve"):
        return self.op(eng, lambda e: e.scalar_tensor_tensor(out=out, in0=in0, scalar=scalar, in1=in1, op0=op0, op1=op1), R, W)

    def copy(self, out, in_, R=(), W=(), eng="dve"):
        if eng == "act":
            return self.op("act", lambda e: e.copy(out=out, in_=in_), R, W)
        return self.op(eng, lambda e: e.tensor_copy(out=out, in_=in_), R, W)

    def memset(self, ap, v, W=(), eng="dve"):
        return self.op(eng, lambda e: e.memset(ap, v), (), W)


class Ctx:
    pass


def dbg_dump(kb, c, ap, R, rows, cols):
    if not c.cfg.get("DBG") or c.dbg_n >= 16:
        return
    kb.dma("sp", c.dbg_d[c.dbg_n, :rows, :cols], ap, R=R)
    c.dbg_n += 1


def token_groups(cfg, G):
    out = []
    for s in range(cfg["NP"]):
        for g in range(cfg["SEQ"] // G):
            out.append(("p", s, s * cfg["SEQ"] + g * G, G))
    out.append(("s", None, 0, cfg["NS"] * LS))
    return out


def subtiles(n):
    return [(r0, min(128, n - r0)) for r0 in range(0, n, 128)]


def load_rows(kb, c, tile, tok, kind, seq, col0, ncols=1024):
    cfg = c.cfg
    if kind == "p":
        src = c.mod_d[seq:seq + 1, col0:col0 + ncols].partition_broadcast(128)
        kb.dma("sp", tile[:, :ncols], src, W=[tok])
    else:
        ts = cfg["NS"] * LS
        src = c.mod_d[cfg["NP"]:cfg["NP"] + ts, col0:col0 + ncols]
        kb.dma("sp", tile[:ts, :ncols], src, W=[tok])


def layer_norm_rows(kb, c, t, nr, Tt, lng, lnb, Tl, st, mv, rstd, Tst):
    nc = c.nc
    for h in range(2):
        kb.op("dve", lambda e, h=h: e.bn_stats(out=st[:nr, h, :], in_=t[:nr, h * 512:(h + 1) * 512]), [Tt], [Tst])
    kb.op("dve", lambda e: e.bn_aggr(out=mv[:nr, :], in_=st[:nr, :, :]), [Tst], [Tst])
    kb.act(rstd[:nr, :], mv[:nr, 1:2], AF.Sqrt, R=[Tst], W=[Tst], bias=1e-5)
    kb.op("dve", lambda e: e.reciprocal(out=rstd[:nr, :], in_=rstd[:nr, :]), [Tst], [Tst])
    kb.ts(t[:nr, :], t[:nr, :], mv[:nr, 0:1], rstd[:nr, 0:1], ALU.subtract, ALU.mult, R=[Tt, Tst], W=[Tt])
    kb.tt(t[:nr, :], t[:nr, :], lng[:nr, :], ALU.mult, R=[Tt, Tl], W=[Tt], eng="pool")
    kb.tt(t[:nr, :], t[:nr, :], lnb[:nr, :], ALU.add, R=[Tt, Tl], W=[Tt], eng="pool")


def phase_mod(kb, c):
    nc, cfg = c.nc, c.cfg
    NR = cfg["NP"] + cfg["NS"] * LS
    with contextlib.ExitStack() as ps:
        sb = lambda n, s, d=F32: ps.enter_context(nc.sbuf_tensor(n, s, d))
        ct = sb("m_c", [NR, D])
        sct = sb("m_scT", [128, 8, NR], BF16)
        modt = sb("m_mod", [NR, 9 * D])
        bt = sb("m_b", [NR, 9 * D])
        wb = [sb("m_w%d" % i, [128, 8, 512], BF16) for i in range(2)]
        pst = [ps.enter_context(nc.psum_tensor("m_ps%d" % i, [128, 512], F32)) for i in range(4)]
        Tc, Tsct, Tmod, Tb = Tok(), Tok(), Tok(), Tok()
        Tw = [Tok(), Tok()]
        Tp = [Tok() for _ in range(4)]
        kb.dma("sp", ct[:], c.c_rows, W=[Tc])
        kb.dma("sp", bt[:], c.b_ada.partition_broadcast(NR), W=[Tb])
        kb.act(ct[:], ct[:], AF.Silu, R=[Tc], W=[Tc])
        for k in range(8):
            b = k // 4
            col = (k % 4) * NR
            kb.tr(pst[b][:, col:col + NR], ct[:, k * 128:(k + 1) * 128], c.ident_f[:NR, :NR], R=[Tc, c.Tconst], W=[Tp[b]])
        for b in range(2):
            kb.copy(sct[:, 4 * b:4 * b + 4, :], pst[b][:, 0:4 * NR].rearrange("p (k n) -> p k n", k=4), R=[Tp[b]], W=[Tsct])
        wv = c.w_ada.rearrange("(k p) f -> p k f", p=128)
        for nb in range(18):
            w = wb[nb % 2]
            kb.dma("pool", w[:], wv[:, :, nb * 512:(nb + 1) * 512], W=[Tw[nb % 2]])
            bank = pst[2 + nb % 2]
            for k in range(8):
                kb.mm(bank[:NR, :], sct[:, k, :], w[:, k, :], k == 0, k == 7, R=[Tsct, Tw[nb % 2]], W=[Tp[2 + nb % 2]])
            kb.tt(modt[:, nb * 512:(nb + 1) * 512], bank[:NR, :], bt[:, nb * 512:(nb + 1) * 512], ALU.add,
                  R=[Tp[2 + nb % 2], Tb], W=[Tmod])
        for i in range(3):
            o = (i * 3 + 1) * D
            kb.ts(modt[:, o:o + D], modt[:, o:o + D], 1.0, None, ALU.add, R=[Tmod], W=[Tmod])
        for i in (0, 2):
            o = (i * 3 + 2) * D
            kb.ts(modt[:, o:o + D], modt[:, o:o + D], 0.5, None, ALU.mult, R=[Tmod], W=[Tmod])
        kb.dma("sp", c.mod_d[:, :], modt[:], R=[Tmod])
        kb.barrier()


def phase_ffn(kb, c, tag, src, dst, w_gu, w_dn, isub):
    nc, cfg = c.nc, c.cfg
    G = 256
    NJ = DFF // 128
    with contextlib.ExitStack() as ps:
        sb = lambda n, s, d=F32: ps.enter_context(nc.sbuf_tensor(tag + n, s, d))
        wgu = sb("wgu", [128, 8, 2 * DFF], BF16)
        wdn = sb("wdn", [128, NJ, D], BF16)
        hT = sb("hT", [128, NJ, G], BF16)
        xmT = sb("xmT", [128, 8, G], BF16)
        xin = [sb("xin%d" % i, [128, D]) for i in range(4)]
        rows = {n: sb("row_" + n, [128, D]) for n in ("sc", "sh", "g", "lng", "lnb")}
        xm = sb("xm", [128, D])
        wk = [sb("wk%d" % i, [128, D]) for i in range(2)]
        sgt = [sb("sg%d" % i, [128, G], BF16) for i in range(2)]
        st = sb("st", [128, 2, nc.vector.BN_STATS_DIM])
        mv = sb("mv", [128, nc.vector.BN_AGGR_DIM])
        rstd = sb("rstd", [128, 1])
        psgu = [ps.enter_context(nc.psum_tensor(tag + "psgu%d" % i, [128, 512], F32)) for i in range(4)]
        psy = [ps.enter_context(nc.psum_tensor(tag + "psy%d" % i, [128, 1024], F32)) for i in range(2)]
        Tgu = [Tok() for _ in range(11)]
        Tdn = [Tok() for _ in range(11)]
        ThT = [Tok() for _ in range(NJ)]
        TxmT, Txm, Tst, Tln, Tmodrows = Tok(), Tok(), Tok(), Tok(), Tok()
        Txin = [Tok() for _ in range(4)]
        Twk = [Tok(), Tok()]
        Tsg = [Tok(), Tok()]
        Tpsgu = [Tok() for _ in range(4)]
        Tpsy = [Tok(), Tok()]
        guv = w_gu.rearrange("(k p) f -> p k f", p=128)
        for i in range(11):
            kb.dma("pool", wgu[:, :, i * 512:(i + 1) * 512], guv[:, :, i * 512:(i + 1) * 512], W=[Tgu[i]])
        dnv = w_dn.rearrange("(j p) d -> p j d", p=128)
        for i in range(11):
            kb.dma("pool", wdn[:, 2 * i:2 * i + 2, :], dnv[:, 2 * i:2 * i + 2, :], W=[Tdn[i]])
        kb.dma("sp", rows["lng"][:], c.ln_g[isub:isub + 1, :].partition_broadcast(128), W=[Tln])
        kb.dma("sp", rows["lnb"][:], c.ln_b[isub:isub + 1, :].partition_broadcast(128), W=[Tln])
        cur = None
        xslot = 0
        wslot = 0
        gi = 0
        for (kind, seq, t0, n) in token_groups(cfg, G):
            if (kind, seq) != cur:
                cur = (kind, seq)
                base = isub * 3 * D
                load_rows(kb, c, rows["sh"], Tmodrows, kind, seq, base)
                load_rows(kb, c, rows["sc"], Tmodrows, kind, seq, base + D)
                load_rows(kb, c, rows["g"], Tmodrows, kind, seq, base + 2 * D)
            subs = subtiles(n)
            xs = []
            for si, (r0, nr) in enumerate(subs):
                xt = xin[xslot]
                Tx = Txin[xslot]
                xslot = (xslot + 1) % 4
                xs.append((xt, Tx))
                kb.dma("sp", xt[:nr, :], src[kind][t0 + r0:t0 + r0 + nr, :], W=[Tx])
                kb.tt(xm[:nr, :], xt[:nr, :], rows["sc"][:nr, :], ALU.mult, R=[Tx, Tmodrows], W=[Txm], eng="pool")
                kb.tt(xm[:nr, :], xm[:nr, :], rows["sh"][:nr, :], ALU.add, R=[Txm, Tmodrows], W=[Txm], eng="pool")
                py = psy[si % 2]
                for k in range(8):
                    kb.tr(py[:, k * 128:k * 128 + nr], xm[:nr, k * 128:(k + 1) * 128], c.ident_f[:nr, :nr],
                          R=[Txm, c.Tconst], W=[Tpsy[si % 2]])
                kb.copy(xmT[:, :, r0:r0 + nr], py[:, :].rearrange("p (k n) -> p k n", k=8)[:, :, :nr],
                        R=[Tpsy[si % 2]], W=[TxmT], eng="act")
            for j in range(NJ):
                pg = psgu[(j % 2) * 2]
                pu = psgu[(j % 2) * 2 + 1]
                Tg_, Tu_ = Tpsgu[(j % 2) * 2], Tpsgu[(j % 2) * 2 + 1]
                cg = j * 128
                cu = DFF + j * 128
                for k in range(8):
                    kb.mm(pg[:, :n], wgu[:, k, cg:cg + 128], xmT[:, k, :n], k == 0, k == 7,
                          R=[TxmT, Tgu[cg // 512]], W=[Tg_])
                for k in range(8):
                    kb.mm(pu[:, :n], wgu[:, k, cu:cu + 128], xmT[:, k, :n], k == 0, k == 7,
                          R=[TxmT, Tgu[cu // 512]], W=[Tu_])
                s = sgt[j % 2]
                kb.act(s[:, :n], pg[:, :n], AF.Silu, R=[Tg_], W=[Tsg[j % 2]])
                kb.tt(hT[:, j, :n], s[:, :n], pu[:, :n], ALU.mult, R=[Tsg[j % 2], Tu_], W=[ThT[j]])
            for si, (r0, nr) in enumerate(subs):
                py = psy[si % 2]
                Tpy = Tpsy[si % 2]
                for half in range(2):
                    for j in range(NJ):
                        kb.mm(py[:nr, half * 512:(half + 1) * 512], hT[:, j, r0:r0 + nr],
                              wdn[:, j, half * 512:(half + 1) * 512], j == 0, j == NJ - 1,
                              R=[ThT[j], Tdn[j // 2]], W=[Tpy])
                w = wk[wslot]
                Tw_ = Twk[wslot]
                wslot = (wslot + 1) % 2
                xt, Tx = xs[si]
                kb.tt(w[:nr, :], py[:nr, :], rows["g"][:nr, :], ALU.mult, R=[Tpy, Tmodrows], W=[Tw_])
                kb.stt(w[:nr, :], xt[:nr, :], ALPHA, w[:nr, :], ALU.mult, ALU.add, R=[Tx, Tw_], W=[Tw_])
                layer_norm_rows(kb, c, w, nr, Tw_, rows["lng"], rows["lnb"], Tln, st, mv, rstd, Tst)
                kb.dma("sp", dst[kind][t0 + r0:t0 + r0 + nr, :], w[:nr, :], R=[Tw_])
            gi += 1
        kb.barrier()


QKV0, Z0, Q0, KV0, GN0, MG0 = 0, 1536, 2064, 2576, 3344, 3368


def phase_win(kb, c):
    nc, cfg = c.nc, c.cfg
    G = 512
    SEQ, NP, NS = cfg["SEQ"], cfg["NP"], cfg["NS"]
    TP, TS = c.TP, c.TS
    with contextlib.ExitStack() as ps:
        sb = lambda n, s, d=F32: ps.enter_context(nc.sbuf_tensor("wi_" + n, s, d))
        win = sb("w", [128, 8, DIN], BF16)
        hT = sb("hT", [128, 8, G], BF16)
        xin = [sb("xin%d" % i, [128, D]) for i in range(2)]
        xm = sb("xm", [128, D])
        rows = {n: sb("row_" + n, [128, D]) for n in ("sc", "sh")}
        stF = [sb("stF%d" % i, [128, 4, G]) for i in range(2)]
        stB = [sb("stB%d" % i, [128, 4, G], BF16) for i in range(3)]
        stT = [sb("stT%d" % i, [128, 1536]) for i in range(2)]
        psx = ps.enter_context(nc.psum_tensor("wi_psx", [128, 1024], F32))
        psF = [ps.enter_context(nc.psum_tensor("wi_psF%d" % i, [128, 512], F32)) for i in range(3)]
        psT = ps.enter_context(nc.psum_tensor("wi_psT", [128, 1536], F32))
        Tw = [Tok() for _ in range(11)]
        ThT, Txm, Tmodrows, Tpsx, TpsT = Tok(), Tok(), Tok(), Tok(), Tok()
        Txin = [Tok(), Tok()]
        TstF = [Tok(), Tok()]
        TstB = [Tok(), Tok(), Tok()]
        TstT = [Tok(), Tok()]
        TpsF = [Tok(), Tok(), Tok()]
        wv = c.w_in.rearrange("(k p) f -> p k f", p=128)
        for i in range(11):
            hi = min(DIN, (i + 1) * 512)
            kb.dma("pool", win[:, :, i * 512:hi], wv[:, :, i * 512:hi], W=[Tw[i]])

        def wtoks(c0, c1):
            return [Tw[i] for i in range(c0 // 512, (c1 - 1) // 512 + 1)]

        wl = cfg["WIN"]
        kb.dma("sp", c.s_win[:, 0:wl - LS, :], c.cache_win[:, LS:wl, :])
        cur = None
        xslot = 0
        cnt = {"F": 0, "B": 0, "T": 0, "pf": 0, "ev": 0}
        for (kind, seq, t0, n) in token_groups(cfg, G):
            tg = t0 if kind == "p" else TP + t0
            if (kind, seq) != cur:
                cur = (kind, seq)
                load_rows(kb, c, rows["sh"], Tmodrows, kind, seq, 3 * D)
                load_rows(kb, c, rows["sc"], Tmodrows, kind, seq, 4 * D)
            subs = subtiles(n)
            for si, (r0, nr) in enumerate(subs):
                xt, Tx = xin[xslot], Txin[xslot]
                xslot = (xslot + 1) % 2
                kb.dma("sp", xt[:nr, :], c.x1[kind][t0 + r0:t0 + r0 + nr, :], W=[Tx])
                kb.tt(xm[:nr, :], xt[:nr, :], rows["sc"][:nr, :], ALU.mult, R=[Tx, Tmodrows], W=[Txm], eng="pool")
                kb.tt(xm[:nr, :], xm[:nr, :], rows["sh"][:nr, :], ALU.add, R=[Txm, Tmodrows], W=[Txm], eng="pool")
                for k in range(8):
                    kb.tr(psx[:, k * 128:k * 128 + nr], xm[:nr, k * 128:(k + 1) * 128], c.ident_f[:nr, :nr],
                          R=[Txm, c.Tconst], W=[Tpsx])
                kb.copy(hT[:, :, r0:r0 + nr], psx[:, :].rearrange("p (k n) -> p k n", k=8)[:, :, :nr],
                        R=[Tpsx], W=[ThT], eng="act")

            def fgroup(cols, stage_kind, dst_ap, func=None):
                if stage_kind == "F":
                    st, Ts = stF[cnt["F"] % 2], TstF[cnt["F"] % 2]
                    cnt["F"] += 1
                else:
                    st, Ts = stB[cnt["B"] % 3], TstB[cnt["B"] % 3]
                    cnt["B"] += 1
                for ci, c0 in enumerate(cols):
                    pf, Tpf = psF[cnt["pf"] % 3], TpsF[cnt["pf"] % 3]
                    cnt["pf"] += 1
                    for k in range(8):
                        kb.mm(pf[:, :n], win[:, k, c0:c0 + 128], hT[:, k, :n], k == 0, k == 7,
                              R=[ThT] + wtoks(c0, c0 + 128), W=[Tpf])
                    if func is not None:
                        kb.act(st[:, ci, :n], pf[:, :n], func, R=[Tpf], W=[Ts])
                    else:
                        eng = "act" if cnt["ev"] % 2 == 0 else "dve"
                        cnt["ev"] += 1
                        kb.copy(st[:, ci, :n], pf[:, :n], R=[Tpf], W=[Ts], eng=eng)
                kb.dma("sp", dst_ap, st[:, 0:len(cols), :n], R=[Ts])

            qv = c.qkvT_d.rearrange("(c p) t -> p c t", p=128)
            for g4 in range(3):
                fgroup([QKV0 + (g4 * 4 + i) * 128 for i in range(4)], "F", qv[:, g4 * 4:g4 * 4 + 4, tg:tg + n])
            fgroup([Q0 + i * 128 for i in range(4)], "B", c.qT_d.rearrange("(c p) t -> p c t", p=128)[:, :, tg:tg + n])
            fgroup([KV0, KV0 + 256, KV0 + 512], "B", c.kT_d.rearrange("i p t -> p i t")[:, :, tg:tg + n])
            mv_ = c.mgT_d.rearrange("(c p) t -> p c t", p=128)
            for g4 in range(4):
                fgroup([MG0 + (g4 * 4 + i) * 128 for i in range(4)], "B", mv_[:, g4 * 4:g4 * 4 + 4, tg:tg + n], func=AF.Sigmoid)

            for si, (r0, nr) in enumerate(subs):
                def tgroup(c0, ncols):
                    st, Ts = stT[cnt["T"] % 2], TstT[cnt["T"] % 2]
                    cnt["T"] += 1
                    for b0 in range(0, ncols, 512):
                        w_ = min(512, ncols - b0)
                        for k in range(8):
                            kb.mm(psT[:nr, b0:b0 + w_], hT[:, k, r0:r0 + nr], win[:, k, c0 + b0:c0 + b0 + w_],
                                  k == 0, k == 7, R=[ThT] + wtoks(c0 + b0, c0 + b0 + w_), W=[TpsT])
                    kb.copy(st[:nr, :ncols], psT[:nr, :ncols], R=[TpsT], W=[Ts])
                    return st, Ts

                a0 = t0 + r0
                st, Ts = tgroup(Z0, 528)
                kb.dma("sp", c.zab_d[tg + r0:tg + r0 + nr, :], st[:nr, :528], R=[Ts])
                st, Ts = tgroup(KV0, 792)
                kb.dma("sp", c.kvg_d[tg + r0:tg + r0 + nr, :], st[:nr, :792], R=[Ts])
                if kind == "p":
                    kb.dma("sp", c.p_cmp[a0:a0 + nr, :], st[:nr, 0:256], R=[Ts])
                    kb.dma("sp", c.p_sel[a0:a0 + nr, :], st[:nr, 256:512], R=[Ts])
                    pos = a0 - seq * SEQ
                    w0 = SEQ - min(wl, SEQ)
                    if pos >= w0:
                        wr = seq * min(wl, SEQ) + pos - w0
                        kb.dma("sp", c.p_win[wr:wr + nr, :], st[:nr, 512:768], R=[Ts])
                else:
                    kb.dma("sp", c.s_cmp[a0:a0 + nr, :], st[:nr, 0:256], R=[Ts])
                    kb.dma("sp", c.s_sel[a0:a0 + nr, :], st[:nr, 256:512], R=[Ts])
                    for l in range(LS):
                        kb.dma("sp", c.s_win[:, wl - LS + l, :], st[l:nr:LS, 512:768], R=[Ts])
                last_of_seq = (kind == "p" and a0 + nr == (seq + 1) * SEQ)
                if last_of_seq or kind == "s":
                    st, Ts = tgroup(QKV0, 1536)
                    if kind == "p":
                        kb.dma("sp", c.p_conv[seq, :, :], st[nr - 3:nr, :1536], R=[Ts])
                    else:
                        for l in range(1, LS):
                            kb.dma("sp", c.s_conv[:, l - 1, :], st[l:nr:LS, :1536], R=[Ts])
        kb.barrier()


def phase_gdn_a(kb, c):
    nc, cfg = c.nc, c.cfg
    G = 512
    SEQ, NP, NS = cfg["SEQ"], cfg["NP"], cfg["NS"]
    TP, TS = c.TP, c.TS
    with contextlib.ExitStack() as ps:
        sb = lambda n, s, d=F32: ps.enter_context(nc.sbuf_tensor("ga_" + n, s, d))
        cwt = sb("cwt", [4, 1536])
        cw = sb("cw", [128, 12, 4])
        blk1 = sb("blk1", [128, 128])
        xr = [sb("xr%d" % i, [128, 12, 3 + G]) for i in range(2)]
        yc = [sb("yc%d" % i, [128, 12, G]) for i in range(2)]
        sq = sb("sq", [128, 8, G])
        rn = [sb("rn%d" % i, [128, G]) for i in range(2)]
        kvt = sb("kvt", [128, 4, 1024])
        cbt = sb("cbt", [NS * 3, 1536])
        xrs = sb("xrs", [128, 12, NS, 3 + LS])
        pss = [ps.enter_context(nc.psum_tensor("ga_pss%d" % i, [128, 512], F32)) for i in range(2)]
        pst = [ps.enter_context(nc.psum_tensor("ga_pst%d" % i, [128, 1024], F32)) for i in range(2)]
        Tcw, Tblk, Tsq, Tkvt, Tcbt, Txrs = Tok(), Tok(), Tok(), Tok(), Tok(), Tok()
        Txr = [Tok(), Tok()]
        Tyc = [Tok(), Tok()]
        Trn = [Tok(), Tok()]
        Tpss = [Tok(), Tok()]
        Tpst = [Tok(), Tok()]
        kb.dma("sp", cwt[:], c.conv_w, W=[Tcw])
        kb.dma("sp", blk1[:], c.blk1_d, W=[Tblk])
        for cc in range(12):
            kb.tr(pss[0][:, cc * 4:cc * 4 + 4], cwt[:, cc * 128:(cc + 1) * 128], c.ident_f[:4, :4], R=[Tcw, c.Tconst], W=[Tpss[0]])
        kb.copy(cw[:, :, :], pss[0][:, 0:48].rearrange("p (c j) -> p c j", j=4), R=[Tpss[0]], W=[Tcw])
        qv = c.qkvT_d.rearrange("(c p) t -> p c t", p=128)
        qn = c.qkvn_d.rearrange("(c p) t -> p c t", p=128)
        bi = 0
        for (kind, seq, t0, n) in token_groups(cfg, G):
            tg = t0 if kind == "p" else TP + t0
            y, Ty = yc[bi % 2], Tyc[bi % 2]
            x, Tx = xr[bi % 2], Txr[bi % 2]
            bi += 1
            if kind == "p":
                if t0 % SEQ == 0:
                    kb.memset(x[:, :, 0:3], 0.0, W=[Tx])
                    kb.dma("sp", x[:, :, 3:3 + n], qv[:, :, tg:tg + n], W=[Tx])
                else:
                    kb.dma("sp", x[:, :, 0:3 + n], qv[:, :, tg - 3:tg + n], W=[Tx])
                for cc in range(12):
                    eng = "dve"
                    kb.ts(y[:, cc, :n], x[:, cc, 0:n], cw[:, cc, 0:1], None, ALU.mult, R=[Tx, Tcw], W=[Ty], eng=eng)
                    for j in range(1, 4):
                        kb.stt(y[:, cc, :n], x[:, cc, j:j + n], cw[:, cc, j:j + 1], y[:, cc, :n], ALU.mult, ALU.add,
                               R=[Tx, Tcw, Ty], W=[Ty], eng=eng)
            else:
                kb.dma("sp", cbt[:], c.conv_buf.rearrange("b r f -> (b r) f"), W=[Tcbt])
                for cc in range(12):
                    kb.tr(pst[0][:, cc * 64:cc * 64 + NS * 3], cbt[:, cc * 128:(cc + 1) * 128], c.ident_f[:NS * 3, :NS * 3],
                          R=[Tcbt, c.Tconst], W=[Tpst[0]])
                kb.copy(xrs[:, :, :, 0:3],
                        pst[0][:, 0:768].rearrange("p (c x) -> p c x", x=64)[:, :, 0:NS * 3].rearrange("p c (b r) -> p c b r", r=3),
                        R=[Tpst[0]], W=[Txrs])
                for cc in range(12):
                    kb.dma("sp", xrs[:, cc, :, 3:3 + LS], qv[:, cc, tg:tg + n].rearrange("p (b l) -> p b l", l=LS), W=[Txrs])
                for cc in range(12):
                    eng = "dve"
                    yv = y[:, cc, :n].rearrange("p (b l) -> p b l", l=LS)
                    kb.ts(yv, xrs[:, cc, :, 0:LS], cw[:, cc, 0:1], None, ALU.mult, R=[Txrs, Tcw], W=[Ty], eng=eng)
                    for j in range(1, 4):
                        kb.stt(yv, xrs[:, cc, :, j:j + LS], cw[:, cc, j:j + 1], yv, ALU.mult, ALU.add,
                               R=[Txrs, Tcw, Ty], W=[Ty], eng=eng)
            kb.act(y[:, :, :n], y[:, :, :n], AF.Silu, R=[Ty], W=[Ty])
            kb.tt(sq[:, :, :n], y[:, 0:8, :n], y[:, 0:8, :n], ALU.mult, R=[Ty], W=[Tsq], eng="pool")
            for cc in range(8):
                p_, Tp_ = pss[cc % 2], Tpss[cc % 2]
                r_, Tr_ = rn[cc % 2], Trn[cc % 2]
                kb.mm(p_[:, :n], blk1[:, :], sq[:, cc, :n], True, True, R=[Tsq, Tblk], W=[Tp_])
                if cc < 4:
                    kb.act(r_[:, :n], p_[:, :n], AF.Sqrt, R=[Tp_], W=[Tr_], scale=64.0, bias=64e-6)
                else:
                    kb.act(r_[:, :n], p_[:, :n], AF.Sqrt, R=[Tp_], W=[Tr_], bias=1e-6)
                kb.op("dve", lambda e, r_=r_: e.reciprocal(out=r_[:, :n], in_=r_[:, :n]), [Tr_], [Tr_])
                kb.tt(y[:, cc, :n], y[:, cc, :n], r_[:, :n], ALU.mult, R=[Ty, Tr_], W=[Ty])
            kb.dma("sp", qn[:, :, tg:tg + n], y[:, :, :n], R=[Ty])
            subs = subtiles(n)
            for si, (r0, nr) in enumerate(subs):
                p_, Tp_ = pst[si % 2], Tpst[si % 2]
                for cc in range(8):
                    kb.tr(p_[:nr, cc * 128:(cc + 1) * 128], y[:, 4 + cc, r0:r0 + nr], c.ident_f[:, :], R=[Ty, c.Tconst], W=[Tp_])
                kb.copy(kvt[:nr, si, :], p_[:nr, :], R=[Tp_], W=[Tkvt], eng=("act" if si % 2 == 0 else "dve"))
                kb.dma("sp", c.kvtok_d[tg + r0:tg + r0 + nr, :], kvt[:nr, si, :], R=[Tkvt])
        kb.barrier()


def phase_gdn_b(kb, c):
    nc, cfg = c.nc, c.cfg
    SEQ, NP, NS = cfg["SEQ"], cfg["NP"], cfg["NS"]
    TP, TS = c.TP, c.TS
    C = 64
    HB = 4
    with contextlib.ExitStack() as ps:
        sb = lambda n, s, d=F32: ps.enter_context(nc.sbuf_tensor("gb_" + n, s, d))
        gc = sb("const", [64, 5, 64])
        nA = sb("nA", [64, 8])
        dtb = sb("dtb", [64, 8])
        normw = sb("normw", [64, 64])
        valid = sb("valid", [64, 1])
        qTb = [sb("qTb%d" % i, [64, 8, HB * C]) for i in range(2)]
        kTb = [sb("kTb%d" % i, [64, 8, HB * C]) for i in range(2)]
        kvb = [sb("kvb%d" % i, [64, HB, 1024]) for i in range(2)]
        zab = [sb("zab%d" % i, [64, HB, 528]) for i in range(2)]
        gt = sb("gt", [64, HB, 8])
        bt = sb("bt", [64, HB, 8])
        nbt = sb("nbt", [64, HB, 8])
        gtmp = sb("gtmp", [64, HB, 8])
        nwz = sb("nwz", [64, HB, 512])
        S = sb("S", [64, 8, 64])
        og = [sb("og%d" % i, [64, HB, 512], BF16) for i in range(2)]
        w3 = lambda n: sb(n, [64, 8, 64])
        GG, EG, E2, bg = sb("GG", [64, 16]), sb("EG", [64, 16]), sb("E2", [64, 8]), sb("bg", [64, 8])
        gmat, tmp1, tmpT, Dt, Dn, EGr = w3("gmat"), w3("tmp1"), w3("tmpT"), w3("Dt"), w3("Dn"), w3("EGr")
        Xb = [w3("X0"), w3("X1")]
        Yb = [w3("Y0"), w3("Y1")]
        Qb = [w3("Q0"), w3("Q1")]
        vb, kbg, u, wT, aqkT, qd, kd, vn, oc, sqo = (w3(n) for n in ("vb", "kbg", "u", "wT", "aqkT", "qd", "kd", "vn", "oc", "sqo"))
        ss, lnv = sb("ss", [64, 8]), sb("lnv", [64, 8])
        banks = [ps.enter_context(nc.psum_tensor("gb_ps%d" % i, [128, 512], F32)) for i in range(8)]
        Tb = [Tok() for _ in range(8)]
        bstate = {"i": 0}

        def bank():
            i = bstate["i"] % 8
            bstate["i"] += 1
            return banks[i], Tb[i]

        names = ("gc nA dtb normw valid gt bt nwz S GG EG E2 bg gmat tmp1 tmpT Dt Dn EGr vb kbg u wT aqkT qd kd vn oc sqo ss lnv").split()
        T = {n: Tok(n) for n in names}
        TX, TY, TQ = [Tok(), Tok()], [Tok(), Tok()], [Tok(), Tok()]
        Tq, Tk, Tkv, Tz, Tog = [Tok(), Tok()], [Tok(), Tok()], [Tok(), Tok()], [Tok(), Tok()], [Tok(), Tok()]
        U_, ones_, mup, mlo, idn = (gc[:, i, :] for i in range(5))
        kb.dma("sp", gc[:], c.gconst_d, W=[T["gc"]])
        kb.dma("sp", nA[:], c.a_log.partition_broadcast(64), W=[T["nA"]])
        kb.dma("sp", dtb[:], c.dt_bias.partition_broadcast(64), W=[T["dtb"]])
        kb.dma("sp", normw[:], c.norm_w.partition_broadcast(64), W=[T["normw"]])
        kb.dma("sp", valid[:], c.valid_d, W=[T["valid"]])
        kb.act(nA[:], nA[:], AF.Exp, R=[T["nA"]], W=[T["nA"]])
        kb.ts(nA[:], nA[:], -1.0, None, ALU.mult, R=[T["nA"]], W=[T["nA"]])
        bc_h = lambda ap: ap.unsqueeze(2).broadcast_to([64, 8, 64])
        bc_m = lambda ap: ap.unsqueeze(1).broadcast_to([64, 8, 64])
        v3 = lambda ap: ap.rearrange("p (h d) -> p h d", h=8)
        qn_q = c.qkvn_d[0:512, :].rearrange("(h d) t -> d h t", d=64)
        qn_k = c.qkvn_d[512:1024, :].rearrange("(h d) t -> d h t", d=64)

        def mm8(lhs, rhs, pair2=None):
            p_, Tp_ = bank()
            return p_, Tp_

        def gates(slot, nch, sample):
            z = zab[slot]
            Tz_ = Tz[slot]
            a_ = z[:, :nch, 512:520]
            b_ = z[:, :nch, 520:528]
            dtb_b = dtb[:].unsqueeze(1).broadcast_to([64, nch, 8])
            nA_b = nA[:].unsqueeze(1).broadcast_to([64, nch, 8])
            kb.tt(gtmp[:, :nch, :], a_, dtb_b, ALU.add, R=[Tz_, T["dtb"]], W=[T["gt"]])
            kb.act(gtmp[:, :nch, :], gtmp[:, :nch, :], AF.Exp, R=[T["gt"]], W=[T["gt"]])
            kb.act(gtmp[:, :nch, :], gtmp[:, :nch, :], AF.Ln, R=[T["gt"]], W=[T["gt"]], bias=1.0)
            kb.tt(gt[:, :nch, :], gtmp[:, :nch, :], nA_b, ALU.mult, R=[T["gt"], T["nA"]], W=[T["gt"]])
            kb.act(bt[:, :nch, :], b_, AF.Exp, R=[Tz_], W=[T["bt"]], scale=-1.0)
            kb.ts(bt[:, :nch, :], bt[:, :nch, :], 1.0, None, ALU.add, R=[T["bt"]], W=[T["bt"]])
            kb.op("dve", lambda e: e.reciprocal(out=bt[:, :nch, :], in_=bt[:, :nch, :]), [T["bt"]], [T["bt"]])
            if sample:
                kb.ts(gt[:, :nch, :], gt[:, :nch, :], valid[:, 0:1], None, ALU.mult, R=[T["gt"], T["valid"]], W=[T["gt"]])
                kb.ts(bt[:, :nch, :], bt[:, :nch, :], valid[:, 0:1], None, ALU.mult, R=[T["bt"], T["valid"]], W=[T["bt"]])
            kb.ts(nbt[:, :nch, :], bt[:, :nch, :], -1.0, None, ALU.mult, R=[T["bt"]], W=[T["bt"]])
            zz = z[:, :nch, 0:512]
            kb.act(nwz[:, :nch, :], zz, AF.Exp, R=[Tz_], W=[T["nwz"]], scale=-1.0)
            kb.ts(nwz[:, :nch, :], nwz[:, :nch, :], 1.0, None, ALU.add, R=[T["nwz"]], W=[T["nwz"]], eng="pool")
            kb.op("dve", lambda e: e.reciprocal(out=nwz[:, :nch, :], in_=nwz[:, :nch, :]), [T["nwz"]], [T["nwz"]])
            kb.tt(nwz[:, :nch, :], nwz[:, :nch, :], zz, ALU.mult, R=[T["nwz"], Tz_], W=[T["nwz"]], eng="pool")
            for ci in range(nch):
                kb.tt(v3(nwz[:, ci, :]), v3(nwz[:, ci, :]), bc_m(normw[:, :]), ALU.mult, R=[T["nwz"], T["normw"]], W=[T["nwz"]], eng="pool")

        def chunk(slot, ci):
            tb = ci * C
            qT = qTb[slot][:, :, tb:tb + C]
            kT = kTb[slot][:, :, tb:tb + C]
            ktok = v3(kvb[slot][:, ci, 0:512])
            vtok = v3(kvb[slot][:, ci, 512:1024])
            g = gt[:, ci, :]
            beta = bt[:, ci, :]
            nbeta = nbt[:, ci, :]
            Rq, Rk, Rkv = Tq[slot], Tk[slot], Tkv[slot]
            pa, Tpa = bank()
            kb.mm(pa[:64, 0:8], U_, g, True, True, R=[T["gc"], T["gt"]], W=[Tpa], sig=False)
            kb.mm(pa[:64, 8:16], ones_, g, True, True, R=[T["gc"], T["gt"]], W=[Tpa])
            kb.copy(GG[:, :], pa[:64, 0:16], R=[Tpa], W=[T["GG"]])
            kb.copy(gmat[:], bc_h(g), R=[T["gt"]], W=[T["gmat"]], eng="pool")
            pgr, Tpgr = bank()
            for h in range(8):
                kb.mm(pgr[:64, h * 64:(h + 1) * 64], gmat[:, h, :], U_, True, True, R=[T["gmat"], T["gc"]], W=[Tpgr], sig=(h == 7))
            kb.act(EG[:, :], GG[:, :], AF.Exp, R=[T["GG"]], W=[T["EG"]])
            kb.tt(E2[:, :], GG[:, 8:16], GG[:, 0:8], ALU.subtract, R=[T["GG"]], W=[T["E2"]])
            kb.act(E2[:, :], E2[:, :], AF.Exp, R=[T["E2"]], W=[T["E2"]])
            kb.tt(tmp1[:], v3(pgr[:64, :]), bc_h(GG[:, 0:8]), ALU.subtract, R=[Tpgr, T["GG"]], W=[T["tmp1"]])
            kb.tt(tmpT[:], tmp1[:], bc_m(mup), ALU.add, R=[T["tmp1"], T["gc"]], W=[T["tmpT"]], eng="pool")
            kb.act(Dt[:], tmpT[:], AF.Exp, R=[T["tmpT"]], W=[T["Dt"]])
            kb.tt(tmpT[:], tmp1[:], bc_m(mlo), ALU.add, R=[T["tmp1"], T["gc"], T["Dt"]], W=[T["tmpT"]], eng="pool")
            kb.act(Dn[:], tmpT[:], AF.Exp, R=[T["tmpT"]], W=[T["Dn"]], scale=-1.0)
            kb.act(EGr[:], v3(pgr[:64, :]), AF.Exp, R=[Tpgr], W=[T["EGr"]])
            pkk, Tpkk = bank()
            for h in range(8):
                kb.mm(pkk[:64, h * 64:(h + 1) * 64], kT[:, h, :], kT[:, h, :], True, True, R=[Rk], W=[Tpkk], sig=(h == 7))
            X, Y, Q = Xb[0], Yb[0], Qb[0]
            kb.tt(X[:], v3(pkk[:64, :]), Dn[:], ALU.mult, R=[Tpkk, T["Dn"]], W=[TX[0]])
            kb.tt(X[:], X[:], bc_h(nbeta), ALU.mult, R=[TX[0], T["bt"]], W=[TX[0]])
            pt_, Tpt = bank()
            for h in range(8):
                kb.tr(pt_[:64, h * 64:(h + 1) * 64], X[:, h, :], idn, R=[TX[0], T["gc"]], W=[Tpt], sig=(h == 7))
            kb.copy(Y[:], v3(pt_[:64, :]), R=[Tpt], W=[TY[0]], eng="act")
            kb.tt(Q[:], Y[:], bc_m(idn), ALU.add, R=[TY[0], T["gc"]], W=[TQ[0]])
            cur = 0
            for k in range(1, 6):
                nx = 1 - cur
                pX, TpX = bank()
                for h in range(8):
                    kb.mm(pX[:64, h * 64:(h + 1) * 64], Yb[cur][:, h, :], Xb[cur][:, h, :], True, True, R=[TX[cur], TY[cur]], W=[TpX], sig=(h == 7))
                if k < 5:
                    pY, TpY = bank()
                    for h in range(8):
                        kb.mm(pY[:64, h * 64:(h + 1) * 64], Xb[cur][:, h, :], Yb[cur][:, h, :], True, True, R=[TX[cur], TY[cur]], W=[TpY], sig=(h == 7))
                kb.copy(Xb[nx][:], v3(pX[:64, :]), R=[TpX], W=[TX[nx]], eng="act")
                if k < 5:
                    kb.copy(Yb[nx][:], v3(pY[:64, :]), R=[TpY], W=[TY[nx]], eng="act")
                pQ, TpQ = bank()
                for h in range(8):
                    kb.mm(pQ[:64, h * 64:(h + 1) * 64], Xb[nx][:, h, :], Qb[cur][:, h, :], True, True, R=[TX[nx], TQ[cur]], W=[TpQ], sig=(h == 7))
                kb.tt(Qb[nx][:], Qb[cur][:], v3(pQ[:64, :]), ALU.add, R=[TQ[cur], TpQ], W=[TQ[nx]])
                cur = nx
            Q, TQc = Qb[cur], TQ[cur]
            kb.tt(vb[:], vtok, bc_h(beta), ALU.mult, R=[Rkv, T["bt"]], W=[T["vb"]])
            kb.tt(bg[:, :], beta, EG[:, 0:8], ALU.mult, R=[T["bt"], T["EG"]], W=[T["bg"]])
            kb.tt(kbg[:], ktok, bc_h(bg[:, :]), ALU.mult, R=[Rkv, T["bg"]], W=[T["kbg"]], eng="pool")
            pu, Tpu = bank()
            for h in range(8):
                kb.mm(pu[:64, h * 64:(h + 1) * 64], Q[:, h, :], vb[:, h, :], True, True, R=[TQc, T["vb"]], W=[Tpu], sig=(h == 7))
            kb.copy(u[:], v3(pu[:64, :]), R=[Tpu], W=[T["u"]], eng="act")
            pw, Tpw = bank()
            for h in range(8):
                kb.mm(pw[:64, h * 64:(h + 1) * 64], kbg[:, h, :], Q[:, h, :], True, True, R=[TQc, T["kbg"]], W=[Tpw], sig=(h == 7))
            kb.copy(wT[:], v3(pw[:64, :]), R=[Tpw], W=[T["wT"]], eng="act")
            pq, Tpq = bank()
            for h in range(8):
                kb.mm(pq[:64, h * 64:(h + 1) * 64], kT[:, h, :], qT[:, h, :], True, True, R=[Rk, Rq], W=[Tpq], sig=(h == 7))
            kb.tt(aqkT[:], v3(pq[:64, :]), Dt[:], ALU.mult, R=[Tpq, T["Dt"]], W=[T["aqkT"]])
            kb.tt(qd[:], qT, EGr[:], ALU.mult, R=[Rq, T["EGr"]], W=[T["qd"]], eng="pool")
            kb.tt(kd[:], ktok, bc_h(E2[:, :]), ALU.mult, R=[Rkv, T["E2"]], W=[T["kd"]], eng="pool")
            pws, Tpws = bank()
            for h in range(8):
                kb.mm(pws[:64, h * 64:(h + 1) * 64], wT[:, h, :], S[:, h, :], True, True, R=[T["wT"], T["S"]], W=[Tpws], sig=(h == 7))
            kb.tt(vn[:], u[:], v3(pws[:64, :]), ALU.subtract, R=[T["u"], Tpws], W=[T["vn"]])
            po, Tpo = bank()
            for h in range(8):
                kb.mm(po[:64, h * 64:(h + 1) * 64], qd[:, h, :], S[:, h, :], True, False, R=[T["qd"], T["S"]], W=[Tpo], sig=False)
                kb.mm(po[:64, h * 64:(h + 1) * 64], aqkT[:, h, :], vn[:, h, :], False, True, R=[T["aqkT"], T["vn"]], W=[Tpo], sig=(h == 7))
            pds, Tpds = bank()
            for h in range(8):
                kb.mm(pds[:64, h * 64:(h + 1) * 64], kd[:, h, :], vn[:, h, :], True, True, R=[T["kd"], T["vn"]], W=[Tpds], sig=(h == 7))
            kb.tt(S[:], S[:], bc_h(EG[:, 8:16]), ALU.mult, R=[T["S"], T["EG"]], W=[T["S"]])
            kb.tt(S[:], S[:], v3(pds[:64, :]), ALU.add, R=[T["S"], Tpds], W=[T["S"]])
            kb.copy(oc[:], v3(po[:64, :]), R=[Tpo], W=[T["oc"]], eng="act")
            kb.tt(sqo[:], oc[:], oc[:], ALU.mult, R=[T["oc"]], W=[T["sqo"]], eng="pool")
            kb.op("dve", lambda e: e.tensor_reduce(out=ss[:, :], in_=sqo[:], axis=AX.X, op=ALU.add), [T["sqo"]], [T["ss"]])
            kb.act(lnv[:, :], ss[:, :], AF.Ln, R=[T["ss"]], W=[T["lnv"]], scale=1.0 / 64.0, bias=1e-6)
            kb.act(lnv[:, :], lnv[:, :], AF.Exp, R=[T["lnv"]], W=[T["lnv"]], scale=-0.5)
            kb.tt(oc[:], oc[:], bc_h(lnv[:, :]), ALU.mult, R=[T["oc"], T["lnv"]], W=[T["oc"]])
            kb.tt(v3(og[slot][:, ci, :]), oc[:], v3(nwz[:, ci, :]), ALU.mult, R=[T["oc"], T["nwz"]], W=[Tog[slot]])

        slot = 0
        for s in range(NP):
            kb.memset(S[:], 0.0, W=[T["S"]])
            for hb in range(SEQ // (HB * C)):
                tg = s * SEQ + hb * HB * C
                n = HB * C
                kb.dma("sp", qTb[slot][:, :, :], qn_q[:, :, tg:tg + n], W=[Tq[slot]])
                kb.dma("sp", kTb[slot][:, :, :], qn_k[:, :, tg:tg + n], W=[Tk[slot]])
                kb.dma("sp", kvb[slot][:, :, :], c.kvtok_d[tg:tg + n, :].rearrange("(c p) f -> p c f", p=C), W=[Tkv[slot]])
                kb.dma("sp", zab[slot][:, :, :], c.zab_d[tg:tg + n, :].rearrange("(c p) f -> p c f", p=C), W=[Tz[slot]])
                gates(slot, HB, False)
                for ci in range(HB):
                    chunk(slot, ci)
                kb.dma("sp", c.og_d[tg:tg + n, :].rearrange("(c p) f -> p c f", p=C), og[slot][:, :, :], R=[Tog[slot]])
                slot = 1 - slot
            kb.dma("sp", c.p_gdn[s].rearrange("h k v -> k h v"), S[:], R=[T["S"]])
        for sl in range(2):
            kb.memset(qTb[sl][:], 0.0, W=[Tq[sl]])
            kb.memset(kTb[sl][:], 0.0, W=[Tk[sl]])
            kb.memset(kvb[sl][:], 0.0, W=[Tkv[sl]])
            kb.memset(zab[sl][:], 0.0, W=[Tz[sl]])
        for b in range(NS):
            tg = TP + b * LS
            kb.dma("sp", S[:], c.state_gdn[b].rearrange("h k v -> k h v"), W=[T["S"]])
            kb.dma("sp", qTb[slot][:, :, 0:LS], qn_q[:, :, tg:tg + LS], W=[Tq[slot]])
            kb.dma("sp", kTb[slot][:, :, 0:LS], qn_k[:, :, tg:tg + LS], W=[Tk[slot]])
            kb.dma("sp", kvb[slot][0:LS, 0, :], c.kvtok_d[tg:tg + LS, :], W=[Tkv[slot]])
            kb.dma("sp", zab[slot][0:LS, 0, :], c.zab_d[tg:tg + LS, :], W=[Tz[slot]])
            gates(slot, 1, True)
            chunk(slot, 0)
            kb.dma("sp", c.og_d[tg:tg + LS, :], og[slot][0:LS, 0, :], R=[Tog[slot]])
            kb.dma("sp", c.s_gdn[b].rearrange("h k v -> k h v"), S[:], R=[T["S"]])
            slot = 1 - slot
        kb.barrier()


def phase_mix(kb, c):
    nc, cfg = c.nc, c.cfg
    G = 512
    TP, TS = c.TP, c.TS
    with contextlib.ExitStack() as ps:
        sb = lambda n, s, d=F32: ps.enter_context(nc.sbuf_tensor("mx_" + n, s, d))
        wbg = sb("wbg", [128, 4, D], BF16)
        wbn = sb("wbn", [128, 4, D], BF16)
        wout = sb("wout", [128, 8, D], BF16)
        ogT = sb("ogT", [128, 4, G], BF16)
        onT = sb("onT", [128, 4, G], BF16)
        mg = sb("mg", [128, 16, G], BF16)
        mT = sb("mT", [128, 8, G], BF16)
        tk = [sb("tk%d" % i, [128, 1024], BF16) for i in range(2)]
        t1 = [sb("t1_%d" % i, [128, G]) for i in range(2)]
        t2 = [sb("t2_%d" % i, [128, G]) for i in range(2)]
        xin = [sb("xin%d" % i, [128, D]) for i in range(2)]
        wk = [sb("wk%d" % i, [128, D]) for i in range(2)]
        rows = {n: sb("row_" + n, [128, D]) for n in ("g", "lng", "lnb")}
        st = sb("st", [128, 2, nc.vector.BN_STATS_DIM])
        mv = sb("mv", [128, nc.vector.BN_AGGR_DIM])
        rstd = sb("rstd", [128, 1])
        pstr = [ps.enter_context(nc.psum_tensor("mx_pstr%d" % i, [128, 1024], BF16)) for i in range(2)]
        psb = [ps.enter_context(nc.psum_tensor("mx_psb%d" % i, [128, 512], F32)) for i in range(4)]
        psy = ps.enter_context(nc.psum_tensor("mx_psy", [128, 1024], F32))
        Tw, TogT, TonT, Tmg, TmT, Tln, Tmodrows, Tst, Tpsy = (Tok() for _ in range(9))
        Ttk, Tt1, Tt2, Txin, Twk, Tpstr = ([Tok(), Tok()] for _ in range(6))
        Tpsb = [Tok() for _ in range(4)]
        kb.dma("pool", wbg[:], c.w_br_gdn.rearrange("(k p) d -> p k d", p=128), W=[Tw])
        kb.dma("pool", wbn[:], c.w_br_nsa.rearrange("(k p) d -> p k d", p=128), W=[Tw])
        kb.dma("pool", wout[:], c.w_out.rearrange("(k p) d -> p k d", p=128), W=[Tw])
        kb.dma("sp", rows["lng"][:], c.ln_g[1:2, :].partition_broadcast(128), W=[Tln])
        kb.dma("sp", rows["lnb"][:], c.ln_b[1:2, :].partition_broadcast(128), W=[Tln])
        mgv = c.mgT_d.rearrange("(c p) t -> p c t", p=128)
        cur = None
        slot = 0
        for (kind, seq, t0, n) in token_groups(cfg, G):
            tg = t0 if kind == "p" else TP + t0
            if (kind, seq) != cur:
                cur = (kind, seq)
                load_rows(kb, c, rows["g"], Tmodrows, kind, seq, 5 * D)
            subs = subtiles(n)
            kb.dma("sp", mg[:, :, :n], mgv[:, :, tg:tg + n], W=[Tmg])
            for si, (r0, nr) in enumerate(subs):
                for which, (src, dstT, Td) in enumerate(((c.og_d, ogT, TogT), (c.on_d, onT, TonT))):
                    i2 = (2 * si + which) % 2
                    kb.dma("sp", tk[i2][:nr, 0:512], src[tg + r0:tg + r0 + nr, :], W=[Ttk[i2]])
                    for k in range(4):
                        kb.tr(pstr[i2][:, k * 128:k * 128 + nr], tk[i2][:nr, k * 128:(k + 1) * 128], c.ident_b[:nr, :nr],
                              R=[Ttk[i2], c.Tconst], W=[Tpstr[i2]])
                    kb.copy(dstT[:, :, r0:r0 + nr], pstr[i2][:, 0:512].rearrange("p (k n) -> p k n", k=4)[:, :, :nr],
                            R=[Tpstr[i2]], W=[Td], eng=("act" if which == 0 else "dve"))
            for dc in range(8):
                pg, Tpg = psb[(dc % 2) * 2], Tpsb[(dc % 2) * 2]
                pn, Tpn = psb[(dc % 2) * 2 + 1], Tpsb[(dc % 2) * 2 + 1]
                for k in range(4):
                    kb.mm(pg[:, :n], wbg[:, k, dc * 128:(dc + 1) * 128], ogT[:, k, :n], k == 0, k == 3, R=[Tw, TogT], W=[Tpg])
                for k in range(4):
                    kb.mm(pn[:, :n], wbn[:, k, dc * 128:(dc + 1) * 128], onT[:, k, :n], k == 0, k == 3, R=[Tw, TonT], W=[Tpn])
                a, Ta = t1[dc % 2], Tt1[dc % 2]
                b, Tb_ = t2[dc % 2], Tt2[dc % 2]
                kb.tt(a[:, :n], pg[:, :n], mg[:, dc, :n], ALU.mult, R=[Tpg, Tmg], W=[Ta])
                kb.tt(b[:, :n], pn[:, :n], mg[:, 8 + dc, :n], ALU.mult, R=[Tpn, Tmg], W=[Tb_])
                kb.tt(mT[:, dc, :n], a[:, :n], b[:, :n], ALU.add, R=[Ta, Tb_], W=[TmT], eng="pool")
            for si, (r0, nr) in enumerate(subs):
                xt, Tx = xin[slot], Txin[slot]
                w, Tw_ = wk[slot], Twk[slot]
                slot = 1 - slot
                kb.dma("sp", xt[:nr, :], c.x1[kind][t0 + r0:t0 + r0 + nr, :], W=[Tx])
                for half in range(2):
                    for k in range(8):
                        kb.mm(psy[:nr, half * 512:(half + 1) * 512], mT[:, k, r0:r0 + nr], wout[:, k, half * 512:(half + 1) * 512],
                              k == 0, k == 7, R=[TmT, Tw], W=[Tpsy])
                kb.tt(w[:nr, :], psy[:nr, :], rows["g"][:nr, :], ALU.mult, R=[Tpsy, Tmodrows], W=[Tw_])
                kb.stt(w[:nr, :], xt[:nr, :], ALPHA, w[:nr, :], ALU.mult, ALU.add, R=[Tx, Tw_], W=[Tw_])
                layer_norm_rows(kb, c, w, nr, Tw_, rows["lng"], rows["lnb"], Tln, st, mv, rstd, Tst)
                kb.dma("sp", c.x2[kind][t0 + r0:t0 + r0 + nr, :], w[:nr, :], R=[Tw_])
        kb.barrier()


NEGM = -30000.0


def nsa_consts(SEQ):
    t = np.arange(SEQ)
    nsel = SEQ // 64
    j = np.arange(nsel)
    valid = (j[None, :] * 64) <= t[:, None]
    cur = (t // 64)[:, None]
    forced = (j[None, :] == 0) | (j[None, :] == cur) | (j[None, :] == cur - 1)
    M1 = (valid & ~forced).astype(np.float32)
    M2 = np.where(valid, np.where(forced, 1e4 + j[None, :], 0.0), -1.0).astype(np.float32)
    ncmp = SEQ // 32
    n = np.arange(ncmp)
    cmask = np.where(((n[:, None] + 1) * 32 - 1) <= t[None, :], 0.0, NEGM).astype(np.float32)
    F = (np.arange(SEQ)[None, :] // 64 == j[:, None]).astype(np.float32)
    sk = np.arange(128)[:, None, None]
    a = np.arange(8)[None, :, None]
    tq = np.arange(512)[None, None, :]
    diff = tq - 128 * (a - 4) - sk
    Wm = np.where((diff >= 0) & (diff < 512), 0.0, NEGM).astype(np.float32)
    pair = (n[:, None] // 2 == j[None, :]).astype(np.float32)
    r = np.arange(128)
    maskW = np.zeros((128, 124), np.float32)
    maskW[r, 60 + r // 32] = 1.0
    return dict(nsa_M1=M1, nsa_M2=M2, nsa_cmask=cmask, nsa_F=F, nsa_Wm=Wm, nsa_pair=pair, nsa_maskW=maskW)


def phase_nsa_prompt(kb, c):
    nc, cfg = c.nc, c.cfg
    SEQ, NP = cfg["SEQ"], cfg["NP"]
    NKT = SEQ // 128
    NQG = SEQ // 512
    NSEL = SEQ // 64
    NCMP = SEQ // 32
    with contextlib.ExitStack() as ps:
        sb = lambda n, s, d=F32: ps.enter_context(nc.sbuf_tensor("np_" + n, s, d))
        qTh = sb("qTh", [64, 8, SEQ], BF16)
        KT = sb("KT", [64, 3, 2, SEQ], BF16)
        Vx = sb("Vx", [128, NKT, 3, 2, 65], BF16)
        kvc = sb("kvc", [128, NKT, 256], BF16)
        gn = sb("gn", [128, NKT, 24])
        M1 = sb("M1", [128, NKT, NSEL])
        M2 = sb("M2", [128, NKT, NSEL])
        cmask = sb("cmask", [NCMP, SEQ], BF16)
        Fm = sb("F", [NSEL, SEQ], BF16)
        Wm = sb("Wm", [128, 8, 512], BF16)
        pair = sb("pair", [NCMP, NSEL], BF16)
        maskW = sb("maskW", [128, 124])
        wcol = sb("wcol", [128, 2, 2])
        Wbig = sb("Wbig", [128, 4, 124], BF16)
        ON = sb("ON", [128, NKT, 512])
        imp = sb("imp", [128, NKT, 2, NSEL])
        selT = sb("selT", [NSEL, 2, SEQ], BF16)
        kvb_sb = sb("kvb", [NCMP, 256], BF16)
        KcT = sb("KcT", [64, 2, NCMP], BF16)
        Vc1P = sb("Vc1P", [NCMP, 2, 65 + NSEL], BF16)
        Pt = [sb("Pt%d" % i, [128, 512], BF16) for i in range(3)]
        sc = sb("sc", [128, NSEL])
        wk_ = sb("wk", [128, NSEL])
        m8a, m8b = sb("m8a", [128, 8]), sb("m8b", [128, 8])
        thr = sb("thr", [128, 1])
        selm = sb("selm", [128, NSEL])
        okm = sb("okm", [128, NSEL])
        rr = [sb("rr%d" % i, [128, 1]) for i in range(4)]
        rg = [sb("rg%d" % i, [128, 1]) for i in range(4)]
        zb = sb("zb", [128, 512], BF16)
        pss = [ps.enter_context(nc.psum_tensor("np_pss%d" % i, [128, 512], F32)) for i in range(3)]
        pacc = [ps.enter_context(nc.psum_tensor("np_pacc%d" % i, [128, 512], F32)) for i in range(3)]
        pmisc = ps.enter_context(nc.psum_tensor("np_pm", [128, 512], F32))
        pmb = ps.enter_context(nc.psum_tensor("np_pmb", [128, 1024], BF16))
        names = "q K V kvc gn M cm F Wm pair maskW wcol Wbig ON imp selT kvb KcT Vc1P sc m8 sel pm pmb zb".split()
        T = {n: Tok(n) for n in names}
        kb.memset(zb[:], 0.0, W=[T["zb"]])
        TPt, Tpss, Tpacc = [Tok() for _ in range(3)], [Tok() for _ in range(3)], [Tok() for _ in range(3)]
        Trr = [Tok() for _ in range(4)]
        kb.dma("sp", M1[:], c.nsa_M1.rearrange("(t p) j -> p t j", p=128), W=[T["M"]])
        kb.dma("sp", M2[:], c.nsa_M2.rearrange("(t p) j -> p t j", p=128), W=[T["M"]])
        kb.dma("pool", cmask[:], c.nsa_cmask, W=[T["cm"]])
        kb.dma("pool", Fm[:], c.nsa_F, W=[T["F"]])
        kb.dma("pool", Wm[:], c.nsa_Wm, W=[T["Wm"]])
        kb.dma("pool", pair[:], c.nsa_pair, W=[T["pair"]])
        kb.dma("sp", maskW[:], c.nsa_maskW, W=[T["maskW"]])
        wsrc = c.w_cmp.rearrange("s j h -> j s h")
        for q4 in range(4):
            kb.dma("sp", wcol[32 * q4:32 * q4 + 32, :, :], wsrc, W=[T["wcol"]])
        for s_ in range(2):
            for hk in range(2):
                kb.ts(Wbig[:, s_ * 2 + hk, :], maskW[:, :], wcol[:, s_, hk:hk + 1], None, ALU.mult,
                      R=[T["maskW"], T["wcol"]], W=[T["Wbig"]])
        kb.memset(Vx[:, :, :, :, 64:65], 1.0, W=[T["V"]])
        kb.memset(Vc1P[:, :, 64:65], 1.0, W=[T["Vc1P"]])
        for hk in range(2):
            kb.copy(Vc1P[:, hk, 65:65 + NSEL], pair[:, :], R=[T["pair"]], W=[T["Vc1P"]])
        st = {"ps": 0, "pt": 0, "rr": 0}

        def score_bank():
            i = st["ps"] % 3
            st["ps"] += 1
            return pss[i], Tpss[i]

        def acc_ap(a):
            return pacc[a // 7][:, (a % 7) * 65:(a % 7) * 65 + 65], Tpacc[a // 7]

        def finish_acc(ap_, Tp_, qt, head, br, first, with_imp=None):
            i = st["rr"] % 4
            st["rr"] += 1
            r_, g_, Tr_ = rr[i], rg[i], Trr[i]
            kb.ts(r_[:, :], ap_[:, 64:65], 1e-30, None, ALU.max, R=[Tp_], W=[Tr_])
            kb.op("dve", lambda e: e.reciprocal(out=r_[:, :], in_=r_[:, :]), [Tr_], [Tr_])
            kb.tt(g_[:, :], r_[:, :], gn[:, qt, br * 8 + head:br * 8 + head + 1], ALU.mult, R=[Tr_, T["gn"]], W=[Tr_])
            o_ = ON[:, qt, head * 64:(head + 1) * 64]
            if first:
                kb.ts(o_, ap_[:, 0:64], g_[:, 0:1], None, ALU.mult, R=[Tp_, Tr_], W=[T["ON"]])
            else:
                kb.stt(o_, ap_[:, 0:64], g_[:, 0:1], o_, ALU.mult, ALU.add, R=[Tp_, Tr_, T["ON"]], W=[T["ON"]])
            return r_, Tr_

        for s in range(NP):
            tg = s * SEQ
            kb.dma("sp", qTh[:, :, :], c.qT_d.rearrange("(h d) t -> d h t", d=64)[:, :, tg:tg + SEQ], W=[T["q"]])
            for i3 in range(3):
                kb.dma("sp", KT[:, i3, :, :], c.kT_d[i3].rearrange("(h d) t -> d h t", d=64)[:, :, tg:tg + SEQ], W=[T["K"]])
            rowsv = c.kvg_d[tg:tg + SEQ, :].rearrange("(t p) f -> p t f", p=128)
            for br in range(3):
                for hk in range(2):
                    c0 = br * 256 + 128 + hk * 64
                    kb.dma("pool", Vx[:, :, br, hk, 0:64], rowsv[:, :, c0:c0 + 64], W=[T["V"]])
            kb.dma("pool", kvc[:, :, :], rowsv[:, :, 0:256], W=[T["kvc"]])
            kb.dma("sp", gn[:, :, :], rowsv[:, :, 768:792], W=[T["gn"]])
            kb.act(gn[:, :, :], gn[:, :, :], AF.Exp, R=[T["gn"]], W=[T["gn"]], scale=-1.0)
            kb.ts(gn[:, :, :], gn[:, :, :], 1.0, None, ALU.add, R=[T["gn"]], W=[T["gn"]])
            kb.op("dve", lambda e: e.reciprocal(out=gn[:, :, :], in_=gn[:, :, :]), [T["gn"]], [T["gn"]])
            for cc in range(4):
                for kt in range(NKT):
                    kb.mm(pmisc[:NCMP, cc * 64:(cc + 1) * 64], Wbig[:, cc, 60 - 4 * kt:60 - 4 * kt + NCMP],
                          kvc[:, kt, cc * 64:(cc + 1) * 64], kt == 0, kt == NKT - 1, R=[T["Wbig"], T["kvc"]], W=[T["pm"]],
                          sig=(kt == NKT - 1 and cc == 3))
            kb.copy(kvb_sb[:, :], pmisc[:NCMP, 0:256], R=[T["pm"]], W=[T["kvb"]])
            for hk in range(2):
                kb.tr(pmb[:64, hk * NCMP:(hk + 1) * NCMP], kvb_sb[:, hk * 64:(hk + 1) * 64], c.ident_b[:NCMP, :NCMP],
                      R=[T["kvb"], c.Tconst], W=[T["pmb"]])
                kb.copy(Vc1P[:, hk, 0:64], kvb_sb[:, 128 + hk * 64:128 + (hk + 1) * 64], R=[T["kvb"]], W=[T["Vc1P"]], eng="pool")
            kb.copy(KcT[:, :, :], pmb[:64, 0:2 * NCMP].rearrange("p (h n) -> p h n", h=2), R=[T["pmb"]], W=[T["KcT"]])
            for qg in range(NQG):
                for hk in range(2):
                    for g in range(4):
                        head = hk * 4 + g
                        p_, Tp_ = score_bank()
                        kb.mm(p_[:NCMP, :], KcT[:, hk, :], qTh[:, head, qg * 512:(qg + 1) * 512], True, False,
                              R=[T["KcT"], T["q"]], W=[Tp_], sig=False)
                        kb.mm(p_[:NCMP, :], c.ident_b[:NCMP, :NCMP], cmask[:, qg * 512:(qg + 1) * 512], False, True,
                              R=[c.Tconst, T["cm"]], W=[Tp_])
                        pt_, Tpt_ = Pt[st["pt"] % 3], TPt[st["pt"] % 3]
                        st["pt"] += 1
                        kb.act(pt_[:NCMP, :], p_[:NCMP, :], AF.Exp, R=[Tp_], W=[Tpt_], scale=0.125)
                        for sub in range(4):
                            qt = qg * 4 + sub
                            kb.mm(pmisc[:, 256 + 0:256 + 65 + NSEL], pt_[:NCMP, sub * 128:(sub + 1) * 128], Vc1P[:, hk, :], True, True,
                                  R=[Tpt_, T["Vc1P"]], W=[T["pm"]])
                            ap_ = pmisc[:, 256:256 + 65 + NSEL]
                            r_, Tr_ = finish_acc(ap_, T["pm"], qt, head, 0, True)
                            if g == 0:
                                kb.ts(imp[:, qt, hk, :], ap_[:, 65:65 + NSEL], r_[:, 0:1], None, ALU.mult, R=[T["pm"], Tr_], W=[T["imp"]])
                            else:
                                kb.stt(imp[:, qt, hk, :], ap_[:, 65:65 + NSEL], r_[:, 0:1], imp[:, qt, hk, :], ALU.mult, ALU.add,
                                       R=[T["pm"], Tr_, T["imp"]], W=[T["imp"]])
            for qt in range(NKT):
                for hk in range(2):
                    kb.tt(sc[:, :], imp[:, qt, hk, :], M1[:, qt, :], ALU.mult, R=[T["imp"], T["M"]], W=[T["sc"]])
                    kb.tt(sc[:, :], sc[:, :], M2[:, qt, :], ALU.add, R=[T["sc"], T["M"]], W=[T["sc"]])
                    kb.op("dve", lambda e: e.max(out=m8a[:, :], in_=sc[:, :]), [T["sc"]], [T["m8"]])
                    kb.op("dve", lambda e: e.match_replace(out=wk_[:, :], in_to_replace=m8a[:, :], in_values=sc[:, :], imm_value=-2.0),
                          [T["sc"], T["m8"]], [T["sel"]])
                    kb.op("dve", lambda e: e.max(out=m8b[:, :], in_=wk_[:, :]), [T["sel"]], [T["m8"]])
                    kb.op("dve", lambda e: e.tensor_reduce(out=thr[:, :], in_=m8b[:, :], axis=AX.X, op=ALU.min), [T["m8"]], [T["m8"]])
                    kb.ts(selm[:, :], sc[:, :], thr[:, 0:1], None, ALU.is_ge, R=[T["sc"], T["m8"]], W=[T["sel"]])
                    kb.ts(okm[:, :], sc[:, :], 0.0, None, ALU.is_ge, R=[T["sc"]], W=[T["sel"]])
                    kb.tt(selm[:, :], selm[:, :], okm[:, :], ALU.mult, R=[T["sel"]], W=[T["sel"]])
                    kb.ts(selm[:, :], selm[:, :], -1.0, -NEGM, ALU.add, ALU.mult, R=[T["sel"]], W=[T["sel"]])
                    kb.tr(pmisc[:NSEL, 0:128], selm[:, :], c.ident_f[:, :], R=[T["sel"], c.Tconst], W=[T["pm"]])
                    kb.copy(selT[:, hk, qt * 128:(qt + 1) * 128], pmisc[:NSEL, 0:128], R=[T["pm"]], W=[T["selT"]], eng="act")
            for br in (1, 2):
                for hk in range(2):
                    for qg in range(NQG):
                        kts = list(range(0, 4 * qg + 4)) if br == 1 else list(range(max(0, 4 * qg - 4), 4 * qg + 4))
                        for b3 in range(3):
                            kb.mm(pacc[b3][:, :], zb[:, 0:128], zb[:, :], True, False, R=[T["zb"]], W=[Tpacc[b3]], sig=False)
                        for ki, kt in enumerate(kts):
                            for g in range(4):
                                head = hk * 4 + g
                                p_, Tp_ = score_bank()
                                diag = kt >= 4 * qg
                                need_w = diag or br == 2
                                kb.mm(p_[:, :], KT[:, br, hk, kt * 128:(kt + 1) * 128], qTh[:, head, qg * 512:(qg + 1) * 512],
                                      True, not (need_w or br == 1), R=[T["K"], T["q"]], W=[Tp_], sig=False)
                                if br == 1:
                                    kb.mm(p_[:, :], Fm[:, kt * 128:(kt + 1) * 128], selT[:, hk, qg * 512:(qg + 1) * 512],
                                          False, not need_w, R=[T["F"], T["selT"]], W=[Tp_], sig=(not need_w))
                                if need_w:
                                    kb.mm(p_[:, :], c.ident_b[:, :], Wm[:, kt - 4 * qg + 4, :], False, True,
                                          R=[c.Tconst, T["Wm"]], W=[Tp_])
                                pt_, Tpt_ = Pt[st["pt"] % 3], TPt[st["pt"] % 3]
                                st["pt"] += 1
                                kb.act(pt_[:, :], p_[:, :], AF.Exp, R=[Tp_], W=[Tpt_], scale=0.125)
                                for sub in range(4):
                                    ap_, Ta_ = acc_ap(g * 4 + sub)
                                    last = (ki == len(kts) - 1)
                                    kb.mm(ap_, pt_[:, sub * 128:(sub + 1) * 128], Vx[:, kt, br, hk, :], False, last,
                                          R=[Tpt_, T["V"]], W=[Ta_], sig=last)
                        for g in range(4):
                            for sub in range(4):
                                ap_, Ta_ = acc_ap(g * 4 + sub)
                                finish_acc(ap_, Ta_, qg * 4 + sub, hk * 4 + g, br, False)
            kb.dma("pool", c.on_d[tg:tg + SEQ, :].rearrange("(t p) f -> p t f", p=128), ON[:, :, :], R=[T["ON"]])
        kb.barrier()


def nsa_s_consts(NPAGES):
    nblk = NPAGES * 2
    ncmp = NPAGES * 4
    nsel = nblk + 1
    j = np.arange(nsel)
    forced = (j == 0) | (j == nblk) | (j == nblk - 1)
    M1 = np.tile((~forced).astype(np.float32)[None, :], (LS, 1))
    M2 = np.tile(np.where(forced, 1e4 + j, 0.0).astype(np.float32)[None, :], (LS, 1))
    nh = ncmp // 128
    nl = np.arange(128)
    pairs = np.zeros((128, nh, nblk), np.float32)
    for h in range(nh):
        pairs[nl, h, 64 * h + nl // 2] = 1.0
    Fs = (np.arange(NPAGES * 128)[None, :] // 64 == np.arange(nblk)[:, None]).astype(np.float32)
    col_l = np.tile(np.arange(LS), 8)[None, :]
    caus = np.where(np.arange(LS)[:, None] <= col_l, 0.0, NEGM).astype(np.float32)
    wm0 = np.where(np.arange(128)[:, None] <= col_l, NEGM, 0.0).astype(np.float32)
    gsum = (np.arange(16)[:, None] % LS == np.arange(LS)[None, :]).astype(np.float32)
    qcol = (np.arange(128) % 4).astype(np.float32).reshape(128, 1)
    pcol = np.arange(128, dtype=np.float32).reshape(128, 1)
    jrow = np.arange(32, dtype=np.float32).reshape(1, 32)
    return dict(ns_jrow=jrow, ns_M1=M1, ns_M2=M2, ns_pairs=pairs, ns_Fs=Fs, ns_caus=caus, ns_wm0=wm0, ns_gsum=gsum, ns_qcol=qcol, ns_pcol=pcol)


def phase_nsa_sample(kb, c):
    nc, cfg = c.nc, c.cfg
    NS, NPAGES, NPHYS, WL = cfg["NS"], cfg["NPAGES"], cfg["NPHYS"], cfg["WIN"]
    TP = c.TP
    NBLK = NPAGES * 2
    NCMP = NPAGES * 4
    NH = NCMP // 128
    NSELS = NBLK + 1
    assert NBLK <= 128 and NCMP % 128 == 0
    with contextlib.ExitStack() as ps:
        sb = lambda n, s, d=F32: ps.enter_context(nc.sbuf_tensor("nss_" + n, s, d))
        ptT = sb("ptT", [128, NS, NH], I32)
        ptb = sb("ptb", [128, NS * NPAGES], I32)
        idxq_f = sb("idxq_f", [128, NS, NH])
        idxr_f = sb("idxr_f", [128, NS * NPAGES])
        idxq = sb("idxq", [128, NS, NH], I32)
        idxr = sb("idxr", [128, NS * NPAGES], I32)
        qcol, pcol = sb("qcol", [128, 1]), sb("pcol", [128, 1])
        jrow = sb("jrow", [128, 32])
        idxq32_f = sb("idxq32_f", [128, NS * NH, 32])
        idxq32 = sb("idxq32", [128, NS * NH * 32], I32)
        wflat = sb("wflat", [128, 128])
        M1, M2 = sb("M1", [LS, NSELS]), sb("M2", [LS, NSELS])
        pairs = sb("pairs", [128, NH, NBLK], BF16)
        Fs = sb("Fs", [NBLK, NPAGES * 128], BF16)
        caus = sb("caus", [LS, 32], BF16)
        wm0 = sb("wm0", [128, 32], BF16)
        gsum = sb("gsum", [16, LS])
        zb = sb("zb", [128, 512], BF16)
        cq = [sb("cq%d" % i, [128, 32, 256]) for i in range(2)]
        kvbs = sb("kvbs", [128, NH, 256])
        kvbK = sb("kvbK", [128, NH, 128], BF16)
        KcTs = sb("KcTs", [64, 2, NCMP], BF16)
        VcS = sb("VcS", [128, NH, 2, 65 + NBLK], BF16)
        qs = sb("qs", [64, 8, LS], BF16)
        KnT = sb("KnT", [64, 3, 2, LS], BF16)
        Vn = sb("Vn", [LS, 3, 2, 2, 65], BF16)
        gns = sb("gns", [16, 3, 2])
        Es = sb("Es", [128, NH, 2, 16], BF16)
        impg = sb("impg", [16, NBLK])
        sc = sb("sc", [LS, NSELS])
        wk_ = sb("wk", [LS, NSELS])
        m8a, m8b, thr = sb("m8a", [LS, 8]), sb("m8b", [LS, 8]), sb("thr", [LS, 1])
        selm = sb("selm", [LS, NBLK])
        selx = sb("selx", [NBLK, 2, 4, LS], BF16)
        pg = [sb("pg%d" % i, [128, 256]) for i in range(3)]
        pgb = [sb("pgb%d" % i, [128, 2, 2, 65], BF16) for i in range(3)]
        KpT = [sb("KpT%d" % i, [64, 2, 128], BF16) for i in range(2)]
        Pp = [sb("Pp%d" % i, [128, 32], BF16) for i in range(2)]
        ONs = sb("ONs", [16, 2, 64])
        ONb = sb("ONb", [16, 2, 64], BF16)
        rr, rg = sb("rr", [16, 1]), sb("rg", [16, 1])
        pS = ps.enter_context(nc.psum_tensor("ns_pS", [128, 512], F32))
        pO = ps.enter_context(nc.psum_tensor("ns_pO", [128, 512], F32))
        pI = ps.enter_context(nc.psum_tensor("ns_pI", [128, 512], F32))
        pT = [ps.enter_context(nc.psum_tensor("ns_pT%d" % i, [128, 1024], BF16)) for i in range(2)]
        pS2 = [ps.enter_context(nc.psum_tensor("ns_pS2%d" % i, [128, 512], F32)) for i in range(2)]
        pA = ps.enter_context(nc.psum_tensor("ns_pA", [128, 512], F32))
        names = ("pt idx const w cq kvbs kvbK KcTs VcS qs KnT Vn gns Es impg sc m8 sel selx ONs rr pS pO pI pA zb").split()
        T = {n: Tok(n) for n in names}
        Tpg, Tpgb = [Tok() for _ in range(3)], [Tok() for _ in range(3)]
        Tcqs = [Tok(), Tok()]
        TKpT, TPp, TpT, TpS2 = ([Tok(), Tok()] for _ in range(4))
        kb.memset(zb[:], 0.0, W=[T["zb"]])
        with nc.allow_non_contiguous_dma(reason="page table transpose (tiny)"):
            kb.dma("sp", ptT[:, :, :], c.pt4.rearrange("b (h p) -> p b h", p=128), W=[T["pt"]])
        kb.dma("sp", ptb[:, :], c.page_table.rearrange("b n -> (b n)").partition_broadcast(128), W=[T["pt"]])
        for nm, t_, src in (("qcol", qcol, c.ns_qcol), ("pcol", pcol, c.ns_pcol), ("M1", M1, c.ns_M1), ("M2", M2, c.ns_M2), ("gsum", gsum, c.ns_gsum)):
            kb.dma("sp", t_[:], src, W=[T["const"]])
        for t_, src in ((pairs, c.ns_pairs), (Fs, c.ns_Fs), (caus, c.ns_caus), (wm0, c.ns_wm0)):
            kb.dma("pool", t_[:], src, W=[T["const"]])
        kb.dma("sp", wflat[:, :], c.w_cmp.rearrange("s j h -> (s j h)").partition_broadcast(128), W=[T["w"]])
        kb.copy(idxq_f[:], ptT[:], R=[T["pt"]], W=[T["idx"]])
        kb.ts(idxq_f[:], idxq_f[:], 4.0, qcol[:, 0:1], ALU.mult, ALU.add, R=[T["idx"], T["const"]], W=[T["idx"]])
        kb.copy(idxq[:], idxq_f[:], R=[T["idx"]], W=[T["idx"]])
        kb.ts(idxq_f[:], idxq_f[:], 32.0, None, ALU.mult, R=[T["idx"]], W=[T["idx"]])
        kb.dma("sp", jrow[:, :], c.ns_jrow.partition_broadcast(128), W=[T["const"]])
        kb.tt(idxq32_f[:, :, :], idxq_f[:].rearrange("p b h -> p (b h)").unsqueeze(2).broadcast_to([128, NS * NH, 32]),
              jrow[:, :].unsqueeze(1).broadcast_to([128, NS * NH, 32]), ALU.add, R=[T["idx"], T["const"]], W=[T["idx"]])
        kb.copy(idxq32[:, :], idxq32_f[:, :, :].rearrange("p a j -> p (a j)"), R=[T["idx"]], W=[T["idx"]])
        kb.copy(idxr_f[:], ptb[:], R=[T["pt"]], W=[T["idx"]])
        kb.ts(idxr_f[:], idxr_f[:], 128.0, pcol[:, 0:1], ALU.mult, ALU.add, R=[T["idx"], T["const"]], W=[T["idx"]])
        kb.copy(idxr[:], idxr_f[:], R=[T["idx"]], W=[T["idx"]])
        for i in range(3):
            kb.memset(pgb[i][:, :, :, 64:65], 1.0, W=[Tpgb[i]])
        kb.memset(VcS[:, :, :, 64:65], 1.0, W=[T["VcS"]])
        kb.memset(Vn[:, :, :, :, 64:65], 1.0, W=[T["Vn"]])
        for hk in range(2):
            kb.copy(VcS[:, :, hk, 65:65 + NBLK], pairs[:, :, :], R=[T["const"]], W=[T["VcS"]])
        cmp_v = c.cache_cmp.rearrange("(n q) f -> n q f", q=32)
        wv = wflat[:, :].rearrange("p (s j h) -> p j s h", s=2, h=2)
        st = {"pg": 0, "k": 0}

        def tile_pipeline(src_tile, Tsrc, nrows, mask_rhs, mask_lhsT, Rmask, hk_list=(0, 1)):
            i3 = st["pg"] % 3
            i2 = st["k"] % 2
            st["pg"] += 1
            st["k"] += 1
            pb, Tpb = pgb[i3], Tpgb[i3]
            kb.copy(pb[:nrows, :, :, 0:64], src_tile, R=[Tsrc], W=[Tpb])
            for hk in range(2):
                kb.tr(pT[i2][:64, hk * 128:hk * 128 + nrows], pb[:nrows, 0, hk, 0:64], c.ident_b[:nrows, :nrows],
                      R=[Tpb, c.Tconst], W=[TpT[i2]])
            kb.copy(KpT[i2][:, :, :nrows], pT[i2][:64, 0:256].rearrange("p (h n) -> p h n", h=2)[:, :, :nrows],
                    R=[TpT[i2]], W=[TKpT[i2]], eng="act")
            p2, Tp2 = pS2[i2], TpS2[i2]
            if mask_rhs is not None:
                kb.mm(p2[:nrows, 0:32], mask_lhsT, mask_rhs, True, False, R=Rmask, W=[Tp2], sig=False)
            for hk in range(2):
                first = (mask_rhs is None)
                kb.mm(p2[:nrows, hk * 16:(hk + 1) * 16], KpT[i2][:, hk, :nrows], qs[:, hk * 4:(hk + 1) * 4, :], first, True,
                      R=[TKpT[i2], T["qs"]], W=[Tp2], sig=(hk == 1))
            kb.act(Pp[i2][:nrows, :], p2[:nrows, 0:32], AF.Exp, R=[Tp2], W=[TPp[i2]], scale=0.125)
            for hk in range(2):
                kb.mm(pA[:16, hk * 65:(hk + 1) * 65], Pp[i2][:nrows, hk * 16:(hk + 1) * 16], pb[:nrows, 1, hk, :], False, True,
                      R=[TPp[i2], Tpb], W=[T["pA"]], sig=(hk == 1))

        def finish_branch(br, first):
            for hk in range(2):
                ap_ = pA[:16, hk * 65:(hk + 1) * 65]
                kb.ts(rr[:, :], ap_[:, 64:65], 1e-30, None, ALU.max, R=[T["pA"]], W=[T["rr"]])
                kb.op("dve", lambda e: e.reciprocal(out=rr[:, :], in_=rr[:, :]), [T["rr"]], [T["rr"]])
                kb.tt(rg[:, :], rr[:, :], gns[:, br, hk:hk + 1], ALU.mult, R=[T["rr"], T["gns"]], W=[T["rr"]])
                if first:
                    kb.ts(ONs[:, hk, :], ap_[:, 0:64], rg[:, 0:1], None, ALU.mult, R=[T["pA"], T["rr"]], W=[T["ONs"]])
                else:
                    kb.stt(ONs[:, hk, :], ap_[:, 0:64], rg[:, 0:1], ONs[:, hk, :], ALU.mult, ALU.add, R=[T["pA"], T["rr"], T["ONs"]], W=[T["ONs"]])

        for b in range(NS):
            tg = TP + b * LS
            kb.dma("sp", qs[:, :, :], c.qT_d.rearrange("(h d) t -> d h t", d=64)[:, :, tg:tg + LS], W=[T["qs"]])
            for i3 in range(3):
                kb.dma("sp", KnT[:, i3, :, :], c.kT_d[i3].rearrange("(h d) t -> d h t", d=64)[:, :, tg:tg + LS], W=[T["KnT"]])
            kb.dma("pool", Vn[:, :, :, :, 0:64], c.kvg_d[tg:tg + LS, 0:768].rearrange("l (b s h d) -> l b s h d", b=3, s=2, h=2),
                   W=[T["Vn"]])
            for g in range(4):
                src = c.kvg_d[tg:tg + LS, 768 + g:768 + g + 21:4].rearrange("l (b h) -> l b h", h=2)
                with nc.allow_non_contiguous_dma(reason="tiny gate gather"):
                    kb.dma("sp", gns[4 * g:4 * g + 4, :, :], src, W=[T["gns"]])
            kb.act(gns[:, :, :], gns[:, :, :], AF.Exp, R=[T["gns"]], W=[T["gns"]], scale=-1.0)
            kb.ts(gns[:, :, :], gns[:, :, :], 1.0, None, ALU.add, R=[T["gns"]], W=[T["gns"]])
            kb.op("dve", lambda e: e.reciprocal(out=gns[:, :, :], in_=gns[:, :, :]), [T["gns"]], [T["gns"]])
            for half in range(NH):
                q_ = cq[(b * NH + half) % 2]
                Tcq = Tcqs[(b * NH + half) % 2]
                for j in range(32):
                    col = ((b * NH + half) * 32 + j)
                    kb.dma_fn("pool", lambda e, q_=q_, j=j, col=col: e.indirect_dma_start(
                        out=q_[:, j, :], out_offset=None, in_=c.cache_cmp,
                        in_offset=bass.IndirectOffsetOnAxis(ap=idxq32[:, col:col + 1], axis=0)),
                        R=[T["idx"]], W=[Tcq])
                q4 = q_[:, :, :].rearrange("p j (s h d) -> p j s h d", s=2, h=2)
                if b == 0 and half == 0:
                    dbg_dump(kb, c, q_[:, 0:2, :].rearrange("p j c -> p (j c)"), [Tcq], 128, 512)
                    dbg_dump(kb, c, wflat[:, :], [T["w"]], 128, 128)
                for s_ in range(2):
                    kb.tt(q4[:, :, s_, :, :], q4[:, :, s_, :, :], wv[:, :, s_, :].unsqueeze(3).broadcast_to([128, 32, 2, 64]), ALU.mult,
                          R=[Tcq, T["w"]], W=[Tcq], eng="pool")
                if b == 0 and half == 0:
                    dbg_dump(kb, c, q_[:, 0:2, :].rearrange("p j c -> p (j c)"), [Tcq], 128, 512)
                kb.op("dve", lambda e, q_=q_, half=half: e.tensor_reduce(out=kvbs[:, half, :], in_=q_[:, :, :].rearrange("p j c -> p c j"),
                                                                        axis=AX.X, op=ALU.add), [Tcq], [T["kvbs"]])
            if b == 0:
                dbg_dump(kb, c, idxq_f[:, 0, :], [T["idx"]], 128, NH)
                dbg_dump(kb, c, kvbs[:, 0, :], [T["kvbs"]], 128, 256)
            kb.copy(kvbK[:, :, :], kvbs[:, :, 0:128], R=[T["kvbs"]], W=[T["kvbK"]])
            for hk in range(2):
                kb.copy(VcS[:, :, hk, 0:64], kvbs[:, :, 128 + hk * 64:128 + (hk + 1) * 64], R=[T["kvbs"]], W=[T["VcS"]])
            for half in range(NH):
                for hk in range(2):
                    kb.tr(pT[0][:64, (half * 2 + hk) * 128:(half * 2 + hk + 1) * 128], kvbK[:, half, hk * 64:(hk + 1) * 64], c.ident_b[:, :],
                          R=[T["kvbK"], c.Tconst], W=[TpT[0]])
            kb.copy(KcTs[:, :, :].rearrange("p h (a n) -> p a h n", a=NH),
                    pT[0][:64, 0:NH * 256].rearrange("p (a h n) -> p a h n", a=NH, h=2), R=[TpT[0]], W=[T["KcTs"]])
            for half in range(NH):
                for hk in range(2):
                    kb.mm(pS[:, (half * 2 + hk) * 16:(half * 2 + hk + 1) * 16], KcTs[:, hk, half * 128:(half + 1) * 128],
                          qs[:, hk * 4:(hk + 1) * 4, :], True, True, R=[T["KcTs"], T["qs"]], W=[T["pS"]], sig=(half == NH - 1 and hk == 1))
            kb.act(Es[:, :, :, :], pS[:, 0:NH * 32].rearrange("p (a h x) -> p a h x", a=NH, h=2), AF.Exp, R=[T["pS"]], W=[T["Es"]], scale=0.125)
            for hk in range(2):
                for half in range(NH):
                    kb.mm(pO[:16, hk * 256:hk * 256 + 65 + NBLK], Es[:, half, hk, :], VcS[:, half, hk, :], half == 0, half == NH - 1,
                          R=[T["Es"], T["VcS"]], W=[T["pO"]], sig=(half == NH - 1))
                ap_ = pO[:16, hk * 256:hk * 256 + 65 + NBLK]
                kb.ts(rr[:, :], ap_[:, 64:65], 1e-30, None, ALU.max, R=[T["pO"]], W=[T["rr"]])
                kb.op("dve", lambda e: e.reciprocal(out=rr[:, :], in_=rr[:, :]), [T["rr"]], [T["rr"]])
                kb.tt(rg[:, :], rr[:, :], gns[:, 0, hk:hk + 1], ALU.mult, R=[T["rr"], T["gns"]], W=[T["rr"]])
                kb.ts(ONs[:, hk, :], ap_[:, 0:64], rg[:, 0:1], None, ALU.mult, R=[T["pO"], T["rr"]], W=[T["ONs"]])
                kb.ts(impg[:, :], ap_[:, 65:65 + NBLK], rr[:, 0:1], None, ALU.mult, R=[T["pO"], T["rr"]], W=[T["impg"]])
                if b == 0:
                    dbg_dump(kb, c, ONs[:, hk, :], [T["ONs"]], 16, 64)
                    dbg_dump(kb, c, impg[:, :], [T["impg"]], 16, NBLK)
                kb.mm(pI[:LS, 0:NBLK], gsum[:, :], impg[:, :], True, True, R=[T["const"], T["impg"]], W=[T["pI"]])
                kb.tt(sc[:, 0:NBLK], pI[:LS, 0:NBLK], M1[:, 0:NBLK], ALU.mult, R=[T["pI"], T["const"]], W=[T["sc"]])
                kb.tt(sc[:, 0:NBLK], sc[:, 0:NBLK], M2[:, 0:NBLK], ALU.add, R=[T["sc"], T["const"]], W=[T["sc"]])
                kb.copy(sc[:, NBLK:NSELS], M2[:, NBLK:NSELS], R=[T["const"]], W=[T["sc"]])
                kb.op("dve", lambda e: e.max(out=m8a[:, :], in_=sc[:, :]), [T["sc"]], [T["m8"]])
                kb.op("dve", lambda e: e.match_replace(out=wk_[:, :], in_to_replace=m8a[:, :], in_values=sc[:, :], imm_value=-2.0),
                      [T["sc"], T["m8"]], [T["sel"]])
                kb.op("dve", lambda e: e.max(out=m8b[:, :], in_=wk_[:, :]), [T["sel"]], [T["m8"]])
                kb.op("dve", lambda e: e.tensor_reduce(out=thr[:, :], in_=m8b[:, :], axis=AX.X, op=ALU.min), [T["m8"]], [T["m8"]])
                kb.ts(selm[:, :], sc[:, 0:NBLK], thr[:, 0:1], None, ALU.is_ge, R=[T["sc"], T["m8"]], W=[T["sel"]])
                kb.ts(selm[:, :], selm[:, :], -1.0, -NEGM, ALU.add, ALU.mult, R=[T["sel"]], W=[T["sel"]])
                if b == 0:
                    dbg_dump(kb, c, selm[:, :], [T["sel"]], LS, NBLK)
                kb.tr(pI[:NBLK, 256:256 + LS], selm[:, :], c.ident_f[:LS, :LS], R=[T["sel"], c.Tconst], W=[T["pI"]])
                kb.copy(selx[:, hk, :, :], pI[:NBLK, 256:256 + LS].unsqueeze(1).broadcast_to([NBLK, 4, LS]), R=[T["pI"]], W=[T["selx"]])
            for br in (1, 2):
                kb.mm(pA[:, :], zb[:, 0:128], zb[:, :], True, False, R=[T["zb"]], W=[T["pA"]], sig=False)
                if br == 1:
                    cache = c.cache_sel
                    for lp in range(NPAGES):
                        i3 = st["pg"] % 3
                        t_, Tt_ = pg[i3], Tpg[i3]
                        kb.dma_fn("pool", lambda e, t_=t_, lp=lp: e.indirect_dma_start(
                            out=t_[:, :], out_offset=None, in_=cache,
                            in_offset=bass.IndirectOffsetOnAxis(ap=idxr[:, b * NPAGES + lp:b * NPAGES + lp + 1], axis=0)),
                            R=[T["idx"]], W=[Tt_])
                        tile_pipeline(t_[:, 0:256].rearrange("p (s h d) -> p s h d", s=2, h=2), Tt_, 128,
                                      selx[:, :, :, :].rearrange("p a g l -> p (a g l)"), Fs[:, lp * 128:(lp + 1) * 128], [T["selx"], T["const"]])
                else:
                    for kt in range(WL // 128):
                        i3 = st["pg"] % 3
                        t_, Tt_ = pg[i3], Tpg[i3]
                        kb.dma("sp", t_[:, 0:256], c.cache_win[b, kt * 128:(kt + 1) * 128, :], W=[Tt_])
                        if kt == 0:
                            tile_pipeline(t_[:, 0:256].rearrange("p (s h d) -> p s h d", s=2, h=2), Tt_, 128,
                                          wm0[:, :], c.ident_b[:, :], [T["const"], c.Tconst])
                        else:
                            tile_pipeline(t_[:, 0:256].rearrange("p (s h d) -> p s h d", s=2, h=2), Tt_, 128, None, None, [])
                p2, Tp2 = pS2[0], TpS2[0]
                kb.mm(p2[:LS, 0:32], c.ident_b[:LS, :LS], caus[:, :], True, False, R=[c.Tconst, T["const"]], W=[Tp2], sig=False)
                for hk in range(2):
                    kb.mm(p2[:LS, hk * 16:(hk + 1) * 16], KnT[:, br, hk, :], qs[:, hk * 4:(hk + 1) * 4, :], False, True,
                          R=[T["KnT"], T["qs"]], W=[Tp2], sig=(hk == 1))
                kb.act(Pp[0][:LS, :], p2[:LS, 0:32], AF.Exp, R=[Tp2], W=[TPp[0]], scale=0.125)
                for hk in range(2):
                    kb.mm(pA[:16, hk * 65:(hk + 1) * 65], Pp[0][:LS, hk * 16:(hk + 1) * 16], Vn[:, br, 1, hk, :], False, True,
                          R=[TPp[0], T["Vn"]], W=[T["pA"]], sig=(hk == 1))
                finish_branch(br, False)
                if b == 0:
                    dbg_dump(kb, c, ONs[:, :, :].rearrange("p a d -> p (a d)"), [T["ONs"]], 16, 128)
            kb.copy(ONb[:, :, :], ONs[:, :, :], R=[T["ONs"]], W=[T["ONs"]])
            for hk in range(2):
                for g in range(4):
                    h = hk * 4 + g
                    kb.dma("sp", c.on_d[tg:tg + LS, h * 64:(h + 1) * 64], ONb[4 * g:4 * g + 4, hk, :], R=[T["ONs"]])
        kb.barrier()


def phase_nsa_zero(kb, c, start=0):
    nc = c.nc
    with contextlib.ExitStack() as ps:
        zt = ps.enter_context(nc.sbuf_tensor("nz_z", [128, 512], BF16))
        Tz = Tok()
        kb.memset(zt[:], 0.0, W=[Tz])
        for r0 in range(start, c.T, 128):
            nr = min(128, c.T - r0)
            kb.dma("sp", c.on_d[r0:r0 + nr, :], zt[:nr, :], R=[Tz])
        kb.barrier()

def build(cfg, stages=("mod", "ffn1", "win", "gdn", "nsap", "nsas", "mix", "ffn2")):
    nc = bass.Bass("TRN2", target_bir_lowering=False)
    c = Ctx()
    c.nc, c.cfg = nc, cfg
    cfg.setdefault("WIN", 512)
    NP, SEQ, NS = cfg["NP"], cfg["SEQ"], cfg["NS"]
    TP, TS = NP * SEQ, NS * LS
    T = TP + TS
    NR = NP + TS
    c.TP, c.TS, c.T = TP, TS, T
    WL = cfg["WIN"]
    PW = min(WL, SEQ)

    def din(name, shape, dt=F32):
        return nc.dram_tensor(name, list(shape), dt, kind="ExternalInput").ap()

    def dout(name, shape, dt=F32):
        return nc.dram_tensor(name, list(shape), dt, kind="ExternalOutput").ap()

    def dscr(name, shape, dt=F32):
        return nc.dram_tensor(name, list(shape), dt, kind="Internal").ap()

    c.xp = din("xp", [TP, D])
    c.xs = din("xs", [TS, D])
    c.c_rows = din("c_rows", [NR, D])
    c.ident_d = din("ident", [128, 128])
    c.ln_g = din("ln_g", [3, D])
    c.ln_b = din("ln_b", [3, D])
    c.w_ada = din("w_ada", [D, 9 * D])
    c.b_ada = din("b_ada", [1, 9 * D])
    c.w_ff1_gu = din("w_ff1_gu", [D, 2 * DFF])
    c.w_ff1_dn = din("w_ff1_dn", [DFF, D])
    c.w_ff2_gu = din("w_ff2_gu", [D, 2 * DFF])
    c.w_ff2_dn = din("w_ff2_dn", [DFF, D])
    c.w_in = din("w_in", [D, DIN])
    c.state_gdn = din("state_gdn", [NS, 8, 64, 64])
    c.conv_buf = din("conv_buf", [NS, 3, 1536])
    c.cache_win = din("cache_win", [NS, WL, 256])
    c.conv_w = din("conv_w", [4, 1536])
    c.w_cmp = din("w_cmp", [2, 32, 2])
    nsel, ncmp = SEQ // 64, SEQ // 32
    c.nsa_M1 = din("nsa_M1", [SEQ, nsel])
    c.nsa_M2 = din("nsa_M2", [SEQ, nsel])
    c.nsa_cmask = din("nsa_cmask", [ncmp, SEQ])
    c.nsa_F = din("nsa_F", [nsel, SEQ])
    c.nsa_Wm = din("nsa_Wm", [128, 8, 512])
    c.nsa_pair = din("nsa_pair", [ncmp, nsel])
    c.nsa_maskW = din("nsa_maskW", [128, 124])
    NPAGES, NPHYS = cfg["NPAGES"], cfg["NPHYS"]
    c.cache_cmp = din("cache_cmp", [NPHYS * 128, 256])
    c.cache_sel = din("cache_sel", [NPHYS * 128, 256])
    c.page_table = din("page_table", [NS, NPAGES], I32)
    c.pt4 = din("pt4", [NS, NPAGES * 4], I32)
    for k_, v_ in nsa_s_consts(NPAGES).items():
        setattr(c, k_, din(k_, list(v_.shape)))
    c.w_br_gdn = din("w_br_gdn", [512, D])
    c.w_br_nsa = din("w_br_nsa", [512, D])
    c.w_out = din("w_out", [D, D])
    c.a_log = din("a_log", [1, 8])
    c.dt_bias = din("dt_bias", [1, 8])
    c.norm_w = din("norm_w", [1, 64])
    c.blk1_d = din("blk1", [128, 128])
    c.gconst_d = din("gconst", [64, 5, 64])
    c.valid_d = din("valid", [64, 1])
    c.yp = dout("y_prompt", [TP, D])
    c.ys = dout("y_sample", [TS, D])
    c.p_gdn = dout("p_gdn", [NP, 8, 64, 64])
    c.p_conv = dout("p_conv", [NP, 3, 1536])
    c.p_cmp = dout("p_cmp", [TP, 256])
    c.p_sel = dout("p_sel", [TP, 256])
    c.p_win = dout("p_win", [NP * PW, 256])
    c.s_gdn = dout("s_gdn", [NS, 8, 64, 64])
    c.s_conv = dout("s_conv", [NS, 3, 1536])
    c.s_cmp = dout("s_cmp", [TS, 256])
    c.s_sel = dout("s_sel", [TS, 256])
    c.s_win = dout("s_win", [NS, WL, 256])
    c.mod_d = dscr("mod_d", [NR, 9 * D])
    c.x1 = {"p": dscr("x1p", [TP, D]), "s": dscr("x1s", [TS, D])}
    c.x2 = {"p": dscr("x2p", [TP, D]), "s": dscr("x2s", [TS, D])}
    c.qkvT_d = dscr("qkvT_d", [1536, T])
    c.qT_d = dscr("qT_d", [512, T], BF16)
    c.kT_d = dscr("kT_d", [3, 128, T], BF16)
    c.mgT_d = dscr("mgT_d", [2048, T], BF16)
    c.zab_d = dscr("zab_d", [T, 528])
    c.kvg_d = dscr("kvg_d", [T, 792])
    c.qkvn_d = dscr("qkvn_d", [1536, T])
    c.kvtok_d = dscr("kvtok_d", [T, 1024])
    dbg = dout if cfg.get("DBG") else dscr
    c.dbg_d = dbg("dbg_d", [16, 128, 512])
    c.dbg_n = 0
    c.og_d = dbg("og_d", [T, 512], BF16)
    c.on_d = dbg("on_d", [T, 512], BF16)

    with contextlib.ExitStack() as es:
        kb = KB(nc, es)
        c.kb = kb
        c.ident_f = es.enter_context(nc.sbuf_tensor("ident_f", [128, 128], F32))
        c.ident_b = es.enter_context(nc.sbuf_tensor("ident_b", [128, 128], BF16))
        c.Tconst = Tok()
        kb.dma("sp", c.ident_f[:], c.ident_d, W=[c.Tconst])
        kb.copy(c.ident_b[:], c.ident_f[:], R=[c.Tconst], W=[c.Tconst])
        if "mod" in stages:
            phase_mod(kb, c)
        if "ffn1" in stages:
            phase_ffn(kb, c, "f1", {"p": c.xp, "s": c.xs}, c.x1, c.w_ff1_gu, c.w_ff1_dn, 0)
        if "win" in stages:
            phase_win(kb, c)
        if "gdn" in stages:
            phase_gdn_a(kb, c)
            phase_gdn_b(kb, c)
        if "nsap" in stages:
            phase_nsa_prompt(kb, c)
        if "nsas" in stages:
            phase_nsa_sample(kb, c)
        if "nsa0" in stages:
            phase_nsa_zero(kb, c)
        if "nsa0s" in stages:
            phase_nsa_zero(kb, c, c.TP)
        if "mix" in stages:
            phase_mix(kb, c)
        if "ffn2" in stages:
            phase_ffn(kb, c, "f2", c.x2, {"p": c.yp, "s": c.ys}, c.w_ff2_gu, c.w_ff2_dn, 2)
        kb.finish()
        c.nops = kb.nops
    return nc, c


def make_in_maps(cfg, inputs, ncores):
    NP, SEQ, NS = cfg["NP"], cfg["SEQ"], cfg["NS"]
    f = lambda a: np.ascontiguousarray(np.asarray(a))
    maps = []
    ident = np.eye(128, dtype=np.float32)
    blk1 = np.kron(np.eye(2, dtype=np.float32), np.ones((64, 64), np.float32))
    ii = np.arange(64)
    gconst = np.stack([
        (ii[:, None] <= ii[None, :]).astype(np.float32),
        np.ones((64, 64), np.float32),
        np.where(ii[None, :] >= ii[:, None], 0.0, -30000.0).astype(np.float32),
        np.where(ii[None, :] >= ii[:, None], 30000.0, 0.0).astype(np.float32),
        np.eye(64, dtype=np.float32)], axis=1)
    valid = (ii < LS).astype(np.float32).reshape(64, 1)
    nconst = nsa_consts(SEQ)
    nconst.update(nsa_s_consts(cfg["NPAGES"]))
    cc_all = f(inputs["cache_cmp_kv"][0]).reshape(-1, 256)
    cs_all = f(inputs["cache_sel_kv"][0]).reshape(-1, 256)
    for i in range(ncores):
        ps, ss = slice(i * NP, (i + 1) * NP), slice(i * NS, (i + 1) * NS)
        m = {
            "xp": f(inputs["x_prompt"][ps]).reshape(NP * SEQ, D),
            "xs": f(inputs["x_sample"][ss]).reshape(NS * LS, D),
            "c_rows": np.concatenate([f(inputs["c_prompt"][ps]), np.repeat(f(inputs["c_sample"][ss]), LS, axis=0)], 0),
            "ident": ident,
            "ln_g": f(inputs["ln_g"][0]), "ln_b": f(inputs["ln_b"][0]),
            "w_ada": f(inputs["w_ada"][0]), "b_ada": f(inputs["b_ada"]).reshape(1, 9 * D),
            "w_ff1_gu": f(inputs["w_ff1_gu"][0]), "w_ff1_dn": f(inputs["w_ff1_dn"][0]),
            "w_ff2_gu": f(inputs["w_ff2_gu"][0]), "w_ff2_dn": f(inputs["w_ff2_dn"][0]),
            "w_in": f(inputs["w_in"][0]),
            "state_gdn": f(inputs["state_gdn"][0][ss]),
            "conv_buf": f(inputs["state_gdn_conv"][0][ss]),
            "cache_win": f(inputs["cache_win_kv"][0][ss]).reshape(NS, -1, 256),
            "conv_w": f(inputs["gdn_conv_w"][0]), "a_log": f(inputs["gdn_a_log"]).reshape(1, 8),
            "dt_bias": f(inputs["gdn_dt_bias"]).reshape(1, 8), "norm_w": f(inputs["gdn_norm_w"]).reshape(1, 64),
            "blk1": blk1, "gconst": gconst, "valid": valid, "w_cmp": f(inputs["nsa_w_cmp"][0]),
            "cache_cmp": cc_all, "cache_sel": cs_all,
            "page_table": f(inputs["page_table"][ss]).astype(np.int32),
            "pt4": np.repeat(f(inputs["page_table"][ss]).astype(np.int32), 4, axis=1),
            "w_br_gdn": f(inputs["w_br_gdn"][0]), "w_br_nsa": f(inputs["w_br_nsa"][0]), "w_out": f(inputs["w_out"][0]),
        }
        m.update(nconst)
        maps.append(m)
    return maps


def gather_outputs(cfg, results, ncores):
    NP, SEQ, NS = cfg["NP"], cfg["SEQ"], cfg["NS"]
    PW = min(cfg["WIN"], SEQ)
    cat = lambda k: np.concatenate([np.asarray(r[k]) for r in results], 0)
    B, Bs = NP * ncores, NS * ncores
    return (
        cat("y_prompt").reshape(B, SEQ, D),
        cat("y_sample").reshape(Bs, LS, D),
        cat("p_gdn").reshape(1, B, 8, 64, 64),
        cat("p_conv").reshape(1, B, 3, 1536),
        cat("p_cmp").reshape(1, B, SEQ, 2, 2, 64),
        cat("p_sel").reshape(1, B, SEQ, 2, 2, 64),
        cat("p_win").reshape(1, B, PW, 2, 2, 64),
        cat("s_gdn").reshape(1, Bs, 8, 64, 64),
        cat("s_conv").reshape(1, Bs, 3, 1536),
        cat("s_cmp").reshape(1, Bs, LS, 2, 2, 64),
        cat("s_sel").reshape(1, Bs, LS, 2, 2, 64),
        cat("s_win").reshape(1, Bs, cfg["WIN"], 2, 2, 64),
    )


_CACHE = {}


def run(cfg, inputs, ncores, stages=None):
    key = (tuple(sorted(cfg.items())), ncores, stages)
    if key not in _CACHE:
        _CACHE[key] = build(dict(cfg), stages) if stages else build(dict(cfg))
    nc, c = _CACHE[key]
    res = run_bass_kernel_spmd(nc, make_in_maps(c.cfg, inputs, ncores), core_ids=list(range(ncores)))
    _CACHE["last"] = res.results
    return gather_outputs(c.cfg, res.results, ncores)


def kernel(**inputs):
    return run(FULL_CFG, inputs, NCORES)
```

```python
import contextlib
import numpy as np
import concourse.bass as bass
import concourse.mybir as mybir
from concourse.bass_utils import run_bass_kernel_spmd

F32 = mybir.dt.float32
BF16 = mybir.dt.bfloat16
I32 = mybir.dt.int32
AF = mybir.ActivationFunctionType
ALU = mybir.AluOpType
AX = mybir.AxisListType

D = 1024
DFF = 2816
DIN = 5416
NCORES = 8
ALPHA = 2.0 ** 0.25
LS = 4

FULL_CFG = dict(NP=4, SEQ=2048, NS=16, NPAGES=64, NPHYS=10240)


class Tok:
    __slots__ = ("w", "r", "name")

    def __init__(self, name=""):
        self.w = None
        self.r = {}
        self.name = name


class Ev:
    __slots__ = ("dim", "val")

    def __init__(self, dim, val):
        self.dim = dim
        self.val = val


class KB:
    def __init__(self, nc, es, kq=None):
        self.nc = nc
        self.eng = {"pe": nc.tensor, "act": nc.scalar, "dve": nc.vector, "pool": nc.gpsimd, "sp": nc.sync}
        self.sem = {e: es.enter_context(nc.semaphore("s_" + e)) for e in self.eng}
        self.cnt = {e: 0 for e in self.eng}
        self.waited = {e: {} for e in self.eng}
        kq = kq or {"sp": 24, "act": 8, "pool": 16}
        self.dq = {q: [es.enter_context(nc.semaphore("d_%s%d" % (q, i))) for i in range(k)] for q, k in kq.items()}
        self.dqn = {q: 0 for q in kq}
        self.pe_pending = []
        self.nops = 0

    def semof(self, dim):
        if isinstance(dim, str):
            return self.sem[dim]
        return self.dq[dim[0]][dim[1]]

    def _collect(self, e, R, W):
        deps = {}

        def add(ev):
            if ev is None:
                return
            if e == "pe" and ev.dim == "pe":
                return
            if ev.val is None:
                raise RuntimeError("dependency on unsignaled PE op")
            if deps.get(ev.dim, 0) < ev.val:
                deps[ev.dim] = ev.val

        for t in R:
            add(t.w)
        for t in W:
            add(t.w)
            for r in t.r.values():
                add(r)
        return deps

    def _waits(self, e, deps):
        wd = self.waited[e]
        for dim, val in deps.items():
            if wd.get(dim, 0) < val:
                self.eng[e].wait_ge(self.semof(dim), val)
                wd[dim] = val

    def _record(self, ev, R, W):
        for t in R:
            t.r[ev.dim] = ev
        for t in W:
            t.w = ev
            t.r = {}

    def op(self, e, fn, R=(), W=(), sig=True):
        self._waits(e, self._collect(e, R, W))
        ins = fn(self.eng[e])
        self.nops += 1
        if sig:
            self.cnt[e] += 1
            ins.then_inc(self.sem[e], 1)
            ev = Ev(e, self.cnt[e])
            if e == "pe":
                for p in self.pe_pending:
                    p.val = self.cnt[e]
                self.pe_pending = []
        else:
            assert e == "pe"
            ev = Ev("pe", None)
            self.pe_pending.append(ev)
        self._record(ev, R, W)
        return ev

    def dma(self, q, out, in_, R=(), W=(), **kw):
        slots = self.dq[q]
        i = self.dqn[q]
        self.dqn[q] += 1
        k = i % len(slots)
        val = 16 * (i // len(slots) + 1)
        deps = self._collect(q, R, W)
        dim = (q, k)
        if val > 16:
            deps[dim] = max(deps.get(dim, 0), val - 16)
        self._waits(q, deps)
        ins = self.eng[q].dma_start(out=out, in_=in_, **kw)
        ins.then_inc(slots[k], 16)
        self.nops += 1
        ev = Ev(dim, val)
        self._record(ev, R, W)
        return ev

    def dma_fn(self, q, fn, R=(), W=()):
        slots = self.dq[q]
        i = self.dqn[q]
        self.dqn[q] += 1
        k = i % len(slots)
        val = 16 * (i // len(slots) + 1)
        deps = self._collect(q, R, W)
        dim = (q, k)
        if val > 16:
            deps[dim] = max(deps.get(dim, 0), val - 16)
        self._waits(q, deps)
        ins = fn(self.eng[q])
        ins.then_inc(slots[k], 16)
        self.nops += 1
        ev = Ev(dim, val)
        self._record(ev, R, W)
        return ev

    def _all_targets(self):
        targets = {}
        for e in self.eng:
            if self.cnt[e] > 0:
                targets[e] = self.cnt[e]
        for q, slots in self.dq.items():
            n = self.dqn[q]
            for k in range(len(slots)):
                uses = (n - k + len(slots) - 1) // len(slots) if n > k else 0
                if uses > 0:
                    targets[(q, k)] = 16 * uses
        return targets

    def barrier(self):
        assert not self.pe_pending
        targets = self._all_targets()
        for e in self.eng:
            self._waits(e, targets)

    def finish(self):
        assert not self.pe_pending
        self._waits("sp", self._all_targets())

    def mm(self, out, lhsT, rhs, start, stop, R=(), W=(), sig=None):
        if sig is None:
            sig = stop
        return self.op("pe", lambda e: e.matmul(out, lhsT, rhs, start=start, stop=stop), R, W, sig)

    def tr(self, out, in_, ident, R=(), W=(), sig=True):
        return self.op("pe", lambda e: e.transpose(out, in_, ident), R, W, sig)

    def act(self, out, in_, func, R=(), W=(), bias=None, scale=None, **kw):
        kws = dict(kw)
        if bias is not None:
            kws["bias"] = bias
        if scale is not None:
            kws["scale"] = scale
        return self.op("act", lambda e: e.activation(out=out, in_=in_, func=func, **kws), R, W)

    def tt(self, out, in0, in1, op, R=(), W=(), eng="dve"):
        return self.op(eng, lambda e: e.tensor_tensor(out=out, in0=in0, in1=in1, op=op), R, W)

    def ts(self, out, in0, s1, s2, op0, op1=None, R=(), W=(), eng="dve"):
        if op1 is None:
            return self.op(eng, lambda e: e.tensor_scalar(out=out, in0=in0, scalar1=s1, scalar2=None, op0=op0), R, W)
        return self.op(eng, lambda e: e.tensor_scalar(out=out, in0=in0, scalar1=s1, scalar2=s2, op0=op0, op1=op1), R, W)

    def stt(self, out, in0, scalar, in1, op0, op1, R=(), W=(), eng="dve"):
        return self.op(eng, lambda e: e.scalar_tensor_tensor(out=out, in0=in0, scalar=scalar, in1=in1, op0=op0, op1=op1), R, W)

    def copy(self, out, in_, R=(), W=(), eng="dve"):
        if eng == "act":
            return self.op("act", lambda e: e.copy(out=out, in_=in_), R, W)
        return self.op(eng, lambda e: e.tensor_copy(out=out, in_=in_), R, W)

    def memset(self, ap, v, W=(), eng="dve"):
        return self.op(eng, lambda e: e.memset(ap, v), (), W)


class Ctx:
    pass


def dbg_dump(kb, c, ap, R, rows, cols):
    if not c.cfg.get("DBG") or c.dbg_n >= 16:
        return
    kb.dma("sp", c.dbg_d[c.dbg_n, :rows, :cols], ap, R=R)
    c.dbg_n += 1


def token_groups(cfg, G):
    out = []
    for s in range(cfg["NP"]):
        for g in range(cfg["SEQ"] // G):
            out.append(("p", s, s * cfg["SEQ"] + g * G, G))
    out.append(("s", None, 0, cfg["NS"] * LS))
    return out


def subtiles(n):
    return [(r0, min(128, n - r0)) for r0 in range(0, n, 128)]


def load_rows(kb, c, tile, tok, kind, seq, col0, ncols=1024, q="sp"):
    cfg = c.cfg
    if kind == "p":
        src = c.mod_d[seq:seq + 1, col0:col0 + ncols].partition_broadcast(128)
        kb.dma(q, tile[:, :ncols], src, W=[tok])
    else:
        ts = cfg["NS"] * LS
        src = c.mod_d[cfg["NP"]:cfg["NP"] + ts, col0:col0 + ncols]
        kb.dma(q, tile[:ts, :ncols], src, W=[tok])


def layer_norm_rows(kb, c, t, nr, Tt, lng, lnb, Tl, st, mv, rstd, Tst):
    nc = c.nc
    for h in range(2):
        kb.op("dve", lambda e, h=h: e.bn_stats(out=st[:nr, h, :], in_=t[:nr, h * 512:(h + 1) * 512]), [Tt], [Tst])
    kb.op("dve", lambda e: e.bn_aggr(out=mv[:nr, :], in_=st[:nr, :, :]), [Tst], [Tst])
    kb.act(rstd[:nr, :], mv[:nr, 1:2], AF.Sqrt, R=[Tst], W=[Tst], bias=1e-5)
    kb.op("dve", lambda e: e.reciprocal(out=rstd[:nr, :], in_=rstd[:nr, :]), [Tst], [Tst])
    kb.ts(t[:nr, :], t[:nr, :], mv[:nr, 0:1], rstd[:nr, 0:1], ALU.subtract, ALU.mult, R=[Tt, Tst], W=[Tt])
    kb.tt(t[:nr, :], t[:nr, :], lng[:nr, :], ALU.mult, R=[Tt, Tl], W=[Tt], eng="pool")
    kb.tt(t[:nr, :], t[:nr, :], lnb[:nr, :], ALU.add, R=[Tt, Tl], W=[Tt], eng="pool")


def phase_mod(kb, c):
    nc, cfg = c.nc, c.cfg
    NR = cfg["NP"] + cfg["NS"] * LS
    with contextlib.ExitStack() as ps:
        sb = lambda n, s, d=F32: ps.enter_context(nc.sbuf_tensor(n, s, d))
        ct = sb("m_c", [NR, D])
        sct = sb("m_scT", [128, 8, NR], BF16)
        modt = sb("m_mod", [NR, 9 * D])
        bt = sb("m_b", [NR, 9 * D])
        wb = [sb("m_w%d" % i, [128, 8, 512], BF16) for i in range(2)]
        pst = [ps.enter_context(nc.psum_tensor("m_ps%d" % i, [128, 512], F32)) for i in range(4)]
        Tc, Tsct, Tmod, Tb = Tok(), Tok(), Tok(), Tok()
        Tw = [Tok(), Tok()]
        Tp = [Tok() for _ in range(4)]
        kb.dma("sp", ct[:], c.c_rows, W=[Tc])
        kb.dma("sp", bt[:], c.b_ada.partition_broadcast(NR), W=[Tb])
        kb.act(ct[:], ct[:], AF.Silu, R=[Tc], W=[Tc])
        for k in range(8):
            b = k // 4
            col = (k % 4) * NR
            kb.tr(pst[b][:, col:col + NR], ct[:, k * 128:(k + 1) * 128], c.ident_f[:NR, :NR], R=[Tc, c.Tconst], W=[Tp[b]])
        for b in range(2):
            kb.copy(sct[:, 4 * b:4 * b + 4, :], pst[b][:, 0:4 * NR].rearrange("p (k n) -> p k n", k=4), R=[Tp[b]], W=[Tsct])
        wv = c.w_ada.rearrange("(k p) f -> p k f", p=128)
        for nb in range(18):
            w = wb[nb % 2]
            kb.dma("pool", w[:], wv[:, :, nb * 512:(nb + 1) * 512], W=[Tw[nb % 2]])
            bank = pst[2 + nb % 2]
            for k in range(8):
                kb.mm(bank[:NR, :], sct[:, k, :], w[:, k, :], k == 0, k == 7, R=[Tsct, Tw[nb % 2]], W=[Tp[2 + nb % 2]])
            kb.tt(modt[:, nb * 512:(nb + 1) * 512], bank[:NR, :], bt[:, nb * 512:(nb + 1) * 512], ALU.add,
                  R=[Tp[2 + nb % 2], Tb], W=[Tmod])
        for i in range(3):
            o = (i * 3 + 1) * D
            kb.ts(modt[:, o:o + D], modt[:, o:o + D], 1.0, None, ALU.add, R=[Tmod], W=[Tmod])
        for i in (0, 2):
            o = (i * 3 + 2) * D
            kb.ts(modt[:, o:o + D], modt[:, o:o + D], 0.5, None, ALU.mult, R=[Tmod], W=[Tmod])
        kb.dma("sp", c.mod_d[:, :], modt[:], R=[Tmod])
        kb.barrier()


def phase_ffn(kb, c, tag, src, dst, w_gu, w_dn, isub):
    nc, cfg = c.nc, c.cfg
    G = 512
    NJ = DFF // 128
    with contextlib.ExitStack() as ps:
        sb = lambda n, s, d=F32: ps.enter_context(nc.sbuf_tensor(tag + n, s, d))
        wgu = sb("wgu", [128, 8, 2 * DFF], BF16)
        wdn = sb("wdn", [128, NJ, D], BF16)
        hT = sb("hT", [128, NJ, G], BF16)
        xmT = sb("xmT", [128, 8, G], BF16)
        xin = [sb("xin%d" % i, [128, D]) for i in range(4)]
        rows = {n: sb("row_" + n, [128, D], BF16 if n in ("sc", "sh", "g") else F32) for n in ("sc", "sh", "g", "lng", "lnb")}
        xm = sb("xm", [128, D])
        wk = [sb("wk%d" % i, [128, D]) for i in range(2)]
        sgt = [sb("sg%d" % i, [128, G], BF16) for i in range(2)]
        st = sb("st", [128, 2, nc.vector.BN_STATS_DIM])
        mv = sb("mv", [128, nc.vector.BN_AGGR_DIM])
        rstd = sb("rstd", [128, 1])
        psgu = [ps.enter_context(nc.psum_tensor(tag + "psgu%d" % i, [128, 512], F32)) for i in range(4)]
        psy = [ps.enter_context(nc.psum_tensor(tag + "psy%d" % i, [128, 1024], F32)) for i in range(2)]
        Tgu = [Tok() for _ in range(11)]
        Tdn = [Tok() for _ in range(11)]
        ThT = [Tok() for _ in range(NJ)]
        TxmT, Txm, Tst, Tln, Tmodrows = Tok(), Tok(), Tok(), Tok(), Tok()
        Txin = [Tok() for _ in range(4)]
        Twk = [Tok(), Tok()]
        Tsg = [Tok(), Tok()]
        Tpsgu = [Tok() for _ in range(4)]
        Tpsy = [Tok(), Tok()]
        guv = w_gu.rearrange("(k p) f -> p k f", p=128)
        for i in range(11):
            kb.dma("pool", wgu[:, :, i * 512:(i + 1) * 512], guv[:, :, i * 512:(i + 1) * 512], W=[Tgu[i]])
        dnv = w_dn.rearrange("(j p) d -> p j d", p=128)
        for i in range(11):
            kb.dma("pool", wdn[:, 2 * i:2 * i + 2, :], dnv[:, 2 * i:2 * i + 2, :], W=[Tdn[i]])
        kb.dma("sp", rows["lng"][:], c.ln_g[isub:isub + 1, :].partition_broadcast(128), W=[Tln])
        kb.dma("sp", rows["lnb"][:], c.ln_b[isub:isub + 1, :].partition_broadcast(128), W=[Tln])
        cur = None
        xslot = 0
        wslot = 0
        gi = 0
        for (kind, seq, t0, n) in token_groups(cfg, G):
            if (kind, seq) != cur:
                cur = (kind, seq)
                base = isub * 3 * D
                load_rows(kb, c, rows["sh"], Tmodrows, kind, seq, base, q="pool")
                load_rows(kb, c, rows["sc"], Tmodrows, kind, seq, base + D, q="pool")
                load_rows(kb, c, rows["g"], Tmodrows, kind, seq, base + 2 * D, q="pool")
            subs = subtiles(n)
            xs = []
            for si, (r0, nr) in enumerate(subs):
                xt = xin[xslot]
                Tx = Txin[xslot]
                xslot = (xslot + 1) % 4
                xs.append((xt, Tx))
                kb.dma("sp", xt[:nr, :], src[kind][t0 + r0:t0 + r0 + nr, :], W=[Tx])
                kb.tt(xm[:nr, :], xt[:nr, :], rows["sc"][:nr, :], ALU.mult, R=[Tx, Tmodrows], W=[Txm], eng="pool")
                kb.tt(xm[:nr, :], xm[:nr, :], rows["sh"][:nr, :], ALU.add, R=[Txm, Tmodrows], W=[Txm], eng="pool")
                py = psy[si % 2]
                for k in range(8):
                    kb.tr(py[:, k * 128:k * 128 + nr], xm[:nr, k * 128:(k + 1) * 128], c.ident_f[:nr, :nr],
                          R=[Txm, c.Tconst], W=[Tpsy[si % 2]])
                kb.copy(xmT[:, :, r0:r0 + nr], py[:, :].rearrange("p (k n) -> p k n", k=8)[:, :, :nr],
                        R=[Tpsy[si % 2]], W=[TxmT], eng="act")
            for j in range(NJ):
                pg = psgu[(j % 2) * 2]
                pu = psgu[(j % 2) * 2 + 1]
                Tg_, Tu_ = Tpsgu[(j % 2) * 2], Tpsgu[(j % 2) * 2 + 1]
                cg = j * 128
                cu = DFF + j * 128
                for k in range(8):
                    kb.mm(pg[:, :n], wgu[:, k, cg:cg + 128], xmT[:, k, :n], k == 0, k == 7,
                          R=[TxmT, Tgu[cg // 512]], W=[Tg_])
                for k in range(8):
                    kb.mm(pu[:, :n], wgu[:, k, cu:cu + 128], xmT[:, k, :n], k == 0, k == 7,
                          R=[TxmT, Tgu[cu // 512]], W=[Tu_])
                s = sgt[j % 2]
                kb.act(s[:, :n], pg[:, :n], AF.Silu, R=[Tg_], W=[Tsg[j % 2]])
                kb.tt(hT[:, j, :n], s[:, :n], pu[:, :n], ALU.mult, R=[Tsg[j % 2], Tu_], W=[ThT[j]])
            for si, (r0, nr) in enumerate(subs):
                py = psy[si % 2]
                Tpy = Tpsy[si % 2]
                for half in range(2):
                    for j in range(NJ):
                        kb.mm(py[:nr, half * 512:(half + 1) * 512], hT[:, j, r0:r0 + nr],
                              wdn[:, j, half * 512:(half + 1) * 512], j == 0, j == NJ - 1,
                              R=[ThT[j], Tdn[j // 2]], W=[Tpy])
                w = wk[wslot]
                Tw_ = Twk[wslot]
                wslot = (wslot + 1) % 2
                xt, Tx = xs[si]
                kb.tt(w[:nr, :], py[:nr, :], rows["g"][:nr, :], ALU.mult, R=[Tpy, Tmodrows], W=[Tw_])
                kb.stt(w[:nr, :], xt[:nr, :], ALPHA, w[:nr, :], ALU.mult, ALU.add, R=[Tx, Tw_], W=[Tw_])
                layer_norm_rows(kb, c, w, nr, Tw_, rows["lng"], rows["lnb"], Tln, st, mv, rstd, Tst)
                kb.dma("sp", dst[kind][t0 + r0:t0 + r0 + nr, :], w[:nr, :], R=[Tw_])
            gi += 1
        kb.barrier()


QKV0, Z0, Q0, KV0, GN0, MG0 = 0, 1536, 2064, 2576, 3344, 3368


def phase_win(kb, c):
    nc, cfg = c.nc, c.cfg
    G = 512
    SEQ, NP, NS = cfg["SEQ"], cfg["NP"], cfg["NS"]
    TP, TS = c.TP, c.TS
    with contextlib.ExitStack() as ps:
        sb = lambda n, s, d=F32: ps.enter_context(nc.sbuf_tensor("wi_" + n, s, d))
        win = sb("w", [128, 8, DIN], BF16)
        hT = sb("hT", [128, 8, G], BF16)
        xin = [sb("xin%d" % i, [128, D]) for i in range(2)]
        xm = sb("xm", [128, D])
        rows = {n: sb("row_" + n, [128, D]) for n in ("sc", "sh")}
        stF = [sb("stF%d" % i, [128, 4, G]) for i in range(2)]
        stB = [sb("stB%d" % i, [128, 4, G], BF16) for i in range(3)]
        stT = [sb("stT%d" % i, [128, 1536]) for i in range(2)]
        psx = ps.enter_context(nc.psum_tensor("wi_psx", [128, 1024], F32))
        psF = [ps.enter_context(nc.psum_tensor("wi_psF%d" % i, [128, 512], F32)) for i in range(3)]
        psT = ps.enter_context(nc.psum_tensor("wi_psT", [128, 1536], F32))
        Tw = [Tok() for _ in range(11)]
        ThT, Txm, Tmodrows, Tpsx, TpsT = Tok(), Tok(), Tok(), Tok(), Tok()
        Txin = [Tok(), Tok()]
        TstF = [Tok(), Tok()]
        TstB = [Tok(), Tok(), Tok()]
        TstT = [Tok(), Tok()]
        TpsF = [Tok(), Tok(), Tok()]
        wv = c.w_in.rearrange("(k p) f -> p k f", p=128)
        for i in range(11):
            hi = min(DIN, (i + 1) * 512)
            kb.dma("pool", win[:, :, i * 512:hi], wv[:, :, i * 512:hi], W=[Tw[i]])

        def wtoks(c0, c1):
            return [Tw[i] for i in range(c0 // 512, (c1 - 1) // 512 + 1)]

        wl = cfg["WIN"]
        kb.dma("sp", c.s_win[:, 0:wl - LS, :], c.cache_win[:, LS:wl, :])
        cur = None
        xslot = 0
        cnt = {"F": 0, "B": 0, "T": 0, "pf": 0, "ev": 0}
        for (kind, seq, t0, n) in token_groups(cfg, G):
            tg = t0 if kind == "p" else TP + t0
            if (kind, seq) != cur:
                cur = (kind, seq)
                load_rows(kb, c, rows["sh"], Tmodrows, kind, seq, 3 * D)
                load_rows(kb, c, rows["sc"], Tmodrows, kind, seq, 4 * D)
            subs = subtiles(n)
            for si, (r0, nr) in enumerate(subs):
                xt, Tx = xin[xslot], Txin[xslot]
                xslot = (xslot + 1) % 2
                kb.dma("sp", xt[:nr, :], c.x1[kind][t0 + r0:t0 + r0 + nr, :], W=[Tx])
                kb.tt(xm[:nr, :], xt[:nr, :], rows["sc"][:nr, :], ALU.mult, R=[Tx, Tmodrows], W=[Txm], eng="pool")
                kb.tt(xm[:nr, :], xm[:nr, :], rows["sh"][:nr, :], ALU.add, R=[Txm, Tmodrows], W=[Txm], eng="pool")
                for k in range(8):
                    kb.tr(psx[:, k * 128:k * 128 + nr], xm[:nr, k * 128:(k + 1) * 128], c.ident_f[:nr, :nr],
                          R=[Txm, c.Tconst], W=[Tpsx])
                kb.copy(hT[:, :, r0:r0 + nr], psx[:, :].rearrange("p (k n) -> p k n", k=8)[:, :, :nr],
                        R=[Tpsx], W=[ThT], eng="act")

            def fgroup(cols, stage_kind, dst_ap, func=None):
                if stage_kind == "F":
                    st, Ts = stF[cnt["F"] % 2], TstF[cnt["F"] % 2]
                    cnt["F"] += 1
                else:
                    st, Ts = stB[cnt["B"] % 3], TstB[cnt["B"] % 3]
                    cnt["B"] += 1
                for ci, c0 in enumerate(cols):
                    pf, Tpf = psF[cnt["pf"] % 3], TpsF[cnt["pf"] % 3]
                    cnt["pf"] += 1
                    for k in range(8):
                        kb.mm(pf[:, :n], win[:, k, c0:c0 + 128], hT[:, k, :n], k == 0, k == 7,
                              R=[ThT] + wtoks(c0, c0 + 128), W=[Tpf])
                    if func is not None:
                        kb.act(st[:, ci, :n], pf[:, :n], func, R=[Tpf], W=[Ts])
                    else:
                        eng = "act" if cnt["ev"] % 2 == 0 else "dve"
                        cnt["ev"] += 1
                        kb.copy(st[:, ci, :n], pf[:, :n], R=[Tpf], W=[Ts], eng=eng)
                kb.dma("sp", dst_ap, st[:, 0:len(cols), :n], R=[Ts])

            qv = c.qkvT_d.rearrange("(c p) t -> p c t", p=128)
            for g4 in range(3):
                fgroup([QKV0 + (g4 * 4 + i) * 128 for i in range(4)], "F", qv[:, g4 * 4:g4 * 4 + 4, tg:tg + n])
            fgroup([Q0 + i * 128 for i in range(4)], "B", c.qT_d.rearrange("(c p) t -> p c t", p=128)[:, :, tg:tg + n])
            fgroup([KV0, KV0 + 256, KV0 + 512], "B", c.kT_d.rearrange("i p t -> p i t")[:, :, tg:tg + n])
            mv_ = c.mgT_d.rearrange("(c p) t -> p c t", p=128)
            for g4 in range(4):
                fgroup([MG0 + (g4 * 4 + i) * 128 for i in range(4)], "B", mv_[:, g4 * 4:g4 * 4 + 4, tg:tg + n], func=AF.Sigmoid)

            for si, (r0, nr) in enumerate(subs):
                def tgroup(c0, ncols):
                    st, Ts = stT[cnt["T"] % 2], TstT[cnt["T"] % 2]
                    cnt["T"] += 1
                    for b0 in range(0, ncols, 512):
                        w_ = min(512, ncols - b0)
                        for k in range(8):
                            kb.mm(psT[:nr, b0:b0 + w_], hT[:, k, r0:r0 + nr], win[:, k, c0 + b0:c0 + b0 + w_],
                                  k == 0, k == 7, R=[ThT] + wtoks(c0 + b0, c0 + b0 + w_), W=[TpsT])
                    kb.copy(st[:nr, :ncols], psT[:nr, :ncols], R=[TpsT], W=[Ts])
                    return st, Ts

                a0 = t0 + r0
                st, Ts = tgroup(Z0, 528)
                kb.dma("sp", c.zab_d[tg + r0:tg + r0 + nr, :], st[:nr, :528], R=[Ts])
                st, Ts = tgroup(KV0, 792)
                kb.dma("sp", c.kvg_d[tg + r0:tg + r0 + nr, :], st[:nr, :792], R=[Ts])
                if kind == "p":
                    kb.dma("sp", c.p_cmp[a0:a0 + nr, :], st[:nr, 0:256], R=[Ts])
                    kb.dma("sp", c.p_sel[a0:a0 + nr, :], st[:nr, 256:512], R=[Ts])
                    pos = a0 - seq * SEQ
                    w0 = SEQ - min(wl, SEQ)
                    if pos >= w0:
                        wr = seq * min(wl, SEQ) + pos - w0
                        kb.dma("sp", c.p_win[wr:wr + nr, :], st[:nr, 512:768], R=[Ts])
                else:
                    kb.dma("sp", c.s_cmp[a0:a0 + nr, :], st[:nr, 0:256], R=[Ts])
                    kb.dma("sp", c.s_sel[a0:a0 + nr, :], st[:nr, 256:512], R=[Ts])
                    for l in range(LS):
                        kb.dma("sp", c.s_win[:, wl - LS + l, :], st[l:nr:LS, 512:768], R=[Ts])
                last_of_seq = (kind == "p" and a0 + nr == (seq + 1) * SEQ)
                if last_of_seq or kind == "s":
                    st, Ts = tgroup(QKV0, 1536)
                    if kind == "p":
                        kb.dma("sp", c.p_conv[seq, :, :], st[nr - 3:nr, :1536], R=[Ts])
                    else:
                        for l in range(1, LS):
                            kb.dma("sp", c.s_conv[:, l - 1, :], st[l:nr:LS, :1536], R=[Ts])
        kb.barrier()


def phase_gdn_a(kb, c):
    nc, cfg = c.nc, c.cfg
    G = 512
    SEQ, NP, NS = cfg["SEQ"], cfg["NP"], cfg["NS"]
    TP, TS = c.TP, c.TS
    with contextlib.ExitStack() as ps:
        sb = lambda n, s, d=F32: ps.enter_context(nc.sbuf_tensor("ga_" + n, s, d))
        cwt = sb("cwt", [4, 1536])
        cw = sb("cw", [128, 12, 4])
        blk1 = sb("blk1", [128, 128])
        xr = [sb("xr%d" % i, [128, 12, 3 + G]) for i in range(2)]
        yc = [sb("yc%d" % i, [128, 12, G]) for i in range(2)]
        sq = sb("sq", [128, 8, G])
        rn = [sb("rn%d" % i, [128, G]) for i in range(2)]
        kvt = sb("kvt", [128, 4, 1024])
        cbt = sb("cbt", [NS * 3, 1536])
        xrs = sb("xrs", [128, 12, NS, 3 + LS])
        pss = [ps.enter_context(nc.psum_tensor("ga_pss%d" % i, [128, 512], F32)) for i in range(2)]
        pst = [ps.enter_context(nc.psum_tensor("ga_pst%d" % i, [128, 1024], F32)) for i in range(2)]
        Tcw, Tblk, Tsq, Tkvt, Tcbt, Txrs = Tok(), Tok(), Tok(), Tok(), Tok(), Tok()
        Txr = [Tok(), Tok()]
        Tyc = [Tok(), Tok()]
        Trn = [Tok(), Tok()]
        Tpss = [Tok(), Tok()]
        Tpst = [Tok(), Tok()]
        kb.dma("sp", cwt[:], c.conv_w, W=[Tcw])
        kb.dma("sp", blk1[:], c.blk1_d, W=[Tblk])
        for cc in range(12):
            kb.tr(pss[0][:, cc * 4:cc * 4 + 4], cwt[:, cc * 128:(cc + 1) * 128], c.ident_f[:4, :4], R=[Tcw, c.Tconst], W=[Tpss[0]])
        kb.copy(cw[:, :, :], pss[0][:, 0:48].rearrange("p (c j) -> p c j", j=4), R=[Tpss[0]], W=[Tcw])
        qv = c.qkvT_d.rearrange("(c p) t -> p c t", p=128)
        qn = c.qkvn_d.rearrange("(c p) t -> p c t", p=128)
        bi = 0
        for (kind, seq, t0, n) in token_groups(cfg, G):
            tg = t0 if kind == "p" else TP + t0
            y, Ty = yc[bi % 2], Tyc[bi % 2]
            x, Tx = xr[bi % 2], Txr[bi % 2]
            bi += 1
            if kind == "p":
                if t0 % SEQ == 0:
                    kb.memset(x[:, :, 0:3], 0.0, W=[Tx])
                    kb.dma("sp", x[:, :, 3:3 + n], qv[:, :, tg:tg + n], W=[Tx])
                else:
                    kb.dma("sp", x[:, :, 0:3 + n], qv[:, :, tg - 3:tg + n], W=[Tx])
                for cc in range(12):
                    eng = "dve"
                    kb.ts(y[:, cc, :n], x[:, cc, 0:n], cw[:, cc, 0:1], None, ALU.mult, R=[Tx, Tcw], W=[Ty], eng=eng)
                    for j in range(1, 4):
                        kb.stt(y[:, cc, :n], x[:, cc, j:j + n], cw[:, cc, j:j + 1], y[:, cc, :n], ALU.mult, ALU.add,
                               R=[Tx, Tcw, Ty], W=[Ty], eng=eng)
            else:
                kb.dma("sp", cbt[:], c.conv_buf.rearrange("b r f -> (b r) f"), W=[Tcbt])
                for cc in range(12):
                    kb.tr(pst[0][:, cc * 64:cc * 64 + NS * 3], cbt[:, cc * 128:(cc + 1) * 128], c.ident_f[:NS * 3, :NS * 3],
                          R=[Tcbt, c.Tconst], W=[Tpst[0]])
                kb.copy(xrs[:, :, :, 0:3],
                        pst[0][:, 0:768].rearrange("p (c x) -> p c x", x=64)[:, :, 0:NS * 3].rearrange("p c (b r) -> p c b r", r=3),
                        R=[Tpst[0]], W=[Txrs])
                for cc in range(12):
                    kb.dma("sp", xrs[:, cc, :, 3:3 + LS], qv[:, cc, tg:tg + n].rearrange("p (b l) -> p b l", l=LS), W=[Txrs])
                for cc in range(12):
                    eng = "dve"
                    yv = y[:, cc, :n].rearrange("p (b l) -> p b l", l=LS)
                    kb.ts(yv, xrs[:, cc, :, 0:LS], cw[:, cc, 0:1], None, ALU.mult, R=[Txrs, Tcw], W=[Ty], eng=eng)
                    for j in range(1, 4):
                        kb.stt(yv, xrs[:, cc, :, j:j + LS], cw[:, cc, j:j + 1], yv, ALU.mult, ALU.add,
                               R=[Txrs, Tcw, Ty], W=[Ty], eng=eng)
            kb.act(y[:, :, :n], y[:, :, :n], AF.Silu, R=[Ty], W=[Ty])
            kb.tt(sq[:, :, :n], y[:, 0:8, :n], y[:, 0:8, :n], ALU.mult, R=[Ty], W=[Tsq], eng="pool")
            for cc in range(8):
                p_, Tp_ = pss[cc % 2], Tpss[cc % 2]
                r_, Tr_ = rn[cc % 2], Trn[cc % 2]
                kb.mm(p_[:, :n], blk1[:, :], sq[:, cc, :n], True, True, R=[Tsq, Tblk], W=[Tp_])
                if cc < 4:
                    kb.act(r_[:, :n], p_[:, :n], AF.Sqrt, R=[Tp_], W=[Tr_], scale=64.0, bias=64e-6)
                else:
                    kb.act(r_[:, :n], p_[:, :n], AF.Sqrt, R=[Tp_], W=[Tr_], bias=1e-6)
                kb.op("dve", lambda e, r_=r_: e.reciprocal(out=r_[:, :n], in_=r_[:, :n]), [Tr_], [Tr_])
                kb.tt(y[:, cc, :n], y[:, cc, :n], r_[:, :n], ALU.mult, R=[Ty, Tr_], W=[Ty])
            kb.dma("sp", qn[:, :, tg:tg + n], y[:, :, :n], R=[Ty])
            subs = subtiles(n)
            for si, (r0, nr) in enumerate(subs):
                p_, Tp_ = pst[si % 2], Tpst[si % 2]
                for cc in range(8):
                    kb.tr(p_[:nr, cc * 128:(cc + 1) * 128], y[:, 4 + cc, r0:r0 + nr], c.ident_f[:, :], R=[Ty, c.Tconst], W=[Tp_])
                kb.copy(kvt[:nr, si, :], p_[:nr, :], R=[Tp_], W=[Tkvt], eng=("act" if si % 2 == 0 else "dve"))
                kb.dma("sp", c.kvtok_d[tg + r0:tg + r0 + nr, :], kvt[:nr, si, :], R=[Tkvt])
        kb.barrier()


def phase_gdn_b(kb, c):
    nc, cfg = c.nc, c.cfg
    SEQ, NP, NS = cfg["SEQ"], cfg["NP"], cfg["NS"]
    TP, TS = c.TP, c.TS
    C = 64
    HB = 4
    with contextlib.ExitStack() as ps:
        sb = lambda n, s, d=F32: ps.enter_context(nc.sbuf_tensor("gb_" + n, s, d))
        gc = sb("const", [64, 5, 64])
        nA = sb("nA", [64, 8])
        dtb = sb("dtb", [64, 8])
        normw = sb("normw", [64, 64])
        valid = sb("valid", [64, 1])
        qTb = [sb("qTb%d" % i, [64, 8, HB * C]) for i in range(2)]
        kTb = [sb("kTb%d" % i, [64, 8, HB * C]) for i in range(2)]
        kvb = [sb("kvb%d" % i, [64, HB, 1024]) for i in range(2)]
        zab = [sb("zab%d" % i, [64, HB, 528]) for i in range(2)]
        gt = sb("gt", [64, HB, 8])
        bt = sb("bt", [64, HB, 8])
        nbt = sb("nbt", [64, HB, 8])
        gtmp = sb("gtmp", [64, HB, 8])
        nwz = sb("nwz", [64, HB, 512])
        S = sb("S", [64, 8, 64])
        og = [sb("og%d" % i, [64, HB, 512], BF16) for i in range(2)]
        w3 = lambda n: sb(n, [64, 8, 64])
        GG, EG, E2, bg = sb("GG", [64, 16]), sb("EG", [64, 16]), sb("E2", [64, 8]), sb("bg", [64, 8])
        gmat, tmp1, tmpT, Dt, Dn, EGr = w3("gmat"), w3("tmp1"), w3("tmpT"), w3("Dt"), w3("Dn"), w3("EGr")
        Xb = [w3("X0"), w3("X1")]
        Yb = [w3("Y0"), w3("Y1")]
        Qb = [w3("Q0"), w3("Q1")]
        vb, kbg, u, wT, aqkT, qd, kd, vn, oc, sqo = (w3(n) for n in ("vb", "kbg", "u", "wT", "aqkT", "qd", "kd", "vn", "oc", "sqo"))
        ss, lnv = sb("ss", [64, 8]), sb("lnv", [64, 8])
        banks = [ps.enter_context(nc.psum_tensor("gb_ps%d" % i, [128, 512], F32)) for i in range(8)]
        Tb = [Tok() for _ in range(8)]
        bstate = {"i": 0}

        def bank():
            i = bstate["i"] % 8
            bstate["i"] += 1
            return banks[i], Tb[i]

        names = ("gc nA dtb normw valid gt bt nwz S GG EG E2 bg gmat tmp1 tmpT Dt Dn EGr vb kbg u wT aqkT qd kd vn oc sqo ss lnv").split()
        T = {n: Tok(n) for n in names}
        TX, TY, TQ = [Tok(), Tok()], [Tok(), Tok()], [Tok(), Tok()]
        Tq, Tk, Tkv, Tz, Tog = [Tok(), Tok()], [Tok(), Tok()], [Tok(), Tok()], [Tok(), Tok()], [Tok(), Tok()]
        U_, ones_, mup, mlo, idn = (gc[:, i, :] for i in range(5))
        kb.dma("sp", gc[:], c.gconst_d, W=[T["gc"]])
        kb.dma("sp", nA[:], c.a_log.partition_broadcast(64), W=[T["nA"]])
        kb.dma("sp", dtb[:], c.dt_bias.partition_broadcast(64), W=[T["dtb"]])
        kb.dma("sp", normw[:], c.norm_w.partition_broadcast(64), W=[T["normw"]])
        kb.dma("sp", valid[:], c.valid_d, W=[T["valid"]])
        kb.act(nA[:], nA[:], AF.Exp, R=[T["nA"]], W=[T["nA"]])
        kb.ts(nA[:], nA[:], -1.0, None, ALU.mult, R=[T["nA"]], W=[T["nA"]])
        bc_h = lambda ap: ap.unsqueeze(2).broadcast_to([64, 8, 64])
        bc_m = lambda ap: ap.unsqueeze(1).broadcast_to([64, 8, 64])
        v3 = lambda ap: ap.rearrange("p (h d) -> p h d", h=8)
        qn_q = c.qkvn_d[0:512, :].rearrange("(h d) t -> d h t", d=64)
        qn_k = c.qkvn_d[512:1024, :].rearrange("(h d) t -> d h t", d=64)

        def mm8(lhs, rhs, pair2=None):
            p_, Tp_ = bank()
            return p_, Tp_

        def gates(slot, nch, sample):
            z = zab[slot]
            Tz_ = Tz[slot]
            a_ = z[:, :nch, 512:520]
            b_ = z[:, :nch, 520:528]
            dtb_b = dtb[:].unsqueeze(1).broadcast_to([64, nch, 8])
            nA_b = nA[:].unsqueeze(1).broadcast_to([64, nch, 8])
            kb.tt(gtmp[:, :nch, :], a_, dtb_b, ALU.add, R=[Tz_, T["dtb"]], W=[T["gt"]])
            kb.act(gtmp[:, :nch, :], gtmp[:, :nch, :], AF.Exp, R=[T["gt"]], W=[T["gt"]])
            kb.act(gtmp[:, :nch, :], gtmp[:, :nch, :], AF.Ln, R=[T["gt"]], W=[T["gt"]], bias=1.0)
            kb.tt(gt[:, :nch, :], gtmp[:, :nch, :], nA_b, ALU.mult, R=[T["gt"], T["nA"]], W=[T["gt"]])
            kb.act(bt[:, :nch, :], b_, AF.Exp, R=[Tz_], W=[T["bt"]], scale=-1.0)
            kb.ts(bt[:, :nch, :], bt[:, :nch, :], 1.0, None, ALU.add, R=[T["bt"]], W=[T["bt"]])
            kb.op("dve", lambda e: e.reciprocal(out=bt[:, :nch, :], in_=bt[:, :nch, :]), [T["bt"]], [T["bt"]])
            if sample:
                kb.ts(gt[:, :nch, :], gt[:, :nch, :], valid[:, 0:1], None, ALU.mult, R=[T["gt"], T["valid"]], W=[T["gt"]])
                kb.ts(bt[:, :nch, :], bt[:, :nch, :], valid[:, 0:1], None, ALU.mult, R=[T["bt"], T["valid"]], W=[T["bt"]])
            kb.ts(nbt[:, :nch, :], bt[:, :nch, :], -1.0, None, ALU.mult, R=[T["bt"]], W=[T["bt"]])
            zz = z[:, :nch, 0:512]
            kb.act(nwz[:, :nch, :], zz, AF.Exp, R=[Tz_], W=[T["nwz"]], scale=-1.0)
            kb.ts(nwz[:, :nch, :], nwz[:, :nch, :], 1.0, None, ALU.add, R=[T["nwz"]], W=[T["nwz"]], eng="pool")
            kb.op("dve", lambda e: e.reciprocal(out=nwz[:, :nch, :], in_=nwz[:, :nch, :]), [T["nwz"]], [T["nwz"]])
            kb.tt(nwz[:, :nch, :], nwz[:, :nch, :], zz, ALU.mult, R=[T["nwz"], Tz_], W=[T["nwz"]], eng="pool")
            for ci in range(nch):
                kb.tt(v3(nwz[:, ci, :]), v3(nwz[:, ci, :]), bc_m(normw[:, :]), ALU.mult, R=[T["nwz"], T["normw"]], W=[T["nwz"]], eng="pool")

        def chunk(slot, ci):
            tb = ci * C
            qT = qTb[slot][:, :, tb:tb + C]
            kT = kTb[slot][:, :, tb:tb + C]
            ktok = v3(kvb[slot][:, ci, 0:512])
            vtok = v3(kvb[slot][:, ci, 512:1024])
            g = gt[:, ci, :]
            beta = bt[:, ci, :]
            nbeta = nbt[:, ci, :]
            Rq, Rk, Rkv = Tq[slot], Tk[slot], Tkv[slot]
            pa, Tpa = bank()
            kb.mm(pa[:64, 0:8], U_, g, True, True, R=[T["gc"], T["gt"]], W=[Tpa], sig=False)
            kb.mm(pa[:64, 8:16], ones_, g, True, True, R=[T["gc"], T["gt"]], W=[Tpa])
            kb.copy(GG[:, :], pa[:64, 0:16], R=[Tpa], W=[T["GG"]])
            kb.copy(gmat[:], bc_h(g), R=[T["gt"]], W=[T["gmat"]], eng="pool")
            pgr, Tpgr = bank()
            for h in range(8):
                kb.mm(pgr[:64, h * 64:(h + 1) * 64], gmat[:, h, :], U_, True, True, R=[T["gmat"], T["gc"]], W=[Tpgr], sig=(h == 7))
            kb.act(EG[:, :], GG[:, :], AF.Exp, R=[T["GG"]], W=[T["EG"]])
            kb.tt(E2[:, :], GG[:, 8:16], GG[:, 0:8], ALU.subtract, R=[T["GG"]], W=[T["E2"]])
            kb.act(E2[:, :], E2[:, :], AF.Exp, R=[T["E2"]], W=[T["E2"]])
            kb.tt(tmp1[:], v3(pgr[:64, :]), bc_h(GG[:, 0:8]), ALU.subtract, R=[Tpgr, T["GG"]], W=[T["tmp1"]])
            kb.tt(tmpT[:], tmp1[:], bc_m(mup), ALU.add, R=[T["tmp1"], T["gc"]], W=[T["tmpT"]], eng="pool")
            kb.act(Dt[:], tmpT[:], AF.Exp, R=[T["tmpT"]], W=[T["Dt"]])
            kb.tt(tmpT[:], tmp1[:], bc_m(mlo), ALU.add, R=[T["tmp1"], T["gc"], T["Dt"]], W=[T["tmpT"]], eng="pool")
            kb.act(Dn[:], tmpT[:], AF.Exp, R=[T["tmpT"]], W=[T["Dn"]], scale=-1.0)
            kb.act(EGr[:], v3(pgr[:64, :]), AF.Exp, R=[Tpgr], W=[T["EGr"]])
            pkk, Tpkk = bank()
            for h in range(8):
                kb.mm(pkk[:64, h * 64:(h + 1) * 64], kT[:, h, :], kT[:, h, :], True, True, R=[Rk], W=[Tpkk], sig=(h == 7))
            X, Y, Q = Xb[0], Yb[0], Qb[0]
            kb.tt(X[:], v3(pkk[:64, :]), Dn[:], ALU.mult, R=[Tpkk, T["Dn"]], W=[TX[0]])
            kb.tt(X[:], X[:], bc_h(nbeta), ALU.mult, R=[TX[0], T["bt"]], W=[TX[0]])
            pt_, Tpt = bank()
            for h in range(8):
                kb.tr(pt_[:64, h * 64:(h + 1) * 64], X[:, h, :], idn, R=[TX[0], T["gc"]], W=[Tpt], sig=(h == 7))
            kb.copy(Y[:], v3(pt_[:64, :]), R=[Tpt], W=[TY[0]], eng="act")
            kb.tt(Q[:], Y[:], bc_m(idn), ALU.add, R=[TY[0], T["gc"]], W=[TQ[0]])
            cur = 0
            for k in range(1, 6):
                nx = 1 - cur
                pX, TpX = bank()
                for h in range(8):
                    kb.mm(pX[:64, h * 64:(h + 1) * 64], Yb[cur][:, h, :], Xb[cur][:, h, :], True, True, R=[TX[cur], TY[cur]], W=[TpX], sig=(h == 7))
                if k < 5:
                    pY, TpY = bank()
                    for h in range(8):
                        kb.mm(pY[:64, h * 64:(h + 1) * 64], Xb[cur][:, h, :], Yb[cur][:, h, :], True, True, R=[TX[cur], TY[cur]], W=[TpY], sig=(h == 7))
                kb.copy(Xb[nx][:], v3(pX[:64, :]), R=[TpX], W=[TX[nx]], eng="act")
                if k < 5:
                    kb.copy(Yb[nx][:], v3(pY[:64, :]), R=[TpY], W=[TY[nx]], eng="act")
                pQ, TpQ = bank()
                for h in range(8):
                    kb.mm(pQ[:64, h * 64:(h + 1) * 64], Xb[nx][:, h, :], Qb[cur][:, h, :], True, True, R=[TX[nx], TQ[cur]], W=[TpQ], sig=(h == 7))
                kb.tt(Qb[nx][:], Qb[cur][:], v3(pQ[:64, :]), ALU.add, R=[TQ[cur], TpQ], W=[TQ[nx]])
                cur = nx
            Q, TQc = Qb[cur], TQ[cur]
            kb.tt(vb[:], vtok, bc_h(beta), ALU.mult, R=[Rkv, T["bt"]], W=[T["vb"]])
            kb.tt(bg[:, :], beta, EG[:, 0:8], ALU.mult, R=[T["bt"], T["EG"]], W=[T["bg"]])
            kb.tt(kbg[:], ktok, bc_h(bg[:, :]), ALU.mult, R=[Rkv, T["bg"]], W=[T["kbg"]], eng="pool")
            pu, Tpu = bank()
            for h in range(8):
                kb.mm(pu[:64, h * 64:(h + 1) * 64], Q[:, h, :], vb[:, h, :], True, True, R=[TQc, T["vb"]], W=[Tpu], sig=(h == 7))
            kb.copy(u[:], v3(pu[:64, :]), R=[Tpu], W=[T["u"]], eng="act")
            pw, Tpw = bank()
            for h in range(8):
                kb.mm(pw[:64, h * 64:(h + 1) * 64], kbg[:, h, :], Q[:, h, :], True, True, R=[TQc, T["kbg"]], W=[Tpw], sig=(h == 7))
            kb.copy(wT[:], v3(pw[:64, :]), R=[Tpw], W=[T["wT"]], eng="act")
            pq, Tpq = bank()
            for h in range(8):
                kb.mm(pq[:64, h * 64:(h + 1) * 64], kT[:, h, :], qT[:, h, :], True, True, R=[Rk, Rq], W=[Tpq], sig=(h == 7))
            kb.tt(aqkT[:], v3(pq[:64, :]), Dt[:], ALU.mult, R=[Tpq, T["Dt"]], W=[T["aqkT"]])
            kb.tt(qd[:], qT, EGr[:], ALU.mult, R=[Rq, T["EGr"]], W=[T["qd"]], eng="pool")
            kb.tt(kd[:], ktok, bc_h(E2[:, :]), ALU.mult, R=[Rkv, T["E2"]], W=[T["kd"]], eng="pool")
            pws, Tpws = bank()
            for h in range(8):
                kb.mm(pws[:64, h * 64:(h + 1) * 64], wT[:, h, :], S[:, h, :], True, True, R=[T["wT"], T["S"]], W=[Tpws], sig=(h == 7))
            kb.tt(vn[:], u[:], v3(pws[:64, :]), ALU.subtract, R=[T["u"], Tpws], W=[T["vn"]])
            po, Tpo = bank()
            for h in range(8):
                kb.mm(po[:64, h * 64:(h + 1) * 64], qd[:, h, :], S[:, h, :], True, False, R=[T["qd"], T["S"]], W=[Tpo], sig=False)
                kb.mm(po[:64, h * 64:(h + 1) * 64], aqkT[:, h, :], vn[:, h, :], False, True, R=[T["aqkT"], T["vn"]], W=[Tpo], sig=(h == 7))
            pds, Tpds = bank()
            for h in range(8):
                kb.mm(pds[:64, h * 64:(h + 1) * 64], kd[:, h, :], vn[:, h, :], True, True, R=[T["kd"], T["vn"]], W=[Tpds], sig=(h == 7))
            kb.tt(S[:], S[:], bc_h(EG[:, 8:16]), ALU.mult, R=[T["S"], T["EG"]], W=[T["S"]])
            kb.tt(S[:], S[:], v3(pds[:64, :]), ALU.add, R=[T["S"], Tpds], W=[T["S"]])
            kb.copy(oc[:], v3(po[:64, :]), R=[Tpo], W=[T["oc"]], eng="act")
            kb.tt(sqo[:], oc[:], oc[:], ALU.mult, R=[T["oc"]], W=[T["sqo"]], eng="pool")
            kb.op("dve", lambda e: e.tensor_reduce(out=ss[:, :], in_=sqo[:], axis=AX.X, op=ALU.add), [T["sqo"]], [T["ss"]])
            kb.act(lnv[:, :], ss[:, :], AF.Ln, R=[T["ss"]], W=[T["lnv"]], scale=1.0 / 64.0, bias=1e-6)
            kb.act(lnv[:, :], lnv[:, :], AF.Exp, R=[T["lnv"]], W=[T["lnv"]], scale=-0.5)
            kb.tt(oc[:], oc[:], bc_h(lnv[:, :]), ALU.mult, R=[T["oc"], T["lnv"]], W=[T["oc"]])
            kb.tt(v3(og[slot][:, ci, :]), oc[:], v3(nwz[:, ci, :]), ALU.mult, R=[T["oc"], T["nwz"]], W=[Tog[slot]])

        slot = 0
        for s in range(NP):
            kb.memset(S[:], 0.0, W=[T["S"]])
            for hb in range(SEQ // (HB * C)):
                tg = s * SEQ + hb * HB * C
                n = HB * C
                kb.dma("sp", qTb[slot][:, :, :], qn_q[:, :, tg:tg + n], W=[Tq[slot]])
                kb.dma("sp", kTb[slot][:, :, :], qn_k[:, :, tg:tg + n], W=[Tk[slot]])
                kb.dma("sp", kvb[slot][:, :, :], c.kvtok_d[tg:tg + n, :].rearrange("(c p) f -> p c f", p=C), W=[Tkv[slot]])
                kb.dma("sp", zab[slot][:, :, :], c.zab_d[tg:tg + n, :].rearrange("(c p) f -> p c f", p=C), W=[Tz[slot]])
                gates(slot, HB, False)
                for ci in range(HB):
                    chunk(slot, ci)
                kb.dma("sp", c.og_d[tg:tg + n, :].rearrange("(c p) f -> p c f", p=C), og[slot][:, :, :], R=[Tog[slot]])
                slot = 1 - slot
            kb.dma("sp", c.p_gdn[s].rearrange("h k v -> k h v"), S[:], R=[T["S"]])
        for sl in range(2):
            kb.memset(qTb[sl][:], 0.0, W=[Tq[sl]])
            kb.memset(kTb[sl][:], 0.0, W=[Tk[sl]])
            kb.memset(kvb[sl][:], 0.0, W=[Tkv[sl]])
            kb.memset(zab[sl][:], 0.0, W=[Tz[sl]])
        for b in range(NS):
            tg = TP + b * LS
            kb.dma("sp", S[:], c.state_gdn[b].rearrange("h k v -> k h v"), W=[T["S"]])
            kb.dma("sp", qTb[slot][:, :, 0:LS], qn_q[:, :, tg:tg + LS], W=[Tq[slot]])
            kb.dma("sp", kTb[slot][:, :, 0:LS], qn_k[:, :, tg:tg + LS], W=[Tk[slot]])
            kb.dma("sp", kvb[slot][0:LS, 0, :], c.kvtok_d[tg:tg + LS, :], W=[Tkv[slot]])
            kb.dma("sp", zab[slot][0:LS, 0, :], c.zab_d[tg:tg + LS, :], W=[Tz[slot]])
            gates(slot, 1, True)
            chunk(slot, 0)
            kb.dma("sp", c.og_d[tg:tg + LS, :], og[slot][0:LS, 0, :], R=[Tog[slot]])
            kb.dma("sp", c.s_gdn[b].rearrange("h k v -> k h v"), S[:], R=[T["S"]])
            slot = 1 - slot
        kb.barrier()


def phase_mix(kb, c):
    nc, cfg = c.nc, c.cfg
    G = 512
    TP, TS = c.TP, c.TS
    with contextlib.ExitStack() as ps:
        sb = lambda n, s, d=F32: ps.enter_context(nc.sbuf_tensor("mx_" + n, s, d))
        wbg = sb("wbg", [128, 4, D], BF16)
        wbn = sb("wbn", [128, 4, D], BF16)
        wout = sb("wout", [128, 8, D], BF16)
        ogT = sb("ogT", [128, 4, G], BF16)
        onT = sb("onT", [128, 4, G], BF16)
        mg = sb("mg", [128, 16, G], BF16)
        mT = sb("mT", [128, 8, G], BF16)
        tk = [sb("tk%d" % i, [128, 1024], BF16) for i in range(2)]
        t1 = [sb("t1_%d" % i, [128, G]) for i in range(2)]
        t2 = [sb("t2_%d" % i, [128, G]) for i in range(2)]
        xin = [sb("xin%d" % i, [128, D]) for i in range(2)]
        wk = [sb("wk%d" % i, [128, D]) for i in range(2)]
        rows = {n: sb("row_" + n, [128, D]) for n in ("g", "lng", "lnb")}
        st = sb("st", [128, 2, nc.vector.BN_STATS_DIM])
        mv = sb("mv", [128, nc.vector.BN_AGGR_DIM])
        rstd = sb("rstd", [128, 1])
        pstr = [ps.enter_context(nc.psum_tensor("mx_pstr%d" % i, [128, 1024], BF16)) for i in range(2)]
        psb = [ps.enter_context(nc.psum_tensor("mx_psb%d" % i, [128, 512], F32)) for i in range(4)]
        psy = ps.enter_context(nc.psum_tensor("mx_psy", [128, 1024], F32))
        Tw, TogT, TonT, Tmg, TmT, Tln, Tmodrows, Tst, Tpsy = (Tok() for _ in range(9))
        Ttk, Tt1, Tt2, Txin, Twk, Tpstr = ([Tok(), Tok()] for _ in range(6))
        Tpsb = [Tok() for _ in range(4)]
        kb.dma("pool", wbg[:], c.w_br_gdn.rearrange("(k p) d -> p k d", p=128), W=[Tw])
        kb.dma("pool", wbn[:], c.w_br_nsa.rearrange("(k p) d -> p k d", p=128), W=[Tw])
        kb.dma("pool", wout[:], c.w_out.rearrange("(k p) d -> p k d", p=128), W=[Tw])
        kb.dma("sp", rows["lng"][:], c.ln_g[1:2, :].partition_broadcast(128), W=[Tln])
        kb.dma("sp", rows["lnb"][:], c.ln_b[1:2, :].partition_broadcast(128), W=[Tln])
        mgv = c.mgT_d.rearrange("(c p) t -> p c t", p=128)
        cur = None
        slot = 0
        for (kind, seq, t0, n) in token_groups(cfg, G):
            tg = t0 if kind == "p" else TP + t0
            if (kind, seq) != cur:
                cur = (kind, seq)
                load_rows(kb, c, rows["g"], Tmodrows, kind, seq, 5 * D)
            subs = subtiles(n)
            kb.dma("sp", mg[:, :, :n], mgv[:, :, tg:tg + n], W=[Tmg])
            for si, (r0, nr) in enumerate(subs):
                for which, (src, dstT, Td) in enumerate(((c.og_d, ogT, TogT), (c.on_d, onT, TonT))):
                    i2 = (2 * si + which) % 2
                    kb.dma("sp", tk[i2][:nr, 0:512], src[tg + r0:tg + r0 + nr, :], W=[Ttk[i2]])
                    for k in range(4):
                        kb.tr(pstr[i2][:, k * 128:k * 128 + nr], tk[i2][:nr, k * 128:(k + 1) * 128], c.ident_b[:nr, :nr],
                              R=[Ttk[i2], c.Tconst], W=[Tpstr[i2]])
                    kb.copy(dstT[:, :, r0:r0 + nr], pstr[i2][:, 0:512].rearrange("p (k n) -> p k n", k=4)[:, :, :nr],
                            R=[Tpstr[i2]], W=[Td], eng=("act" if which == 0 else "dve"))
            for dc in range(8):
                pg, Tpg = psb[(dc % 2) * 2], Tpsb[(dc % 2) * 2]
                pn, Tpn = psb[(dc % 2) * 2 + 1], Tpsb[(dc % 2) * 2 + 1]
                for k in range(4):
                    kb.mm(pg[:, :n], wbg[:, k, dc * 128:(dc + 1) * 128], ogT[:, k, :n], k == 0, k == 3, R=[Tw, TogT], W=[Tpg])
                for k in range(4):
                    kb.mm(pn[:, :n], wbn[:, k, dc * 128:(dc + 1) * 128], onT[:, k, :n], k == 0, k == 3, R=[Tw, TonT], W=[Tpn])
                a, Ta = t1[dc % 2], Tt1[dc % 2]
                b, Tb_ = t2[dc % 2], Tt2[dc % 2]
                kb.tt(a[:, :n], pg[:, :n], mg[:, dc, :n], ALU.mult, R=[Tpg, Tmg], W=[Ta])
                kb.tt(b[:, :n], pn[:, :n], mg[:, 8 + dc, :n], ALU.mult, R=[Tpn, Tmg], W=[Tb_])
                kb.tt(mT[:, dc, :n], a[:, :n], b[:, :n], ALU.add, R=[Ta, Tb_], W=[TmT], eng="pool")
            for si, (r0, nr) in enumerate(subs):
                xt, Tx = xin[slot], Txin[slot]
                w, Tw_ = wk[slot], Twk[slot]
                slot = 1 - slot
                kb.dma("sp", xt[:nr, :], c.x1[kind][t0 + r0:t0 + r0 + nr, :], W=[Tx])
                for half in range(2):
                    for k in range(8):
                        kb.mm(psy[:nr, half * 512:(half + 1) * 512], mT[:, k, r0:r0 + nr], wout[:, k, half * 512:(half + 1) * 512],
                              k == 0, k == 7, R=[TmT, Tw], W=[Tpsy])
                kb.tt(w[:nr, :], psy[:nr, :], rows["g"][:nr, :], ALU.mult, R=[Tpsy, Tmodrows], W=[Tw_])
                kb.stt(w[:nr, :], xt[:nr, :], ALPHA, w[:nr, :], ALU.mult, ALU.add, R=[Tx, Tw_], W=[Tw_])
                layer_norm_rows(kb, c, w, nr, Tw_, rows["lng"], rows["lnb"], Tln, st, mv, rstd, Tst)
                kb.dma("sp", c.x2[kind][t0 + r0:t0 + r0 + nr, :], w[:nr, :], R=[Tw_])
        kb.barrier()


NEGM = -30000.0


def nsa_consts(SEQ):
    t = np.arange(SEQ)
    nsel = SEQ // 64
    j = np.arange(nsel)
    valid = (j[None, :] * 64) <= t[:, None]
    cur = (t // 64)[:, None]
    forced = (j[None, :] == 0) | (j[None, :] == cur) | (j[None, :] == cur - 1)
    M1 = (valid & ~forced).astype(np.float32)
    M2 = np.where(valid, np.where(forced, 1e4 + j[None, :], 0.0), -1.0).astype(np.float32)
    ncmp = SEQ // 32
    n = np.arange(ncmp)
    cmask = np.where(((n[:, None] + 1) * 32 - 1) <= t[None, :], 0.0, NEGM).astype(np.float32)
    F = (np.arange(SEQ)[None, :] // 64 == j[:, None]).astype(np.float32)
    sk = np.arange(128)[:, None, None]
    a = np.arange(8)[None, :, None]
    tq = np.arange(512)[None, None, :]
    diff = tq - 128 * (a - 4) - sk
    Wm = np.where((diff >= 0) & (diff < 512), 0.0, NEGM).astype(np.float32)
    pair = (n[:, None] // 2 == j[None, :]).astype(np.float32)
    r = np.arange(128)
    maskW = np.zeros((128, 124), np.float32)
    maskW[r, 60 + r // 32] = 1.0
    return dict(nsa_M1=M1, nsa_M2=M2, nsa_cmask=cmask, nsa_F=F, nsa_Wm=Wm, nsa_pair=pair, nsa_maskW=maskW)


def phase_nsa_prompt(kb, c):
    nc, cfg = c.nc, c.cfg
    SEQ, NP = cfg["SEQ"], cfg["NP"]
    NKT = SEQ // 128
    NQG = SEQ // 512
    NSEL = SEQ // 64
    NCMP = SEQ // 32
    with contextlib.ExitStack() as ps:
        sb = lambda n, s, d=F32: ps.enter_context(nc.sbuf_tensor("np_" + n, s, d))
        qTh = sb("qTh", [64, 8, SEQ], BF16)
        KT = sb("KT", [64, 3, 2, SEQ], BF16)
        Vx = sb("Vx", [128, NKT, 3, 2, 65], BF16)
        kvc = sb("kvc", [128, NKT, 256], BF16)
        gn = sb("gn", [128, NKT, 24])
        M1 = sb("M1", [128, NKT, NSEL])
        M2 = sb("M2", [128, NKT, NSEL])
        cmask = sb("cmask", [NCMP, SEQ], BF16)
        Fm = sb("F", [NSEL, SEQ], BF16)
        Wm = sb("Wm", [128, 8, 512], BF16)
        pair = sb("pair", [NCMP, NSEL], BF16)
        maskW = sb("maskW", [128, 124])
        wcol = sb("wcol", [128, 2, 2])
        Wbig = sb("Wbig", [128, 4, 124], BF16)
        ON = sb("ON", [128, NKT, 512])
        imp = sb("imp", [128, NKT, 2, NSEL])
        selT = sb("selT", [NSEL, 2, SEQ], BF16)
        kvb_sb = sb("kvb", [NCMP, 256], BF16)
        KcT = sb("KcT", [64, 2, NCMP], BF16)
        Vc1P = sb("Vc1P", [NCMP, 2, 65 + NSEL], BF16)
        Pt = [sb("Pt%d" % i, [128, 512], BF16) for i in range(3)]
        sc = sb("sc", [128, NSEL])
        wk_ = sb("wk", [128, NSEL])
        m8a, m8b = sb("m8a", [128, 8]), sb("m8b", [128, 8])
        thr = sb("thr", [128, 1])
        selm = sb("selm", [128, NSEL])
        okm = sb("okm", [128, NSEL])
        rr = [sb("rr%d" % i, [128, 1]) for i in range(4)]
        rg = [sb("rg%d" % i, [128, 1]) for i in range(4)]
        zb = sb("zb", [128, 512], BF16)
        pss = [ps.enter_context(nc.psum_tensor("np_pss%d" % i, [128, 512], F32)) for i in range(2)]
        paccs = [[ps.enter_context(nc.psum_tensor("np_pacc%d_%d" % (j, i), [128, 512], F32)) for i in range(3)] for j in range(2)]
        pmisc = paccs[1][0]
        pmb = paccs[1][1]
        kvb_f = sb("kvb_f", [NCMP, 128])
        rr16 = [sb("rr16_%d" % i, [128, 16]) for i in range(2)]
        rg16 = [sb("rg16_%d" % i, [128, 16]) for i in range(2)]
        Trr16 = [Tok(), Tok()]
        TON = [[Tok() for _ in range(8)] for _ in range(NKT)]
        names = "q K V kvc gn M cm F Wm pair maskW wcol Wbig ON imp selT kvb KcT Vc1P sc m8 sel pm pmb zb".split()
        T = {n: Tok(n) for n in names}
        kb.memset(zb[:], 0.0, W=[T["zb"]])
        TPt, Tpss = [Tok() for _ in range(3)], [Tok() for _ in range(2)]
        Tpaccs = [[Tok() for _ in range(3)] for _ in range(2)]
        T["pm"] = Tpaccs[1][0]
        T["pmb"] = Tpaccs[1][1]
        Trr = [Tok() for _ in range(4)]
        kb.dma("sp", M1[:], c.nsa_M1.rearrange("(t p) j -> p t j", p=128), W=[T["M"]])
        kb.dma("sp", M2[:], c.nsa_M2.rearrange("(t p) j -> p t j", p=128), W=[T["M"]])
        kb.dma("pool", cmask[:], c.nsa_cmask, W=[T["cm"]])
        kb.dma("pool", Fm[:], c.nsa_F, W=[T["F"]])
        kb.dma("pool", Wm[:], c.nsa_Wm, W=[T["Wm"]])
        kb.dma("pool", pair[:], c.nsa_pair, W=[T["pair"]])
        kb.dma("sp", maskW[:], c.nsa_maskW, W=[T["maskW"]])
        wsrc = c.w_cmp.rearrange("s j h -> j s h")
        for q4 in range(4):
            kb.dma("sp", wcol[32 * q4:32 * q4 + 32, :, :], wsrc, W=[T["wcol"]])
        for s_ in range(2):
            for hk in range(2):
                kb.ts(Wbig[:, s_ * 2 + hk, :], maskW[:, :], wcol[:, s_, hk:hk + 1], None, ALU.mult,
                      R=[T["maskW"], T["wcol"]], W=[T["Wbig"]])
        kb.memset(Vx[:, :, :, :, 64:65], 1.0, W=[T["V"]])
        kb.memset(Vc1P[:, :, 64:65], 1.0, W=[T["Vc1P"]])
        for hk in range(2):
            kb.copy(Vc1P[:, hk, 65:65 + NSEL], pair[:, :], R=[T["pair"]], W=[T["Vc1P"]])
        st = {"ps": 0, "pt": 0, "rr": 0, "aset": 0}

        def score_bank():
            i = st["ps"] % 2
            st["ps"] += 1
            return pss[i], Tpss[i]

        def acc_ap(a, aset=0):
            return paccs[aset][a // 7][:, (a % 7) * 65:(a % 7) * 65 + 65], Tpaccs[aset][a // 7]

        def finish_group(aset, qg, hk, br):
            i = st["rr"] % 2
            st["rr"] += 1
            r16, g16, Tr = rr16[i], rg16[i], Trr16[i]
            for b3 in range(3):
                na = min(7, 16 - 7 * b3)
                den = paccs[aset][b3][:, 0:na * 65].rearrange("p (a c) -> p a c", c=65)[:, :, 64]
                kb.ts(r16[:, 7 * b3:7 * b3 + na], den, 1e-30, None, ALU.max, R=[Tpaccs[aset][b3]], W=[Tr])
            kb.op("dve", lambda e: e.reciprocal(out=r16[:, :], in_=r16[:, :]), [Tr], [Tr])
            gview = gn[:, qg * 4:(qg + 1) * 4, br * 8 + hk * 4:br * 8 + hk * 4 + 4].rearrange("p s g -> p g s")
            kb.tt(g16[:, :].rearrange("p (g s) -> p g s", g=4), r16[:, :].rearrange("p (g s) -> p g s", g=4), gview, ALU.mult,
                  R=[Tr, T["gn"]], W=[Tr])
            for g in range(4):
                for sub in range(4):
                    a = g * 4 + sub
                    ap_, Ta_ = acc_ap(a, aset)
                    qt, head = qg * 4 + sub, hk * 4 + g
                    o_ = ON[:, qt, head * 64:(head + 1) * 64]
                    kb.stt(o_, ap_[:, 0:64], g16[:, a:a + 1], o_, ALU.mult, ALU.add, R=[Ta_, Tr, TON[qt][head]], W=[TON[qt][head]])

        def finish_acc(ap_, Tp_, qt, head, br, first, with_imp=None):
            i = st["rr"] % 4
            st["rr"] += 1
            r_, g_, Tr_ = rr[i], rg[i], Trr[i]
            kb.ts(r_[:, :], ap_[:, 64:65], 1e-30, None, ALU.max, R=[Tp_], W=[Tr_])
            kb.op("dve", lambda e: e.reciprocal(out=r_[:, :], in_=r_[:, :]), [Tr_], [Tr_])
            kb.tt(g_[:, :], r_[:, :], gn[:, qt, br * 8 + head:br * 8 + head + 1], ALU.mult, R=[Tr_, T["gn"]], W=[Tr_])
            o_ = ON[:, qt, head * 64:(head + 1) * 64]
            if first:
                kb.ts(o_, ap_[:, 0:64], g_[:, 0:1], None, ALU.mult, R=[Tp_, Tr_], W=[TON[qt][head]])
            else:
                kb.stt(o_, ap_[:, 0:64], g_[:, 0:1], o_, ALU.mult, ALU.add, R=[Tp_, Tr_, TON[qt][head]], W=[TON[qt][head]])
            return r_, Tr_

        for s in range(NP):
            tg = s * SEQ
            kb.dma("sp", qTh[:, :, :], c.qT_d.rearrange("(h d) t -> d h t", d=64)[:, :, tg:tg + SEQ], W=[T["q"]])
            for i3 in range(3):
                kb.dma("sp", KT[:, i3, :, :], c.kT_d[i3].rearrange("(h d) t -> d h t", d=64)[:, :, tg:tg + SEQ], W=[T["K"]])
            rowsv = c.kvg_d[tg:tg + SEQ, :].rearrange("(t p) f -> p t f", p=128)
            for br in range(3):
                for hk in range(2):
                    c0 = br * 256 + 128 + hk * 64
                    kb.dma("pool", Vx[:, :, br, hk, 0:64], rowsv[:, :, c0:c0 + 64], W=[T["V"]])
            kb.dma("pool", kvc[:, :, :], rowsv[:, :, 0:256], W=[T["kvc"]])
            kb.dma("sp", gn[:, :, :], rowsv[:, :, 768:792], W=[T["gn"]])
            kb.act(gn[:, :, :], gn[:, :, :], AF.Exp, R=[T["gn"]], W=[T["gn"]], scale=-1.0)
            kb.ts(gn[:, :, :], gn[:, :, :], 1.0, None, ALU.add, R=[T["gn"]], W=[T["gn"]])
            kb.op("dve", lambda e: e.reciprocal(out=gn[:, :, :], in_=gn[:, :, :]), [T["gn"]], [T["gn"]])
            for cc in range(4):
                for kt in range(NKT):
                    kb.mm(pmisc[:NCMP, cc * 64:(cc + 1) * 64], Wbig[:, cc, 60 - 4 * kt:60 - 4 * kt + NCMP],
                          kvc[:, kt, cc * 64:(cc + 1) * 64], kt == 0, kt == NKT - 1, R=[T["Wbig"], T["kvc"]], W=[T["pm"]],
                          sig=(kt == NKT - 1 and cc == 3))
            kb.copy(kvb_sb[:, :], pmisc[:NCMP, 0:256], R=[T["pm"]], W=[T["kvb"]])
            kb.copy(kvb_f[:, :], pmisc[:NCMP, 0:128], R=[T["pm"]], W=[T["kvb"]], eng="act")
            for hk in range(2):
                kb.tr(pmb[:64, hk * NCMP:(hk + 1) * NCMP], kvb_f[:, hk * 64:(hk + 1) * 64], c.ident_f[:NCMP, :NCMP],
                      R=[T["kvb"], c.Tconst], W=[T["pmb"]])
                kb.copy(Vc1P[:, hk, 0:64], kvb_sb[:, 128 + hk * 64:128 + (hk + 1) * 64], R=[T["kvb"]], W=[T["Vc1P"]], eng="pool")
            kb.copy(KcT[:, :, :], pmb[:64, 0:2 * NCMP].rearrange("p (h n) -> p h n", h=2), R=[T["pmb"]], W=[T["KcT"]])
            for qg in range(NQG):
                for hk in range(2):
                    for g in range(4):
                        head = hk * 4 + g
                        p_, Tp_ = score_bank()
                        kb.mm(p_[:NCMP, :], KcT[:, hk, :], qTh[:, head, qg * 512:(qg + 1) * 512], True, False,
                              R=[T["KcT"], T["q"]], W=[Tp_], sig=False)
                        kb.mm(p_[:NCMP, :], c.ident_b[:NCMP, :NCMP], cmask[:, qg * 512:(qg + 1) * 512], False, True,
                              R=[c.Tconst, T["cm"]], W=[Tp_])
                        pt_, Tpt_ = Pt[st["pt"] % 3], TPt[st["pt"] % 3]
                        st["pt"] += 1
                        kb.act(pt_[:NCMP, :], p_[:NCMP, :], AF.Exp, R=[Tp_], W=[Tpt_], scale=0.125)
                        for sub in range(4):
                            qt = qg * 4 + sub
                            kb.mm(pmisc[:, 256 + 0:256 + 65 + NSEL], pt_[:NCMP, sub * 128:(sub + 1) * 128], Vc1P[:, hk, :], True, True,
                                  R=[Tpt_, T["Vc1P"]], W=[T["pm"]])
                            ap_ = pmisc[:, 256:256 + 65 + NSEL]
                            r_, Tr_ = finish_acc(ap_, T["pm"], qt, head, 0, True)
                            if g == 0:
                                kb.ts(imp[:, qt, hk, :], ap_[:, 65:65 + NSEL], r_[:, 0:1], None, ALU.mult, R=[T["pm"], Tr_], W=[T["imp"]])
                            else:
                                kb.stt(imp[:, qt, hk, :], ap_[:, 65:65 + NSEL], r_[:, 0:1], imp[:, qt, hk, :], ALU.mult, ALU.add,
                                       R=[T["pm"], Tr_, T["imp"]], W=[T["imp"]])
            for qt in range(NKT):
                for hk in range(2):
                    kb.tt(sc[:, :], imp[:, qt, hk, :], M1[:, qt, :], ALU.mult, R=[T["imp"], T["M"]], W=[T["sc"]])
                    kb.tt(sc[:, :], sc[:, :], M2[:, qt, :], ALU.add, R=[T["sc"], T["M"]], W=[T["sc"]])
                    kb.op("dve", lambda e: e.max(out=m8a[:, :], in_=sc[:, :]), [T["sc"]], [T["m8"]])
                    kb.op("dve", lambda e: e.match_replace(out=wk_[:, :], in_to_replace=m8a[:, :], in_values=sc[:, :], imm_value=-2.0),
                          [T["sc"], T["m8"]], [T["sel"]])
                    kb.op("dve", lambda e: e.max(out=m8b[:, :], in_=wk_[:, :]), [T["sel"]], [T["m8"]])
                    kb.op("dve", lambda e: e.tensor_reduce(out=thr[:, :], in_=m8b[:, :], axis=AX.X, op=ALU.min), [T["m8"]], [T["m8"]])
                    kb.ts(selm[:, :], sc[:, :], thr[:, 0:1], None, ALU.is_ge, R=[T["sc"], T["m8"]], W=[T["sel"]])
                    kb.ts(okm[:, :], sc[:, :], 0.0, None, ALU.is_ge, R=[T["sc"]], W=[T["sel"]])
                    kb.tt(selm[:, :], selm[:, :], okm[:, :], ALU.mult, R=[T["sel"]], W=[T["sel"]])
                    kb.ts(selm[:, :], selm[:, :], -1.0, -NEGM, ALU.add, ALU.mult, R=[T["sel"]], W=[T["sel"]])
                    kb.tr(pmisc[:NSEL, 0:128], selm[:, :], c.ident_f[:, :], R=[T["sel"], c.Tconst], W=[T["pm"]])
                    kb.copy(selT[:, hk, qt * 128:(qt + 1) * 128], pmisc[:NSEL, 0:128], R=[T["pm"]], W=[T["selT"]], eng="act")
            for br in (1, 2):
                for hk in range(2):
                    for qg in range(NQG):
                        kts = list(range(0, 4 * qg + 4)) if br == 1 else list(range(max(0, 4 * qg - 4), 4 * qg + 4))
                        aset = st["aset"] % 2
                        st["aset"] += 1
                        for b3 in range(3):
                            kb.mm(paccs[aset][b3][:, :], zb[:, 0:128], zb[:, :], True, False, R=[T["zb"]], W=[Tpaccs[aset][b3]], sig=False)
                        for ki, kt in enumerate(kts):
                            for g in range(4):
                                head = hk * 4 + g
                                p_, Tp_ = score_bank()
                                diag = kt >= 4 * qg
                                need_w = diag or br == 2
                                kb.mm(p_[:, :], KT[:, br, hk, kt * 128:(kt + 1) * 128], qTh[:, head, qg * 512:(qg + 1) * 512],
                                      True, not (need_w or br == 1), R=[T["K"], T["q"]], W=[Tp_], sig=False)
                                if br == 1:
                                    kb.mm(p_[:, :], Fm[:, kt * 128:(kt + 1) * 128], selT[:, hk, qg * 512:(qg + 1) * 512],
                                          False, not need_w, R=[T["F"], T["selT"]], W=[Tp_], sig=(not need_w))
                                if need_w:
                                    kb.mm(p_[:, :], c.ident_b[:, :], Wm[:, kt - 4 * qg + 4, :], False, True,
                                          R=[c.Tconst, T["Wm"]], W=[Tp_])
                                pt_, Tpt_ = Pt[st["pt"] % 3], TPt[st["pt"] % 3]
                                st["pt"] += 1
                                kb.act(pt_[:, :], p_[:, :], AF.Exp, R=[Tp_], W=[Tpt_], scale=0.125)
                                for sub in range(4):
                                    ap_, Ta_ = acc_ap(g * 4 + sub, aset)
                                    last = (ki == len(kts) - 1)
                                    kb.mm(ap_, pt_[:, sub * 128:(sub + 1) * 128], Vx[:, kt, br, hk, :], False, last,
                                          R=[Tpt_, T["V"]], W=[Ta_], sig=last)
                        finish_group(aset, qg, hk, br)
            kb.dma("pool", c.on_d[tg:tg + SEQ, :].rearrange("(t p) f -> p t f", p=128), ON[:, :, :],
                   R=[TON[a_][b_] for a_ in range(NKT) for b_ in range(8)])
        kb.barrier()


def nsa_s_consts(NPAGES):
    nblk = NPAGES * 2
    ncmp = NPAGES * 4
    nsel = nblk + 1
    j = np.arange(nsel)
    forced = (j == 0) | (j == nblk) | (j == nblk - 1)
    M1 = np.tile((~forced).astype(np.float32)[None, :], (LS, 1))
    M2 = np.tile(np.where(forced, 1e4 + j, 0.0).astype(np.float32)[None, :], (LS, 1))
    nh = ncmp // 128
    nl = np.arange(128)
    pairs = np.zeros((128, nh, nblk), np.float32)
    for h in range(nh):
        pairs[nl, h, 64 * h + nl // 2] = 1.0
    Fs = (np.arange(NPAGES * 128)[None, :] // 64 == np.arange(nblk)[:, None]).astype(np.float32)
    col_l = np.tile(np.arange(LS), 8)[None, :]
    caus = np.where(np.arange(LS)[:, None] <= col_l, 0.0, NEGM).astype(np.float32)
    wm0 = np.where(np.arange(128)[:, None] <= col_l, NEGM, 0.0).astype(np.float32)
    gsum = (np.arange(16)[:, None] % LS == np.arange(LS)[None, :]).astype(np.float32)
    qcol = (np.arange(128) % 4).astype(np.float32).reshape(128, 1)
    pcol = np.arange(128, dtype=np.float32).reshape(128, 1)
    jrow = np.arange(32, dtype=np.float32).reshape(1, 32)
    maskWs = np.zeros((128, 252), np.float32)
    maskWs[np.arange(128), 124 + np.arange(128) // 32] = 1.0
    return dict(ns_maskWs=maskWs, ns_jrow=jrow, ns_M1=M1, ns_M2=M2, ns_pairs=pairs, ns_Fs=Fs, ns_caus=caus, ns_wm0=wm0, ns_gsum=gsum, ns_qcol=qcol, ns_pcol=pcol)


def phase_nsa_sample(kb, c):
    nc, cfg = c.nc, c.cfg
    NS, NPAGES, NPHYS, WL = cfg["NS"], cfg["NPAGES"], cfg["NPHYS"], cfg["WIN"]
    TP = c.TP
    NBLK = NPAGES * 2
    NCMP = NPAGES * 4
    NH = NCMP // 128
    NSELS = NBLK + 1
    assert NBLK <= 128 and NCMP % 128 == 0
    with contextlib.ExitStack() as ps:
        sb = lambda n, s, d=F32: ps.enter_context(nc.sbuf_tensor("nss_" + n, s, d))
        ptT = sb("ptT", [128, NS, NH], I32)
        ptb = sb("ptb", [128, NS * NPAGES], I32)
        idxq_f = sb("idxq_f", [128, NS, NH])
        idxr_f = sb("idxr_f", [128, NS * NPAGES])
        idxq = sb("idxq", [128, NS, NH], I32)
        idxr = sb("idxr", [128, NS * NPAGES], I32)
        qcol, pcol = sb("qcol", [128, 1]), sb("pcol", [128, 1])
        jrow = sb("jrow", [128, 32])
        idxq32_f = sb("idxq32_f", [128, NS * NH, 32])
        idxq32 = sb("idxq32", [128, NS * NH * 32], I32)
        wflat = sb("wflat", [128, 128])
        M1, M2 = sb("M1", [LS, NSELS]), sb("M2", [LS, NSELS])
        pairs = sb("pairs", [128, NH, NBLK], BF16)
        Fs = sb("Fs", [NBLK, NPAGES * 128], BF16)
        caus = sb("caus", [LS, 32], BF16)
        wm0 = sb("wm0", [128, 32], BF16)
        gsum = sb("gsum", [16, LS])
        zb = sb("zb", [128, 512], BF16)
        cq = [sb("cq%d" % i, [128, 32, 256]) for i in range(2)]
        kvbs = sb("kvbs", [128, NH, 256])
        kvbK = sb("kvbK", [128, NH, 128], BF16)
        KcTs = sb("KcTs", [64, 2, NCMP], BF16)
        VcS = sb("VcS", [128, NH, 2, 65 + NBLK], BF16)
        qs = sb("qs", [64, 8, LS], BF16)
        KnT = sb("KnT", [64, 3, 2, LS], BF16)
        Vn = sb("Vn", [LS, 3, 2, 2, 65], BF16)
        gns = sb("gns", [16, 3, 2])
        Es = sb("Es", [128, NH, 2, 16], BF16)
        impg = sb("impg", [16, NBLK])
        sc = sb("sc", [LS, NSELS])
        wk_ = sb("wk", [LS, NSELS])
        m8a, m8b, thr = sb("m8a", [LS, 8]), sb("m8b", [LS, 8]), sb("thr", [LS, 1])
        selm = sb("selm", [LS, NBLK])
        selx = sb("selx", [NBLK, 2, 4, LS], BF16)
        pg = [sb("pg%d" % i, [128, 256]) for i in range(8)]
        pgb = [sb("pgb%d" % i, [128, 2, 2, 65], BF16) for i in range(3)]
        KpT = [sb("KpT%d" % i, [64, 2, 128], BF16) for i in range(2)]
        Pp = [sb("Pp%d" % i, [128, 32], BF16) for i in range(2)]
        ONs = sb("ONs", [16, 2, 64])
        ONb = sb("ONb", [16, 2, 64], BF16)
        rr, rg = sb("rr", [16, 1]), sb("rg", [16, 1])
        pS = ps.enter_context(nc.psum_tensor("ns_pS", [128, 512], F32))
        pO = ps.enter_context(nc.psum_tensor("ns_pO", [128, 512], F32))
        pI = ps.enter_context(nc.psum_tensor("ns_pI", [128, 512], F32))
        pT = [ps.enter_context(nc.psum_tensor("ns_pT%d" % i, [128, 1024], BF16)) for i in range(2)]
        pS2 = [ps.enter_context(nc.psum_tensor("ns_pS2%d" % i, [128, 512], F32)) for i in range(2)]
        pA = ps.enter_context(nc.psum_tensor("ns_pA", [128, 512], F32))
        names = ("pt idx const w cq kvbs kvbK KcTs VcS qs KnT Vn gns Es impg sc m8 sel selx ONs rr pS pO pI pA zb").split()
        T = {n: Tok(n) for n in names}
        Tpg, Tpgb = [Tok() for _ in range(8)], [Tok() for _ in range(3)]
        Tcqs = [Tok(), Tok()]
        TKpT, TPp, TpT, TpS2 = ([Tok(), Tok()] for _ in range(4))
        kb.memset(zb[:], 0.0, W=[T["zb"]])
        with nc.allow_non_contiguous_dma(reason="page table transpose (tiny)"):
            kb.dma("sp", ptT[:, :, :], c.pt4.rearrange("b (h p) -> p b h", p=128), W=[T["pt"]])
        kb.dma("sp", ptb[:, :], c.page_table.rearrange("b n -> (b n)").partition_broadcast(128), W=[T["pt"]])
        for nm, t_, src in (("qcol", qcol, c.ns_qcol), ("pcol", pcol, c.ns_pcol), ("M1", M1, c.ns_M1), ("M2", M2, c.ns_M2), ("gsum", gsum, c.ns_gsum)):
            kb.dma("sp", t_[:], src, W=[T["const"]])
        for t_, src in ((pairs, c.ns_pairs), (Fs, c.ns_Fs), (caus, c.ns_caus), (wm0, c.ns_wm0)):
            kb.dma("pool", t_[:], src, W=[T["const"]])
        kb.dma("sp", wflat[:, :], c.w_cmp.rearrange("s j h -> (s j h)").partition_broadcast(128), W=[T["w"]])
        kb.copy(idxq_f[:], ptT[:], R=[T["pt"]], W=[T["idx"]])
        kb.ts(idxq_f[:], idxq_f[:], 4.0, qcol[:, 0:1], ALU.mult, ALU.add, R=[T["idx"], T["const"]], W=[T["idx"]])
        kb.copy(idxq[:], idxq_f[:], R=[T["idx"]], W=[T["idx"]])
        kb.ts(idxq_f[:], idxq_f[:], 32.0, None, ALU.mult, R=[T["idx"]], W=[T["idx"]])
        kb.dma("sp", jrow[:, :], c.ns_jrow.partition_broadcast(128), W=[T["const"]])
        kb.tt(idxq32_f[:, :, :], idxq_f[:].rearrange("p b h -> p (b h)").unsqueeze(2).broadcast_to([128, NS * NH, 32]),
              jrow[:, :].unsqueeze(1).broadcast_to([128, NS * NH, 32]), ALU.add, R=[T["idx"], T["const"]], W=[T["idx"]])
        kb.copy(idxq32[:, :], idxq32_f[:, :, :].rearrange("p a j -> p (a j)"), R=[T["idx"]], W=[T["idx"]])
        kb.copy(idxr_f[:], ptb[:], R=[T["pt"]], W=[T["idx"]])
        kb.ts(idxr_f[:], idxr_f[:], 128.0, pcol[:, 0:1], ALU.mult, ALU.add, R=[T["idx"], T["const"]], W=[T["idx"]])
        kb.copy(idxr[:], idxr_f[:], R=[T["idx"]], W=[T["idx"]])
        for i in range(3):
            kb.memset(pgb[i][:, :, :, 64:65], 1.0, W=[Tpgb[i]])
        kb.memset(VcS[:, :, :, 64:65], 1.0, W=[T["VcS"]])
        kb.memset(Vn[:, :, :, :, 64:65], 1.0, W=[T["Vn"]])
        for hk in range(2):
            kb.copy(VcS[:, :, hk, 65:65 + NBLK], pairs[:, :, :], R=[T["const"]], W=[T["VcS"]])
        cmp_v = c.cache_cmp.rearrange("(n q) f -> n q f", q=32)
        wv = wflat[:, :].rearrange("p (s j h) -> p j s h", s=2, h=2)
        st = {"pg": 0, "k": 0, "g8": 0, "rg": 0}

        def page_dma(dst, cache, b, lp, W):
            col = b * NPAGES + lp
            return kb.dma_fn("pool", lambda e: e.indirect_dma_start(
                out=dst, out_offset=None, in_=cache,
                in_offset=bass.IndirectOffsetOnAxis(ap=idxr[:, col:col + 1], axis=0)), R=[T["idx"]], W=W)

        maskWs = sb("maskWs", [128, 252])
        wcol = sb("wcol", [128, 2, 2])
        Wc = sb("Wc", [128, 4, 252])
        kb.dma("sp", maskWs[:], c.ns_maskWs, W=[T["w"]])
        wsrc = c.w_cmp.rearrange("s j h -> j s h")
        with nc.allow_non_contiguous_dma(reason="tiny weight gather"):
            for q4 in range(4):
                kb.dma("sp", wcol[32 * q4:32 * q4 + 32, :, :], wsrc, W=[T["w"]])
        for s_ in range(2):
            for hk in range(2):
                kb.ts(Wc[:, s_ * 2 + hk, :], maskWs[:, :], wcol[:, s_, hk:hk + 1], None, ALU.mult, R=[T["w"]], W=[T["w"]])
        pK = pS

        def tile_pipeline(src_tile, Tsrc, nrows, mask_rhs, mask_lhsT, Rmask, hk_list=(0, 1)):
            i3 = st["pg"] % 3
            i2 = st["k"] % 2
            st["pg"] += 1
            st["k"] += 1
            pb, Tpb = pgb[i3], Tpgb[i3]
            kb.copy(pb[:nrows, :, :, 0:64], src_tile, R=[Tsrc], W=[Tpb])
            for hk in range(2):
                kb.tr(pT[i2][:64, hk * 128:hk * 128 + nrows], pb[:nrows, 0, hk, 0:64], c.ident_b[:nrows, :nrows],
                      R=[Tpb, c.Tconst], W=[TpT[i2]])
            kb.copy(KpT[i2][:, :, :nrows], pT[i2][:64, 0:256].rearrange("p (h n) -> p h n", h=2)[:, :, :nrows],
                    R=[TpT[i2]], W=[TKpT[i2]], eng="act")
            p2, Tp2 = pS2[i2], TpS2[i2]
            if mask_rhs is not None:
                kb.mm(p2[:nrows, 0:32], mask_lhsT, mask_rhs, True, False, R=Rmask, W=[Tp2], sig=False)
            for hk in range(2):
                first = (mask_rhs is None)
                kb.mm(p2[:nrows, hk * 16:(hk + 1) * 16], KpT[i2][:, hk, :nrows], qs[:, hk * 4:(hk + 1) * 4, :], first, True,
                      R=[TKpT[i2], T["qs"]], W=[Tp2], sig=(hk == 1))
            kb.act(Pp[i2][:nrows, :], p2[:nrows, 0:32], AF.Exp, R=[Tp2], W=[TPp[i2]], scale=0.125)
            for hk in range(2):
                kb.mm(pA[:16, hk * 65:(hk + 1) * 65], Pp[i2][:nrows, hk * 16:(hk + 1) * 16], pb[:nrows, 1, hk, :], False, True,
                      R=[TPp[i2], Tpb], W=[T["pA"]], sig=(hk == 1))

        def finish_branch(br, first):
            for hk in range(2):
                ap_ = pA[:16, hk * 65:(hk + 1) * 65]
                kb.ts(rr[:, :], ap_[:, 64:65], 1e-30, None, ALU.max, R=[T["pA"]], W=[T["rr"]])
                kb.op("dve", lambda e: e.reciprocal(out=rr[:, :], in_=rr[:, :]), [T["rr"]], [T["rr"]])
                kb.tt(rg[:, :], rr[:, :], gns[:, br, hk:hk + 1], ALU.mult, R=[T["rr"], T["gns"]], W=[T["rr"]])
                if first:
                    kb.ts(ONs[:, hk, :], ap_[:, 0:64], rg[:, 0:1], None, ALU.mult, R=[T["pA"], T["rr"]], W=[T["ONs"]])
                else:
                    kb.stt(ONs[:, hk, :], ap_[:, 0:64], rg[:, 0:1], ONs[:, hk, :], ALU.mult, ALU.add, R=[T["pA"], T["rr"], T["ONs"]], W=[T["ONs"]])

        for b in range(NS):
            tg = TP + b * LS
            kb.dma("sp", qs[:, :, :], c.qT_d.rearrange("(h d) t -> d h t", d=64)[:, :, tg:tg + LS], W=[T["qs"]])
            for i3 in range(3):
                kb.dma("sp", KnT[:, i3, :, :], c.kT_d[i3].rearrange("(h d) t -> d h t", d=64)[:, :, tg:tg + LS], W=[T["KnT"]])
            kb.dma("pool", Vn[:, :, :, :, 0:64], c.kvg_d[tg:tg + LS, 0:768].rearrange("l (b s h d) -> l b s h d", b=3, s=2, h=2),
                   W=[T["Vn"]])
            for g in range(4):
                src = c.kvg_d[tg:tg + LS, 768 + g:768 + g + 21:4].rearrange("l (b h) -> l b h", h=2)
                with nc.allow_non_contiguous_dma(reason="tiny gate gather"):
                    kb.dma("sp", gns[4 * g:4 * g + 4, :, :], src, W=[T["gns"]])
            kb.act(gns[:, :, :], gns[:, :, :], AF.Exp, R=[T["gns"]], W=[T["gns"]], scale=-1.0)
            kb.ts(gns[:, :, :], gns[:, :, :], 1.0, None, ALU.add, R=[T["gns"]], W=[T["gns"]])
            kb.op("dve", lambda e: e.reciprocal(out=gns[:, :, :], in_=gns[:, :, :]), [T["gns"]], [T["gns"]])
            kb.mm(pK[:, :], zb[:, 0:128], zb[:, :], True, False, R=[T["zb"]], W=[T["pS"]], sig=False)
            for half in range(NH):
                for lpl in range(32):
                    lp = half * 32 + lpl
                    i3 = st["g8"] % 8
                    st["g8"] += 1
                    t_, Tt_ = pg[i3], Tpg[i3]
                    page_dma(t_[:, 0:256], c.cache_cmp, b, lp, [Tt_])
                    for cc in range(4):
                        kb.mm(pK[:, half * 256 + cc * 64:half * 256 + (cc + 1) * 64], Wc[:, cc, 124 - 4 * lpl:124 - 4 * lpl + 128],
                              t_[:, cc * 64:(cc + 1) * 64], False, True, R=[T["w"], Tt_], W=[T["pS"]],
                              sig=(cc == 3))
            kb.copy(kvbs[:, :, :], pK[:, 0:NH * 256].rearrange("p (a c) -> p a c", a=NH), R=[T["pS"]], W=[T["kvbs"]])
            kb.copy(kvbK[:, :, :], kvbs[:, :, 0:128], R=[T["kvbs"]], W=[T["kvbK"]])
            for hk in range(2):
                kb.copy(VcS[:, :, hk, 0:64], kvbs[:, :, 128 + hk * 64:128 + (hk + 1) * 64], R=[T["kvbs"]], W=[T["VcS"]])
            for half in range(NH):
                for hk in range(2):
                    kb.tr(pT[0][:64, (half * 2 + hk) * 128:(half * 2 + hk + 1) * 128], kvbK[:, half, hk * 64:(hk + 1) * 64], c.ident_b[:, :],
                          R=[T["kvbK"], c.Tconst], W=[TpT[0]])
            kb.copy(KcTs[:, :, :].rearrange("p h (a n) -> p a h n", a=NH),
                    pT[0][:64, 0:NH * 256].rearrange("p (a h n) -> p a h n", a=NH, h=2), R=[TpT[0]], W=[T["KcTs"]])
            for half in range(NH):
                for hk in range(2):
                    kb.mm(pI[:, 384 + (half * 2 + hk) * 16:384 + (half * 2 + hk + 1) * 16], KcTs[:, hk, half * 128:(half + 1) * 128],
                          qs[:, hk * 4:(hk + 1) * 4, :], True, True, R=[T["KcTs"], T["qs"]], W=[T["pI"]], sig=(half == NH - 1 and hk == 1))
            kb.act(Es[:, :, :, :], pI[:, 384:384 + NH * 32].rearrange("p (a h x) -> p a h x", a=NH, h=2), AF.Exp, R=[T["pI"]], W=[T["Es"]], scale=0.125)
            for hk in range(2):
                for half in range(NH):
                    kb.mm(pO[:16, hk * 256:hk * 256 + 65 + NBLK], Es[:, half, hk, :], VcS[:, half, hk, :], half == 0, half == NH - 1,
                          R=[T["Es"], T["VcS"]], W=[T["pO"]], sig=(half == NH - 1))
                ap_ = pO[:16, hk * 256:hk * 256 + 65 + NBLK]
                kb.ts(rr[:, :], ap_[:, 64:65], 1e-30, None, ALU.max, R=[T["pO"]], W=[T["rr"]])
                kb.op("dve", lambda e: e.reciprocal(out=rr[:, :], in_=rr[:, :]), [T["rr"]], [T["rr"]])
                kb.tt(rg[:, :], rr[:, :], gns[:, 0, hk:hk + 1], ALU.mult, R=[T["rr"], T["gns"]], W=[T["rr"]])
                kb.ts(ONs[:, hk, :], ap_[:, 0:64], rg[:, 0:1], None, ALU.mult, R=[T["pO"], T["rr"]], W=[T["ONs"]])
                kb.ts(impg[:, :], ap_[:, 65:65 + NBLK], rr[:, 0:1], None, ALU.mult, R=[T["pO"], T["rr"]], W=[T["impg"]])
                if b == 0:
                    dbg_dump(kb, c, ONs[:, hk, :], [T["ONs"]], 16, 64)
                    dbg_dump(kb, c, impg[:, :], [T["impg"]], 16, NBLK)
                kb.mm(pI[:LS, 0:NBLK], gsum[:, :], impg[:, :], True, True, R=[T["const"], T["impg"]], W=[T["pI"]])
                kb.tt(sc[:, 0:NBLK], pI[:LS, 0:NBLK], M1[:, 0:NBLK], ALU.mult, R=[T["pI"], T["const"]], W=[T["sc"]])
                kb.tt(sc[:, 0:NBLK], sc[:, 0:NBLK], M2[:, 0:NBLK], ALU.add, R=[T["sc"], T["const"]], W=[T["sc"]])
                kb.copy(sc[:, NBLK:NSELS], M2[:, NBLK:NSELS], R=[T["const"]], W=[T["sc"]])
                kb.op("dve", lambda e: e.max(out=m8a[:, :], in_=sc[:, :]), [T["sc"]], [T["m8"]])
                kb.op("dve", lambda e: e.match_replace(out=wk_[:, :], in_to_replace=m8a[:, :], in_values=sc[:, :], imm_value=-2.0),
                      [T["sc"], T["m8"]], [T["sel"]])
                kb.op("dve", lambda e: e.max(out=m8b[:, :], in_=wk_[:, :]), [T["sel"]], [T["m8"]])
                kb.op("dve", lambda e: e.tensor_reduce(out=thr[:, :], in_=m8b[:, :], axis=AX.X, op=ALU.min), [T["m8"]], [T["m8"]])
                kb.ts(selm[:, :], sc[:, 0:NBLK], thr[:, 0:1], None, ALU.is_ge, R=[T["sc"], T["m8"]], W=[T["sel"]])
                kb.ts(selm[:, :], selm[:, :], -1.0, -NEGM, ALU.add, ALU.mult, R=[T["sel"]], W=[T["sel"]])
                if b == 0:
                    dbg_dump(kb, c, selm[:, :], [T["sel"]], LS, NBLK)
                kb.tr(pI[:NBLK, 256:256 + LS], selm[:, :], c.ident_f[:LS, :LS], R=[T["sel"], c.Tconst], W=[T["pI"]])
                kb.copy(selx[:, hk, :, :], pI[:NBLK, 256:256 + LS].unsqueeze(1).broadcast_to([NBLK, 4, LS]), R=[T["pI"]], W=[T["selx"]])
            for br in (1, 2):
                kb.mm(pA[:, :], zb[:, 0:128], zb[:, :], True, False, R=[T["zb"]], W=[T["pA"]], sig=False)
                if br == 1:
                    cache = c.cache_sel
                    for lp in range(NPAGES if not cfg.get("NS_SKIP_SEL") else 0):
                        i3 = st["g8"] % 8
                        st["g8"] += 1
                        t_, Tt_ = pg[i3], Tpg[i3]
                        page_dma(t_[:, 0:256], cache, b, lp, [Tt_])
                        tile_pipeline(t_[:, 0:256].rearrange("p (s h d) -> p s h d", s=2, h=2), Tt_, 128,
                                      selx[:, :, :, :].rearrange("p a g l -> p (a g l)"), Fs[:, lp * 128:(lp + 1) * 128], [T["selx"], T["const"]])
                else:
                    for kt in range(WL // 128):
                        i3 = st["g8"] % 8
                        st["g8"] += 1
                        t_, Tt_ = pg[i3], Tpg[i3]
                        kb.dma("sp", t_[:, 0:256], c.cache_win[b, kt * 128:(kt + 1) * 128, :], W=[Tt_])
                        if kt == 0:
                            tile_pipeline(t_[:, 0:256].rearrange("p (s h d) -> p s h d", s=2, h=2), Tt_, 128,
                                          wm0[:, :], c.ident_b[:, :], [T["const"], c.Tconst])
                        else:
                            tile_pipeline(t_[:, 0:256].rearrange("p (s h d) -> p s h d", s=2, h=2), Tt_, 128, None, None, [])
                p2, Tp2 = pS2[0], TpS2[0]
                kb.mm(p2[:LS, 0:32], c.ident_b[:LS, :LS], caus[:, :], True, False, R=[c.Tconst, T["const"]], W=[Tp2], sig=False)
                for hk in range(2):
                    kb.mm(p2[:LS, hk * 16:(hk + 1) * 16], KnT[:, br, hk, :], qs[:, hk * 4:(hk + 1) * 4, :], False, True,
                          R=[T["KnT"], T["qs"]], W=[Tp2], sig=(hk == 1))
                kb.act(Pp[0][:LS, :], p2[:LS, 0:32], AF.Exp, R=[Tp2], W=[TPp[0]], scale=0.125)
                for hk in range(2):
                    kb.mm(pA[:16, hk * 65:(hk + 1) * 65], Pp[0][:LS, hk * 16:(hk + 1) * 16], Vn[:, br, 1, hk, :], False, True,
                          R=[TPp[0], T["Vn"]], W=[T["pA"]], sig=(hk == 1))
                finish_branch(br, False)
                if b == 0:
                    dbg_dump(kb, c, ONs[:, :, :].rearrange("p a d -> p (a d)"), [T["ONs"]], 16, 128)
            kb.copy(ONb[:, :, :], ONs[:, :, :], R=[T["ONs"]], W=[T["ONs"]])
            for hk in range(2):
                for g in range(4):
                    h = hk * 4 + g
                    kb.dma("sp", c.on_d[tg:tg + LS, h * 64:(h + 1) * 64], ONb[4 * g:4 * g + 4, hk, :], R=[T["ONs"]])
        kb.barrier()


def phase_nsa_zero(kb, c, start=0):
    nc = c.nc
    with contextlib.ExitStack() as ps:
        zt = ps.enter_context(nc.sbuf_tensor("nz_z", [128, 512], BF16))
        Tz = Tok()
        kb.memset(zt[:], 0.0, W=[Tz])
        for r0 in range(start, c.T, 128):
            nr = min(128, c.T - r0)
            kb.dma("sp", c.on_d[r0:r0 + nr, :], zt[:nr, :], R=[Tz])
        kb.barrier()

def build(cfg, stages=("mod", "ffn1", "win", "gdn", "nsap", "nsas", "mix", "ffn2")):
    nc = bass.Bass("TRN2", target_bir_lowering=False)
    c = Ctx()
    c.nc, c.cfg = nc, cfg
    cfg.setdefault("WIN", 512)
    NP, SEQ, NS = cfg["NP"], cfg["SEQ"], cfg["NS"]
    TP, TS = NP * SEQ, NS * LS
    T = TP + TS
    NR = NP + TS
    c.TP, c.TS, c.T = TP, TS, T
    WL = cfg["WIN"]
    PW = min(WL, SEQ)

    def din(name, shape, dt=F32):
        return nc.dram_tensor(name, list(shape), dt, kind="ExternalInput").ap()

    def dout(name, shape, dt=F32):
        return nc.dram_tensor(name, list(shape), dt, kind="ExternalOutput").ap()

    def dscr(name, shape, dt=F32):
        return nc.dram_tensor(name, list(shape), dt, kind="Internal").ap()

    c.xp = din("xp", [TP, D])
    c.xs = din("xs", [TS, D])
    c.c_rows = din("c_rows", [NR, D])
    c.ident_d = din("ident", [128, 128])
    c.ln_g = din("ln_g", [3, D])
    c.ln_b = din("ln_b", [3, D])
    c.w_ada = din("w_ada", [D, 9 * D])
    c.b_ada = din("b_ada", [1, 9 * D])
    c.w_ff1_gu = din("w_ff1_gu", [D, 2 * DFF])
    c.w_ff1_dn = din("w_ff1_dn", [DFF, D])
    c.w_ff2_gu = din("w_ff2_gu", [D, 2 * DFF])
    c.w_ff2_dn = din("w_ff2_dn", [DFF, D])
    c.w_in = din("w_in", [D, DIN])
    c.state_gdn = din("state_gdn", [NS, 8, 64, 64])
    c.conv_buf = din("conv_buf", [NS, 3, 1536])
    c.cache_win = din("cache_win", [NS, WL, 256])
    c.conv_w = din("conv_w", [4, 1536])
    c.w_cmp = din("w_cmp", [2, 32, 2])
    nsel, ncmp = SEQ // 64, SEQ // 32
    c.nsa_M1 = din("nsa_M1", [SEQ, nsel])
    c.nsa_M2 = din("nsa_M2", [SEQ, nsel])
    c.nsa_cmask = din("nsa_cmask", [ncmp, SEQ])
    c.nsa_F = din("nsa_F", [nsel, SEQ])
    c.nsa_Wm = din("nsa_Wm", [128, 8, 512])
    c.nsa_pair = din("nsa_pair", [ncmp, nsel])
    c.nsa_maskW = din("nsa_maskW", [128, 124])
    NPAGES, NPHYS = cfg["NPAGES"], cfg["NPHYS"]
    c.cache_cmp = din("cache_cmp", [NPHYS * 128, 256])
    c.cache_sel = din("cache_sel", [NPHYS * 128, 256])
    c.page_table = din("page_table", [NS, NPAGES], I32)
    c.pt4 = din("pt4", [NS, NPAGES * 4], I32)
    for k_, v_ in nsa_s_consts(NPAGES).items():
        setattr(c, k_, din(k_, list(v_.shape)))
    c.w_br_gdn = din("w_br_gdn", [512, D])
    c.w_br_nsa = din("w_br_nsa", [512, D])
    c.w_out = din("w_out", [D, D])
    c.a_log = din("a_log", [1, 8])
    c.dt_bias = din("dt_bias", [1, 8])
    c.norm_w = din("norm_w", [1, 64])
    c.blk1_d = din("blk1", [128, 128])
    c.gconst_d = din("gconst", [64, 5, 64])
    c.valid_d = din("valid", [64, 1])
    c.yp = dout("y_prompt", [TP, D])
    c.ys = dout("y_sample", [TS, D])
    c.p_gdn = dout("p_gdn", [NP, 8, 64, 64])
    c.p_conv = dout("p_conv", [NP, 3, 1536])
    c.p_cmp = dout("p_cmp", [TP, 256])
    c.p_sel = dout("p_sel", [TP, 256])
    c.p_win = dout("p_win", [NP * PW, 256])
    c.s_gdn = dout("s_gdn", [NS, 8, 64, 64])
    c.s_conv = dout("s_conv", [NS, 3, 1536])
    c.s_cmp = dout("s_cmp", [TS, 256])
    c.s_sel = dout("s_sel", [TS, 256])
    c.s_win = dout("s_win", [NS, WL, 256])
    c.mod_d = dscr("mod_d", [NR, 9 * D])
    c.x1 = {"p": dscr("x1p", [TP, D]), "s": dscr("x1s", [TS, D])}
    c.x2 = {"p": dscr("x2p", [TP, D]), "s": dscr("x2s", [TS, D])}
    c.qkvT_d = dscr("qkvT_d", [1536, T])
    c.qT_d = dscr("qT_d", [512, T], BF16)
    c.kT_d = dscr("kT_d", [3, 128, T], BF16)
    c.mgT_d = dscr("mgT_d", [2048, T], BF16)
    c.zab_d = dscr("zab_d", [T, 528])
    c.kvg_d = dscr("kvg_d", [T, 792])
    c.qkvn_d = dscr("qkvn_d", [1536, T])
    c.kvtok_d = dscr("kvtok_d", [T, 1024])
    dbg = dout if cfg.get("DBG") else dscr
    c.dbg_d = dbg("dbg_d", [16, 128, 512])
    c.dbg_n = 0
    c.og_d = dbg("og_d", [T, 512], BF16)
    c.on_d = dbg("on_d", [T, 512], BF16)

    with contextlib.ExitStack() as es:
        kb = KB(nc, es)
        c.kb = kb
        c.ident_f = es.enter_context(nc.sbuf_tensor("ident_f", [128, 128], F32))
        c.ident_b = es.enter_context(nc.sbuf_tensor("ident_b", [128, 128], BF16))
        c.Tconst = Tok()
        kb.dma("sp", c.ident_f[:], c.ident_d, W=[c.Tconst])
        kb.copy(c.ident_b[:], c.ident_f[:], R=[c.Tconst], W=[c.Tconst])
        if "mod" in stages:
            phase_mod(kb, c)
        if "ffn1" in stages:
            phase_ffn(kb, c, "f1", {"p": c.xp, "s": c.xs}, c.x1, c.w_ff1_gu, c.w_ff1_dn, 0)
        if "win" in stages:
            phase_win(kb, c)
        if "gdn" in stages:
            phase_gdn_a(kb, c)
            phase_gdn_b(kb, c)
        if "nsap" in stages:
            phase_nsa_prompt(kb, c)
        if "nsas" in stages:
            phase_nsa_sample(kb, c)
        if "nsa0" in stages:
            phase_nsa_zero(kb, c)
        if "nsa0s" in stages:
            phase_nsa_zero(kb, c, c.TP)
        if "mix" in stages:
            phase_mix(kb, c)
        if "ffn2" in stages:
            phase_ffn(kb, c, "f2", c.x2, {"p": c.yp, "s": c.ys}, c.w_ff2_gu, c.w_ff2_dn, 2)
        kb.finish()
        c.nops = kb.nops
    return nc, c


def make_in_maps(cfg, inputs, ncores):
    NP, SEQ, NS = cfg["NP"], cfg["SEQ"], cfg["NS"]
    f = lambda a: np.ascontiguousarray(np.asarray(a))
    maps = []
    ident = np.eye(128, dtype=np.float32)
    blk1 = np.kron(np.eye(2, dtype=np.float32), np.ones((64, 64), np.float32))
    ii = np.arange(64)
    gconst = np.stack([
        (ii[:, None] <= ii[None, :]).astype(np.float32),
        np.ones((64, 64), np.float32),
        np.where(ii[None, :] >= ii[:, None], 0.0, -30000.0).astype(np.float32),
        np.where(ii[None, :] >= ii[:, None], 30000.0, 0.0).astype(np.float32),
        np.eye(64, dtype=np.float32)], axis=1)
    valid = (ii < LS).astype(np.float32).reshape(64, 1)
    nconst = nsa_consts(SEQ)
    nconst.update(nsa_s_consts(cfg["NPAGES"]))
    cc_all = f(inputs["cache_cmp_kv"][0]).reshape(-1, 256)
    cs_all = f(inputs["cache_sel_kv"][0]).reshape(-1, 256)
    for i in range(ncores):
        ps, ss = slice(i * NP, (i + 1) * NP), slice(i * NS, (i + 1) * NS)
        m = {
            "xp": f(inputs["x_prompt"][ps]).reshape(NP * SEQ, D),
            "xs": f(inputs["x_sample"][ss]).reshape(NS * LS, D),
            "c_rows": np.concatenate([f(inputs["c_prompt"][ps]), np.repeat(f(inputs["c_sample"][ss]), LS, axis=0)], 0),
            "ident": ident,
            "ln_g": f(inputs["ln_g"][0]), "ln_b": f(inputs["ln_b"][0]),
            "w_ada": f(inputs["w_ada"][0]), "b_ada": f(inputs["b_ada"]).reshape(1, 9 * D),
            "w_ff1_gu": f(inputs["w_ff1_gu"][0]), "w_ff1_dn": f(inputs["w_ff1_dn"][0]),
            "w_ff2_gu": f(inputs["w_ff2_gu"][0]), "w_ff2_dn": f(inputs["w_ff2_dn"][0]),
            "w_in": f(inputs["w_in"][0]),
            "state_gdn": f(inputs["state_gdn"][0][ss]),
            "conv_buf": f(inputs["state_gdn_conv"][0][ss]),
            "cache_win": f(inputs["cache_win_kv"][0][ss]).reshape(NS, -1, 256),
            "conv_w": f(inputs["gdn_conv_w"][0]), "a_log": f(inputs["gdn_a_log"]).reshape(1, 8),
            "dt_bias": f(inputs["gdn_dt_bias"]).reshape(1, 8), "norm_w": f(inputs["gdn_norm_w"]).reshape(1, 64),
            "blk1": blk1, "gconst": gconst, "valid": valid, "w_cmp": f(inputs["nsa_w_cmp"][0]),
            "cache_cmp": cc_all, "cache_sel": cs_all,
            "page_table": f(inputs["page_table"][ss]).astype(np.int32),
            "pt4": np.repeat(f(inputs["page_table"][ss]).astype(np.int32), 4, axis=1),
            "w_br_gdn": f(inputs["w_br_gdn"][0]), "w_br_nsa": f(inputs["w_br_nsa"][0]), "w_out": f(inputs["w_out"][0]),
        }
        m.update(nconst)
        maps.append(m)
    return maps


def gather_outputs(cfg, results, ncores):
    NP, SEQ, NS = cfg["NP"], cfg["SEQ"], cfg["NS"]
    PW = min(cfg["WIN"], SEQ)
    cat = lambda k: np.concatenate([np.asarray(r[k]) for r in results], 0)
    B, Bs = NP * ncores, NS * ncores
    return (
        cat("y_prompt").reshape(B, SEQ, D),
        cat("y_sample").reshape(Bs, LS, D),
        cat("p_gdn").reshape(1, B, 8, 64, 64),
        cat("p_conv").reshape(1, B, 3, 1536),
        cat("p_cmp").reshape(1, B, SEQ, 2, 2, 64),
        cat("p_sel").reshape(1, B, SEQ, 2, 2, 64),
        cat("p_win").reshape(1, B, PW, 2, 2, 64),
        cat("s_gdn").reshape(1, Bs, 8, 64, 64),
        cat("s_conv").reshape(1, Bs, 3, 1536),
        cat("s_cmp").reshape(1, Bs, LS, 2, 2, 64),
        cat("s_sel").reshape(1, Bs, LS, 2, 2, 64),
        cat("s_win").reshape(1, Bs, cfg["WIN"], 2, 2, 64),
    )


_CACHE = {}


def run(cfg, inputs, ncores, stages=None):
    key = (tuple(sorted(cfg.items())), ncores, stages)
    if key not in _CACHE:
        _CACHE[key] = build(dict(cfg), stages) if stages else build(dict(cfg))
    nc, c = _CACHE[key]
    res = run_bass_kernel_spmd(nc, make_in_maps(c.cfg, inputs, ncores), core_ids=list(range(ncores)))
    _CACHE["last"] = res.results
    return gather_outputs(c.cfg, res.results, ncores)


def kernel(**inputs):
    return run(FULL_CFG, inputs, NCORES)
```

```python
import contextlib
import numpy as np
import concourse.bass as bass
import concourse.mybir as mybir
from concourse.bass_utils import run_bass_kernel_spmd

F32 = mybir.dt.float32
BF16 = mybir.dt.bfloat16
I32 = mybir.dt.int32
AF = mybir.ActivationFunctionType
ALU = mybir.AluOpType
AX = mybir.AxisListType

D = 1024
DFF = 2816
DIN = 5416
NCORES = 8
ALPHA = 2.0 ** 0.25
LS = 4

FULL_CFG = dict(NP=4, SEQ=2048, NS=16, NPAGES=64, NPHYS=10240)


class Tok:
    __slots__ = ("w", "r", "name")

    def __init__(self, name=""):
        self.w = None
        self.r = {}
        self.name = name


class Ev:
    __slots__ = ("dim", "val")

    def __init__(self, dim, val):
        self.dim = dim
        self.val = val


class KB:
    def __init__(self, nc, es, kq=None):
        self.nc = nc
        self.eng = {"pe": nc.tensor, "act": nc.scalar, "dve": nc.vector, "pool": nc.gpsimd, "sp": nc.sync}
        self.sem = {e: es.enter_context(nc.semaphore("s_" + e)) for e in self.eng}
        self.cnt = {e: 0 for e in self.eng}
        self.waited = {e: {} for e in self.eng}
        kq = kq or {"sp": 24, "act": 8, "pool": 16}
        self.dq = {q: [es.enter_context(nc.semaphore("d_%s%d" % (q, i))) for i in range(k)] for q, k in kq.items()}
        self.dqn = {q: 0 for q in kq}
        self.pe_pending = []
        self.nops = 0

    def semof(self, dim):
        if isinstance(dim, str):
            return self.sem[dim]
        return self.dq[dim[0]][dim[1]]

    def _collect(self, e, R, W):
        deps = {}

        def add(ev):
            if ev is None:
                return
            if e == "pe" and ev.dim == "pe":
                return
            if ev.val is None:
                raise RuntimeError("dependency on unsignaled PE op")
            if deps.get(ev.dim, 0) < ev.val:
                deps[ev.dim] = ev.val

        for t in R:
            add(t.w)
        for t in W:
            add(t.w)
            for r in t.r.values():
                add(r)
        return deps

    def _waits(self, e, deps):
        wd = self.waited[e]
        for dim, val in deps.items():
            if wd.get(dim, 0) < val:
                self.eng[e].wait_ge(self.semof(dim), val)
                wd[dim] = val

    def _record(self, ev, R, W):
        for t in R:
            t.r[ev.dim] = ev
        for t in W:
            t.w = ev
            t.r = {}

    def op(self, e, fn, R=(), W=(), sig=True):
        self._waits(e, self._collect(e, R, W))
        ins = fn(self.eng[e])
        self.nops += 1
        if sig:
            self.cnt[e] += 1
            ins.then_inc(self.sem[e], 1)
            ev = Ev(e, self.cnt[e])
            if e == "pe":
                for p in self.pe_pending:
                    p.val = self.cnt[e]
                self.pe_pending = []
        else:
            assert e == "pe"
            ev = Ev("pe", None)
            self.pe_pending.append(ev)
        self._record(ev, R, W)
        return ev

    def dma(self, q, out, in_, R=(), W=(), **kw):
        slots = self.dq[q]
        i = self.dqn[q]
        self.dqn[q] += 1
        k = i % len(slots)
        val = 16 * (i // len(slots) + 1)
        deps = self._collect(q, R, W)
        dim = (q, k)
        if val > 16:
            deps[dim] = max(deps.get(dim, 0), val - 16)
        self._waits(q, deps)
        ins = self.eng[q].dma_start(out=out, in_=in_, **kw)
        ins.then_inc(slots[k], 16)
        self.nops += 1
        ev = Ev(dim, val)
        self._record(ev, R, W)
        return ev

    def dma_fn(self, q, fn, R=(), W=()):
        slots = self.dq[q]
        i = self.dqn[q]
        self.dqn[q] += 1
        k = i % len(slots)
        val = 16 * (i // len(slots) + 1)
        deps = self._collect(q, R, W)
        dim = (q, k)
        if val > 16:
            deps[dim] = max(deps.get(dim, 0), val - 16)
        self._waits(q, deps)
        ins = fn(self.eng[q])
        ins.then_inc(slots[k], 16)
        self.nops += 1
        ev = Ev(dim, val)
        self._record(ev, R, W)
        return ev

    def _all_targets(self):
        targets = {}
        for e in self.eng:
            if self.cnt[e] > 0:
                targets[e] = self.cnt[e]
        for q, slots in self.dq.items():
            n = self.dqn[q]
            for k in range(len(slots)):
                uses = (n - k + len(slots) - 1) // len(slots) if n > k else 0
                if uses > 0:
                    targets[(q, k)] = 16 * uses
        return targets

    def barrier(self):
        assert not self.pe_pending
        targets = self._all_targets()
        for e in self.eng:
            self._waits(e, targets)

    def finish(self):
        assert not self.pe_pending
        self._waits("sp", self._all_targets())

    def mm(self, out, lhsT, rhs, start, stop, R=(), W=(), sig=None):
        if sig is None:
            sig = stop
        return self.op("pe", lambda e: e.matmul(out, lhsT, rhs, start=start, stop=stop), R, W, sig)

    def tr(self, out, in_, ident, R=(), W=(), sig=True):
        return self.op("pe", lambda e: e.transpose(out, in_, ident), R, W, sig)

    def act(self, out, in_, func, R=(), W=(), bias=None, scale=None, **kw):
        kws = dict(kw)
        if bias is not None:
            kws["bias"] = bias
        if scale is not None:
            kws["scale"] = scale
        return self.op("act", lambda e: e.activation(out=out, in_=in_, func=func, **kws), R, W)

    def tt(self, out, in0, in1, op, R=(), W=(), eng="dve"):
        return self.op(eng, lambda e: e.tensor_tensor(out=out, in0=in0, in1=in1, op=op), R, W)

    def ts(self, out, in0, s1, s2, op0, op1=None, R=(), W=(), eng="dve"):
        if op1 is None:
            return self.op(eng, lambda e: e.tensor_scalar(out=out, in0=in0, scalar1=s1, scalar2=None, op0=op0), R, W)
        return self.op(eng, lambda e: e.tensor_scalar(out=out, in0=in0, scalar1=s1, scalar2=s2, op0=op0, op1=op1), R, W)

    def stt(self, out, in0, scalar, in1, op0, op1, R=(), W=(), eng="dve"):
        return self.op(eng, lambda e: e.scalar_tensor_tensor(out=out, in0=in0, scalar=scalar, in1=in1, op0=op0, op1=op1), R, W)

    def copy(self, out, in_, R=(), W=(), eng="dve"):
        if eng == "act":
            return self.op("act", lambda e: e.copy(out=out, in_=in_), R, W)
        return self.op(eng, lambda e: e.tensor_copy(out=out, in_=in_), R, W)

    def memset(self, ap, v, W=(), eng="dve"):
        return self.op(eng, lambda e: e.memset(ap, v), (), W)


class Ctx:
    pass


def dbg_dump(kb, c, ap, R, rows, cols):
    if not c.cfg.get("DBG") or c.dbg_n >= 16:
        return
    kb.dma("sp", c.dbg_d[c.dbg_n, :rows, :cols], ap, R=R)
    c.dbg_n += 1


def token_groups(cfg, G):
    out = []
    for s in range(cfg["NP"]):
        for g in range(cfg["SEQ"] // G):
            out.append(("p", s, s * cfg["SEQ"] + g * G, G))
    out.append(("s", None, 0, cfg["NS"] * LS))
    return out


def subtiles(n):
    return [(r0, min(128, n - r0)) for r0 in range(0, n, 128)]


def load_rows(kb, c, tile, tok, kind, seq, col0, ncols=1024, q="sp"):
    cfg = c.cfg
    if kind == "p":
        src = c.mod_d[seq:seq + 1, col0:col0 + ncols].partition_broadcast(128)
        kb.dma(q, tile[:, :ncols], src, W=[tok])
    else:
        ts = cfg["NS"] * LS
        src = c.mod_d[cfg["NP"]:cfg["NP"] + ts, col0:col0 + ncols]
        kb.dma(q, tile[:ts, :ncols], src, W=[tok])


def layer_norm_rows(kb, c, t, nr, Tt, lng, lnb, Tl, st, mv, rstd, Tst):
    nc = c.nc
    for h in range(2):
        kb.op("dve", lambda e, h=h: e.bn_stats(out=st[:nr, h, :], in_=t[:nr, h * 512:(h + 1) * 512]), [Tt], [Tst])
    kb.op("dve", lambda e: e.bn_aggr(out=mv[:nr, :], in_=st[:nr, :, :]), [Tst], [Tst])
    kb.act(rstd[:nr, :], mv[:nr, 1:2], AF.Sqrt, R=[Tst], W=[Tst], bias=1e-5)
    kb.op("dve", lambda e: e.reciprocal(out=rstd[:nr, :], in_=rstd[:nr, :]), [Tst], [Tst])
    kb.ts(t[:nr, :], t[:nr, :], mv[:nr, 0:1], rstd[:nr, 0:1], ALU.subtract, ALU.mult, R=[Tt, Tst], W=[Tt])
    kb.tt(t[:nr, :], t[:nr, :], lng[:nr, :], ALU.mult, R=[Tt, Tl], W=[Tt], eng="pool")
    kb.tt(t[:nr, :], t[:nr, :], lnb[:nr, :], ALU.add, R=[Tt, Tl], W=[Tt], eng="pool")


def phase_mod(kb, c):
    nc, cfg = c.nc, c.cfg
    NR = cfg["NP"] + cfg["NS"] * LS
    with contextlib.ExitStack() as ps:
        sb = lambda n, s, d=F32: ps.enter_context(nc.sbuf_tensor(n, s, d))
        ct = sb("m_c", [NR, D])
        sct = sb("m_scT", [128, 8, NR], BF16)
        modt = sb("m_mod", [NR, 9 * D])
        bt = sb("m_b", [NR, 9 * D])
        wb = [sb("m_w%d" % i, [128, 8, 512], BF16) for i in range(2)]
        pst = [ps.enter_context(nc.psum_tensor("m_ps%d" % i, [128, 512], F32)) for i in range(4)]
        Tc, Tsct, Tmod, Tb = Tok(), Tok(), Tok(), Tok()
        Tw = [Tok(), Tok()]
        Tp = [Tok() for _ in range(4)]
        kb.dma("sp", ct[:], c.c_rows, W=[Tc])
        kb.dma("sp", bt[:], c.b_ada.partition_broadcast(NR), W=[Tb])
        kb.act(ct[:], ct[:], AF.Silu, R=[Tc], W=[Tc])
        for k in range(8):
            b = k // 4
            col = (k % 4) * NR
            kb.tr(pst[b][:, col:col + NR], ct[:, k * 128:(k + 1) * 128], c.ident_f[:NR, :NR], R=[Tc, c.Tconst], W=[Tp[b]])
        for b in range(2):
            kb.copy(sct[:, 4 * b:4 * b + 4, :], pst[b][:, 0:4 * NR].rearrange("p (k n) -> p k n", k=4), R=[Tp[b]], W=[Tsct])
        wv = c.w_ada.rearrange("(k p) f -> p k f", p=128)
        for nb in range(18):
            w = wb[nb % 2]
            kb.dma("pool", w[:], wv[:, :, nb * 512:(nb + 1) * 512], W=[Tw[nb % 2]])
            bank = pst[2 + nb % 2]
            for k in range(8):
                kb.mm(bank[:NR, :], sct[:, k, :], w[:, k, :], k == 0, k == 7, R=[Tsct, Tw[nb % 2]], W=[Tp[2 + nb % 2]])
            kb.tt(modt[:, nb * 512:(nb + 1) * 512], bank[:NR, :], bt[:, nb * 512:(nb + 1) * 512], ALU.add,
                  R=[Tp[2 + nb % 2], Tb], W=[Tmod])
        for i in range(3):
            o = (i * 3 + 1) * D
            kb.ts(modt[:, o:o + D], modt[:, o:o + D], 1.0, None, ALU.add, R=[Tmod], W=[Tmod])
        for i in (0, 2):
            o = (i * 3 + 2) * D
            kb.ts(modt[:, o:o + D], modt[:, o:o + D], 0.5, None, ALU.mult, R=[Tmod], W=[Tmod])
        kb.dma("sp", c.mod_d[:, :], modt[:], R=[Tmod])
        kb.barrier()


def phase_ffn(kb, c, tag, src, dst, w_gu, w_dn, isub):
    nc, cfg = c.nc, c.cfg
    G = 512
    NJ = DFF // 128
    with contextlib.ExitStack() as ps:
        sb = lambda n, s, d=F32: ps.enter_context(nc.sbuf_tensor(tag + n, s, d))
        wgu = sb("wgu", [128, 8, 2 * DFF], BF16)
        wdn = sb("wdn", [128, NJ, D], BF16)
        hT = sb("hT", [128, NJ, G], BF16)
        xmT = sb("xmT", [128, 8, G], BF16)
        xin = [sb("xin%d" % i, [128, D]) for i in range(4)]
        rows = {n: sb("row_" + n, [128, D], BF16 if n in ("sc", "sh", "g") else F32) for n in ("sc", "sh", "g", "lng", "lnb")}
        xm = sb("xm", [128, D])
        wk = [sb("wk%d" % i, [128, D]) for i in range(2)]
        sgt = [sb("sg%d" % i, [128, G], BF16) for i in range(2)]
        st = sb("st", [128, 2, nc.vector.BN_STATS_DIM])
        mv = sb("mv", [128, nc.vector.BN_AGGR_DIM])
        rstd = sb("rstd", [128, 1])
        psgu = [ps.enter_context(nc.psum_tensor(tag + "psgu%d" % i, [128, 512], F32)) for i in range(4)]
        psy = [ps.enter_context(nc.psum_tensor(tag + "psy%d" % i, [128, 1024], F32)) for i in range(2)]
        Tgu = [Tok() for _ in range(11)]
        Tdn = [Tok() for _ in range(11)]
        ThT = [Tok() for _ in range(NJ)]
        TxmT, Txm, Tst, Tln, Tmodrows = Tok(), Tok(), Tok(), Tok(), Tok()
        Txin = [Tok() for _ in range(4)]
        Twk = [Tok(), Tok()]
        Tsg = [Tok(), Tok()]
        Tpsgu = [Tok() for _ in range(4)]
        Tpsy = [Tok(), Tok()]
        guv = w_gu.rearrange("(k p) f -> p k f", p=128)
        for i in range(11):
            kb.dma("pool", wgu[:, :, i * 512:(i + 1) * 512], guv[:, :, i * 512:(i + 1) * 512], W=[Tgu[i]])
        dnv = w_dn.rearrange("(j p) d -> p j d", p=128)
        for i in range(11):
            kb.dma("pool", wdn[:, 2 * i:2 * i + 2, :], dnv[:, 2 * i:2 * i + 2, :], W=[Tdn[i]])
        kb.dma("sp", rows["lng"][:], c.ln_g[isub:isub + 1, :].partition_broadcast(128), W=[Tln])
        kb.dma("sp", rows["lnb"][:], c.ln_b[isub:isub + 1, :].partition_broadcast(128), W=[Tln])
        cur = None
        xslot = 0
        wslot = 0
        gi = 0
        for (kind, seq, t0, n) in token_groups(cfg, G):
            if (kind, seq) != cur:
                cur = (kind, seq)
                base = isub * 3 * D
                load_rows(kb, c, rows["sh"], Tmodrows, kind, seq, base, q="pool")
                load_rows(kb, c, rows["sc"], Tmodrows, kind, seq, base + D, q="pool")
                load_rows(kb, c, rows["g"], Tmodrows, kind, seq, base + 2 * D, q="pool")
            subs = subtiles(n)
            xs = []
            for si, (r0, nr) in enumerate(subs):
                xt = xin[xslot]
                Tx = Txin[xslot]
                xslot = (xslot + 1) % 4
                xs.append((xt, Tx))
                kb.dma("sp", xt[:nr, :], src[kind][t0 + r0:t0 + r0 + nr, :], W=[Tx])
                kb.tt(xm[:nr, :], xt[:nr, :], rows["sc"][:nr, :], ALU.mult, R=[Tx, Tmodrows], W=[Txm], eng="pool")
                kb.tt(xm[:nr, :], xm[:nr, :], rows["sh"][:nr, :], ALU.add, R=[Txm, Tmodrows], W=[Txm], eng="pool")
                py = psy[si % 2]
                for k in range(8):
                    kb.tr(py[:, k * 128:k * 128 + nr], xm[:nr, k * 128:(k + 1) * 128], c.ident_f[:nr, :nr],
                          R=[Txm, c.Tconst], W=[Tpsy[si % 2]])
                kb.copy(xmT[:, :, r0:r0 + nr], py[:, :].rearrange("p (k n) -> p k n", k=8)[:, :, :nr],
                        R=[Tpsy[si % 2]], W=[TxmT], eng="act")
            for j in range(NJ):
                pg = psgu[(j % 2) * 2]
                pu = psgu[(j % 2) * 2 + 1]
                Tg_, Tu_ = Tpsgu[(j % 2) * 2], Tpsgu[(j % 2) * 2 + 1]
                cg = j * 128
                cu = DFF + j * 128
                for k in range(8):
                    kb.mm(pg[:, :n], wgu[:, k, cg:cg + 128], xmT[:, k, :n], k == 0, k == 7,
                          R=[TxmT, Tgu[cg // 512]], W=[Tg_])
                for k in range(8):
                    kb.mm(pu[:, :n], wgu[:, k, cu:cu + 128], xmT[:, k, :n], k == 0, k == 7,
                          R=[TxmT, Tgu[cu // 512]], W=[Tu_])
                s = sgt[j % 2]
                kb.act(s[:, :n], pg[:, :n], AF.Silu, R=[Tg_], W=[Tsg[j % 2]])
                kb.tt(hT[:, j, :n], s[:, :n], pu[:, :n], ALU.mult, R=[Tsg[j % 2], Tu_], W=[ThT[j]])
            for si, (r0, nr) in enumerate(subs):
                py = psy[si % 2]
                Tpy = Tpsy[si % 2]
                for half in range(2):
                    for j in range(NJ):
                        kb.mm(py[:nr, half * 512:(half + 1) * 512], hT[:, j, r0:r0 + nr],
                              wdn[:, j, half * 512:(half + 1) * 512], j == 0, j == NJ - 1,
                              R=[ThT[j], Tdn[j // 2]], W=[Tpy])
                w = wk[wslot]
                Tw_ = Twk[wslot]
                wslot = (wslot + 1) % 2
                xt, Tx = xs[si]
                kb.tt(w[:nr, :], py[:nr, :], rows["g"][:nr, :], ALU.mult, R=[Tpy, Tmodrows], W=[Tw_])
                kb.stt(w[:nr, :], xt[:nr, :], ALPHA, w[:nr, :], ALU.mult, ALU.add, R=[Tx, Tw_], W=[Tw_])
                layer_norm_rows(kb, c, w, nr, Tw_, rows["lng"], rows["lnb"], Tln, st, mv, rstd, Tst)
                kb.dma("sp", dst[kind][t0 + r0:t0 + r0 + nr, :], w[:nr, :], R=[Tw_])
            gi += 1
        kb.barrier()


QKV0, Z0, Q0, KV0, GN0, MG0 = 0, 1536, 2064, 2576, 3344, 3368


def phase_win(kb, c):
    nc, cfg = c.nc, c.cfg
    G = 512
    SEQ, NP, NS = cfg["SEQ"], cfg["NP"], cfg["NS"]
    TP, TS = c.TP, c.TS
    with contextlib.ExitStack() as ps:
        sb = lambda n, s, d=F32: ps.enter_context(nc.sbuf_tensor("wi_" + n, s, d))
        win = sb("w", [128, 8, DIN], BF16)
        hT = sb("hT", [128, 8, G], BF16)
        xin = [sb("xin%d" % i, [128, D]) for i in range(2)]
        xm = sb("xm", [128, D])
        rows = {n: sb("row_" + n, [128, D]) for n in ("sc", "sh")}
        stF = [sb("stF%d" % i, [128, 4, G]) for i in range(2)]
        stB = [sb("stB%d" % i, [128, 4, G], BF16) for i in range(3)]
        stT = [sb("stT%d" % i, [128, 1536]) for i in range(2)]
        psx = ps.enter_context(nc.psum_tensor("wi_psx", [128, 1024], F32))
        psF = [ps.enter_context(nc.psum_tensor("wi_psF%d" % i, [128, 512], F32)) for i in range(3)]
        psT = ps.enter_context(nc.psum_tensor("wi_psT", [128, 1536], F32))
        Tw = [Tok() for _ in range(11)]
        ThT, Txm, Tmodrows, Tpsx, TpsT = Tok(), Tok(), Tok(), Tok(), Tok()
        Txin = [Tok(), Tok()]
        TstF = [Tok(), Tok()]
        TstB = [Tok(), Tok(), Tok()]
        TstT = [Tok(), Tok()]
        TpsF = [Tok(), Tok(), Tok()]
        wv = c.w_in.rearrange("(k p) f -> p k f", p=128)
        for i in range(11):
            hi = min(DIN, (i + 1) * 512)
            kb.dma("pool", win[:, :, i * 512:hi], wv[:, :, i * 512:hi], W=[Tw[i]])

        def wtoks(c0, c1):
            return [Tw[i] for i in range(c0 // 512, (c1 - 1) // 512 + 1)]

        wl = cfg["WIN"]
        kb.dma("sp", c.s_win[:, 0:wl - LS, :], c.cache_win[:, LS:wl, :])
        cur = None
        xslot = 0
        cnt = {"F": 0, "B": 0, "T": 0, "pf": 0, "ev": 0}
        for (kind, seq, t0, n) in token_groups(cfg, G):
            tg = t0 if kind == "p" else TP + t0
            if (kind, seq) != cur:
                cur = (kind, seq)
                load_rows(kb, c, rows["sh"], Tmodrows, kind, seq, 3 * D)
                load_rows(kb, c, rows["sc"], Tmodrows, kind, seq, 4 * D)
            subs = subtiles(n)
            for si, (r0, nr) in enumerate(subs):
                xt, Tx = xin[xslot], Txin[xslot]
                xslot = (xslot + 1) % 2
                kb.dma("sp", xt[:nr, :], c.x1[kind][t0 + r0:t0 + r0 + nr, :], W=[Tx])
                kb.tt(xm[:nr, :], xt[:nr, :], rows["sc"][:nr, :], ALU.mult, R=[Tx, Tmodrows], W=[Txm], eng="pool")
                kb.tt(xm[:nr, :], xm[:nr, :], rows["sh"][:nr, :], ALU.add, R=[Txm, Tmodrows], W=[Txm], eng="pool")
                for k in range(8):
                    kb.tr(psx[:, k * 128:k * 128 + nr], xm[:nr, k * 128:(k + 1) * 128], c.ident_f[:nr, :nr],
                          R=[Txm, c.Tconst], W=[Tpsx])
                kb.copy(hT[:, :, r0:r0 + nr], psx[:, :].rearrange("p (k n) -> p k n", k=8)[:, :, :nr],
                        R=[Tpsx], W=[ThT], eng="act")

            def fgroup(cols, stage_kind, dst_ap, func=None):
                if stage_kind == "F":
                    st, Ts = stF[cnt["F"] % 2], TstF[cnt["F"] % 2]
                    cnt["F"] += 1
                else:
                    st, Ts = stB[cnt["B"] % 3], TstB[cnt["B"] % 3]
                    cnt["B"] += 1
                for ci, c0 in enumerate(cols):
                    pf, Tpf = psF[cnt["pf"] % 3], TpsF[cnt["pf"] % 3]
                    cnt["pf"] += 1
                    for k in range(8):
                        kb.mm(pf[:, :n], win[:, k, c0:c0 + 128], hT[:, k, :n], k == 0, k == 7,
                              R=[ThT] + wtoks(c0, c0 + 128), W=[Tpf])
                    if func is not None:
                        kb.act(st[:, ci, :n], pf[:, :n], func, R=[Tpf], W=[Ts])
                    else:
                        eng = "act" if cnt["ev"] % 2 == 0 else "dve"
                        cnt["ev"] += 1
                        kb.copy(st[:, ci, :n], pf[:, :n], R=[Tpf], W=[Ts], eng=eng)
                kb.dma("sp", dst_ap, st[:, 0:len(cols), :n], R=[Ts])

            qv = c.qkvT_d.rearrange("(c p) t -> p c t", p=128)
            for g4 in range(3):
                fgroup([QKV0 + (g4 * 4 + i) * 128 for i in range(4)], "F", qv[:, g4 * 4:g4 * 4 + 4, tg:tg + n])
            fgroup([Q0 + i * 128 for i in range(4)], "B", c.qT_d.rearrange("(c p) t -> p c t", p=128)[:, :, tg:tg + n])
            fgroup([KV0, KV0 + 256, KV0 + 512], "B", c.kT_d.rearrange("i p t -> p i t")[:, :, tg:tg + n])
            mv_ = c.mgT_d.rearrange("(c p) t -> p c t", p=128)
            for g4 in range(4):
                fgroup([MG0 + (g4 * 4 + i) * 128 for i in range(4)], "B", mv_[:, g4 * 4:g4 * 4 + 4, tg:tg + n], func=AF.Sigmoid)

            for si, (r0, nr) in enumerate(subs):
                def tgroup(c0, ncols):
                    st, Ts = stT[cnt["T"] % 2], TstT[cnt["T"] % 2]
                    cnt["T"] += 1
                    for b0 in range(0, ncols, 512):
                        w_ = min(512, ncols - b0)
                        for k in range(8):
                            kb.mm(psT[:nr, b0:b0 + w_], hT[:, k, r0:r0 + nr], win[:, k, c0 + b0:c0 + b0 + w_],
                                  k == 0, k == 7, R=[ThT] + wtoks(c0 + b0, c0 + b0 + w_), W=[TpsT])
                    kb.copy(st[:nr, :ncols], psT[:nr, :ncols], R=[TpsT], W=[Ts])
                    return st, Ts

                a0 = t0 + r0
                st, Ts = tgroup(Z0, 528)
                kb.dma("sp", c.zab_d[tg + r0:tg + r0 + nr, :], st[:nr, :528], R=[Ts])
                st, Ts = tgroup(KV0, 792)
                kb.dma("sp", c.kvg_d[tg + r0:tg + r0 + nr, :], st[:nr, :792], R=[Ts])
                if kind == "p":
                    kb.dma("sp", c.p_cmp[a0:a0 + nr, :], st[:nr, 0:256], R=[Ts])
                    kb.dma("sp", c.p_sel[a0:a0 + nr, :], st[:nr, 256:512], R=[Ts])
                    pos = a0 - seq * SEQ
                    w0 = SEQ - min(wl, SEQ)
                    if pos >= w0:
                        wr = seq * min(wl, SEQ) + pos - w0
                        kb.dma("sp", c.p_win[wr:wr + nr, :], st[:nr, 512:768], R=[Ts])
                else:
                    kb.dma("sp", c.s_cmp[a0:a0 + nr, :], st[:nr, 0:256], R=[Ts])
                    kb.dma("sp", c.s_sel[a0:a0 + nr, :], st[:nr, 256:512], R=[Ts])
                    for l in range(LS):
                        kb.dma("sp", c.s_win[:, wl - LS + l, :], st[l:nr:LS, 512:768], R=[Ts])
                last_of_seq = (kind == "p" and a0 + nr == (seq + 1) * SEQ)
                if last_of_seq or kind == "s":
                    st, Ts = tgroup(QKV0, 1536)
                    if kind == "p":
                        kb.dma("sp", c.p_conv[seq, :, :], st[nr - 3:nr, :1536], R=[Ts])
                    else:
                        for l in range(1, LS):
                            kb.dma("sp", c.s_conv[:, l - 1, :], st[l:nr:LS, :1536], R=[Ts])
        kb.barrier()


def phase_gdn_a(kb, c):
    nc, cfg = c.nc, c.cfg
    G = 512
    SEQ, NP, NS = cfg["SEQ"], cfg["NP"], cfg["NS"]
    TP, TS = c.TP, c.TS
    with contextlib.ExitStack() as ps:
        sb = lambda n, s, d=F32: ps.enter_context(nc.sbuf_tensor("ga_" + n, s, d))
        cwt = sb("cwt", [4, 1536])
        cw = sb("cw", [128, 12, 4])
        blk1 = sb("blk1", [128, 128])
        xr = [sb("xr%d" % i, [128, 12, 3 + G]) for i in range(2)]
        yc = [sb("yc%d" % i, [128, 12, G]) for i in range(2)]
        sq = sb("sq", [128, 8, G])
        rn = [sb("rn%d" % i, [128, G]) for i in range(2)]
        kvt = sb("kvt", [128, 4, 1024])
        cbt = sb("cbt", [NS * 3, 1536])
        xrs = sb("xrs", [128, 12, NS, 3 + LS])
        pss = [ps.enter_context(nc.psum_tensor("ga_pss%d" % i, [128, 512], F32)) for i in range(2)]
        pst = [ps.enter_context(nc.psum_tensor("ga_pst%d" % i, [128, 1024], F32)) for i in range(2)]
        Tcw, Tblk, Tsq, Tkvt, Tcbt, Txrs = Tok(), Tok(), Tok(), Tok(), Tok(), Tok()
        Txr = [Tok(), Tok()]
        Tyc = [Tok(), Tok()]
        Trn = [Tok(), Tok()]
        Tpss = [Tok(), Tok()]
        Tpst = [Tok(), Tok()]
        kb.dma("sp", cwt[:], c.conv_w, W=[Tcw])
        kb.dma("sp", blk1[:], c.blk1_d, W=[Tblk])
        for cc in range(12):
            kb.tr(pss[0][:, cc * 4:cc * 4 + 4], cwt[:, cc * 128:(cc + 1) * 128], c.ident_f[:4, :4], R=[Tcw, c.Tconst], W=[Tpss[0]])
        kb.copy(cw[:, :, :], pss[0][:, 0:48].rearrange("p (c j) -> p c j", j=4), R=[Tpss[0]], W=[Tcw])
        qv = c.qkvT_d.rearrange("(c p) t -> p c t", p=128)
        qn = c.qkvn_d.rearrange("(c p) t -> p c t", p=128)
        bi = 0
        for (kind, seq, t0, n) in token_groups(cfg, G):
            tg = t0 if kind == "p" else TP + t0
            y, Ty = yc[bi % 2], Tyc[bi % 2]
            x, Tx = xr[bi % 2], Txr[bi % 2]
            bi += 1
            if kind == "p":
                if t0 % SEQ == 0:
                    kb.memset(x[:, :, 0:3], 0.0, W=[Tx])
                    kb.dma("sp", x[:, :, 3:3 + n], qv[:, :, tg:tg + n], W=[Tx])
                else:
                    kb.dma("sp", x[:, :, 0:3 + n], qv[:, :, tg - 3:tg + n], W=[Tx])
                for cc in range(12):
                    eng = "dve"
                    kb.ts(y[:, cc, :n], x[:, cc, 0:n], cw[:, cc, 0:1], None, ALU.mult, R=[Tx, Tcw], W=[Ty], eng=eng)
                    for j in range(1, 4):
                        kb.stt(y[:, cc, :n], x[:, cc, j:j + n], cw[:, cc, j:j + 1], y[:, cc, :n], ALU.mult, ALU.add,
                               R=[Tx, Tcw, Ty], W=[Ty], eng=eng)
            else:
                kb.dma("sp", cbt[:], c.conv_buf.rearrange("b r f -> (b r) f"), W=[Tcbt])
                for cc in range(12):
                    kb.tr(pst[0][:, cc * 64:cc * 64 + NS * 3], cbt[:, cc * 128:(cc + 1) * 128], c.ident_f[:NS * 3, :NS * 3],
                          R=[Tcbt, c.Tconst], W=[Tpst[0]])
                kb.copy(xrs[:, :, :, 0:3],
                        pst[0][:, 0:768].rearrange("p (c x) -> p c x", x=64)[:, :, 0:NS * 3].rearrange("p c (b r) -> p c b r", r=3),
                        R=[Tpst[0]], W=[Txrs])
                for cc in range(12):
                    kb.dma("sp", xrs[:, cc, :, 3:3 + LS], qv[:, cc, tg:tg + n].rearrange("p (b l) -> p b l", l=LS), W=[Txrs])
                for cc in range(12):
                    eng = "dve"
                    yv = y[:, cc, :n].rearrange("p (b l) -> p b l", l=LS)
                    kb.ts(yv, xrs[:, cc, :, 0:LS], cw[:, cc, 0:1], None, ALU.mult, R=[Txrs, Tcw], W=[Ty], eng=eng)
                    for j in range(1, 4):
                        kb.stt(yv, xrs[:, cc, :, j:j + LS], cw[:, cc, j:j + 1], yv, ALU.mult, ALU.add,
                               R=[Txrs, Tcw, Ty], W=[Ty], eng=eng)
            kb.act(y[:, :, :n], y[:, :, :n], AF.Silu, R=[Ty], W=[Ty])
            kb.tt(sq[:, :, :n], y[:, 0:8, :n], y[:, 0:8, :n], ALU.mult, R=[Ty], W=[Tsq], eng="pool")
            for cc in range(8):
                p_, Tp_ = pss[cc % 2], Tpss[cc % 2]
                r_, Tr_ = rn[cc % 2], Trn[cc % 2]
                kb.mm(p_[:, :n], blk1[:, :], sq[:, cc, :n], True, True, R=[Tsq, Tblk], W=[Tp_])
                if cc < 4:
                    kb.act(r_[:, :n], p_[:, :n], AF.Sqrt, R=[Tp_], W=[Tr_], scale=64.0, bias=64e-6)
                else:
                    kb.act(r_[:, :n], p_[:, :n], AF.Sqrt, R=[Tp_], W=[Tr_], bias=1e-6)
                kb.op("dve", lambda e, r_=r_: e.reciprocal(out=r_[:, :n], in_=r_[:, :n]), [Tr_], [Tr_])
                kb.tt(y[:, cc, :n], y[:, cc, :n], r_[:, :n], ALU.mult, R=[Ty, Tr_], W=[Ty])
            kb.dma("sp", qn[:, :, tg:tg + n], y[:, :, :n], R=[Ty])
            subs = subtiles(n)
            for si, (r0, nr) in enumerate(subs):
                p_, Tp_ = pst[si % 2], Tpst[si % 2]
                for cc in range(8):
                    kb.tr(p_[:nr, cc * 128:(cc + 1) * 128], y[:, 4 + cc, r0:r0 + nr], c.ident_f[:, :], R=[Ty, c.Tconst], W=[Tp_])
                kb.copy(kvt[:nr, si, :], p_[:nr, :], R=[Tp_], W=[Tkvt], eng=("act" if si % 2 == 0 else "dve"))
                kb.dma("sp", c.kvtok_d[tg + r0:tg + r0 + nr, :], kvt[:nr, si, :], R=[Tkvt])
        kb.barrier()


def phase_gdn_b(kb, c):
    nc, cfg = c.nc, c.cfg
    SEQ, NP, NS = cfg["SEQ"], cfg["NP"], cfg["NS"]
    TP, TS = c.TP, c.TS
    C = 64
    NL = cfg.get("GDN_LANES", 2)
    with contextlib.ExitStack() as ps:
        sb = lambda n, s, d=F32: ps.enter_context(nc.sbuf_tensor("gb_" + n, s, d))
        gc = sb("const", [64, 5, 64])
        nA = sb("nA", [64, 8])
        dtb = sb("dtb", [64, 8])
        normw = sb("normw", [64, 64])
        valid = sb("valid", [64, 1])
        banks = [ps.enter_context(nc.psum_tensor("gb_ps%d" % i, [128, 512], F32)) for i in range(8)]
        Tb = [Tok() for _ in range(8)]
        bstate = {"i": 0}

        def bank():
            i = bstate["i"] % 8
            bstate["i"] += 1
            return banks[i], Tb[i]

        Tg = {n: Tok(n) for n in ("gc", "nA", "dtb", "normw", "valid")}
        U_, ones_, mup, mlo, idn = (gc[:, i, :] for i in range(5))
        kb.dma("sp", gc[:], c.gconst_d, W=[Tg["gc"]])
        kb.dma("sp", nA[:], c.a_log.partition_broadcast(64), W=[Tg["nA"]])
        kb.dma("sp", dtb[:], c.dt_bias.partition_broadcast(64), W=[Tg["dtb"]])
        kb.dma("sp", normw[:], c.norm_w.partition_broadcast(64), W=[Tg["normw"]])
        kb.dma("sp", valid[:], c.valid_d, W=[Tg["valid"]])
        kb.act(nA[:], nA[:], AF.Exp, R=[Tg["nA"]], W=[Tg["nA"]])
        kb.ts(nA[:], nA[:], -1.0, None, ALU.mult, R=[Tg["nA"]], W=[Tg["nA"]])
        bc_h = lambda ap: ap.unsqueeze(2).broadcast_to([64, 8, 64])
        bc_m = lambda ap: ap.unsqueeze(1).broadcast_to([64, 8, 64])
        v3 = lambda ap: ap.rearrange("p (h d) -> p h d", h=8)
        qn_q = c.qkvn_d[0:512, :].rearrange("(h d) t -> d h t", d=64)
        qn_k = c.qkvn_d[512:1024, :].rearrange("(h d) t -> d h t", d=64)

        class Lane:
            pass

        lanes = []
        for li in range(NL):
            L = Lane()
            w3 = lambda n: (sb("l%d_%s" % (li, n), [64, 8, 64]), Tok())
            L.qT, L.Tq = w3("qT")
            L.kT, L.Tk = w3("kT")
            L.kv, L.Tkv = sb("l%d_kv" % li, [64, 1024]), Tok()
            L.zab, L.Tz = sb("l%d_zab" % li, [64, 528]), Tok()
            L.nwz, L.Tnwz = sb("l%d_nwz" % li, [64, 512]), Tok()
            L.og, L.Tog = sb("l%d_og" % li, [64, 512], BF16), Tok()
            L.S = w3("S")
            L.Tsm = Tok()
            L.small = {n: sb("l%d_%s" % (li, n), [64, w]) for n, w in
                       (("g", 8), ("beta", 8), ("nbeta", 8), ("gtmp", 8), ("GG", 16), ("EG", 16), ("E2", 8), ("bg", 8), ("ss", 8), ("lnv", 8))}
            alias = cfg.get("GDN_ALIAS", 0)
            if alias:
                a1, a2, a3, a4, a5 = w3("a1"), w3("a2"), w3("a3"), w3("a4"), w3("a5")
                L.gmat, L.vb, L.oc = a1, a1, a1
                L.tmp1, L.kbg, L.sqo = a2, a2, a2
                L.tmpT, L.u = a3, a3
                L.Dn, L.wT = a4, a4
                L.EGr, L.vn = a5, a5
            else:
                for n_ in ("gmat", "vb", "oc", "tmp1", "kbg", "sqo", "tmpT", "u", "Dn", "wT", "EGr", "vn"):
                    setattr(L, n_, w3(n_))
            L.Dt, L.aqkT, L.qd, L.kd = w3("Dt"), w3("aqkT"), w3("qd"), w3("kd")
            L.X = [w3("X0"), w3("X1")]
            L.Y = [w3("Y0"), w3("Y1")]
            L.Q = [w3("Q0"), w3("Q1")]
            lanes.append(L)

        def chunk(L, sample):
            Tsm = L.Tsm
            g, beta, nbeta, gtmp, GG, EG, E2, bg, ss, lnv = (L.small[n][:, :] for n in
                                                            ("g", "beta", "nbeta", "gtmp", "GG", "EG", "E2", "bg", "ss", "lnv"))
            z = L.zab
            qT, kT = L.qT[:], L.kT[:]
            ktok, vtok = v3(L.kv[:, 0:512]), v3(L.kv[:, 512:1024])
            Rq, Rk, Rkv, Tz_ = L.Tq, L.Tk, L.Tkv, L.Tz
            kb.tt(gtmp, z[:, 512:520], dtb[:, :], ALU.add, R=[Tz_, Tg["dtb"]], W=[Tsm])
            kb.act(gtmp, gtmp, AF.Exp, R=[Tsm], W=[Tsm])
            yield
            kb.act(gtmp, gtmp, AF.Ln, R=[Tsm], W=[Tsm], bias=1.0)
            kb.act(beta, z[:, 520:528], AF.Exp, R=[Tz_], W=[Tsm], scale=-1.0)
            yield
            kb.tt(g, gtmp, nA[:, :], ALU.mult, R=[Tsm, Tg["nA"]], W=[Tsm])
            kb.ts(beta, beta, 1.0, None, ALU.add, R=[Tsm], W=[Tsm])
            kb.op("dve", lambda e: e.reciprocal(out=beta, in_=beta), [Tsm], [Tsm])
            if sample:
                kb.ts(g, g, valid[:, 0:1], None, ALU.mult, R=[Tsm, Tg["valid"]], W=[Tsm])
                kb.ts(beta, beta, valid[:, 0:1], None, ALU.mult, R=[Tsm, Tg["valid"]], W=[Tsm])
            kb.ts(nbeta, beta, -1.0, None, ALU.mult, R=[Tsm], W=[Tsm])
            kb.act(L.nwz[:, :], z[:, 0:512], AF.Exp, R=[Tz_], W=[L.Tnwz], scale=-1.0)
            yield
            kb.ts(L.nwz[:, :], L.nwz[:, :], 1.0, None, ALU.add, R=[L.Tnwz], W=[L.Tnwz], eng="pool")
            kb.op("dve", lambda e: e.reciprocal(out=L.nwz[:, :], in_=L.nwz[:, :]), [L.Tnwz], [L.Tnwz])
            kb.tt(L.nwz[:, :], L.nwz[:, :], z[:, 0:512], ALU.mult, R=[L.Tnwz, Tz_], W=[L.Tnwz], eng="pool")
            kb.tt(v3(L.nwz[:, :]), v3(L.nwz[:, :]), bc_m(normw[:, :]), ALU.mult, R=[L.Tnwz, Tg["normw"]], W=[L.Tnwz], eng="pool")
            pa, Tpa = bank()
            kb.mm(pa[:64, 0:8], U_, g, True, True, R=[Tg["gc"], Tsm], W=[Tpa], sig=False)
            kb.mm(pa[:64, 8:16], ones_, g, True, True, R=[Tg["gc"], Tsm], W=[Tpa])
            gmat, Tgmat = L.gmat
            kb.copy(gmat[:], bc_h(g), R=[Tsm], W=[Tgmat], eng="pool")
            yield
            kb.copy(GG, pa[:64, 0:16], R=[Tpa], W=[Tsm])
            pgr, Tpgr = bank()
            for h in range(8):
                kb.mm(pgr[:64, h * 64:(h + 1) * 64], gmat[:, h, :], U_, True, True, R=[Tgmat, Tg["gc"]], W=[Tpgr], sig=(h == 7))
            pkk, Tpkk = bank()
            for h in range(8):
                kb.mm(pkk[:64, h * 64:(h + 1) * 64], kT[:, h, :], kT[:, h, :], True, True, R=[Rk], W=[Tpkk], sig=(h == 7))
            yield
            kb.act(EG, GG, AF.Exp, R=[Tsm], W=[Tsm])
            kb.tt(E2, GG[:, 8:16], GG[:, 0:8], ALU.subtract, R=[Tsm], W=[Tsm])
            tmp1, Ttmp1 = L.tmp1
            tmpT, TtmpT = L.tmpT
            Dt, TDt = L.Dt
            Dn, TDn = L.Dn
            EGr, TEGr = L.EGr
            kb.tt(tmp1[:], v3(pgr[:64, :]), bc_h(GG[:, 0:8]), ALU.subtract, R=[Tpgr, Tsm], W=[Ttmp1])
            kb.act(EGr[:], v3(pgr[:64, :]), AF.Exp, R=[Tpgr, Ttmp1], W=[TEGr])
            yield
            kb.act(E2, E2, AF.Exp, R=[Tsm], W=[Tsm])
            kb.tt(tmpT[:], tmp1[:], bc_m(mup), ALU.add, R=[Ttmp1, Tg["gc"]], W=[TtmpT], eng="pool")
            yield
            kb.act(Dt[:], tmpT[:], AF.Exp, R=[TtmpT], W=[TDt])
            yield
            kb.tt(tmpT[:], tmp1[:], bc_m(mlo), ALU.add, R=[Ttmp1, Tg["gc"]], W=[TtmpT], eng="pool")
            yield
            kb.act(Dn[:], tmpT[:], AF.Exp, R=[TtmpT], W=[TDn], scale=-1.0)
            yield
            (X0, TX0), (Y0, TY0), (Q0, TQ0) = L.X[0], L.Y[0], L.Q[0]
            kb.tt(X0[:], v3(pkk[:64, :]), Dn[:], ALU.mult, R=[Tpkk, TDn], W=[TX0])
            kb.tt(X0[:], X0[:], bc_h(nbeta), ALU.mult, R=[TX0, Tsm], W=[TX0])
            yield
            pt_, Tpt = bank()
            for h in range(8):
                kb.tr(pt_[:64, h * 64:(h + 1) * 64], X0[:, h, :], idn, R=[TX0, Tg["gc"]], W=[Tpt], sig=(h == 7))
            yield
            kb.copy(Y0[:], v3(pt_[:64, :]), R=[Tpt], W=[TY0], eng="act")
            yield
            kb.tt(Q0[:], Y0[:], bc_m(idn), ALU.add, R=[TY0, Tg["gc"]], W=[TQ0])
            cur = 0
            for k in range(1, 6):
                nx = 1 - cur
                (Xc, TXc), (Yc, TYc), (Qc, TQc) = L.X[cur], L.Y[cur], L.Q[cur]
                (Xn, TXn), (Yn, TYn), (Qn, TQn) = L.X[nx], L.Y[nx], L.Q[nx]
                pX, TpX = bank()
                for h in range(8):
                    kb.mm(pX[:64, h * 64:(h + 1) * 64], Yc[:, h, :], Xc[:, h, :], True, True, R=[TXc, TYc], W=[TpX], sig=(h == 7))
                if k < 5:
                    pY, TpY = bank()
                    for h in range(8):
                        kb.mm(pY[:64, h * 64:(h + 1) * 64], Xc[:, h, :], Yc[:, h, :], True, True, R=[TXc, TYc], W=[TpY], sig=(h == 7))
                yield
                kb.copy(Xn[:], v3(pX[:64, :]), R=[TpX], W=[TXn], eng="act")
                if k < 5:
                    kb.copy(Yn[:], v3(pY[:64, :]), R=[TpY], W=[TYn], eng=("dve" if k % 2 == 0 else "act"))
                yield
                pQ, TpQ = bank()
                for h in range(8):
                    kb.mm(pQ[:64, h * 64:(h + 1) * 64], Xn[:, h, :], Qc[:, h, :], True, True, R=[TXn, TQc], W=[TpQ], sig=(h == 7))
                yield
                kb.tt(Qn[:], Qc[:], v3(pQ[:64, :]), ALU.add, R=[TQc, TpQ], W=[TQn])
                cur = nx
            Q, TQc = L.Q[cur]
            vb, Tvb = L.vb
            kbg, Tkbg = L.kbg
            u, Tu = L.u
            wT, TwT = L.wT
            aqkT, Taqk = L.aqkT
            qd, Tqd = L.qd
            kd, Tkd = L.kd
            vn, Tvn = L.vn
            kb.tt(vb[:], vtok, bc_h(beta), ALU.mult, R=[Rkv, Tsm], W=[Tvb])
            kb.tt(bg, beta, EG[:, 0:8], ALU.mult, R=[Tsm], W=[Tsm])
            kb.tt(qd[:], qT, EGr[:], ALU.mult, R=[Rq, TEGr], W=[Tqd], eng="pool")
            kb.tt(kd[:], ktok, bc_h(E2), ALU.mult, R=[Rkv, Tsm], W=[Tkd], eng="pool")
            yield
            kb.tt(kbg[:], ktok, bc_h(bg), ALU.mult, R=[Rkv, Tsm], W=[Tkbg], eng="pool")
            pu, Tpu = bank()
            for h in range(8):
                kb.mm(pu[:64, h * 64:(h + 1) * 64], Q[:, h, :], vb[:, h, :], True, True, R=[TQc, Tvb], W=[Tpu], sig=(h == 7))
            pq, Tpq = bank()
            for h in range(8):
                kb.mm(pq[:64, h * 64:(h + 1) * 64], kT[:, h, :], qT[:, h, :], True, True, R=[Rk, Rq], W=[Tpq], sig=(h == 7))
            yield
            pw, Tpw = bank()
            for h in range(8):
                kb.mm(pw[:64, h * 64:(h + 1) * 64], kbg[:, h, :], Q[:, h, :], True, True, R=[TQc, Tkbg], W=[Tpw], sig=(h == 7))
            kb.copy(u[:], v3(pu[:64, :]), R=[Tpu], W=[Tu], eng="act")
            kb.tt(aqkT[:], v3(pq[:64, :]), Dt[:], ALU.mult, R=[Tpq, TDt], W=[Taqk])
            yield
            kb.copy(wT[:], v3(pw[:64, :]), R=[Tpw], W=[TwT], eng="act")
            yield
            S, TS_ = L.S, L.TS
            pws, Tpws = bank()
            for h in range(8):
                kb.mm(pws[:64, h * 64:(h + 1) * 64], wT[:, h, :], S[:, h, :], True, True, R=[TwT, TS_], W=[Tpws], sig=(h == 7))
            yield
            kb.tt(vn[:], u[:], v3(pws[:64, :]), ALU.subtract, R=[Tu, Tpws], W=[Tvn])
            yield
            po, Tpo = bank()
            for h in range(8):
                kb.mm(po[:64, h * 64:(h + 1) * 64], qd[:, h, :], S[:, h, :], True, False, R=[Tqd, TS_], W=[Tpo], sig=False)
                kb.mm(po[:64, h * 64:(h + 1) * 64], aqkT[:, h, :], vn[:, h, :], False, True, R=[Taqk, Tvn], W=[Tpo], sig=(h == 7))
            pds, Tpds = bank()
            for h in range(8):
                kb.mm(pds[:64, h * 64:(h + 1) * 64], kd[:, h, :], vn[:, h, :], True, True, R=[Tkd, Tvn], W=[Tpds], sig=(h == 7))
            yield
            kb.tt(S[:], S[:], bc_h(EG[:, 8:16]), ALU.mult, R=[TS_, Tsm], W=[TS_])
            oc, Toc = L.oc
            kb.copy(oc[:], v3(po[:64, :]), R=[Tpo], W=[Toc], eng="act")
            yield
            kb.tt(S[:], S[:], v3(pds[:64, :]), ALU.add, R=[TS_, Tpds], W=[TS_])
            sqo, Tsqo = L.sqo
            kb.tt(sqo[:], oc[:], oc[:], ALU.mult, R=[Toc], W=[Tsqo], eng="pool")
            yield
            kb.op("dve", lambda e: e.tensor_reduce(out=ss, in_=sqo[:], axis=AX.X, op=ALU.add), [Tsqo], [Tsm])
            yield
            kb.act(lnv, ss, AF.Ln, R=[Tsm], W=[Tsm], scale=1.0 / 64.0, bias=1e-6)
            yield
            kb.act(lnv, lnv, AF.Exp, R=[Tsm], W=[Tsm], scale=-0.5)
            yield
            kb.tt(oc[:], oc[:], bc_h(lnv), ALU.mult, R=[Toc, Tsm], W=[Toc])
            yield
            kb.tt(v3(L.og[:, :]), oc[:], v3(L.nwz[:, :]), ALU.mult, R=[Toc, L.Tnwz], W=[L.Tog])

        def run_lockstep(gens):
            gens = list(gens)
            while gens:
                nxt = []
                for g_ in gens:
                    try:
                        next(g_)
                        nxt.append(g_)
                    except StopIteration:
                        pass
                gens = nxt

        def seq_gen(L, kind, idx):
            if kind == "p":
                kb.memset(L.S[0][:], 0.0, W=[L.S[1]]) if False else None
            return

        def lane_prompt(L, s):
            S, TS_ = L.S, L.TS
            kb.memset(S[:], 0.0, W=[TS_])
            for ci in range(SEQ // C):
                tg = s * SEQ + ci * C
                kb.dma("sp", L.qT[:], qn_q[:, :, tg:tg + C], W=[L.Tq])
                kb.dma("sp", L.kT[:], qn_k[:, :, tg:tg + C], W=[L.Tk])
                kb.dma("sp", L.kv[:, :], c.kvtok_d[tg:tg + C, :], W=[L.Tkv])
                kb.dma("sp", L.zab[:, :], c.zab_d[tg:tg + C, :], W=[L.Tz])
                yield
                yield from chunk(L, False)
                kb.dma("sp", c.og_d[tg:tg + C, :], L.og[:, :], R=[L.Tog])
                yield
            kb.dma("sp", c.p_gdn[s].rearrange("h k v -> k h v"), S[:], R=[TS_])

        def lane_sample(L, b):
            S, TS_ = L.S, L.TS
            tg = TP + b * LS
            kb.dma("sp", S[:], c.state_gdn[b].rearrange("h k v -> k h v"), W=[TS_])
            kb.dma("sp", L.qT[:, :, 0:LS], qn_q[:, :, tg:tg + LS], W=[L.Tq])
            kb.dma("sp", L.kT[:, :, 0:LS], qn_k[:, :, tg:tg + LS], W=[L.Tk])
            kb.dma("sp", L.kv[0:LS, :], c.kvtok_d[tg:tg + LS, :], W=[L.Tkv])
            kb.dma("sp", L.zab[0:LS, :], c.zab_d[tg:tg + LS, :], W=[L.Tz])
            yield
            yield from chunk(L, True)
            kb.dma("sp", c.og_d[tg:tg + LS, :], L.og[0:LS, :], R=[L.Tog])
            kb.dma("sp", c.s_gdn[b].rearrange("h k v -> k h v"), S[:], R=[TS_])

        for L in lanes:
            L.S, L.TS = L.S
        for L in lanes:
            pass
        for s0 in range(0, NP if not cfg.get("GDN_SKIP_P") else 0, NL):
            run_lockstep([lane_prompt(lanes[i], s0 + i) for i in range(min(NL, NP - s0))])
        for L in lanes:
            kb.memset(L.qT[:], 0.0, W=[L.Tq])
            kb.memset(L.kT[:], 0.0, W=[L.Tk])
            kb.memset(L.kv[:], 0.0, W=[L.Tkv])
            kb.memset(L.zab[:], 0.0, W=[L.Tz])
        for b0 in range(0, NS if not cfg.get("GDN_SKIP_S") else 0, NL):
            run_lockstep([lane_sample(lanes[i], b0 + i) for i in range(min(NL, NS - b0))])
        kb.barrier()


def phase_mix(kb, c):
    nc, cfg = c.nc, c.cfg
    G = 512
    TP, TS = c.TP, c.TS
    with contextlib.ExitStack() as ps:
        sb = lambda n, s, d=F32: ps.enter_context(nc.sbuf_tensor("mx_" + n, s, d))
        wbg = sb("wbg", [128, 4, D], BF16)
        wbn = sb("wbn", [128, 4, D], BF16)
        wout = sb("wout", [128, 8, D], BF16)
        ogT = sb("ogT", [128, 4, G], BF16)
        onT = sb("onT", [128, 4, G], BF16)
        mg = sb("mg", [128, 16, G], BF16)
        mT = sb("mT", [128, 8, G], BF16)
        tk = [sb("tk%d" % i, [128, 1024], BF16) for i in range(2)]
        t1 = [sb("t1_%d" % i, [128, G]) for i in range(2)]
        t2 = [sb("t2_%d" % i, [128, G]) for i in range(2)]
        xin = [sb("xin%d" % i, [128, D]) for i in range(2)]
        wk = [sb("wk%d" % i, [128, D]) for i in range(2)]
        rows = {n: sb("row_" + n, [128, D]) for n in ("g", "lng", "lnb")}
        st = sb("st", [128, 2, nc.vector.BN_STATS_DIM])
        mv = sb("mv", [128, nc.vector.BN_AGGR_DIM])
        rstd = sb("rstd", [128, 1])
        pstr = [ps.enter_context(nc.psum_tensor("mx_pstr%d" % i, [128, 1024], BF16)) for i in range(2)]
        psb = [ps.enter_context(nc.psum_tensor("mx_psb%d" % i, [128, 512], F32)) for i in range(4)]
        psy = ps.enter_context(nc.psum_tensor("mx_psy", [128, 1024], F32))
        Tw, TogT, TonT, Tmg, TmT, Tln, Tmodrows, Tst, Tpsy = (Tok() for _ in range(9))
        Ttk, Tt1, Tt2, Txin, Twk, Tpstr = ([Tok(), Tok()] for _ in range(6))
        Tpsb = [Tok() for _ in range(4)]
        kb.dma("pool", wbg[:], c.w_br_gdn.rearrange("(k p) d -> p k d", p=128), W=[Tw])
        kb.dma("pool", wbn[:], c.w_br_nsa.rearrange("(k p) d -> p k d", p=128), W=[Tw])
        kb.dma("pool", wout[:], c.w_out.rearrange("(k p) d -> p k d", p=128), W=[Tw])
        kb.dma("sp", rows["lng"][:], c.ln_g[1:2, :].partition_broadcast(128), W=[Tln])
        kb.dma("sp", rows["lnb"][:], c.ln_b[1:2, :].partition_broadcast(128), W=[Tln])
        mgv = c.mgT_d.rearrange("(c p) t -> p c t", p=128)
        cur = None
        slot = 0
        for (kind, seq, t0, n) in token_groups(cfg, G):
            tg = t0 if kind == "p" else TP + t0
            if (kind, seq) != cur:
                cur = (kind, seq)
                load_rows(kb, c, rows["g"], Tmodrows, kind, seq, 5 * D)
            subs = subtiles(n)
            kb.dma("sp", mg[:, :, :n], mgv[:, :, tg:tg + n], W=[Tmg])
            for si, (r0, nr) in enumerate(subs):
                for which, (src, dstT, Td) in enumerate(((c.og_d, ogT, TogT), (c.on_d, onT, TonT))):
                    i2 = (2 * si + which) % 2
                    kb.dma("sp", tk[i2][:nr, 0:512], src[tg + r0:tg + r0 + nr, :], W=[Ttk[i2]])
                    for k in range(4):
                        kb.tr(pstr[i2][:, k * 128:k * 128 + nr], tk[i2][:nr, k * 128:(k + 1) * 128], c.ident_b[:nr, :nr],
                              R=[Ttk[i2], c.Tconst], W=[Tpstr[i2]])
                    kb.copy(dstT[:, :, r0:r0 + nr], pstr[i2][:, 0:512].rearrange("p (k n) -> p k n", k=4)[:, :, :nr],
                            R=[Tpstr[i2]], W=[Td], eng=("act" if which == 0 else "dve"))
            for dc in range(8):
                pg, Tpg = psb[(dc % 2) * 2], Tpsb[(dc % 2) * 2]
                pn, Tpn = psb[(dc % 2) * 2 + 1], Tpsb[(dc % 2) * 2 + 1]
                for k in range(4):
                    kb.mm(pg[:, :n], wbg[:, k, dc * 128:(dc + 1) * 128], ogT[:, k, :n], k == 0, k == 3, R=[Tw, TogT], W=[Tpg])
                for k in range(4):
                    kb.mm(pn[:, :n], wbn[:, k, dc * 128:(dc + 1) * 128], onT[:, k, :n], k == 0, k == 3, R=[Tw, TonT], W=[Tpn])
                a, Ta = t1[dc % 2], Tt1[dc % 2]
                b, Tb_ = t2[dc % 2], Tt2[dc % 2]
                kb.tt(a[:, :n], pg[:, :n], mg[:, dc, :n], ALU.mult, R=[Tpg, Tmg], W=[Ta])
                kb.tt(b[:, :n], pn[:, :n], mg[:, 8 + dc, :n], ALU.mult, R=[Tpn, Tmg], W=[Tb_])
                kb.tt(mT[:, dc, :n], a[:, :n], b[:, :n], ALU.add, R=[Ta, Tb_], W=[TmT], eng="pool")
            for si, (r0, nr) in enumerate(subs):
                xt, Tx = xin[slot], Txin[slot]
                w, Tw_ = wk[slot], Twk[slot]
                slot = 1 - slot
                kb.dma("sp", xt[:nr, :], c.x1[kind][t0 + r0:t0 + r0 + nr, :], W=[Tx])
                for half in range(2):
                    for k in range(8):
                        kb.mm(psy[:nr, half * 512:(half + 1) * 512], mT[:, k, r0:r0 + nr], wout[:, k, half * 512:(half + 1) * 512],
                              k == 0, k == 7, R=[TmT, Tw], W=[Tpsy])
                kb.tt(w[:nr, :], psy[:nr, :], rows["g"][:nr, :], ALU.mult, R=[Tpsy, Tmodrows], W=[Tw_])
                kb.stt(w[:nr, :], xt[:nr, :], ALPHA, w[:nr, :], ALU.mult, ALU.add, R=[Tx, Tw_], W=[Tw_])
                layer_norm_rows(kb, c, w, nr, Tw_, rows["lng"], rows["lnb"], Tln, st, mv, rstd, Tst)
                kb.dma("sp", c.x2[kind][t0 + r0:t0 + r0 + nr, :], w[:nr, :], R=[Tw_])
        kb.barrier()


NEGM = -30000.0


def nsa_consts(SEQ):
    t = np.arange(SEQ)
    nsel = SEQ // 64
    j = np.arange(nsel)
    valid = (j[None, :] * 64) <= t[:, None]
    cur = (t // 64)[:, None]
    forced = (j[None, :] == 0) | (j[None, :] == cur) | (j[None, :] == cur - 1)
    M1 = (valid & ~forced).astype(np.float32)
    M2 = np.where(valid, np.where(forced, 1e4 + j[None, :], 0.0), -1.0).astype(np.float32)
    ncmp = SEQ // 32
    n = np.arange(ncmp)
    cmask = np.where(((n[:, None] + 1) * 32 - 1) <= t[None, :], 0.0, NEGM).astype(np.float32)
    F = (np.arange(SEQ)[None, :] // 64 == j[:, None]).astype(np.float32)
    sk = np.arange(128)[:, None, None]
    a = np.arange(8)[None, :, None]
    tq = np.arange(512)[None, None, :]
    diff = tq - 128 * (a - 4) - sk
    Wm = np.where((diff >= 0) & (diff < 512), 0.0, NEGM).astype(np.float32)
    pair = (n[:, None] // 2 == j[None, :]).astype(np.float32)
    r = np.arange(128)
    maskW = np.zeros((128, 124), np.float32)
    maskW[r, 60 + r // 32] = 1.0
    return dict(nsa_M1=M1, nsa_M2=M2, nsa_cmask=cmask, nsa_F=F, nsa_Wm=Wm, nsa_pair=pair, nsa_maskW=maskW)


def phase_nsa_prompt(kb, c):
    nc, cfg = c.nc, c.cfg
    SEQ, NP = cfg["SEQ"], cfg["NP"]
    NKT = SEQ // 128
    NQG = SEQ // 512
    NSEL = SEQ // 64
    NCMP = SEQ // 32
    with contextlib.ExitStack() as ps:
        sb = lambda n, s, d=F32: ps.enter_context(nc.sbuf_tensor("np_" + n, s, d))
        qTh = sb("qTh", [64, 8, SEQ], BF16)
        KT = sb("KT", [64, 3, 2, SEQ], BF16)
        Vx = sb("Vx", [128, NKT, 3, 2, 65], BF16)
        kvc = sb("kvc", [128, NKT, 256], BF16)
        gn = sb("gn", [128, NKT, 24])
        M1 = sb("M1", [128, NKT, NSEL])
        M2 = sb("M2", [128, NKT, NSEL])
        cmask = sb("cmask", [NCMP, SEQ], BF16)
        Fm = sb("F", [NSEL, SEQ], BF16)
        Wm = sb("Wm", [128, 8, 512], BF16)
        pair = sb("pair", [NCMP, NSEL], BF16)
        maskW = sb("maskW", [128, 124])
        wcol = sb("wcol", [128, 2, 2])
        Wbig = sb("Wbig", [128, 4, 124], BF16)
        ON = sb("ON", [128, NKT, 512])
        imp = sb("imp", [128, NKT, 2, NSEL])
        selT = sb("selT", [NSEL, 2, SEQ], BF16)
        kvb_sb = sb("kvb", [NCMP, 256], BF16)
        KcT = sb("KcT", [64, 2, NCMP], BF16)
        Vc1P = sb("Vc1P", [NCMP, 2, 65 + NSEL], BF16)
        Pt = [sb("Pt%d" % i, [128, 512], BF16) for i in range(3)]
        sc = sb("sc", [128, NSEL])
        wk_ = sb("wk", [128, NSEL])
        m8a, m8b = sb("m8a", [128, 8]), sb("m8b", [128, 8])
        thr = sb("thr", [128, 1])
        selm = sb("selm", [128, NSEL])
        okm = sb("okm", [128, NSEL])
        rr = [sb("rr%d" % i, [128, 1]) for i in range(4)]
        rg = [sb("rg%d" % i, [128, 1]) for i in range(4)]
        zb = sb("zb", [128, 512], BF16)
        pss = [ps.enter_context(nc.psum_tensor("np_pss%d" % i, [128, 512], F32)) for i in range(2)]
        paccs = [[ps.enter_context(nc.psum_tensor("np_pacc%d_%d" % (j, i), [128, 512], F32)) for i in range(3)] for j in range(2)]
        pmisc = paccs[1][0]
        pmb = paccs[1][1]
        kvb_f = sb("kvb_f", [NCMP, 128])
        rr16 = [sb("rr16_%d" % i, [128, 16]) for i in range(2)]
        rg16 = [sb("rg16_%d" % i, [128, 16]) for i in range(2)]
        Trr16 = [Tok(), Tok()]
        TON = [[Tok() for _ in range(8)] for _ in range(NKT)]
        names = "q K V kvc gn M cm F Wm pair maskW wcol Wbig ON imp selT kvb KcT Vc1P sc m8 sel pm pmb zb".split()
        T = {n: Tok(n) for n in names}
        kb.memset(zb[:], 0.0, W=[T["zb"]])
        TPt, Tpss = [Tok() for _ in range(3)], [Tok() for _ in range(2)]
        Tpaccs = [[Tok() for _ in range(3)] for _ in range(2)]
        T["pm"] = Tpaccs[1][0]
        T["pmb"] = Tpaccs[1][1]
        Trr = [Tok() for _ in range(4)]
        kb.dma("sp", M1[:], c.nsa_M1.rearrange("(t p) j -> p t j", p=128), W=[T["M"]])
        kb.dma("sp", M2[:], c.nsa_M2.rearrange("(t p) j -> p t j", p=128), W=[T["M"]])
        kb.dma("pool", cmask[:], c.nsa_cmask, W=[T["cm"]])
        kb.dma("pool", Fm[:], c.nsa_F, W=[T["F"]])
        kb.dma("pool", Wm[:], c.nsa_Wm, W=[T["Wm"]])
        kb.dma("pool", pair[:], c.nsa_pair, W=[T["pair"]])
        kb.dma("sp", maskW[:], c.nsa_maskW, W=[T["maskW"]])
        wsrc = c.w_cmp.rearrange("s j h -> j s h")
        for q4 in range(4):
            kb.dma("sp", wcol[32 * q4:32 * q4 + 32, :, :], wsrc, W=[T["wcol"]])
        for s_ in range(2):
            for hk in range(2):
                kb.ts(Wbig[:, s_ * 2 + hk, :], maskW[:, :], wcol[:, s_, hk:hk + 1], None, ALU.mult,
                      R=[T["maskW"], T["wcol"]], W=[T["Wbig"]])
        kb.memset(Vx[:, :, :, :, 64:65], 1.0, W=[T["V"]])
        kb.memset(Vc1P[:, :, 64:65], 1.0, W=[T["Vc1P"]])
        for hk in range(2):
            kb.copy(Vc1P[:, hk, 65:65 + NSEL], pair[:, :], R=[T["pair"]], W=[T["Vc1P"]])
        st = {"ps": 0, "pt": 0, "rr": 0, "aset": 0}

        def score_bank():
            i = st["ps"] % 2
            st["ps"] += 1
            return pss[i], Tpss[i]

        def acc_ap(a, aset=0):
            return paccs[aset][a // 7][:, (a % 7) * 65:(a % 7) * 65 + 65], Tpaccs[aset][a // 7]

        def finish_group(aset, qg, hk, br):
            i = st["rr"] % 2
            st["rr"] += 1
            r16, g16, Tr = rr16[i], rg16[i], Trr16[i]
            for b3 in range(3):
                na = min(7, 16 - 7 * b3)
                den = paccs[aset][b3][:, 0:na * 65].rearrange("p (a c) -> p a c", c=65)[:, :, 64]
                kb.ts(r16[:, 7 * b3:7 * b3 + na], den, 1e-30, None, ALU.max, R=[Tpaccs[aset][b3]], W=[Tr])
            kb.op("dve", lambda e: e.reciprocal(out=r16[:, :], in_=r16[:, :]), [Tr], [Tr])
            gview = gn[:, qg * 4:(qg + 1) * 4, br * 8 + hk * 4:br * 8 + hk * 4 + 4].rearrange("p s g -> p g s")
            kb.tt(g16[:, :].rearrange("p (g s) -> p g s", g=4), r16[:, :].rearrange("p (g s) -> p g s", g=4), gview, ALU.mult,
                  R=[Tr, T["gn"]], W=[Tr])
            for g in range(4):
                for sub in range(4):
                    a = g * 4 + sub
                    ap_, Ta_ = acc_ap(a, aset)
                    qt, head = qg * 4 + sub, hk * 4 + g
                    o_ = ON[:, qt, head * 64:(head + 1) * 64]
                    kb.stt(o_, ap_[:, 0:64], g16[:, a:a + 1], o_, ALU.mult, ALU.add, R=[Ta_, Tr, TON[qt][head]], W=[TON[qt][head]])

        def finish_acc(ap_, Tp_, qt, head, br, first, with_imp=None):
            i = st["rr"] % 4
            st["rr"] += 1
            r_, g_, Tr_ = rr[i], rg[i], Trr[i]
            kb.ts(r_[:, :], ap_[:, 64:65], 1e-30, None, ALU.max, R=[Tp_], W=[Tr_])
            kb.op("dve", lambda e: e.reciprocal(out=r_[:, :], in_=r_[:, :]), [Tr_], [Tr_])
            kb.tt(g_[:, :], r_[:, :], gn[:, qt, br * 8 + head:br * 8 + head + 1], ALU.mult, R=[Tr_, T["gn"]], W=[Tr_])
            o_ = ON[:, qt, head * 64:(head + 1) * 64]
            if first:
                kb.ts(o_, ap_[:, 0:64], g_[:, 0:1], None, ALU.mult, R=[Tp_, Tr_], W=[TON[qt][head]])
            else:
                kb.stt(o_, ap_[:, 0:64], g_[:, 0:1], o_, ALU.mult, ALU.add, R=[Tp_, Tr_, TON[qt][head]], W=[TON[qt][head]])
            return r_, Tr_

        for s in range(NP):
            tg = s * SEQ
            kb.dma("sp", qTh[:, :, :], c.qT_d.rearrange("(h d) t -> d h t", d=64)[:, :, tg:tg + SEQ], W=[T["q"]])
            for i3 in range(3):
                kb.dma("sp", KT[:, i3, :, :], c.kT_d[i3].rearrange("(h d) t -> d h t", d=64)[:, :, tg:tg + SEQ], W=[T["K"]])
            rowsv = c.kvg_d[tg:tg + SEQ, :].rearrange("(t p) f -> p t f", p=128)
            for br in range(3):
                for hk in range(2):
                    c0 = br * 256 + 128 + hk * 64
                    kb.dma("pool", Vx[:, :, br, hk, 0:64], rowsv[:, :, c0:c0 + 64], W=[T["V"]])
            kb.dma("pool", kvc[:, :, :], rowsv[:, :, 0:256], W=[T["kvc"]])
            kb.dma("sp", gn[:, :, :], rowsv[:, :, 768:792], W=[T["gn"]])
            kb.act(gn[:, :, :], gn[:, :, :], AF.Exp, R=[T["gn"]], W=[T["gn"]], scale=-1.0)
            kb.ts(gn[:, :, :], gn[:, :, :], 1.0, None, ALU.add, R=[T["gn"]], W=[T["gn"]])
            kb.op("dve", lambda e: e.reciprocal(out=gn[:, :, :], in_=gn[:, :, :]), [T["gn"]], [T["gn"]])
            for cc in range(4):
                for kt in range(NKT):
                    kb.mm(pmisc[:NCMP, cc * 64:(cc + 1) * 64], Wbig[:, cc, 60 - 4 * kt:60 - 4 * kt + NCMP],
                          kvc[:, kt, cc * 64:(cc + 1) * 64], kt == 0, kt == NKT - 1, R=[T["Wbig"], T["kvc"]], W=[T["pm"]],
                          sig=(kt == NKT - 1 and cc == 3))
            kb.copy(kvb_sb[:, :], pmisc[:NCMP, 0:256], R=[T["pm"]], W=[T["kvb"]])
            kb.copy(kvb_f[:, :], pmisc[:NCMP, 0:128], R=[T["pm"]], W=[T["kvb"]], eng="act")
            for hk in range(2):
                kb.tr(pmb[:64, hk * NCMP:(hk + 1) * NCMP], kvb_f[:, hk * 64:(hk + 1) * 64], c.ident_f[:NCMP, :NCMP],
                      R=[T["kvb"], c.Tconst], W=[T["pmb"]])
                kb.copy(Vc1P[:, hk, 0:64], kvb_sb[:, 128 + hk * 64:128 + (hk + 1) * 64], R=[T["kvb"]], W=[T["Vc1P"]], eng="pool")
            kb.copy(KcT[:, :, :], pmb[:64, 0:2 * NCMP].rearrange("p (h n) -> p h n", h=2), R=[T["pmb"]], W=[T["KcT"]])
            for qg in range(NQG):
                for hk in range(2):
                    for g in range(4):
                        head = hk * 4 + g
                        p_, Tp_ = score_bank()
                        kb.mm(p_[:NCMP, :], KcT[:, hk, :], qTh[:, head, qg * 512:(qg + 1) * 512], True, False,
                              R=[T["KcT"], T["q"]], W=[Tp_], sig=False)
                        kb.mm(p_[:NCMP, :], c.ident_b[:NCMP, :NCMP], cmask[:, qg * 512:(qg + 1) * 512], False, True,
                              R=[c.Tconst, T["cm"]], W=[Tp_])
                        pt_, Tpt_ = Pt[st["pt"] % 3], TPt[st["pt"] % 3]
                        st["pt"] += 1
                        kb.act(pt_[:NCMP, :], p_[:NCMP, :], AF.Exp, R=[Tp_], W=[Tpt_], scale=0.125)
                        for sub in range(4):
                            qt = qg * 4 + sub
                            kb.mm(pmisc[:, 256 + 0:256 + 65 + NSEL], pt_[:NCMP, sub * 128:(sub + 1) * 128], Vc1P[:, hk, :], True, True,
                                  R=[Tpt_, T["Vc1P"]], W=[T["pm"]])
                            ap_ = pmisc[:, 256:256 + 65 + NSEL]
                            r_, Tr_ = finish_acc(ap_, T["pm"], qt, head, 0, True)
                            if g == 0:
                                kb.ts(imp[:, qt, hk, :], ap_[:, 65:65 + NSEL], r_[:, 0:1], None, ALU.mult, R=[T["pm"], Tr_], W=[T["imp"]])
                            else:
                                kb.stt(imp[:, qt, hk, :], ap_[:, 65:65 + NSEL], r_[:, 0:1], imp[:, qt, hk, :], ALU.mult, ALU.add,
                                       R=[T["pm"], Tr_, T["imp"]], W=[T["imp"]])
            for qt in range(NKT):
                for hk in range(2):
                    kb.tt(sc[:, :], imp[:, qt, hk, :], M1[:, qt, :], ALU.mult, R=[T["imp"], T["M"]], W=[T["sc"]])
                    kb.tt(sc[:, :], sc[:, :], M2[:, qt, :], ALU.add, R=[T["sc"], T["M"]], W=[T["sc"]])
                    kb.op("dve", lambda e: e.max(out=m8a[:, :], in_=sc[:, :]), [T["sc"]], [T["m8"]])
                    kb.op("dve", lambda e: e.match_replace(out=wk_[:, :], in_to_replace=m8a[:, :], in_values=sc[:, :], imm_value=-2.0),
                          [T["sc"], T["m8"]], [T["sel"]])
                    kb.op("dve", lambda e: e.max(out=m8b[:, :], in_=wk_[:, :]), [T["sel"]], [T["m8"]])
                    kb.op("dve", lambda e: e.tensor_reduce(out=thr[:, :], in_=m8b[:, :], axis=AX.X, op=ALU.min), [T["m8"]], [T["m8"]])
                    kb.ts(selm[:, :], sc[:, :], thr[:, 0:1], None, ALU.is_ge, R=[T["sc"], T["m8"]], W=[T["sel"]])
                    kb.ts(okm[:, :], sc[:, :], 0.0, None, ALU.is_ge, R=[T["sc"]], W=[T["sel"]])
                    kb.tt(selm[:, :], selm[:, :], okm[:, :], ALU.mult, R=[T["sel"]], W=[T["sel"]])
                    kb.ts(selm[:, :], selm[:, :], -1.0, -NEGM, ALU.add, ALU.mult, R=[T["sel"]], W=[T["sel"]])
                    kb.tr(pmisc[:NSEL, 0:128], selm[:, :], c.ident_f[:, :], R=[T["sel"], c.Tconst], W=[T["pm"]])
                    kb.copy(selT[:, hk, qt * 128:(qt + 1) * 128], pmisc[:NSEL, 0:128], R=[T["pm"]], W=[T["selT"]], eng="act")
            for br in (1, 2):
                for hk in range(2):
                    for qg in range(NQG):
                        kts = list(range(0, 4 * qg + 4)) if br == 1 else list(range(max(0, 4 * qg - 4), 4 * qg + 4))
                        aset = st["aset"] % 2
                        st["aset"] += 1
                        for b3 in range(3):
                            kb.mm(paccs[aset][b3][:, :], zb[:, 0:128], zb[:, :], True, False, R=[T["zb"]], W=[Tpaccs[aset][b3]], sig=False)
                        for ki, kt in enumerate(kts):
                            for g in range(4):
                                head = hk * 4 + g
                                p_, Tp_ = score_bank()
                                diag = kt >= 4 * qg
                                need_w = diag or br == 2
                                kb.mm(p_[:, :], KT[:, br, hk, kt * 128:(kt + 1) * 128], qTh[:, head, qg * 512:(qg + 1) * 512],
                                      True, not (need_w or br == 1), R=[T["K"], T["q"]], W=[Tp_], sig=False)
                                if br == 1:
                                    kb.mm(p_[:, :], Fm[:, kt * 128:(kt + 1) * 128], selT[:, hk, qg * 512:(qg + 1) * 512],
                                          False, not need_w, R=[T["F"], T["selT"]], W=[Tp_], sig=(not need_w))
                                if need_w:
                                    kb.mm(p_[:, :], c.ident_b[:, :], Wm[:, kt - 4 * qg + 4, :], False, True,
                                          R=[c.Tconst, T["Wm"]], W=[Tp_])
                                pt_, Tpt_ = Pt[st["pt"] % 3], TPt[st["pt"] % 3]
                                st["pt"] += 1
                                kb.act(pt_[:, :], p_[:, :], AF.Exp, R=[Tp_], W=[Tpt_], scale=0.125)
                                for sub in range(4):
                                    ap_, Ta_ = acc_ap(g * 4 + sub, aset)
                                    last = (ki == len(kts) - 1)
                                    kb.mm(ap_, pt_[:, sub * 128:(sub + 1) * 128], Vx[:, kt, br, hk, :], False, last,
                                          R=[Tpt_, T["V"]], W=[Ta_], sig=last)
                        finish_group(aset, qg, hk, br)
            kb.dma("pool", c.on_d[tg:tg + SEQ, :].rearrange("(t p) f -> p t f", p=128), ON[:, :, :],
                   R=[TON[a_][b_] for a_ in range(NKT) for b_ in range(8)])
        kb.barrier()


def nsa_s_consts(NPAGES):
    nblk = NPAGES * 2
    ncmp = NPAGES * 4
    nsel = nblk + 1
    j = np.arange(nsel)
    forced = (j == 0) | (j == nblk) | (j == nblk - 1)
    M1 = np.tile((~forced).astype(np.float32)[None, :], (LS, 1))
    M2 = np.tile(np.where(forced, 1e4 + j, 0.0).astype(np.float32)[None, :], (LS, 1))
    nh = ncmp // 128
    nl = np.arange(128)
    pairs = np.zeros((128, nh, nblk), np.float32)
    for h in range(nh):
        pairs[nl, h, 64 * h + nl // 2] = 1.0
    Fs = (np.arange(NPAGES * 128)[None, :] // 64 == np.arange(nblk)[:, None]).astype(np.float32)
    col_l = np.tile(np.arange(LS), 8)[None, :]
    caus = np.where(np.arange(LS)[:, None] <= col_l, 0.0, NEGM).astype(np.float32)
    wm0 = np.where(np.arange(128)[:, None] <= col_l, NEGM, 0.0).astype(np.float32)
    gsum = (np.arange(16)[:, None] % LS == np.arange(LS)[None, :]).astype(np.float32)
    qcol = (np.arange(128) % 4).astype(np.float32).reshape(128, 1)
    pcol = np.arange(128, dtype=np.float32).reshape(128, 1)
    jrow = np.arange(32, dtype=np.float32).reshape(1, 32)
    maskWs = np.zeros((128, 252), np.float32)
    maskWs[np.arange(128), 124 + np.arange(128) // 32] = 1.0
    return dict(ns_maskWs=maskWs, ns_jrow=jrow, ns_M1=M1, ns_M2=M2, ns_pairs=pairs, ns_Fs=Fs, ns_caus=caus, ns_wm0=wm0, ns_gsum=gsum, ns_qcol=qcol, ns_pcol=pcol)


def phase_nsa_sample(kb, c):
    nc, cfg = c.nc, c.cfg
    NS, NPAGES, NPHYS, WL = cfg["NS"], cfg["NPAGES"], cfg["NPHYS"], cfg["WIN"]
    TP = c.TP
    NBLK = NPAGES * 2
    NCMP = NPAGES * 4
    NH = NCMP // 128
    NSELS = NBLK + 1
    assert NBLK <= 128 and NCMP % 128 == 0
    with contextlib.ExitStack() as ps:
        sb = lambda n, s, d=F32: ps.enter_context(nc.sbuf_tensor("nss_" + n, s, d))
        ptT = sb("ptT", [128, NS, NH], I32)
        ptb = sb("ptb", [128, NS * NPAGES], I32)
        idxq_f = sb("idxq_f", [128, NS, NH])
        idxr_f = sb("idxr_f", [128, NS * NPAGES])
        idxq = sb("idxq", [128, NS, NH], I32)
        idxr = sb("idxr", [128, NS * NPAGES], I32)
        qcol, pcol = sb("qcol", [128, 1]), sb("pcol", [128, 1])
        jrow = sb("jrow", [128, 32])
        idxq32_f = sb("idxq32_f", [128, NS * NH, 32])
        idxq32 = sb("idxq32", [128, NS * NH * 32], I32)
        wflat = sb("wflat", [128, 128])
        M1, M2 = sb("M1", [LS, NSELS]), sb("M2", [LS, NSELS])
        pairs = sb("pairs", [128, NH, NBLK], BF16)
        Fs = sb("Fs", [NBLK, NPAGES * 128], BF16)
        caus = sb("caus", [LS, 32], BF16)
        wm0 = sb("wm0", [128, 32], BF16)
        gsum = sb("gsum", [16, LS])
        zb = sb("zb", [128, 512], BF16)
        cq = [sb("cq%d" % i, [128, 32, 256]) for i in range(2)]
        kvbs = sb("kvbs", [128, NH, 256])
        kvbK = sb("kvbK", [128, NH, 128], BF16)
        KcTs = sb("KcTs", [64, 2, NCMP], BF16)
        VcS = sb("VcS", [128, NH, 2, 65 + NBLK], BF16)
        qs = sb("qs", [64, 8, LS], BF16)
        KnT = sb("KnT", [64, 3, 2, LS], BF16)
        Vn = sb("Vn", [LS, 3, 2, 2, 65], BF16)
        gns = sb("gns", [16, 3, 2])
        Es = sb("Es", [128, NH, 2, 16], BF16)
        impg = sb("impg", [16, NBLK])
        sc = sb("sc", [LS, NSELS])
        wk_ = sb("wk", [LS, NSELS])
        m8a, m8b, thr = sb("m8a", [LS, 8]), sb("m8b", [LS, 8]), sb("thr", [LS, 1])
        selm = sb("selm", [LS, NBLK])
        selx = sb("selx", [NBLK, 2, 4, LS], BF16)
        pg = [sb("pg%d" % i, [128, 256]) for i in range(8)]
        pgb = [sb("pgb%d" % i, [128, 2, 2, 65], BF16) for i in range(3)]
        KpT = [sb("KpT%d" % i, [64, 2, 128], BF16) for i in range(2)]
        Pp = [sb("Pp%d" % i, [128, 32], BF16) for i in range(2)]
        ONs = sb("ONs", [16, 2, 64])
        ONb = sb("ONb", [16, 2, 64], BF16)
        rr, rg = sb("rr", [16, 1]), sb("rg", [16, 1])
        pS = ps.enter_context(nc.psum_tensor("ns_pS", [128, 512], F32))
        pO = ps.enter_context(nc.psum_tensor("ns_pO", [128, 512], F32))
        pI = ps.enter_context(nc.psum_tensor("ns_pI", [128, 512], F32))
        pT = [ps.enter_context(nc.psum_tensor("ns_pT%d" % i, [128, 1024], BF16)) for i in range(2)]
        pS2 = [ps.enter_context(nc.psum_tensor("ns_pS2%d" % i, [128, 512], F32)) for i in range(2)]
        pA = ps.enter_context(nc.psum_tensor("ns_pA", [128, 512], F32))
        names = ("pt idx const w cq kvbs kvbK KcTs VcS qs KnT Vn gns Es impg sc m8 sel selx ONs rr pS pO pI pA zb").split()
        T = {n: Tok(n) for n in names}
        Tpg, Tpgb = [Tok() for _ in range(8)], [Tok() for _ in range(3)]
        Tcqs = [Tok(), Tok()]
        TKpT, TPp, TpT, TpS2 = ([Tok(), Tok()] for _ in range(4))
        kb.memset(zb[:], 0.0, W=[T["zb"]])
        with nc.allow_non_contiguous_dma(reason="page table transpose (tiny)"):
            kb.dma("sp", ptT[:, :, :], c.pt4.rearrange("b (h p) -> p b h", p=128), W=[T["pt"]])
        kb.dma("sp", ptb[:, :], c.page_table.rearrange("b n -> (b n)").partition_broadcast(128), W=[T["pt"]])
        for nm, t_, src in (("qcol", qcol, c.ns_qcol), ("pcol", pcol, c.ns_pcol), ("M1", M1, c.ns_M1), ("M2", M2, c.ns_M2), ("gsum", gsum, c.ns_gsum)):
            kb.dma("sp", t_[:], src, W=[T["const"]])
        for t_, src in ((pairs, c.ns_pairs), (Fs, c.ns_Fs), (caus, c.ns_caus), (wm0, c.ns_wm0)):
            kb.dma("pool", t_[:], src, W=[T["const"]])
        kb.dma("sp", wflat[:, :], c.w_cmp.rearrange("s j h -> (s j h)").partition_broadcast(128), W=[T["w"]])
        kb.copy(idxq_f[:], ptT[:], R=[T["pt"]], W=[T["idx"]])
        kb.ts(idxq_f[:], idxq_f[:], 4.0, qcol[:, 0:1], ALU.mult, ALU.add, R=[T["idx"], T["const"]], W=[T["idx"]])
        kb.copy(idxq[:], idxq_f[:], R=[T["idx"]], W=[T["idx"]])
        kb.ts(idxq_f[:], idxq_f[:], 32.0, None, ALU.mult, R=[T["idx"]], W=[T["idx"]])
        kb.dma("sp", jrow[:, :], c.ns_jrow.partition_broadcast(128), W=[T["const"]])
        kb.tt(idxq32_f[:, :, :], idxq_f[:].rearrange("p b h -> p (b h)").unsqueeze(2).broadcast_to([128, NS * NH, 32]),
              jrow[:, :].unsqueeze(1).broadcast_to([128, NS * NH, 32]), ALU.add, R=[T["idx"], T["const"]], W=[T["idx"]])
        kb.copy(idxq32[:, :], idxq32_f[:, :, :].rearrange("p a j -> p (a j)"), R=[T["idx"]], W=[T["idx"]])
        kb.copy(idxr_f[:], ptb[:], R=[T["pt"]], W=[T["idx"]])
        kb.ts(idxr_f[:], idxr_f[:], 128.0, pcol[:, 0:1], ALU.mult, ALU.add, R=[T["idx"], T["const"]], W=[T["idx"]])
        kb.copy(idxr[:], idxr_f[:], R=[T["idx"]], W=[T["idx"]])
        for i in range(3):
            kb.memset(pgb[i][:, :, :, 64:65], 1.0, W=[Tpgb[i]])
        kb.memset(VcS[:, :, :, 64:65], 1.0, W=[T["VcS"]])
        kb.memset(Vn[:, :, :, :, 64:65], 1.0, W=[T["Vn"]])
        for hk in range(2):
            kb.copy(VcS[:, :, hk, 65:65 + NBLK], pairs[:, :, :], R=[T["const"]], W=[T["VcS"]])
        cmp_v = c.cache_cmp.rearrange("(n q) f -> n q f", q=32)
        wv = wflat[:, :].rearrange("p (s j h) -> p j s h", s=2, h=2)
        st = {"pg": 0, "k": 0, "g8": 0, "rg": 0}

        def page_dma(dst, cache, b, lp, W):
            col = b * NPAGES + lp
            return kb.dma_fn("pool", lambda e: e.indirect_dma_start(
                out=dst, out_offset=None, in_=cache,
                in_offset=bass.IndirectOffsetOnAxis(ap=idxr[:, col:col + 1], axis=0)), R=[T["idx"]], W=W)

        maskWs = sb("maskWs", [128, 252])
        wcol = sb("wcol", [128, 2, 2])
        Wc = sb("Wc", [128, 4, 252])
        kb.dma("sp", maskWs[:], c.ns_maskWs, W=[T["w"]])
        wsrc = c.w_cmp.rearrange("s j h -> j s h")
        with nc.allow_non_contiguous_dma(reason="tiny weight gather"):
            for q4 in range(4):
                kb.dma("sp", wcol[32 * q4:32 * q4 + 32, :, :], wsrc, W=[T["w"]])
        for s_ in range(2):
            for hk in range(2):
                kb.ts(Wc[:, s_ * 2 + hk, :], maskWs[:, :], wcol[:, s_, hk:hk + 1], None, ALU.mult, R=[T["w"]], W=[T["w"]])
        pK = pS

        def tile_pipeline(src_tile, Tsrc, nrows, mask_rhs, mask_lhsT, Rmask, hk_list=(0, 1)):
            i3 = st["pg"] % 3
            i2 = st["k"] % 2
            st["pg"] += 1
            st["k"] += 1
            pb, Tpb = pgb[i3], Tpgb[i3]
            kb.copy(pb[:nrows, :, :, 0:64], src_tile, R=[Tsrc], W=[Tpb])
            for hk in range(2):
                kb.tr(pT[i2][:64, hk * 128:hk * 128 + nrows], pb[:nrows, 0, hk, 0:64], c.ident_b[:nrows, :nrows],
                      R=[Tpb, c.Tconst], W=[TpT[i2]])
            kb.copy(KpT[i2][:, :, :nrows], pT[i2][:64, 0:256].rearrange("p (h n) -> p h n", h=2)[:, :, :nrows],
                    R=[TpT[i2]], W=[TKpT[i2]], eng="act")
            p2, Tp2 = pS2[i2], TpS2[i2]
            if mask_rhs is not None:
                kb.mm(p2[:nrows, 0:32], mask_lhsT, mask_rhs, True, False, R=Rmask, W=[Tp2], sig=False)
            for hk in range(2):
                first = (mask_rhs is None)
                kb.mm(p2[:nrows, hk * 16:(hk + 1) * 16], KpT[i2][:, hk, :nrows], qs[:, hk * 4:(hk + 1) * 4, :], first, True,
                      R=[TKpT[i2], T["qs"]], W=[Tp2], sig=(hk == 1))
            kb.act(Pp[i2][:nrows, :], p2[:nrows, 0:32], AF.Exp, R=[Tp2], W=[TPp[i2]], scale=0.125)
            for hk in range(2):
                kb.mm(pA[:16, hk * 65:(hk + 1) * 65], Pp[i2][:nrows, hk * 16:(hk + 1) * 16], pb[:nrows, 1, hk, :], False, True,
                      R=[TPp[i2], Tpb], W=[T["pA"]], sig=(hk == 1))

        def finish_branch(br, first):
            for hk in range(2):
                ap_ = pA[:16, hk * 65:(hk + 1) * 65]
                kb.ts(rr[:, :], ap_[:, 64:65], 1e-30, None, ALU.max, R=[T["pA"]], W=[T["rr"]])
                kb.op("dve", lambda e: e.reciprocal(out=rr[:, :], in_=rr[:, :]), [T["rr"]], [T["rr"]])
                kb.tt(rg[:, :], rr[:, :], gns[:, br, hk:hk + 1], ALU.mult, R=[T["rr"], T["gns"]], W=[T["rr"]])
                if first:
                    kb.ts(ONs[:, hk, :], ap_[:, 0:64], rg[:, 0:1], None, ALU.mult, R=[T["pA"], T["rr"]], W=[T["ONs"]])
                else:
                    kb.stt(ONs[:, hk, :], ap_[:, 0:64], rg[:, 0:1], ONs[:, hk, :], ALU.mult, ALU.add, R=[T["pA"], T["rr"], T["ONs"]], W=[T["ONs"]])

        for b in range(NS):
            tg = TP + b * LS
            kb.dma("sp", qs[:, :, :], c.qT_d.rearrange("(h d) t -> d h t", d=64)[:, :, tg:tg + LS], W=[T["qs"]])
            for i3 in range(3):
                kb.dma("sp", KnT[:, i3, :, :], c.kT_d[i3].rearrange("(h d) t -> d h t", d=64)[:, :, tg:tg + LS], W=[T["KnT"]])
            kb.dma("pool", Vn[:, :, :, :, 0:64], c.kvg_d[tg:tg + LS, 0:768].rearrange("l (b s h d) -> l b s h d", b=3, s=2, h=2),
                   W=[T["Vn"]])
            for g in range(4):
                src = c.kvg_d[tg:tg + LS, 768 + g:768 + g + 21:4].rearrange("l (b h) -> l b h", h=2)
                with nc.allow_non_contiguous_dma(reason="tiny gate gather"):
                    kb.dma("sp", gns[4 * g:4 * g + 4, :, :], src, W=[T["gns"]])
            kb.act(gns[:, :, :], gns[:, :, :], AF.Exp, R=[T["gns"]], W=[T["gns"]], scale=-1.0)
            kb.ts(gns[:, :, :], gns[:, :, :], 1.0, None, ALU.add, R=[T["gns"]], W=[T["gns"]])
            kb.op("dve", lambda e: e.reciprocal(out=gns[:, :, :], in_=gns[:, :, :]), [T["gns"]], [T["gns"]])
            kb.mm(pK[:, :], zb[:, 0:128], zb[:, :], True, False, R=[T["zb"]], W=[T["pS"]], sig=False)
            for half in range(NH):
                for lpl in range(32):
                    lp = half * 32 + lpl
                    i3 = st["g8"] % 8
                    st["g8"] += 1
                    t_, Tt_ = pg[i3], Tpg[i3]
                    page_dma(t_[:, 0:256], c.cache_cmp, b, lp, [Tt_])
                    for cc in range(4):
                        kb.mm(pK[:, half * 256 + cc * 64:half * 256 + (cc + 1) * 64], Wc[:, cc, 124 - 4 * lpl:124 - 4 * lpl + 128],
                              t_[:, cc * 64:(cc + 1) * 64], False, True, R=[T["w"], Tt_], W=[T["pS"]],
                              sig=(cc == 3))
            kb.copy(kvbs[:, :, :], pK[:, 0:NH * 256].rearrange("p (a c) -> p a c", a=NH), R=[T["pS"]], W=[T["kvbs"]])
            kb.copy(kvbK[:, :, :], kvbs[:, :, 0:128], R=[T["kvbs"]], W=[T["kvbK"]])
            for hk in range(2):
                kb.copy(VcS[:, :, hk, 0:64], kvbs[:, :, 128 + hk * 64:128 + (hk + 1) * 64], R=[T["kvbs"]], W=[T["VcS"]])
            for half in range(NH):
                for hk in range(2):
                    kb.tr(pT[0][:64, (half * 2 + hk) * 128:(half * 2 + hk + 1) * 128], kvbK[:, half, hk * 64:(hk + 1) * 64], c.ident_b[:, :],
                          R=[T["kvbK"], c.Tconst], W=[TpT[0]])
            kb.copy(KcTs[:, :, :].rearrange("p h (a n) -> p a h n", a=NH),
                    pT[0][:64, 0:NH * 256].rearrange("p (a h n) -> p a h n", a=NH, h=2), R=[TpT[0]], W=[T["KcTs"]])
            for half in range(NH):
                for hk in range(2):
                    kb.mm(pI[:, 384 + (half * 2 + hk) * 16:384 + (half * 2 + hk + 1) * 16], KcTs[:, hk, half * 128:(half + 1) * 128],
                          qs[:, hk * 4:(hk + 1) * 4, :], True, True, R=[T["KcTs"], T["qs"]], W=[T["pI"]], sig=(half == NH - 1 and hk == 1))
            kb.act(Es[:, :, :, :], pI[:, 384:384 + NH * 32].rearrange("p (a h x) -> p a h x", a=NH, h=2), AF.Exp, R=[T["pI"]], W=[T["Es"]], scale=0.125)
            for hk in range(2):
                for half in range(NH):
                    kb.mm(pO[:16, hk * 256:hk * 256 + 65 + NBLK], Es[:, half, hk, :], VcS[:, half, hk, :], half == 0, half == NH - 1,
                          R=[T["Es"], T["VcS"]], W=[T["pO"]], sig=(half == NH - 1))
                ap_ = pO[:16, hk * 256:hk * 256 + 65 + NBLK]
                kb.ts(rr[:, :], ap_[:, 64:65], 1e-30, None, ALU.max, R=[T["pO"]], W=[T["rr"]])
                kb.op("dve", lambda e: e.reciprocal(out=rr[:, :], in_=rr[:, :]), [T["rr"]], [T["rr"]])
                kb.tt(rg[:, :], rr[:, :], gns[:, 0, hk:hk + 1], ALU.mult, R=[T["rr"], T["gns"]], W=[T["rr"]])
                kb.ts(ONs[:, hk, :], ap_[:, 0:64], rg[:, 0:1], None, ALU.mult, R=[T["pO"], T["rr"]], W=[T["ONs"]])
                kb.ts(impg[:, :], ap_[:, 65:65 + NBLK], rr[:, 0:1], None, ALU.mult, R=[T["pO"], T["rr"]], W=[T["impg"]])
                if b == 0:
                    dbg_dump(kb, c, ONs[:, hk, :], [T["ONs"]], 16, 64)
                    dbg_dump(kb, c, impg[:, :], [T["impg"]], 16, NBLK)
                kb.mm(pI[:LS, 0:NBLK], gsum[:, :], impg[:, :], True, True, R=[T["const"], T["impg"]], W=[T["pI"]])
                kb.tt(sc[:, 0:NBLK], pI[:LS, 0:NBLK], M1[:, 0:NBLK], ALU.mult, R=[T["pI"], T["const"]], W=[T["sc"]])
                kb.tt(sc[:, 0:NBLK], sc[:, 0:NBLK], M2[:, 0:NBLK], ALU.add, R=[T["sc"], T["const"]], W=[T["sc"]])
                kb.copy(sc[:, NBLK:NSELS], M2[:, NBLK:NSELS], R=[T["const"]], W=[T["sc"]])
                kb.op("dve", lambda e: e.max(out=m8a[:, :], in_=sc[:, :]), [T["sc"]], [T["m8"]])
                kb.op("dve", lambda e: e.match_replace(out=wk_[:, :], in_to_replace=m8a[:, :], in_values=sc[:, :], imm_value=-2.0),
                      [T["sc"], T["m8"]], [T["sel"]])
                kb.op("dve", lambda e: e.max(out=m8b[:, :], in_=wk_[:, :]), [T["sel"]], [T["m8"]])
                kb.op("dve", lambda e: e.tensor_reduce(out=thr[:, :], in_=m8b[:, :], axis=AX.X, op=ALU.min), [T["m8"]], [T["m8"]])
                kb.ts(selm[:, :], sc[:, 0:NBLK], thr[:, 0:1], None, ALU.is_ge, R=[T["sc"], T["m8"]], W=[T["sel"]])
                kb.ts(selm[:, :], selm[:, :], -1.0, -NEGM, ALU.add, ALU.mult, R=[T["sel"]], W=[T["sel"]])
                if b == 0:
                    dbg_dump(kb, c, selm[:, :], [T["sel"]], LS, NBLK)
                kb.tr(pI[:NBLK, 256:256 + LS], selm[:, :], c.ident_f[:LS, :LS], R=[T["sel"], c.Tconst], W=[T["pI"]])
                kb.copy(selx[:, hk, :, :], pI[:NBLK, 256:256 + LS].unsqueeze(1).broadcast_to([NBLK, 4, LS]), R=[T["pI"]], W=[T["selx"]])
            for br in (1, 2):
                kb.mm(pA[:, :], zb[:, 0:128], zb[:, :], True, False, R=[T["zb"]], W=[T["pA"]], sig=False)
                if br == 1:
                    cache = c.cache_sel
                    for lp in range(NPAGES if not cfg.get("NS_SKIP_SEL") else 0):
                        i3 = st["g8"] % 8
                        st["g8"] += 1
                        t_, Tt_ = pg[i3], Tpg[i3]
                        page_dma(t_[:, 0:256], cache, b, lp, [Tt_])
                        tile_pipeline(t_[:, 0:256].rearrange("p (s h d) -> p s h d", s=2, h=2), Tt_, 128,
                                      selx[:, :, :, :].rearrange("p a g l -> p (a g l)"), Fs[:, lp * 128:(lp + 1) * 128], [T["selx"], T["const"]])
                else:
                    for kt in range(WL // 128):
                        i3 = st["g8"] % 8
                        st["g8"] += 1
                        t_, Tt_ = pg[i3], Tpg[i3]
                        kb.dma("sp", t_[:, 0:256], c.cache_win[b, kt * 128:(kt + 1) * 128, :], W=[Tt_])
                        if kt == 0:
                            tile_pipeline(t_[:, 0:256].rearrange("p (s h d) -> p s h d", s=2, h=2), Tt_, 128,
                                          wm0[:, :], c.ident_b[:, :], [T["const"], c.Tconst])
                        else:
                            tile_pipeline(t_[:, 0:256].rearrange("p (s h d) -> p s h d", s=2, h=2), Tt_, 128, None, None, [])
                p2, Tp2 = pS2[0], TpS2[0]
                kb.mm(p2[:LS, 0:32], c.ident_b[:LS, :LS], caus[:, :], True, False, R=[c.Tconst, T["const"]], W=[Tp2], sig=False)
                for hk in range(2):
                    kb.mm(p2[:LS, hk * 16:(hk + 1) * 16], KnT[:, br, hk, :], qs[:, hk * 4:(hk + 1) * 4, :], False, True,
                          R=[T["KnT"], T["qs"]], W=[Tp2], sig=(hk == 1))
                kb.act(Pp[0][:LS, :], p2[:LS, 0:32], AF.Exp, R=[Tp2], W=[TPp[0]], scale=0.125)
                for hk in range(2):
                    kb.mm(pA[:16, hk * 65:(hk + 1) * 65], Pp[0][:LS, hk * 16:(hk + 1) * 16], Vn[:, br, 1, hk, :], False, True,
                          R=[TPp[0], T["Vn"]], W=[T["pA"]], sig=(hk == 1))
                finish_branch(br, False)
                if b == 0:
                    dbg_dump(kb, c, ONs[:, :, :].rearrange("p a d -> p (a d)"), [T["ONs"]], 16, 128)
            kb.copy(ONb[:, :, :], ONs[:, :, :], R=[T["ONs"]], W=[T["ONs"]])
            for hk in range(2):
                for g in range(4):
                    h = hk * 4 + g
                    kb.dma("sp", c.on_d[tg:tg + LS, h * 64:(h + 1) * 64], ONb[4 * g:4 * g + 4, hk, :], R=[T["ONs"]])
        kb.barrier()


def phase_nsa_zero(kb, c, start=0):
    nc = c.nc
    with contextlib.ExitStack() as ps:
        zt = ps.enter_context(nc.sbuf_tensor("nz_z", [128, 512], BF16))
        Tz = Tok()
        kb.memset(zt[:], 0.0, W=[Tz])
        for r0 in range(start, c.T, 128):
            nr = min(128, c.T - r0)
            kb.dma("sp", c.on_d[r0:r0 + nr, :], zt[:nr, :], R=[Tz])
        kb.barrier()

def build(cfg, stages=("mod", "ffn1", "win", "gdn", "nsap", "nsas", "mix", "ffn2")):
    nc = bass.Bass("TRN2", target_bir_lowering=False)
    c = Ctx()
    c.nc, c.cfg = nc, cfg
    cfg.setdefault("WIN", 512)
    NP, SEQ, NS = cfg["NP"], cfg["SEQ"], cfg["NS"]
    TP, TS = NP * SEQ, NS * LS
    T = TP + TS
    NR = NP + TS
    c.TP, c.TS, c.T = TP, TS, T
    WL = cfg["WIN"]
    PW = min(WL, SEQ)

    def din(name, shape, dt=F32):
        return nc.dram_tensor(name, list(shape), dt, kind="ExternalInput").ap()

    def dout(name, shape, dt=F32):
        return nc.dram_tensor(name, list(shape), dt, kind="ExternalOutput").ap()

    def dscr(name, shape, dt=F32):
        return nc.dram_tensor(name, list(shape), dt, kind="Internal").ap()

    c.xp = din("xp", [TP, D])
    c.xs = din("xs", [TS, D])
    c.c_rows = din("c_rows", [NR, D])
    c.ident_d = din("ident", [128, 128])
    c.ln_g = din("ln_g", [3, D])
    c.ln_b = din("ln_b", [3, D])
    c.w_ada = din("w_ada", [D, 9 * D])
    c.b_ada = din("b_ada", [1, 9 * D])
    c.w_ff1_gu = din("w_ff1_gu", [D, 2 * DFF])
    c.w_ff1_dn = din("w_ff1_dn", [DFF, D])
    c.w_ff2_gu = din("w_ff2_gu", [D, 2 * DFF])
    c.w_ff2_dn = din("w_ff2_dn", [DFF, D])
    c.w_in = din("w_in", [D, DIN])
    c.state_gdn = din("state_gdn", [NS, 8, 64, 64])
    c.conv_buf = din("conv_buf", [NS, 3, 1536])
    c.cache_win = din("cache_win", [NS, WL, 256])
    c.conv_w = din("conv_w", [4, 1536])
    c.w_cmp = din("w_cmp", [2, 32, 2])
    nsel, ncmp = SEQ // 64, SEQ // 32
    c.nsa_M1 = din("nsa_M1", [SEQ, nsel])
    c.nsa_M2 = din("nsa_M2", [SEQ, nsel])
    c.nsa_cmask = din("nsa_cmask", [ncmp, SEQ])
    c.nsa_F = din("nsa_F", [nsel, SEQ])
    c.nsa_Wm = din("nsa_Wm", [128, 8, 512])
    c.nsa_pair = din("nsa_pair", [ncmp, nsel])
    c.nsa_maskW = din("nsa_maskW", [128, 124])
    NPAGES, NPHYS = cfg["NPAGES"], cfg["NPHYS"]
    c.cache_cmp = din("cache_cmp", [NPHYS * 128, 256])
    c.cache_sel = din("cache_sel", [NPHYS * 128, 256])
    c.page_table = din("page_table", [NS, NPAGES], I32)
    c.pt4 = din("pt4", [NS, NPAGES * 4], I32)
    for k_, v_ in nsa_s_consts(NPAGES).items():
        setattr(c, k_, din(k_, list(v_.shape)))
    c.w_br_gdn = din("w_br_gdn", [512, D])
    c.w_br_nsa = din("w_br_nsa", [512, D])
    c.w_out = din("w_out", [D, D])
    c.a_log = din("a_log", [1, 8])
    c.dt_bias = din("dt_bias", [1, 8])
    c.norm_w = din("norm_w", [1, 64])
    c.blk1_d = din("blk1", [128, 128])
    c.gconst_d = din("gconst", [64, 5, 64])
    c.valid_d = din("valid", [64, 1])
    c.yp = dout("y_prompt", [TP, D])
    c.ys = dout("y_sample", [TS, D])
    c.p_gdn = dout("p_gdn", [NP, 8, 64, 64])
    c.p_conv = dout("p_conv", [NP, 3, 1536])
    c.p_cmp = dout("p_cmp", [TP, 256])
    c.p_sel = dout("p_sel", [TP, 256])
    c.p_win = dout("p_win", [NP * PW, 256])
    c.s_gdn = dout("s_gdn", [NS, 8, 64, 64])
    c.s_conv = dout("s_conv", [NS, 3, 1536])
    c.s_cmp = dout("s_cmp", [TS, 256])
    c.s_sel = dout("s_sel", [TS, 256])
    c.s_win = dout("s_win", [NS, WL, 256])
    c.mod_d = dscr("mod_d", [NR, 9 * D])
    c.x1 = {"p": dscr("x1p", [TP, D]), "s": dscr("x1s", [TS, D])}
    c.x2 = {"p": dscr("x2p", [TP, D]), "s": dscr("x2s", [TS, D])}
    c.qkvT_d = dscr("qkvT_d", [1536, T])
    c.qT_d = dscr("qT_d", [512, T], BF16)
    c.kT_d = dscr("kT_d", [3, 128, T], BF16)
    c.mgT_d = dscr("mgT_d", [2048, T], BF16)
    c.zab_d = dscr("zab_d", [T, 528])
    c.kvg_d = dscr("kvg_d", [T, 792])
    c.qkvn_d = dscr("qkvn_d", [1536, T])
    c.kvtok_d = dscr("kvtok_d", [T, 1024])
    dbg = dout if cfg.get("DBG") else dscr
    c.dbg_d = dbg("dbg_d", [16, 128, 512])
    c.dbg_n = 0
    c.og_d = dbg("og_d", [T, 512], BF16)
    c.on_d = dbg("on_d", [T, 512], BF16)

    with contextlib.ExitStack() as es:
        kb = KB(nc, es)
        c.kb = kb
        c.ident_f = es.enter_context(nc.sbuf_tensor("ident_f", [128, 128], F32))
        c.ident_b = es.enter_context(nc.sbuf_tensor("ident_b", [128, 128], BF16))
        c.Tconst = Tok()
        kb.dma("sp", c.ident_f[:], c.ident_d, W=[c.Tconst])
        kb.copy(c.ident_b[:], c.ident_f[:], R=[c.Tconst], W=[c.Tconst])
        if "mod" in stages:
            phase_mod(kb, c)
        if "ffn1" in stages:
            phase_ffn(kb, c, "f1", {"p": c.xp, "s": c.xs}, c.x1, c.w_ff1_gu, c.w_ff1_dn, 0)
        if "win" in stages:
            phase_win(kb, c)
        if "gdn" in stages:
            phase_gdn_a(kb, c)
            phase_gdn_b(kb, c)
        if "nsap" in stages:
            phase_nsa_prompt(kb, c)
        if "nsas" in stages:
            phase_nsa_sample(kb, c)
        if "nsa0" in stages:
            phase_nsa_zero(kb, c)
        if "nsa0s" in stages:
            phase_nsa_zero(kb, c, c.TP)
        if "mix" in stages:
            phase_mix(kb, c)
        if "ffn2" in stages:
            phase_ffn(kb, c, "f2", c.x2, {"p": c.yp, "s": c.ys}, c.w_ff2_gu, c.w_ff2_dn, 2)
        kb.finish()
        c.nops = kb.nops
    return nc, c


def make_in_maps(cfg, inputs, ncores):
    NP, SEQ, NS = cfg["NP"], cfg["SEQ"], cfg["NS"]
    f = lambda a: np.ascontiguousarray(np.asarray(a))
    maps = []
    ident = np.eye(128, dtype=np.float32)
    blk1 = np.kron(np.eye(2, dtype=np.float32), np.ones((64, 64), np.float32))
    ii = np.arange(64)
    gconst = np.stack([
        (ii[:, None] <= ii[None, :]).astype(np.float32),
        np.ones((64, 64), np.float32),
        np.where(ii[None, :] >= ii[:, None], 0.0, -30000.0).astype(np.float32),
        np.where(ii[None, :] >= ii[:, None], 30000.0, 0.0).astype(np.float32),
        np.eye(64, dtype=np.float32)], axis=1)
    valid = (ii < LS).astype(np.float32).reshape(64, 1)
    nconst = nsa_consts(SEQ)
    nconst.update(nsa_s_consts(cfg["NPAGES"]))
    cc_all = f(inputs["cache_cmp_kv"][0]).reshape(-1, 256)
    cs_all = f(inputs["cache_sel_kv"][0]).reshape(-1, 256)
    for i in range(ncores):
        ps, ss = slice(i * NP, (i + 1) * NP), slice(i * NS, (i + 1) * NS)
        m = {
            "xp": f(inputs["x_prompt"][ps]).reshape(NP * SEQ, D),
            "xs": f(inputs["x_sample"][ss]).reshape(NS * LS, D),
            "c_rows": np.concatenate([f(inputs["c_prompt"][ps]), np.repeat(f(inputs["c_sample"][ss]), LS, axis=0)], 0),
            "ident": ident,
            "ln_g": f(inputs["ln_g"][0]), "ln_b": f(inputs["ln_b"][0]),
            "w_ada": f(inputs["w_ada"][0]), "b_ada": f(inputs["b_ada"]).reshape(1, 9 * D),
            "w_ff1_gu": f(inputs["w_ff1_gu"][0]), "w_ff1_dn": f(inputs["w_ff1_dn"][0]),
            "w_ff2_gu": f(inputs["w_ff2_gu"][0]), "w_ff2_dn": f(inputs["w_ff2_dn"][0]),
            "w_in": f(inputs["w_in"][0]),
            "state_gdn": f(inputs["state_gdn"][0][ss]),
            "conv_buf": f(inputs["state_gdn_conv"][0][ss]),
            "cache_win": f(inputs["cache_win_kv"][0][ss]).reshape(NS, -1, 256),
            "conv_w": f(inputs["gdn_conv_w"][0]), "a_log": f(inputs["gdn_a_log"]).reshape(1, 8),
            "dt_bias": f(inputs["gdn_dt_bias"]).reshape(1, 8), "norm_w": f(inputs["gdn_norm_w"]).reshape(1, 64),
            "blk1": blk1, "gconst": gconst, "valid": valid, "w_cmp": f(inputs["nsa_w_cmp"][0]),
            "cache_cmp": cc_all, "cache_sel": cs_all,
            "page_table": f(inputs["page_table"][ss]).astype(np.int32),
            "pt4": np.repeat(f(inputs["page_table"][ss]).astype(np.int32), 4, axis=1),
            "w_br_gdn": f(inputs["w_br_gdn"][0]), "w_br_nsa": f(inputs["w_br_nsa"][0]), "w_out": f(inputs["w_out"][0]),
        }
        m.update(nconst)
        maps.append(m)
    return maps


def gather_outputs(cfg, results, ncores):
    NP, SEQ, NS = cfg["NP"], cfg["SEQ"], cfg["NS"]
    PW = min(cfg["WIN"], SEQ)
    cat = lambda k: np.concatenate([np.asarray(r[k]) for r in results], 0)
    B, Bs = NP * ncores, NS * ncores
    return (
        cat("y_prompt").reshape(B, SEQ, D),
        cat("y_sample").reshape(Bs, LS, D),
        cat("p_gdn").reshape(1, B, 8, 64, 64),
        cat("p_conv").reshape(1, B, 3, 1536),
        cat("p_cmp").reshape(1, B, SEQ, 2, 2, 64),
        cat("p_sel").reshape(1, B, SEQ, 2, 2, 64),
        cat("p_win").reshape(1, B, PW, 2, 2, 64),
        cat("s_gdn").reshape(1, Bs, 8, 64, 64),
        cat("s_conv").reshape(1, Bs, 3, 1536),
        cat("s_cmp").reshape(1, Bs, LS, 2, 2, 64),
        cat("s_sel").reshape(1, Bs, LS, 2, 2, 64),
        cat("s_win").reshape(1, Bs, cfg["WIN"], 2, 2, 64),
    )


_CACHE = {}


def run(cfg, inputs, ncores, stages=None):
    key = (tuple(sorted(cfg.items())), ncores, stages)
    if key not in _CACHE:
        _CACHE[key] = build(dict(cfg), stages) if stages else build(dict(cfg))
    nc, c = _CACHE[key]
    res = run_bass_kernel_spmd(nc, make_in_maps(c.cfg, inputs, ncores), core_ids=list(range(ncores)))
    _CACHE["last"] = res.results
    return gather_outputs(c.cfg, res.results, ncores)


def kernel(**inputs):
    return run(FULL_CFG, inputs, NCORES)
```

```python
import contextlib
import numpy as np
import concourse.bass as bass
import concourse.mybir as mybir
from concourse.bass_utils import run_bass_kernel_spmd

F32 = mybir.dt.float32
BF16 = mybir.dt.bfloat16
I32 = mybir.dt.int32
AF = mybir.ActivationFunctionType
ALU = mybir.AluOpType
AX = mybir.AxisListType

D = 1024
DFF = 2816
DIN = 5416
NCORES = 8
ALPHA = 2.0 ** 0.25
LS = 4

FULL_CFG = dict(NP=4, SEQ=2048, NS=16, NPAGES=64, NPHYS=10240)


class Tok:
    __slots__ = ("w", "r", "name")

    def __init__(self, name=""):
        self.w = None
        self.r = {}
        self.name = name


class Ev:
    __slots__ = ("dim", "val")

    def __init__(self, dim, val):
        self.dim = dim
        self.val = val


class KB:
    def __init__(self, nc, es, kq=None):
        self.nc = nc
        self.eng = {"pe": nc.tensor, "act": nc.scalar, "dve": nc.vector, "pool": nc.gpsimd, "sp": nc.sync}
        self.sem = {e: es.enter_context(nc.semaphore("s_" + e)) for e in self.eng}
        self.cnt = {e: 0 for e in self.eng}
        self.waited = {e: {} for e in self.eng}
        kq = kq or {"sp": 24, "act": 8, "pool": 16}
        self.dq = {q: [es.enter_context(nc.semaphore("d_%s%d" % (q, i))) for i in range(k)] for q, k in kq.items()}
        self.dqn = {q: 0 for q in kq}
        self.pe_pending = []
        self.nops = 0

    def semof(self, dim):
        if isinstance(dim, str):
            return self.sem[dim]
        return self.dq[dim[0]][dim[1]]

    def _collect(self, e, R, W):
        deps = {}

        def add(ev):
            if ev is None:
                return
            if e == "pe" and ev.dim == "pe":
                return
            if ev.val is None:
                raise RuntimeError("dependency on unsignaled PE op")
            if deps.get(ev.dim, 0) < ev.val:
                deps[ev.dim] = ev.val

        for t in R:
            add(t.w)
        for t in W:
            add(t.w)
            for r in t.r.values():
                add(r)
        return deps

    def _waits(self, e, deps):
        wd = self.waited[e]
        for dim, val in deps.items():
            if wd.get(dim, 0) < val:
                self.eng[e].wait_ge(self.semof(dim), val)
                wd[dim] = val

    def _record(self, ev, R, W):
        for t in R:
            t.r[ev.dim] = ev
        for t in W:
            t.w = ev
            t.r = {}

    def op(self, e, fn, R=(), W=(), sig=True):
        self._waits(e, self._collect(e, R, W))
        ins = fn(self.eng[e])
        self.nops += 1
        if sig:
            self.cnt[e] += 1
            ins.then_inc(self.sem[e], 1)
            ev = Ev(e, self.cnt[e])
            if e == "pe":
                for p in self.pe_pending:
                    p.val = self.cnt[e]
                self.pe_pending = []
        else:
            assert e == "pe"
            ev = Ev("pe", None)
            self.pe_pending.append(ev)
        self._record(ev, R, W)
        return ev

    def dma(self, q, out, in_, R=(), W=(), **kw):
        slots = self.dq[q]
        i = self.dqn[q]
        self.dqn[q] += 1
        k = i % len(slots)
        val = 16 * (i // len(slots) + 1)
        deps = self._collect(q, R, W)
        dim = (q, k)
        if val > 16:
            deps[dim] = max(deps.get(dim, 0), val - 16)
        self._waits(q, deps)
        ins = self.eng[q].dma_start(out=out, in_=in_, **kw)
        ins.then_inc(slots[k], 16)
        self.nops += 1
        ev = Ev(dim, val)
        self._record(ev, R, W)
        return ev

    def dma_fn(self, q, fn, R=(), W=()):
        slots = self.dq[q]
        i = self.dqn[q]
        self.dqn[q] += 1
        k = i % len(slots)
        val = 16 * (i // len(slots) + 1)
        deps = self._collect(q, R, W)
        dim = (q, k)
        if val > 16:
            deps[dim] = max(deps.get(dim, 0), val - 16)
        self._waits(q, deps)
        ins = fn(self.eng[q])
        ins.then_inc(slots[k], 16)
        self.nops += 1
        ev = Ev(dim, val)
        self._record(ev, R, W)
        return ev

    def _all_targets(self):
        targets = {}
        for e in self.eng:
            if self.cnt[e] > 0:
                targets[e] = self.cnt[e]
        for q, slots in self.dq.items():
            n = self.dqn[q]
            for k in range(len(slots)):
                uses = (n - k + len(slots) - 1) // len(slots) if n > k else 0
                if uses > 0:
                    targets[(q, k)] = 16 * uses
        return targets

    def barrier(self):
        assert not self.pe_pending
        targets = self._all_targets()
        for e in self.eng:
            self._waits(e, targets)

    def finish(self):
        assert not self.pe_pending
        self._waits("sp", self._all_targets())

    def mm(self, out, lhsT, rhs, start, stop, R=(), W=(), sig=None):
        if sig is None:
            sig = stop
        return self.op("pe", lambda e: e.matmul(out, lhsT, rhs, start=start, stop=stop), R, W, sig)

    def tr(self, out, in_, ident, R=(), W=(), sig=True):
        return self.op("pe", lambda e: e.transpose(out, in_, ident), R, W, sig)

    def act(self, out, in_, func, R=(), W=(), bias=None, scale=None, **kw):
        kws = dict(kw)
        if bias is not None:
            kws["bias"] = bias
        if scale is not None:
            kws["scale"] = scale
        return self.op("act", lambda e: e.activation(out=out, in_=in_, func=func, **kws), R, W)

    def tt(self, out, in0, in1, op, R=(), W=(), eng="dve"):
        return self.op(eng, lambda e: e.tensor_tensor(out=out, in0=in0, in1=in1, op=op), R, W)

    def ts(self, out, in0, s1, s2, op0, op1=None, R=(), W=(), eng="dve"):
        if op1 is None:
            return self.op(eng, lambda e: e.tensor_scalar(out=out, in0=in0, scalar1=s1, scalar2=None, op0=op0), R, W)
        return self.op(eng, lambda e: e.tensor_scalar(out=out, in0=in0, scalar1=s1, scalar2=s2, op0=op0, op1=op1), R, W)

    def stt(self, out, in0, scalar, in1, op0, op1, R=(), W=(), eng="dve"):
        return self.op(eng, lambda e: e.scalar_tensor_tensor(out=out, in0=in0, scalar=scalar, in1=in1, op0=op0, op1=op1), R, W)

    def copy(self, out, in_, R=(), W=(), eng="dve"):
        if eng == "act":
            return self.op("act", lambda e: e.copy(out=out, in_=in_), R, W)
        return self.op(eng, lambda e: e.tensor_copy(out=out, in_=in_), R, W)

    def memset(self, ap, v, W=(), eng="dve"):
        return self.op(eng, lambda e: e.memset(ap, v), (), W)


class Ctx:
    pass


def dbg_dump(kb, c, ap, R, rows, cols):
    if not c.cfg.get("DBG") or c.dbg_n >= 16:
        return
    kb.dma("sp", c.dbg_d[c.dbg_n, :rows, :cols], ap, R=R)
    c.dbg_n += 1


def token_groups(cfg, G):
    out = []
    for s in range(cfg["NP"]):
        for g in range(cfg["SEQ"] // G):
            out.append(("p", s, s * cfg["SEQ"] + g * G, G))
    out.append(("s", None, 0, cfg["NS"] * LS))
    return out


def subtiles(n):
    return [(r0, min(128, n - r0)) for r0 in range(0, n, 128)]


def load_rows(kb, c, tile, tok, kind, seq, col0, ncols=1024, q="sp"):
    cfg = c.cfg
    if kind == "p":
        src = c.mod_d[seq:seq + 1, col0:col0 + ncols].partition_broadcast(128)
        kb.dma(q, tile[:, :ncols], src, W=[tok])
    else:
        ts = cfg["NS"] * LS
        src = c.mod_d[cfg["NP"]:cfg["NP"] + ts, col0:col0 + ncols]
        kb.dma(q, tile[:ts, :ncols], src, W=[tok])


def layer_norm_rows(kb, c, t, nr, Tt, lng, lnb, Tl, st, mv, rstd, Tst):
    nc = c.nc
    for h in range(2):
        kb.op("dve", lambda e, h=h: e.bn_stats(out=st[:nr, h, :], in_=t[:nr, h * 512:(h + 1) * 512]), [Tt], [Tst])
    kb.op("dve", lambda e: e.bn_aggr(out=mv[:nr, :], in_=st[:nr, :, :]), [Tst], [Tst])
    kb.act(rstd[:nr, :], mv[:nr, 1:2], AF.Sqrt, R=[Tst], W=[Tst], bias=1e-5)
    kb.op("dve", lambda e: e.reciprocal(out=rstd[:nr, :], in_=rstd[:nr, :]), [Tst], [Tst])
    kb.ts(t[:nr, :], t[:nr, :], mv[:nr, 0:1], rstd[:nr, 0:1], ALU.subtract, ALU.mult, R=[Tt, Tst], W=[Tt])
    kb.tt(t[:nr, :], t[:nr, :], lng[:nr, :], ALU.mult, R=[Tt, Tl], W=[Tt], eng="pool")
    kb.tt(t[:nr, :], t[:nr, :], lnb[:nr, :], ALU.add, R=[Tt, Tl], W=[Tt], eng="pool")


def phase_mod(kb, c):
    nc, cfg = c.nc, c.cfg
    NR = cfg["NP"] + cfg["NS"] * LS
    with contextlib.ExitStack() as ps:
        sb = lambda n, s, d=F32: ps.enter_context(nc.sbuf_tensor(n, s, d))
        ct = sb("m_c", [NR, D])
        sct = sb("m_scT", [128, 8, NR], BF16)
        modt = sb("m_mod", [NR, 9 * D])
        bt = sb("m_b", [NR, 9 * D])
        wb = [sb("m_w%d" % i, [128, 8, 512], BF16) for i in range(2)]
        pst = [ps.enter_context(nc.psum_tensor("m_ps%d" % i, [128, 512], F32)) for i in range(4)]
        Tc, Tsct, Tmod, Tb = Tok(), Tok(), Tok(), Tok()
        Tw = [Tok(), Tok()]
        Tp = [Tok() for _ in range(4)]
        kb.dma("sp", ct[:], c.c_rows, W=[Tc])
        kb.dma("sp", bt[:], c.b_ada.partition_broadcast(NR), W=[Tb])
        kb.act(ct[:], ct[:], AF.Silu, R=[Tc], W=[Tc])
        for k in range(8):
            b = k // 4
            col = (k % 4) * NR
            kb.tr(pst[b][:, col:col + NR], ct[:, k * 128:(k + 1) * 128], c.ident_f[:NR, :NR], R=[Tc, c.Tconst], W=[Tp[b]])
        for b in range(2):
            kb.copy(sct[:, 4 * b:4 * b + 4, :], pst[b][:, 0:4 * NR].rearrange("p (k n) -> p k n", k=4), R=[Tp[b]], W=[Tsct])
        wv = c.w_ada.rearrange("(k p) f -> p k f", p=128)
        for nb in range(18):
            w = wb[nb % 2]
            kb.dma("pool", w[:], wv[:, :, nb * 512:(nb + 1) * 512], W=[Tw[nb % 2]])
            bank = pst[2 + nb % 2]
            for k in range(8):
                kb.mm(bank[:NR, :], sct[:, k, :], w[:, k, :], k == 0, k == 7, R=[Tsct, Tw[nb % 2]], W=[Tp[2 + nb % 2]])
            kb.tt(modt[:, nb * 512:(nb + 1) * 512], bank[:NR, :], bt[:, nb * 512:(nb + 1) * 512], ALU.add,
                  R=[Tp[2 + nb % 2], Tb], W=[Tmod])
        for i in range(3):
            o = (i * 3 + 1) * D
            kb.ts(modt[:, o:o + D], modt[:, o:o + D], 1.0, None, ALU.add, R=[Tmod], W=[Tmod])
        for i in (0, 2):
            o = (i * 3 + 2) * D
            kb.ts(modt[:, o:o + D], modt[:, o:o + D], 0.5, None, ALU.mult, R=[Tmod], W=[Tmod])
        kb.dma("sp", c.mod_d[:, :], modt[:], R=[Tmod])
        kb.barrier()


def phase_ffn(kb, c, tag, src, dst, w_gu, w_dn, isub):
    nc, cfg = c.nc, c.cfg
    G = 512
    NJ = DFF // 128
    with contextlib.ExitStack() as ps:
        sb = lambda n, s, d=F32: ps.enter_context(nc.sbuf_tensor(tag + n, s, d))
        wgu = sb("wgu", [128, 8, 2 * DFF], BF16)
        wdn = sb("wdn", [128, NJ, D], BF16)
        hT = sb("hT", [128, NJ, G], BF16)
        xmT = sb("xmT", [128, 8, G], BF16)
        xin = [sb("xin%d" % i, [128, D]) for i in range(4)]
        rows = {n: sb("row_" + n, [128, D], BF16 if n in ("sc", "sh", "g") else F32) for n in ("sc", "sh", "g", "lng", "lnb")}
        xm = sb("xm", [128, D])
        wk = [sb("wk%d" % i, [128, D]) for i in range(2)]
        sgt = [sb("sg%d" % i, [128, G], BF16) for i in range(2)]
        st = sb("st", [128, 2, nc.vector.BN_STATS_DIM])
        mv = sb("mv", [128, nc.vector.BN_AGGR_DIM])
        rstd = sb("rstd", [128, 1])
        psgu = [ps.enter_context(nc.psum_tensor(tag + "psgu%d" % i, [128, 512], F32)) for i in range(4)]
        psy = [ps.enter_context(nc.psum_tensor(tag + "psy%d" % i, [128, 1024], F32)) for i in range(2)]
        Tgu = [Tok() for _ in range(11)]
        Tdn = [Tok() for _ in range(11)]
        ThT = [Tok() for _ in range(NJ)]
        TxmT, Txm, Tst, Tln, Tmodrows = Tok(), Tok(), Tok(), Tok(), Tok()
        Txin = [Tok() for _ in range(4)]
        Twk = [Tok(), Tok()]
        Tsg = [Tok(), Tok()]
        Tpsgu = [Tok() for _ in range(4)]
        Tpsy = [Tok(), Tok()]
        guv = w_gu.rearrange("(k p) f -> p k f", p=128)
        for i in range(11):
            kb.dma("pool", wgu[:, :, i * 512:(i + 1) * 512], guv[:, :, i * 512:(i + 1) * 512], W=[Tgu[i]])
        dnv = w_dn.rearrange("(j p) d -> p j d", p=128)
        for i in range(11):
            kb.dma("pool", wdn[:, 2 * i:2 * i + 2, :], dnv[:, 2 * i:2 * i + 2, :], W=[Tdn[i]])
        kb.dma("sp", rows["lng"][:], c.ln_g[isub:isub + 1, :].partition_broadcast(128), W=[Tln])
        kb.dma("sp", rows["lnb"][:], c.ln_b[isub:isub + 1, :].partition_broadcast(128), W=[Tln])
        cur = None
        xslot = 0
        wslot = 0
        gi = 0
        for (kind, seq, t0, n) in token_groups(cfg, G):
            if (kind, seq) != cur:
                cur = (kind, seq)
                base = isub * 3 * D
                load_rows(kb, c, rows["sh"], Tmodrows, kind, seq, base, q="pool")
                load_rows(kb, c, rows["sc"], Tmodrows, kind, seq, base + D, q="pool")
                load_rows(kb, c, rows["g"], Tmodrows, kind, seq, base + 2 * D, q="pool")
            subs = subtiles(n)
            xs = []
            for si, (r0, nr) in enumerate(subs):
                xt = xin[xslot]
                Tx = Txin[xslot]
                xslot = (xslot + 1) % 4
                xs.append((xt, Tx))
                kb.dma("sp", xt[:nr, :], src[kind][t0 + r0:t0 + r0 + nr, :], W=[Tx])
                kb.tt(xm[:nr, :], xt[:nr, :], rows["sc"][:nr, :], ALU.mult, R=[Tx, Tmodrows], W=[Txm])
                kb.tt(xm[:nr, :], xm[:nr, :], rows["sh"][:nr, :], ALU.add, R=[Txm, Tmodrows], W=[Txm], eng="pool")
                py = psy[si % 2]
                for k in range(8):
                    kb.tr(py[:, k * 128:k * 128 + nr], xm[:nr, k * 128:(k + 1) * 128], c.ident_f[:nr, :nr],
                          R=[Txm, c.Tconst], W=[Tpsy[si % 2]])
                kb.copy(xmT[:, :, r0:r0 + nr], py[:, :].rearrange("p (k n) -> p k n", k=8)[:, :, :nr],
                        R=[Tpsy[si % 2]], W=[TxmT], eng="act")
            for j in range(NJ):
                pg = psgu[(j % 2) * 2]
                pu = psgu[(j % 2) * 2 + 1]
                Tg_, Tu_ = Tpsgu[(j % 2) * 2], Tpsgu[(j % 2) * 2 + 1]
                cg = j * 128
                cu = DFF + j * 128
                for k in range(8):
                    kb.mm(pg[:, :n], wgu[:, k, cg:cg + 128], xmT[:, k, :n], k == 0, k == 7,
                          R=[TxmT, Tgu[cg // 512]], W=[Tg_])
                for k in range(8):
                    kb.mm(pu[:, :n], wgu[:, k, cu:cu + 128], xmT[:, k, :n], k == 0, k == 7,
                          R=[TxmT, Tgu[cu // 512]], W=[Tu_])
                s = sgt[j % 2]
                kb.act(s[:, :n], pg[:, :n], AF.Silu, R=[Tg_], W=[Tsg[j % 2]])
                kb.tt(hT[:, j, :n], s[:, :n], pu[:, :n], ALU.mult, R=[Tsg[j % 2], Tu_], W=[ThT[j]])
            for si, (r0, nr) in enumerate(subs):
                py = psy[si % 2]
                Tpy = Tpsy[si % 2]
                for half in range(2):
                    for j in range(NJ):
                        kb.mm(py[:nr, half * 512:(half + 1) * 512], hT[:, j, r0:r0 + nr],
                              wdn[:, j, half * 512:(half + 1) * 512], j == 0, j == NJ - 1,
                              R=[ThT[j], Tdn[j // 2]], W=[Tpy])
                w = wk[wslot]
                Tw_ = Twk[wslot]
                wslot = (wslot + 1) % 2
                xt, Tx = xs[si]
                kb.tt(w[:nr, :], py[:nr, :], rows["g"][:nr, :], ALU.mult, R=[Tpy, Tmodrows], W=[Tw_])
                kb.stt(w[:nr, :], xt[:nr, :], ALPHA, w[:nr, :], ALU.mult, ALU.add, R=[Tx, Tw_], W=[Tw_])
                layer_norm_rows(kb, c, w, nr, Tw_, rows["lng"], rows["lnb"], Tln, st, mv, rstd, Tst)
                kb.dma("sp", dst[kind][t0 + r0:t0 + r0 + nr, :], w[:nr, :], R=[Tw_])
            gi += 1
        kb.barrier()


QKV0, Z0, Q0, KV0, GN0, MG0 = 0, 1536, 2064, 2576, 3344, 3368


def phase_win(kb, c):
    nc, cfg = c.nc, c.cfg
    G = 512
    SEQ, NP, NS = cfg["SEQ"], cfg["NP"], cfg["NS"]
    TP, TS = c.TP, c.TS
    with contextlib.ExitStack() as ps:
        sb = lambda n, s, d=F32: ps.enter_context(nc.sbuf_tensor("wi_" + n, s, d))
        win = sb("w", [128, 8, DIN], BF16)
        hT = sb("hT", [128, 8, G], BF16)
        xin = [sb("xin%d" % i, [128, D]) for i in range(2)]
        xm = sb("xm", [128, D])
        rows = {n: sb("row_" + n, [128, D]) for n in ("sc", "sh")}
        stF = [sb("stF%d" % i, [128, 4, G]) for i in range(2)]
        stB = [sb("stB%d" % i, [128, 4, G], BF16) for i in range(3)]
        stT = [sb("stT%d" % i, [128, 1536]) for i in range(2)]
        psx = ps.enter_context(nc.psum_tensor("wi_psx", [128, 1024], F32))
        psF = [ps.enter_context(nc.psum_tensor("wi_psF%d" % i, [128, 512], F32)) for i in range(3)]
        psT = ps.enter_context(nc.psum_tensor("wi_psT", [128, 1536], F32))
        Tw = [Tok() for _ in range(11)]
        ThT, Txm, Tmodrows, Tpsx, TpsT = Tok(), Tok(), Tok(), Tok(), Tok()
        Txin = [Tok(), Tok()]
        TstF = [Tok(), Tok()]
        TstB = [Tok(), Tok(), Tok()]
        TstT = [Tok(), Tok()]
        TpsF = [Tok(), Tok(), Tok()]
        wv = c.w_in.rearrange("(k p) f -> p k f", p=128)
        for i in range(11):
            hi = min(DIN, (i + 1) * 512)
            kb.dma("pool", win[:, :, i * 512:hi], wv[:, :, i * 512:hi], W=[Tw[i]])

        def wtoks(c0, c1):
            return [Tw[i] for i in range(c0 // 512, (c1 - 1) // 512 + 1)]

        wl = cfg["WIN"]
        kb.dma("sp", c.s_win[:, 0:wl - LS, :], c.cache_win[:, LS:wl, :])
        cur = None
        xslot = 0
        cnt = {"F": 0, "B": 0, "T": 0, "pf": 0, "ev": 0}
        for (kind, seq, t0, n) in token_groups(cfg, G):
            tg = t0 if kind == "p" else TP + t0
            if (kind, seq) != cur:
                cur = (kind, seq)
                load_rows(kb, c, rows["sh"], Tmodrows, kind, seq, 3 * D)
                load_rows(kb, c, rows["sc"], Tmodrows, kind, seq, 4 * D)
            subs = subtiles(n)
            for si, (r0, nr) in enumerate(subs):
                xt, Tx = xin[xslot], Txin[xslot]
                xslot = (xslot + 1) % 2
                kb.dma("sp", xt[:nr, :], c.x1[kind][t0 + r0:t0 + r0 + nr, :], W=[Tx])
                kb.tt(xm[:nr, :], xt[:nr, :], rows["sc"][:nr, :], ALU.mult, R=[Tx, Tmodrows], W=[Txm])
                kb.tt(xm[:nr, :], xm[:nr, :], rows["sh"][:nr, :], ALU.add, R=[Txm, Tmodrows], W=[Txm], eng="pool")
                for k in range(8):
                    kb.tr(psx[:, k * 128:k * 128 + nr], xm[:nr, k * 128:(k + 1) * 128], c.ident_f[:nr, :nr],
                          R=[Txm, c.Tconst], W=[Tpsx])
                kb.copy(hT[:, :, r0:r0 + nr], psx[:, :].rearrange("p (k n) -> p k n", k=8)[:, :, :nr],
                        R=[Tpsx], W=[ThT], eng="act")

            def fgroup(cols, stage_kind, dst_ap, func=None):
                if stage_kind == "F":
                    st, Ts = stF[cnt["F"] % 2], TstF[cnt["F"] % 2]
                    cnt["F"] += 1
                else:
                    st, Ts = stB[cnt["B"] % 3], TstB[cnt["B"] % 3]
                    cnt["B"] += 1
                for ci, c0 in enumerate(cols):
                    pf, Tpf = psF[cnt["pf"] % 3], TpsF[cnt["pf"] % 3]
                    cnt["pf"] += 1
                    for k in range(8):
                        kb.mm(pf[:, :n], win[:, k, c0:c0 + 128], hT[:, k, :n], k == 0, k == 7,
                              R=[ThT] + wtoks(c0, c0 + 128), W=[Tpf])
                    if func is not None:
                        kb.act(st[:, ci, :n], pf[:, :n], func, R=[Tpf], W=[Ts])
                    else:
                        eng = "act" if cnt["ev"] % 2 == 0 else "dve"
                        cnt["ev"] += 1
                        kb.copy(st[:, ci, :n], pf[:, :n], R=[Tpf], W=[Ts], eng=eng)
                kb.dma("sp", dst_ap, st[:, 0:len(cols), :n], R=[Ts])

            qv = c.qkvT_d.rearrange("(c p) t -> p c t", p=128)
            for g4 in range(3):
                fgroup([QKV0 + (g4 * 4 + i) * 128 for i in range(4)], "F", qv[:, g4 * 4:g4 * 4 + 4, tg:tg + n])
            fgroup([Q0 + i * 128 for i in range(4)], "B", c.qT_d.rearrange("(c p) t -> p c t", p=128)[:, :, tg:tg + n])
            fgroup([KV0, KV0 + 256, KV0 + 512], "B", c.kT_d.rearrange("i p t -> p i t")[:, :, tg:tg + n])
            mv_ = c.mgT_d.rearrange("(c p) t -> p c t", p=128)
            for g4 in range(4):
                fgroup([MG0 + (g4 * 4 + i) * 128 for i in range(4)], "B", mv_[:, g4 * 4:g4 * 4 + 4, tg:tg + n], func=AF.Sigmoid)

            for si, (r0, nr) in enumerate(subs):
                def tgroup(c0, ncols):
                    st, Ts = stT[cnt["T"] % 2], TstT[cnt["T"] % 2]
                    cnt["T"] += 1
                    for b0 in range(0, ncols, 512):
                        w_ = min(512, ncols - b0)
                        for k in range(8):
                            kb.mm(psT[:nr, b0:b0 + w_], hT[:, k, r0:r0 + nr], win[:, k, c0 + b0:c0 + b0 + w_],
                                  k == 0, k == 7, R=[ThT] + wtoks(c0 + b0, c0 + b0 + w_), W=[TpsT])
                    kb.copy(st[:nr, :ncols], psT[:nr, :ncols], R=[TpsT], W=[Ts])
                    return st, Ts

                a0 = t0 + r0
                st, Ts = tgroup(Z0, 528)
                kb.dma("sp", c.zab_d[tg + r0:tg + r0 + nr, :], st[:nr, :528], R=[Ts])
                st, Ts = tgroup(KV0, 792)
                kb.dma("sp", c.kvg_d[tg + r0:tg + r0 + nr, :], st[:nr, :792], R=[Ts])
                if kind == "p":
                    kb.dma("sp", c.p_cmp[a0:a0 + nr, :], st[:nr, 0:256], R=[Ts])
                    kb.dma("sp", c.p_sel[a0:a0 + nr, :], st[:nr, 256:512], R=[Ts])
                    pos = a0 - seq * SEQ
                    w0 = SEQ - min(wl, SEQ)
                    if pos >= w0:
                        wr = seq * min(wl, SEQ) + pos - w0
                        kb.dma("sp", c.p_win[wr:wr + nr, :], st[:nr, 512:768], R=[Ts])
                else:
                    kb.dma("sp", c.s_cmp[a0:a0 + nr, :], st[:nr, 0:256], R=[Ts])
                    kb.dma("sp", c.s_sel[a0:a0 + nr, :], st[:nr, 256:512], R=[Ts])
                    for l in range(LS):
                        kb.dma("sp", c.s_win[:, wl - LS + l, :], st[l:nr:LS, 512:768], R=[Ts])
                last_of_seq = (kind == "p" and a0 + nr == (seq + 1) * SEQ)
                if last_of_seq or kind == "s":
                    st, Ts = tgroup(QKV0, 1536)
                    if kind == "p":
                        kb.dma("sp", c.p_conv[seq, :, :], st[nr - 3:nr, :1536], R=[Ts])
                    else:
                        for l in range(1, LS):
                            kb.dma("sp", c.s_conv[:, l - 1, :], st[l:nr:LS, :1536], R=[Ts])
        kb.barrier()


def phase_gdn_a(kb, c):
    nc, cfg = c.nc, c.cfg
    G = 512
    SEQ, NP, NS = cfg["SEQ"], cfg["NP"], cfg["NS"]
    TP, TS = c.TP, c.TS
    with contextlib.ExitStack() as ps:
        sb = lambda n, s, d=F32: ps.enter_context(nc.sbuf_tensor("ga_" + n, s, d))
        cwt = sb("cwt", [4, 1536])
        cw = sb("cw", [128, 12, 4])
        blk1 = sb("blk1", [128, 128])
        xr = [sb("xr%d" % i, [128, 12, 3 + G]) for i in range(2)]
        yc = [sb("yc%d" % i, [128, 12, G]) for i in range(2)]
        sq = sb("sq", [128, 8, G])
        rn = [sb("rn%d" % i, [128, G]) for i in range(2)]
        kvt = sb("kvt", [128, 4, 1024])
        cbt = sb("cbt", [NS * 3, 1536])
        xrs = sb("xrs", [128, 12, NS, 3 + LS])
        pss = [ps.enter_context(nc.psum_tensor("ga_pss%d" % i, [128, 512], F32)) for i in range(2)]
        pst = [ps.enter_context(nc.psum_tensor("ga_pst%d" % i, [128, 1024], F32)) for i in range(2)]
        Tcw, Tblk, Tsq, Tkvt, Tcbt, Txrs = Tok(), Tok(), Tok(), Tok(), Tok(), Tok()
        Txr = [Tok(), Tok()]
        Tyc = [Tok(), Tok()]
        Trn = [Tok(), Tok()]
        Tpss = [Tok(), Tok()]
        Tpst = [Tok(), Tok()]
        kb.dma("sp", cwt[:], c.conv_w, W=[Tcw])
        kb.dma("sp", blk1[:], c.blk1_d, W=[Tblk])
        for cc in range(12):
            kb.tr(pss[0][:, cc * 4:cc * 4 + 4], cwt[:, cc * 128:(cc + 1) * 128], c.ident_f[:4, :4], R=[Tcw, c.Tconst], W=[Tpss[0]])
        kb.copy(cw[:, :, :], pss[0][:, 0:48].rearrange("p (c j) -> p c j", j=4), R=[Tpss[0]], W=[Tcw])
        qv = c.qkvT_d.rearrange("(c p) t -> p c t", p=128)
        qn = c.qkvn_d.rearrange("(c p) t -> p c t", p=128)
        bi = 0
        for (kind, seq, t0, n) in token_groups(cfg, G):
            tg = t0 if kind == "p" else TP + t0
            y, Ty = yc[bi % 2], Tyc[bi % 2]
            x, Tx = xr[bi % 2], Txr[bi % 2]
            bi += 1
            if kind == "p":
                if t0 % SEQ == 0:
                    kb.memset(x[:, :, 0:3], 0.0, W=[Tx])
                    kb.dma("sp", x[:, :, 3:3 + n], qv[:, :, tg:tg + n], W=[Tx])
                else:
                    kb.dma("sp", x[:, :, 0:3 + n], qv[:, :, tg - 3:tg + n], W=[Tx])
                for cc in range(12):
                    eng = "dve"
                    kb.ts(y[:, cc, :n], x[:, cc, 0:n], cw[:, cc, 0:1], None, ALU.mult, R=[Tx, Tcw], W=[Ty], eng=eng)
                    for j in range(1, 4):
                        kb.stt(y[:, cc, :n], x[:, cc, j:j + n], cw[:, cc, j:j + 1], y[:, cc, :n], ALU.mult, ALU.add,
                               R=[Tx, Tcw, Ty], W=[Ty], eng=eng)
            else:
                kb.dma("sp", cbt[:], c.conv_buf.rearrange("b r f -> (b r) f"), W=[Tcbt])
                for cc in range(12):
                    kb.tr(pst[0][:, cc * 64:cc * 64 + NS * 3], cbt[:, cc * 128:(cc + 1) * 128], c.ident_f[:NS * 3, :NS * 3],
                          R=[Tcbt, c.Tconst], W=[Tpst[0]])
                kb.copy(xrs[:, :, :, 0:3],
                        pst[0][:, 0:768].rearrange("p (c x) -> p c x", x=64)[:, :, 0:NS * 3].rearrange("p c (b r) -> p c b r", r=3),
                        R=[Tpst[0]], W=[Txrs])
                for cc in range(12):
                    kb.dma("sp", xrs[:, cc, :, 3:3 + LS], qv[:, cc, tg:tg + n].rearrange("p (b l) -> p b l", l=LS), W=[Txrs])
                for cc in range(12):
                    eng = "dve"
                    yv = y[:, cc, :n].rearrange("p (b l) -> p b l", l=LS)
                    kb.ts(yv, xrs[:, cc, :, 0:LS], cw[:, cc, 0:1], None, ALU.mult, R=[Txrs, Tcw], W=[Ty], eng=eng)
                    for j in range(1, 4):
                        kb.stt(yv, xrs[:, cc, :, j:j + LS], cw[:, cc, j:j + 1], yv, ALU.mult, ALU.add,
                               R=[Txrs, Tcw, Ty], W=[Ty], eng=eng)
            kb.act(y[:, :, :n], y[:, :, :n], AF.Silu, R=[Ty], W=[Ty])
            kb.tt(sq[:, :, :n], y[:, 0:8, :n], y[:, 0:8, :n], ALU.mult, R=[Ty], W=[Tsq], eng="pool")
            for cc in range(8):
                p_, Tp_ = pss[cc % 2], Tpss[cc % 2]
                r_, Tr_ = rn[cc % 2], Trn[cc % 2]
                kb.mm(p_[:, :n], blk1[:, :], sq[:, cc, :n], True, True, R=[Tsq, Tblk], W=[Tp_])
                if cc < 4:
                    kb.act(r_[:, :n], p_[:, :n], AF.Sqrt, R=[Tp_], W=[Tr_], scale=64.0, bias=64e-6)
                else:
                    kb.act(r_[:, :n], p_[:, :n], AF.Sqrt, R=[Tp_], W=[Tr_], bias=1e-6)
                kb.op("dve", lambda e, r_=r_: e.reciprocal(out=r_[:, :n], in_=r_[:, :n]), [Tr_], [Tr_])
                kb.tt(y[:, cc, :n], y[:, cc, :n], r_[:, :n], ALU.mult, R=[Ty, Tr_], W=[Ty])
            kb.dma("sp", qn[:, :, tg:tg + n], y[:, :, :n], R=[Ty])
            subs = subtiles(n)
            for si, (r0, nr) in enumerate(subs):
                p_, Tp_ = pst[si % 2], Tpst[si % 2]
                for cc in range(8):
                    kb.tr(p_[:nr, cc * 128:(cc + 1) * 128], y[:, 4 + cc, r0:r0 + nr], c.ident_f[:, :], R=[Ty, c.Tconst], W=[Tp_])
                kb.copy(kvt[:nr, si, :], p_[:nr, :], R=[Tp_], W=[Tkvt], eng=("act" if si % 2 == 0 else "dve"))
                kb.dma("sp", c.kvtok_d[tg + r0:tg + r0 + nr, :], kvt[:nr, si, :], R=[Tkvt])
        kb.barrier()


def phase_gdn_b(kb, c):
    nc, cfg = c.nc, c.cfg
    SEQ, NP, NS = cfg["SEQ"], cfg["NP"], cfg["NS"]
    TP, TS = c.TP, c.TS
    C = 64
    NL = cfg.get("GDN_LANES", 2)
    with contextlib.ExitStack() as ps:
        sb = lambda n, s, d=F32: ps.enter_context(nc.sbuf_tensor("gb_" + n, s, d))
        gc = sb("const", [64, 5, 64])
        nA = sb("nA", [64, 8])
        dtb = sb("dtb", [64, 8])
        normw = sb("normw", [64, 64])
        valid = sb("valid", [64, 1])
        banks = [ps.enter_context(nc.psum_tensor("gb_ps%d" % i, [128, 512], F32)) for i in range(8)]
        Tb = [Tok() for _ in range(8)]
        bstate = {"i": 0}

        def bank():
            i = bstate["i"] % 8
            bstate["i"] += 1
            return banks[i], Tb[i]

        Tg = {n: Tok(n) for n in ("gc", "nA", "dtb", "normw", "valid")}
        U_, ones_, mup, mlo, idn = (gc[:, i, :] for i in range(5))
        kb.dma("sp", gc[:], c.gconst_d, W=[Tg["gc"]])
        kb.dma("sp", nA[:], c.a_log.partition_broadcast(64), W=[Tg["nA"]])
        kb.dma("sp", dtb[:], c.dt_bias.partition_broadcast(64), W=[Tg["dtb"]])
        kb.dma("sp", normw[:], c.norm_w.partition_broadcast(64), W=[Tg["normw"]])
        kb.dma("sp", valid[:], c.valid_d, W=[Tg["valid"]])
        kb.act(nA[:], nA[:], AF.Exp, R=[Tg["nA"]], W=[Tg["nA"]])
        kb.ts(nA[:], nA[:], -1.0, None, ALU.mult, R=[Tg["nA"]], W=[Tg["nA"]])
        bc_h = lambda ap: ap.unsqueeze(2).broadcast_to([64, 8, 64])
        bc_m = lambda ap: ap.unsqueeze(1).broadcast_to([64, 8, 64])
        v3 = lambda ap: ap.rearrange("p (h d) -> p h d", h=8)
        qn_q = c.qkvn_d[0:512, :].rearrange("(h d) t -> d h t", d=64)
        qn_k = c.qkvn_d[512:1024, :].rearrange("(h d) t -> d h t", d=64)

        class Lane:
            pass

        lanes = []
        for li in range(NL):
            L = Lane()
            w3 = lambda n: (sb("l%d_%s" % (li, n), [64, 8, 64]), Tok())
            L.qT, L.Tq = w3("qT")
            L.kT, L.Tk = w3("kT")
            L.kv, L.Tkv = sb("l%d_kv" % li, [64, 1024]), Tok()
            L.zab, L.Tz = sb("l%d_zab" % li, [64, 528]), Tok()
            L.nwz, L.Tnwz = sb("l%d_nwz" % li, [64, 512]), Tok()
            L.og, L.Tog = sb("l%d_og" % li, [64, 512], BF16), Tok()
            L.S = w3("S")
            L.Tsm = Tok()
            L.small = {n: sb("l%d_%s" % (li, n), [64, w]) for n, w in
                       (("g", 8), ("beta", 8), ("nbeta", 8), ("gtmp", 8), ("GG", 16), ("EG", 16), ("E2", 8), ("bg", 8), ("ss", 8), ("lnv", 8))}
            alias = cfg.get("GDN_ALIAS", 0)
            if alias:
                a1, a2, a3, a4, a5 = w3("a1"), w3("a2"), w3("a3"), w3("a4"), w3("a5")
                L.gmat, L.vb, L.oc = a1, a1, a1
                L.tmp1, L.kbg, L.sqo = a2, a2, a2
                L.tmpT, L.u = a3, a3
                L.Dn, L.wT = a4, a4
                L.EGr, L.vn = a5, a5
            else:
                for n_ in ("gmat", "vb", "oc", "tmp1", "kbg", "sqo", "tmpT", "u", "Dn", "wT", "EGr", "vn"):
                    setattr(L, n_, w3(n_))
            L.Dt, L.aqkT, L.qd, L.kd = w3("Dt"), w3("aqkT"), w3("qd"), w3("kd")
            L.X = [w3("X0"), w3("X1")]
            L.Y = [w3("Y0"), w3("Y1")]
            L.Q = [w3("Q0"), w3("Q1")]
            lanes.append(L)

        def chunk(L, sample):
            Tsm = L.Tsm
            g, beta, nbeta, gtmp, GG, EG, E2, bg, ss, lnv = (L.small[n][:, :] for n in
                                                            ("g", "beta", "nbeta", "gtmp", "GG", "EG", "E2", "bg", "ss", "lnv"))
            z = L.zab
            qT, kT = L.qT[:], L.kT[:]
            ktok, vtok = v3(L.kv[:, 0:512]), v3(L.kv[:, 512:1024])
            Rq, Rk, Rkv, Tz_ = L.Tq, L.Tk, L.Tkv, L.Tz
            kb.tt(gtmp, z[:, 512:520], dtb[:, :], ALU.add, R=[Tz_, Tg["dtb"]], W=[Tsm])
            kb.act(gtmp, gtmp, AF.Exp, R=[Tsm], W=[Tsm])
            yield
            kb.act(gtmp, gtmp, AF.Ln, R=[Tsm], W=[Tsm], bias=1.0)
            kb.act(beta, z[:, 520:528], AF.Exp, R=[Tz_], W=[Tsm], scale=-1.0)
            yield
            kb.tt(g, gtmp, nA[:, :], ALU.mult, R=[Tsm, Tg["nA"]], W=[Tsm])
            kb.ts(beta, beta, 1.0, None, ALU.add, R=[Tsm], W=[Tsm])
            kb.op("dve", lambda e: e.reciprocal(out=beta, in_=beta), [Tsm], [Tsm])
            if sample:
                kb.ts(g, g, valid[:, 0:1], None, ALU.mult, R=[Tsm, Tg["valid"]], W=[Tsm])
                kb.ts(beta, beta, valid[:, 0:1], None, ALU.mult, R=[Tsm, Tg["valid"]], W=[Tsm])
            kb.ts(nbeta, beta, -1.0, None, ALU.mult, R=[Tsm], W=[Tsm])
            kb.act(L.nwz[:, :], z[:, 0:512], AF.Exp, R=[Tz_], W=[L.Tnwz], scale=-1.0)
            yield
            kb.ts(L.nwz[:, :], L.nwz[:, :], 1.0, None, ALU.add, R=[L.Tnwz], W=[L.Tnwz], eng="pool")
            kb.op("dve", lambda e: e.reciprocal(out=L.nwz[:, :], in_=L.nwz[:, :]), [L.Tnwz], [L.Tnwz])
            kb.tt(L.nwz[:, :], L.nwz[:, :], z[:, 0:512], ALU.mult, R=[L.Tnwz, Tz_], W=[L.Tnwz], eng="pool")
            kb.tt(v3(L.nwz[:, :]), v3(L.nwz[:, :]), bc_m(normw[:, :]), ALU.mult, R=[L.Tnwz, Tg["normw"]], W=[L.Tnwz], eng="pool")
            pa, Tpa = bank()
            kb.mm(pa[:64, 0:8], U_, g, True, True, R=[Tg["gc"], Tsm], W=[Tpa], sig=False)
            kb.mm(pa[:64, 8:16], ones_, g, True, True, R=[Tg["gc"], Tsm], W=[Tpa])
            gmat, Tgmat = L.gmat
            kb.copy(gmat[:], bc_h(g), R=[Tsm], W=[Tgmat], eng="pool")
            yield
            kb.copy(GG, pa[:64, 0:16], R=[Tpa], W=[Tsm])
            pgr, Tpgr = bank()
            for h in range(8):
                kb.mm(pgr[:64, h * 64:(h + 1) * 64], gmat[:, h, :], U_, True, True, R=[Tgmat, Tg["gc"]], W=[Tpgr], sig=(h == 7))
            pkk, Tpkk = bank()
            for h in range(8):
                kb.mm(pkk[:64, h * 64:(h + 1) * 64], kT[:, h, :], kT[:, h, :], True, True, R=[Rk], W=[Tpkk], sig=(h == 7))
            yield
            kb.act(EG, GG, AF.Exp, R=[Tsm], W=[Tsm])
            kb.tt(E2, GG[:, 8:16], GG[:, 0:8], ALU.subtract, R=[Tsm], W=[Tsm])
            tmp1, Ttmp1 = L.tmp1
            tmpT, TtmpT = L.tmpT
            Dt, TDt = L.Dt
            Dn, TDn = L.Dn
            EGr, TEGr = L.EGr
            kb.tt(tmp1[:], v3(pgr[:64, :]), bc_h(GG[:, 0:8]), ALU.subtract, R=[Tpgr, Tsm], W=[Ttmp1])
            kb.act(EGr[:], v3(pgr[:64, :]), AF.Exp, R=[Tpgr, Ttmp1], W=[TEGr])
            yield
            kb.act(E2, E2, AF.Exp, R=[Tsm], W=[Tsm])
            kb.tt(tmpT[:], tmp1[:], bc_m(mup), ALU.add, R=[Ttmp1, Tg["gc"]], W=[TtmpT], eng="pool")
            yield
            kb.act(Dt[:], tmpT[:], AF.Exp, R=[TtmpT], W=[TDt])
            yield
            kb.tt(tmpT[:], tmp1[:], bc_m(mlo), ALU.add, R=[Ttmp1, Tg["gc"]], W=[TtmpT], eng="pool")
            yield
            kb.act(Dn[:], tmpT[:], AF.Exp, R=[TtmpT], W=[TDn], scale=-1.0)
            yield
            (X0, TX0), (Y0, TY0), (Q0, TQ0) = L.X[0], L.Y[0], L.Q[0]
            kb.tt(X0[:], v3(pkk[:64, :]), Dn[:], ALU.mult, R=[Tpkk, TDn], W=[TX0])
            kb.tt(X0[:], X0[:], bc_h(nbeta), ALU.mult, R=[TX0, Tsm], W=[TX0])
            yield
            pt_, Tpt = bank()
            for h in range(8):
                kb.tr(pt_[:64, h * 64:(h + 1) * 64], X0[:, h, :], idn, R=[TX0, Tg["gc"]], W=[Tpt], sig=(h == 7))
            yield
            kb.copy(Y0[:], v3(pt_[:64, :]), R=[Tpt], W=[TY0], eng="act")
            yield
            kb.tt(Q0[:], Y0[:], bc_m(idn), ALU.add, R=[TY0, Tg["gc"]], W=[TQ0])
            cur = 0
            for k in range(1, 6):
                nx = 1 - cur
                (Xc, TXc), (Yc, TYc), (Qc, TQc) = L.X[cur], L.Y[cur], L.Q[cur]
                (Xn, TXn), (Yn, TYn), (Qn, TQn) = L.X[nx], L.Y[nx], L.Q[nx]
                pX, TpX = bank()
                for h in range(8):
                    kb.mm(pX[:64, h * 64:(h + 1) * 64], Yc[:, h, :], Xc[:, h, :], True, True, R=[TXc, TYc], W=[TpX], sig=(h == 7))
                if k < 5:
                    pY, TpY = bank()
                    for h in range(8):
                        kb.mm(pY[:64, h * 64:(h + 1) * 64], Xc[:, h, :], Yc[:, h, :], True, True, R=[TXc, TYc], W=[TpY], sig=(h == 7))
                yield
                kb.copy(Xn[:], v3(pX[:64, :]), R=[TpX], W=[TXn], eng="act")
                if k < 5:
                    kb.copy(Yn[:], v3(pY[:64, :]), R=[TpY], W=[TYn], eng=("dve" if k % 2 == 0 else "act"))
                yield
                pQ, TpQ = bank()
                for h in range(8):
                    kb.mm(pQ[:64, h * 64:(h + 1) * 64], Xn[:, h, :], Qc[:, h, :], True, True, R=[TXn, TQc], W=[TpQ], sig=(h == 7))
                yield
                kb.tt(Qn[:], Qc[:], v3(pQ[:64, :]), ALU.add, R=[TQc, TpQ], W=[TQn])
                cur = nx
            Q, TQc = L.Q[cur]
            vb, Tvb = L.vb
            kbg, Tkbg = L.kbg
            u, Tu = L.u
            wT, TwT = L.wT
            aqkT, Taqk = L.aqkT
            qd, Tqd = L.qd
            kd, Tkd = L.kd
            vn, Tvn = L.vn
            kb.tt(vb[:], vtok, bc_h(beta), ALU.mult, R=[Rkv, Tsm], W=[Tvb])
            kb.tt(bg, beta, EG[:, 0:8], ALU.mult, R=[Tsm], W=[Tsm])
            kb.tt(qd[:], qT, EGr[:], ALU.mult, R=[Rq, TEGr], W=[Tqd], eng="pool")
            kb.tt(kd[:], ktok, bc_h(E2), ALU.mult, R=[Rkv, Tsm], W=[Tkd], eng="pool")
            yield
            kb.tt(kbg[:], ktok, bc_h(bg), ALU.mult, R=[Rkv, Tsm], W=[Tkbg], eng="pool")
            pu, Tpu = bank()
            for h in range(8):
                kb.mm(pu[:64, h * 64:(h + 1) * 64], Q[:, h, :], vb[:, h, :], True, True, R=[TQc, Tvb], W=[Tpu], sig=(h == 7))
            pq, Tpq = bank()
            for h in range(8):
                kb.mm(pq[:64, h * 64:(h + 1) * 64], kT[:, h, :], qT[:, h, :], True, True, R=[Rk, Rq], W=[Tpq], sig=(h == 7))
            yield
            pw, Tpw = bank()
            for h in range(8):
                kb.mm(pw[:64, h * 64:(h + 1) * 64], kbg[:, h, :], Q[:, h, :], True, True, R=[TQc, Tkbg], W=[Tpw], sig=(h == 7))
            kb.copy(u[:], v3(pu[:64, :]), R=[Tpu], W=[Tu], eng="act")
            kb.tt(aqkT[:], v3(pq[:64, :]), Dt[:], ALU.mult, R=[Tpq, TDt], W=[Taqk])
            yield
            kb.copy(wT[:], v3(pw[:64, :]), R=[Tpw], W=[TwT], eng="act")
            yield
            S, TS_ = L.S, L.TS
            pws, Tpws = bank()
            for h in range(8):
                kb.mm(pws[:64, h * 64:(h + 1) * 64], wT[:, h, :], S[:, h, :], True, True, R=[TwT, TS_], W=[Tpws], sig=(h == 7))
            yield
            kb.tt(vn[:], u[:], v3(pws[:64, :]), ALU.subtract, R=[Tu, Tpws], W=[Tvn])
            yield
            po, Tpo = bank()
            for h in range(8):
                kb.mm(po[:64, h * 64:(h + 1) * 64], qd[:, h, :], S[:, h, :], True, False, R=[Tqd, TS_], W=[Tpo], sig=False)
                kb.mm(po[:64, h * 64:(h + 1) * 64], aqkT[:, h, :], vn[:, h, :], False, True, R=[Taqk, Tvn], W=[Tpo], sig=(h == 7))
            pds, Tpds = bank()
            for h in range(8):
                kb.mm(pds[:64, h * 64:(h + 1) * 64], kd[:, h, :], vn[:, h, :], True, True, R=[Tkd, Tvn], W=[Tpds], sig=(h == 7))
            yield
            kb.tt(S[:], S[:], bc_h(EG[:, 8:16]), ALU.mult, R=[TS_, Tsm], W=[TS_])
            oc, Toc = L.oc
            kb.copy(oc[:], v3(po[:64, :]), R=[Tpo], W=[Toc], eng="act")
            yield
            kb.tt(S[:], S[:], v3(pds[:64, :]), ALU.add, R=[TS_, Tpds], W=[TS_])
            sqo, Tsqo = L.sqo
            kb.tt(sqo[:], oc[:], oc[:], ALU.mult, R=[Toc], W=[Tsqo], eng="pool")
            yield
            kb.op("dve", lambda e: e.tensor_reduce(out=ss, in_=sqo[:], axis=AX.X, op=ALU.add), [Tsqo], [Tsm])
            yield
            kb.act(lnv, ss, AF.Ln, R=[Tsm], W=[Tsm], scale=1.0 / 64.0, bias=1e-6)
            yield
            kb.act(lnv, lnv, AF.Exp, R=[Tsm], W=[Tsm], scale=-0.5)
            yield
            kb.tt(oc[:], oc[:], bc_h(lnv), ALU.mult, R=[Toc, Tsm], W=[Toc])
            yield
            kb.tt(v3(L.og[:, :]), oc[:], v3(L.nwz[:, :]), ALU.mult, R=[Toc, L.Tnwz], W=[L.Tog])

        def run_lockstep(gens):
            gens = list(gens)
            while gens:
                nxt = []
                for g_ in gens:
                    try:
                        next(g_)
                        nxt.append(g_)
                    except StopIteration:
                        pass
                gens = nxt

        def seq_gen(L, kind, idx):
            if kind == "p":
                kb.memset(L.S[0][:], 0.0, W=[L.S[1]]) if False else None
            return

        def lane_prompt(L, s):
            S, TS_ = L.S, L.TS
            kb.memset(S[:], 0.0, W=[TS_])
            for ci in range(SEQ // C):
                tg = s * SEQ + ci * C
                kb.dma("sp", L.qT[:], qn_q[:, :, tg:tg + C], W=[L.Tq])
                kb.dma("sp", L.kT[:], qn_k[:, :, tg:tg + C], W=[L.Tk])
                kb.dma("sp", L.kv[:, :], c.kvtok_d[tg:tg + C, :], W=[L.Tkv])
                kb.dma("sp", L.zab[:, :], c.zab_d[tg:tg + C, :], W=[L.Tz])
                yield
                yield from chunk(L, False)
                kb.dma("sp", c.og_d[tg:tg + C, :], L.og[:, :], R=[L.Tog])
                yield
            kb.dma("sp", c.p_gdn[s].rearrange("h k v -> k h v"), S[:], R=[TS_])

        def lane_sample(L, b):
            S, TS_ = L.S, L.TS
            tg = TP + b * LS
            kb.dma("sp", S[:], c.state_gdn[b].rearrange("h k v -> k h v"), W=[TS_])
            kb.dma("sp", L.qT[:, :, 0:LS], qn_q[:, :, tg:tg + LS], W=[L.Tq])
            kb.dma("sp", L.kT[:, :, 0:LS], qn_k[:, :, tg:tg + LS], W=[L.Tk])
            kb.dma("sp", L.kv[0:LS, :], c.kvtok_d[tg:tg + LS, :], W=[L.Tkv])
            kb.dma("sp", L.zab[0:LS, :], c.zab_d[tg:tg + LS, :], W=[L.Tz])
            yield
            yield from chunk(L, True)
            kb.dma("sp", c.og_d[tg:tg + LS, :], L.og[0:LS, :], R=[L.Tog])
            kb.dma("sp", c.s_gdn[b].rearrange("h k v -> k h v"), S[:], R=[TS_])

        for L in lanes:
            L.S, L.TS = L.S
        for L in lanes:
            pass
        for s0 in range(0, NP if not cfg.get("GDN_SKIP_P") else 0, NL):
            run_lockstep([lane_prompt(lanes[i], s0 + i) for i in range(min(NL, NP - s0))])
        for L in lanes:
            kb.memset(L.qT[:], 0.0, W=[L.Tq])
            kb.memset(L.kT[:], 0.0, W=[L.Tk])
            kb.memset(L.kv[:], 0.0, W=[L.Tkv])
            kb.memset(L.zab[:], 0.0, W=[L.Tz])
        for b0 in range(0, NS if not cfg.get("GDN_SKIP_S") else 0, NL):
            run_lockstep([lane_sample(lanes[i], b0 + i) for i in range(min(NL, NS - b0))])
        kb.barrier()


def phase_mix(kb, c):
    nc, cfg = c.nc, c.cfg
    G = 512
    TP, TS = c.TP, c.TS
    with contextlib.ExitStack() as ps:
        sb = lambda n, s, d=F32: ps.enter_context(nc.sbuf_tensor("mx_" + n, s, d))
        wbg = sb("wbg", [128, 4, D], BF16)
        wbn = sb("wbn", [128, 4, D], BF16)
        wout = sb("wout", [128, 8, D], BF16)
        ogT = sb("ogT", [128, 4, G], BF16)
        onT = sb("onT", [128, 4, G], BF16)
        mg = sb("mg", [128, 16, G], BF16)
        mT = sb("mT", [128, 8, G], BF16)
        tk = [sb("tk%d" % i, [128, 1024], BF16) for i in range(2)]
        t1 = [sb("t1_%d" % i, [128, G]) for i in range(2)]
        t2 = [sb("t2_%d" % i, [128, G]) for i in range(2)]
        xin = [sb("xin%d" % i, [128, D]) for i in range(2)]
        wk = [sb("wk%d" % i, [128, D]) for i in range(2)]
        rows = {n: sb("row_" + n, [128, D]) for n in ("g", "lng", "lnb")}
        st = sb("st", [128, 2, nc.vector.BN_STATS_DIM])
        mv = sb("mv", [128, nc.vector.BN_AGGR_DIM])
        rstd = sb("rstd", [128, 1])
        pstr = [ps.enter_context(nc.psum_tensor("mx_pstr%d" % i, [128, 1024], BF16)) for i in range(2)]
        psb = [ps.enter_context(nc.psum_tensor("mx_psb%d" % i, [128, 512], F32)) for i in range(4)]
        psy = ps.enter_context(nc.psum_tensor("mx_psy", [128, 1024], F32))
        Tw, TogT, TonT, Tmg, TmT, Tln, Tmodrows, Tst, Tpsy = (Tok() for _ in range(9))
        Ttk, Tt1, Tt2, Txin, Twk, Tpstr = ([Tok(), Tok()] for _ in range(6))
        Tpsb = [Tok() for _ in range(4)]
        kb.dma("pool", wbg[:], c.w_br_gdn.rearrange("(k p) d -> p k d", p=128), W=[Tw])
        kb.dma("pool", wbn[:], c.w_br_nsa.rearrange("(k p) d -> p k d", p=128), W=[Tw])
        kb.dma("pool", wout[:], c.w_out.rearrange("(k p) d -> p k d", p=128), W=[Tw])
        kb.dma("sp", rows["lng"][:], c.ln_g[1:2, :].partition_broadcast(128), W=[Tln])
        kb.dma("sp", rows["lnb"][:], c.ln_b[1:2, :].partition_broadcast(128), W=[Tln])
        mgv = c.mgT_d.rearrange("(c p) t -> p c t", p=128)
        cur = None
        slot = 0
        for (kind, seq, t0, n) in token_groups(cfg, G):
            tg = t0 if kind == "p" else TP + t0
            if (kind, seq) != cur:
                cur = (kind, seq)
                load_rows(kb, c, rows["g"], Tmodrows, kind, seq, 5 * D)
            subs = subtiles(n)
            kb.dma("sp", mg[:, :, :n], mgv[:, :, tg:tg + n], W=[Tmg])
            for si, (r0, nr) in enumerate(subs):
                for which, (src, dstT, Td) in enumerate(((c.og_d, ogT, TogT), (c.on_d, onT, TonT))):
                    i2 = (2 * si + which) % 2
                    kb.dma("sp", tk[i2][:nr, 0:512], src[tg + r0:tg + r0 + nr, :], W=[Ttk[i2]])
                    for k in range(4):
                        kb.tr(pstr[i2][:, k * 128:k * 128 + nr], tk[i2][:nr, k * 128:(k + 1) * 128], c.ident_b[:nr, :nr],
                              R=[Ttk[i2], c.Tconst], W=[Tpstr[i2]])
                    kb.copy(dstT[:, :, r0:r0 + nr], pstr[i2][:, 0:512].rearrange("p (k n) -> p k n", k=4)[:, :, :nr],
                            R=[Tpstr[i2]], W=[Td], eng=("act" if which == 0 else "dve"))
            for dc in range(8):
                pg, Tpg = psb[(dc % 2) * 2], Tpsb[(dc % 2) * 2]
                pn, Tpn = psb[(dc % 2) * 2 + 1], Tpsb[(dc % 2) * 2 + 1]
                for k in range(4):
                    kb.mm(pg[:, :n], wbg[:, k, dc * 128:(dc + 1) * 128], ogT[:, k, :n], k == 0, k == 3, R=[Tw, TogT], W=[Tpg])
                for k in range(4):
                    kb.mm(pn[:, :n], wbn[:, k, dc * 128:(dc + 1) * 128], onT[:, k, :n], k == 0, k == 3, R=[Tw, TonT], W=[Tpn])
                a, Ta = t1[dc % 2], Tt1[dc % 2]
                b, Tb_ = t2[dc % 2], Tt2[dc % 2]
                kb.tt(a[:, :n], pg[:, :n], mg[:, dc, :n], ALU.mult, R=[Tpg, Tmg], W=[Ta])
                kb.tt(b[:, :n], pn[:, :n], mg[:, 8 + dc, :n], ALU.mult, R=[Tpn, Tmg], W=[Tb_])
                kb.tt(mT[:, dc, :n], a[:, :n], b[:, :n], ALU.add, R=[Ta, Tb_], W=[TmT], eng="pool")
            for si, (r0, nr) in enumerate(subs):
                xt, Tx = xin[slot], Txin[slot]
                w, Tw_ = wk[slot], Twk[slot]
                slot = 1 - slot
                kb.dma("sp", xt[:nr, :], c.x1[kind][t0 + r0:t0 + r0 + nr, :], W=[Tx])
                for half in range(2):
                    for k in range(8):
                        kb.mm(psy[:nr, half * 512:(half + 1) * 512], mT[:, k, r0:r0 + nr], wout[:, k, half * 512:(half + 1) * 512],
                              k == 0, k == 7, R=[TmT, Tw], W=[Tpsy])
                kb.tt(w[:nr, :], psy[:nr, :], rows["g"][:nr, :], ALU.mult, R=[Tpsy, Tmodrows], W=[Tw_])
                kb.stt(w[:nr, :], xt[:nr, :], ALPHA, w[:nr, :], ALU.mult, ALU.add, R=[Tx, Tw_], W=[Tw_])
                layer_norm_rows(kb, c, w, nr, Tw_, rows["lng"], rows["lnb"], Tln, st, mv, rstd, Tst)
                kb.dma("sp", c.x2[kind][t0 + r0:t0 + r0 + nr, :], w[:nr, :], R=[Tw_])
        kb.barrier()


NEGM = -30000.0


def nsa_consts(SEQ):
    t = np.arange(SEQ)
    nsel = SEQ // 64
    j = np.arange(nsel)
    valid = (j[None, :] * 64) <= t[:, None]
    cur = (t // 64)[:, None]
    forced = (j[None, :] == 0) | (j[None, :] == cur) | (j[None, :] == cur - 1)
    M1 = (valid & ~forced).astype(np.float32)
    M2 = np.where(valid, np.where(forced, 1e4 + j[None, :], 0.0), -1.0).astype(np.float32)
    ncmp = SEQ // 32
    n = np.arange(ncmp)
    cmask = np.where(((n[:, None] + 1) * 32 - 1) <= t[None, :], 0.0, NEGM).astype(np.float32)
    F = (np.arange(SEQ)[None, :] // 64 == j[:, None]).astype(np.float32)
    sk = np.arange(128)[:, None, None]
    a = np.arange(8)[None, :, None]
    tq = np.arange(512)[None, None, :]
    diff = tq - 128 * (a - 4) - sk
    Wm = np.where((diff >= 0) & (diff < 512), 0.0, NEGM).astype(np.float32)
    pair = (n[:, None] // 2 == j[None, :]).astype(np.float32)
    r = np.arange(128)
    maskW = np.zeros((128, 124), np.float32)
    maskW[r, 60 + r // 32] = 1.0
    return dict(nsa_M1=M1, nsa_M2=M2, nsa_cmask=cmask, nsa_F=F, nsa_Wm=Wm, nsa_pair=pair, nsa_maskW=maskW)


def phase_nsa_prompt(kb, c):
    nc, cfg = c.nc, c.cfg
    SEQ, NP = cfg["SEQ"], cfg["NP"]
    NKT = SEQ // 128
    NQG = SEQ // 512
    NSEL = SEQ // 64
    NCMP = SEQ // 32
    with contextlib.ExitStack() as ps:
        sb = lambda n, s, d=F32: ps.enter_context(nc.sbuf_tensor("np_" + n, s, d))
        qTh = sb("qTh", [64, 8, SEQ], BF16)
        KT = sb("KT", [64, 3, 2, SEQ], BF16)
        Vx = sb("Vx", [128, NKT, 3, 2, 65], BF16)
        kvc = sb("kvc", [128, NKT, 256], BF16)
        gn = sb("gn", [128, NKT, 24])
        M1 = sb("M1", [128, NKT, NSEL])
        M2 = sb("M2", [128, NKT, NSEL])
        cmask = sb("cmask", [NCMP, SEQ], BF16)
        Fm = sb("F", [NSEL, SEQ], BF16)
        Wm = sb("Wm", [128, 8, 512], BF16)
        pair = sb("pair", [NCMP, NSEL], BF16)
        maskW = sb("maskW", [128, 124])
        wcol = sb("wcol", [128, 2, 2])
        Wbig = sb("Wbig", [128, 4, 124], BF16)
        ON = sb("ON", [128, NKT, 512])
        imp = sb("imp", [128, NKT, 2, NSEL])
        selT = sb("selT", [NSEL, 2, SEQ], BF16)
        kvb_sb = sb("kvb", [NCMP, 256], BF16)
        KcT = sb("KcT", [64, 2, NCMP], BF16)
        Vc1P = sb("Vc1P", [NCMP, 2, 65 + NSEL], BF16)
        Pt = [sb("Pt%d" % i, [128, 512], BF16) for i in range(3)]
        sc = sb("sc", [128, NSEL])
        wk_ = sb("wk", [128, NSEL])
        m8a, m8b = sb("m8a", [128, 8]), sb("m8b", [128, 8])
        thr = sb("thr", [128, 1])
        selm = sb("selm", [128, NSEL])
        okm = sb("okm", [128, NSEL])
        rr = [sb("rr%d" % i, [128, 1]) for i in range(4)]
        rg = [sb("rg%d" % i, [128, 1]) for i in range(4)]
        zb = sb("zb", [128, 512], BF16)
        pss = [ps.enter_context(nc.psum_tensor("np_pss%d" % i, [128, 512], F32)) for i in range(2)]
        paccs = [[ps.enter_context(nc.psum_tensor("np_pacc%d_%d" % (j, i), [128, 512], F32)) for i in range(3)] for j in range(2)]
        pmisc = paccs[1][0]
        pmb = paccs[1][1]
        kvb_f = sb("kvb_f", [NCMP, 128])
        rr16 = [sb("rr16_%d" % i, [128, 16]) for i in range(2)]
        rg16 = [sb("rg16_%d" % i, [128, 16]) for i in range(2)]
        Trr16 = [Tok(), Tok()]
        TON = [[Tok() for _ in range(8)] for _ in range(NKT)]
        names = "q K V kvc gn M cm F Wm pair maskW wcol Wbig ON imp selT kvb KcT Vc1P sc m8 sel pm pmb zb".split()
        T = {n: Tok(n) for n in names}
        kb.memset(zb[:], 0.0, W=[T["zb"]])
        TPt, Tpss = [Tok() for _ in range(3)], [Tok() for _ in range(2)]
        Tpaccs = [[Tok() for _ in range(3)] for _ in range(2)]
        T["pm"] = Tpaccs[1][0]
        T["pmb"] = Tpaccs[1][1]
        Trr = [Tok() for _ in range(4)]
        kb.dma("sp", M1[:], c.nsa_M1.rearrange("(t p) j -> p t j", p=128), W=[T["M"]])
        kb.dma("sp", M2[:], c.nsa_M2.rearrange("(t p) j -> p t j", p=128), W=[T["M"]])
        kb.dma("pool", cmask[:], c.nsa_cmask, W=[T["cm"]])
        kb.dma("pool", Fm[:], c.nsa_F, W=[T["F"]])
        kb.dma("pool", Wm[:], c.nsa_Wm, W=[T["Wm"]])
        kb.dma("pool", pair[:], c.nsa_pair, W=[T["pair"]])
        kb.dma("sp", maskW[:], c.nsa_maskW, W=[T["maskW"]])
        wsrc = c.w_cmp.rearrange("s j h -> j s h")
        for q4 in range(4):
            kb.dma("sp", wcol[32 * q4:32 * q4 + 32, :, :], wsrc, W=[T["wcol"]])
        for s_ in range(2):
            for hk in range(2):
                kb.ts(Wbig[:, s_ * 2 + hk, :], maskW[:, :], wcol[:, s_, hk:hk + 1], None, ALU.mult,
                      R=[T["maskW"], T["wcol"]], W=[T["Wbig"]])
        kb.memset(Vx[:, :, :, :, 64:65], 1.0, W=[T["V"]])
        kb.memset(Vc1P[:, :, 64:65], 1.0, W=[T["Vc1P"]])
        for hk in range(2):
            kb.copy(Vc1P[:, hk, 65:65 + NSEL], pair[:, :], R=[T["pair"]], W=[T["Vc1P"]])
        st = {"ps": 0, "pt": 0, "rr": 0, "aset": 0}

        def score_bank():
            i = st["ps"] % 2
            st["ps"] += 1
            return pss[i], Tpss[i]

        def acc_ap(a, aset=0):
            return paccs[aset][a // 7][:, (a % 7) * 65:(a % 7) * 65 + 65], Tpaccs[aset][a // 7]

        def finish_group(aset, qg, hk, br):
            i = st["rr"] % 2
            st["rr"] += 1
            r16, g16, Tr = rr16[i], rg16[i], Trr16[i]
            for b3 in range(3):
                na = min(7, 16 - 7 * b3)
                den = paccs[aset][b3][:, 0:na * 65].rearrange("p (a c) -> p a c", c=65)[:, :, 64]
                kb.ts(r16[:, 7 * b3:7 * b3 + na], den, 1e-30, None, ALU.max, R=[Tpaccs[aset][b3]], W=[Tr])
            kb.op("dve", lambda e: e.reciprocal(out=r16[:, :], in_=r16[:, :]), [Tr], [Tr])
            gview = gn[:, qg * 4:(qg + 1) * 4, br * 8 + hk * 4:br * 8 + hk * 4 + 4].rearrange("p s g -> p g s")
            kb.tt(g16[:, :].rearrange("p (g s) -> p g s", g=4), r16[:, :].rearrange("p (g s) -> p g s", g=4), gview, ALU.mult,
                  R=[Tr, T["gn"]], W=[Tr])
            for g in range(4):
                for sub in range(4):
                    a = g * 4 + sub
                    ap_, Ta_ = acc_ap(a, aset)
                    qt, head = qg * 4 + sub, hk * 4 + g
                    o_ = ON[:, qt, head * 64:(head + 1) * 64]
                    kb.stt(o_, ap_[:, 0:64], g16[:, a:a + 1], o_, ALU.mult, ALU.add, R=[Ta_, Tr, TON[qt][head]], W=[TON[qt][head]])

        def finish_acc(ap_, Tp_, qt, head, br, first, with_imp=None):
            i = st["rr"] % 4
            st["rr"] += 1
            r_, g_, Tr_ = rr[i], rg[i], Trr[i]
            kb.ts(r_[:, :], ap_[:, 64:65], 1e-30, None, ALU.max, R=[Tp_], W=[Tr_])
            kb.op("dve", lambda e: e.reciprocal(out=r_[:, :], in_=r_[:, :]), [Tr_], [Tr_])
            kb.tt(g_[:, :], r_[:, :], gn[:, qt, br * 8 + head:br * 8 + head + 1], ALU.mult, R=[Tr_, T["gn"]], W=[Tr_])
            o_ = ON[:, qt, head * 64:(head + 1) * 64]
            if first:
                kb.ts(o_, ap_[:, 0:64], g_[:, 0:1], None, ALU.mult, R=[Tp_, Tr_], W=[TON[qt][head]])
            else:
                kb.stt(o_, ap_[:, 0:64], g_[:, 0:1], o_, ALU.mult, ALU.add, R=[Tp_, Tr_, TON[qt][head]], W=[TON[qt][head]])
            return r_, Tr_

        for s in range(NP):
            tg = s * SEQ
            kb.dma("sp", qTh[:, :, :], c.qT_d.rearrange("(h d) t -> d h t", d=64)[:, :, tg:tg + SEQ], W=[T["q"]])
            for i3 in range(3):
                kb.dma("sp", KT[:, i3, :, :], c.kT_d[i3].rearrange("(h d) t -> d h t", d=64)[:, :, tg:tg + SEQ], W=[T["K"]])
            rowsv = c.kvg_d[tg:tg + SEQ, :].rearrange("(t p) f -> p t f", p=128)
            for br in range(3):
                for hk in range(2):
                    c0 = br * 256 + 128 + hk * 64
                    kb.dma("pool", Vx[:, :, br, hk, 0:64], rowsv[:, :, c0:c0 + 64], W=[T["V"]])
            kb.dma("pool", kvc[:, :, :], rowsv[:, :, 0:256], W=[T["kvc"]])
            kb.dma("sp", gn[:, :, :], rowsv[:, :, 768:792], W=[T["gn"]])
            kb.act(gn[:, :, :], gn[:, :, :], AF.Exp, R=[T["gn"]], W=[T["gn"]], scale=-1.0)
            kb.ts(gn[:, :, :], gn[:, :, :], 1.0, None, ALU.add, R=[T["gn"]], W=[T["gn"]])
            kb.op("dve", lambda e: e.reciprocal(out=gn[:, :, :], in_=gn[:, :, :]), [T["gn"]], [T["gn"]])
            for cc in range(4):
                for kt in range(NKT):
                    kb.mm(pmisc[:NCMP, cc * 64:(cc + 1) * 64], Wbig[:, cc, 60 - 4 * kt:60 - 4 * kt + NCMP],
                          kvc[:, kt, cc * 64:(cc + 1) * 64], kt == 0, kt == NKT - 1, R=[T["Wbig"], T["kvc"]], W=[T["pm"]],
                          sig=(kt == NKT - 1 and cc == 3))
            kb.copy(kvb_sb[:, :], pmisc[:NCMP, 0:256], R=[T["pm"]], W=[T["kvb"]])
            kb.copy(kvb_f[:, :], pmisc[:NCMP, 0:128], R=[T["pm"]], W=[T["kvb"]], eng="act")
            for hk in range(2):
                kb.tr(pmb[:64, hk * NCMP:(hk + 1) * NCMP], kvb_f[:, hk * 64:(hk + 1) * 64], c.ident_f[:NCMP, :NCMP],
                      R=[T["kvb"], c.Tconst], W=[T["pmb"]])
                kb.copy(Vc1P[:, hk, 0:64], kvb_sb[:, 128 + hk * 64:128 + (hk + 1) * 64], R=[T["kvb"]], W=[T["Vc1P"]], eng="pool")
            kb.copy(KcT[:, :, :], pmb[:64, 0:2 * NCMP].rearrange("p (h n) -> p h n", h=2), R=[T["pmb"]], W=[T["KcT"]])
            for qg in range(NQG):
                for hk in range(2):
                    for g in range(4):
                        head = hk * 4 + g
                        p_, Tp_ = score_bank()
                        kb.mm(p_[:NCMP, :], KcT[:, hk, :], qTh[:, head, qg * 512:(qg + 1) * 512], True, False,
                              R=[T["KcT"], T["q"]], W=[Tp_], sig=False)
                        kb.mm(p_[:NCMP, :], c.ident_b[:NCMP, :NCMP], cmask[:, qg * 512:(qg + 1) * 512], False, True,
                              R=[c.Tconst, T["cm"]], W=[Tp_])
                        pt_, Tpt_ = Pt[st["pt"] % 3], TPt[st["pt"] % 3]
                        st["pt"] += 1
                        kb.act(pt_[:NCMP, :], p_[:NCMP, :], AF.Exp, R=[Tp_], W=[Tpt_], scale=0.125)
                        for sub in range(4):
                            qt = qg * 4 + sub
                            kb.mm(pmisc[:, 256 + 0:256 + 65 + NSEL], pt_[:NCMP, sub * 128:(sub + 1) * 128], Vc1P[:, hk, :], True, True,
                                  R=[Tpt_, T["Vc1P"]], W=[T["pm"]])
                            ap_ = pmisc[:, 256:256 + 65 + NSEL]
                            r_, Tr_ = finish_acc(ap_, T["pm"], qt, head, 0, True)
                            if g == 0:
                                kb.ts(imp[:, qt, hk, :], ap_[:, 65:65 + NSEL], r_[:, 0:1], None, ALU.mult, R=[T["pm"], Tr_], W=[T["imp"]])
                            else:
                                kb.stt(imp[:, qt, hk, :], ap_[:, 65:65 + NSEL], r_[:, 0:1], imp[:, qt, hk, :], ALU.mult, ALU.add,
                                       R=[T["pm"], Tr_, T["imp"]], W=[T["imp"]])
            for qt in range(NKT):
                for hk in range(2):
                    kb.tt(sc[:, :], imp[:, qt, hk, :], M1[:, qt, :], ALU.mult, R=[T["imp"], T["M"]], W=[T["sc"]])
                    kb.tt(sc[:, :], sc[:, :], M2[:, qt, :], ALU.add, R=[T["sc"], T["M"]], W=[T["sc"]])
                    kb.op("dve", lambda e: e.max(out=m8a[:, :], in_=sc[:, :]), [T["sc"]], [T["m8"]])
                    kb.op("dve", lambda e: e.match_replace(out=wk_[:, :], in_to_replace=m8a[:, :], in_values=sc[:, :], imm_value=-2.0),
                          [T["sc"], T["m8"]], [T["sel"]])
                    kb.op("dve", lambda e: e.max(out=m8b[:, :], in_=wk_[:, :]), [T["sel"]], [T["m8"]])
                    kb.op("dve", lambda e: e.tensor_reduce(out=thr[:, :], in_=m8b[:, :], axis=AX.X, op=ALU.min), [T["m8"]], [T["m8"]])
                    kb.ts(selm[:, :], sc[:, :], thr[:, 0:1], None, ALU.is_ge, R=[T["sc"], T["m8"]], W=[T["sel"]])
                    kb.ts(okm[:, :], sc[:, :], 0.0, None, ALU.is_ge, R=[T["sc"]], W=[T["sel"]])
                    kb.tt(selm[:, :], selm[:, :], okm[:, :], ALU.mult, R=[T["sel"]], W=[T["sel"]])
                    kb.ts(selm[:, :], selm[:, :], -1.0, -NEGM, ALU.add, ALU.mult, R=[T["sel"]], W=[T["sel"]])
                    kb.tr(pmisc[:NSEL, 0:128], selm[:, :], c.ident_f[:, :], R=[T["sel"], c.Tconst], W=[T["pm"]])
                    kb.copy(selT[:, hk, qt * 128:(qt + 1) * 128], pmisc[:NSEL, 0:128], R=[T["pm"]], W=[T["selT"]], eng="act")
            for br in (1, 2):
                for hk in range(2):
                    for qg in range(NQG):
                        kts = list(range(0, 4 * qg + 4)) if br == 1 else list(range(max(0, 4 * qg - 4), 4 * qg + 4))
                        aset = st["aset"] % 2
                        st["aset"] += 1
                        for b3 in range(3):
                            kb.mm(paccs[aset][b3][:, :], zb[:, 0:128], zb[:, :], True, False, R=[T["zb"]], W=[Tpaccs[aset][b3]], sig=False)
                        for ki, kt in enumerate(kts):
                            for g in range(4):
                                head = hk * 4 + g
                                p_, Tp_ = score_bank()
                                diag = kt >= 4 * qg
                                need_w = diag or br == 2
                                kb.mm(p_[:, :], KT[:, br, hk, kt * 128:(kt + 1) * 128], qTh[:, head, qg * 512:(qg + 1) * 512],
                                      True, not (need_w or br == 1), R=[T["K"], T["q"]], W=[Tp_], sig=False)
                                if br == 1:
                                    kb.mm(p_[:, :], Fm[:, kt * 128:(kt + 1) * 128], selT[:, hk, qg * 512:(qg + 1) * 512],
                                          False, not need_w, R=[T["F"], T["selT"]], W=[Tp_], sig=(not need_w))
                                if need_w:
                                    kb.mm(p_[:, :], c.ident_b[:, :], Wm[:, kt - 4 * qg + 4, :], False, True,
                                          R=[c.Tconst, T["Wm"]], W=[Tp_])
                                pt_, Tpt_ = Pt[st["pt"] % 3], TPt[st["pt"] % 3]
                                st["pt"] += 1
                                kb.act(pt_[:, :], p_[:, :], AF.Exp, R=[Tp_], W=[Tpt_], scale=0.125)
                                a_off = kt - 4 * qg
                                subs_ok = [sb_ for sb_ in range(4) if (sb_ >= a_off if a_off >= 0 else (br == 1 or sb_ <= a_off + 4))]
                                for sub in subs_ok:
                                    ap_, Ta_ = acc_ap(g * 4 + sub, aset)
                                    last = (ki == len(kts) - 1 and g == 3 and sub == subs_ok[-1])
                                    kb.mm(ap_, pt_[:, sub * 128:(sub + 1) * 128], Vx[:, kt, br, hk, :], False, True,
                                          R=[Tpt_, T["V"]], W=[Ta_], sig=last)
                        finish_group(aset, qg, hk, br)
            kb.dma("pool", c.on_d[tg:tg + SEQ, :].rearrange("(t p) f -> p t f", p=128), ON[:, :, :],
                   R=[TON[a_][b_] for a_ in range(NKT) for b_ in range(8)])
        kb.barrier()


def nsa_s_consts(NPAGES):
    nblk = NPAGES * 2
    ncmp = NPAGES * 4
    nsel = nblk + 1
    j = np.arange(nsel)
    forced = (j == 0) | (j == nblk) | (j == nblk - 1)
    M1 = np.tile((~forced).astype(np.float32)[None, :], (LS, 1))
    M2 = np.tile(np.where(forced, 1e4 + j, 0.0).astype(np.float32)[None, :], (LS, 1))
    nh = ncmp // 128
    nl = np.arange(128)
    pairs = np.zeros((128, nh, nblk), np.float32)
    for h in range(nh):
        pairs[nl, h, 64 * h + nl // 2] = 1.0
    Fs = (np.arange(NPAGES * 128)[None, :] // 64 == np.arange(nblk)[:, None]).astype(np.float32)
    col_l = np.tile(np.arange(LS), 8)[None, :]
    caus = np.where(np.arange(LS)[:, None] <= col_l, 0.0, NEGM).astype(np.float32)
    wm0 = np.where(np.arange(128)[:, None] <= col_l, NEGM, 0.0).astype(np.float32)
    gsum = (np.arange(16)[:, None] % LS == np.arange(LS)[None, :]).astype(np.float32)
    qcol = (np.arange(128) % 4).astype(np.float32).reshape(128, 1)
    pcol = np.arange(128, dtype=np.float32).reshape(128, 1)
    jrow = np.arange(32, dtype=np.float32).reshape(1, 32)
    maskWs = np.zeros((128, 252), np.float32)
    maskWs[np.arange(128), 124 + np.arange(128) // 32] = 1.0
    return dict(ns_maskWs=maskWs, ns_jrow=jrow, ns_M1=M1, ns_M2=M2, ns_pairs=pairs, ns_Fs=Fs, ns_caus=caus, ns_wm0=wm0, ns_gsum=gsum, ns_qcol=qcol, ns_pcol=pcol)


def phase_nsa_sample(kb, c):
    nc, cfg = c.nc, c.cfg
    NS, NPAGES, NPHYS, WL = cfg["NS"], cfg["NPAGES"], cfg["NPHYS"], cfg["WIN"]
    TP = c.TP
    NBLK = NPAGES * 2
    NCMP = NPAGES * 4
    NH = NCMP // 128
    NSELS = NBLK + 1
    assert NBLK <= 128 and NCMP % 128 == 0
    with contextlib.ExitStack() as ps:
        sb = lambda n, s, d=F32: ps.enter_context(nc.sbuf_tensor("nss_" + n, s, d))
        ptT = sb("ptT", [128, NS, NH], I32)
        ptb = sb("ptb", [128, NS * NPAGES], I32)
        idxq_f = sb("idxq_f", [128, NS, NH])
        idxr_f = sb("idxr_f", [128, NS * NPAGES])
        idxq = sb("idxq", [128, NS, NH], I32)
        idxr = sb("idxr", [128, NS * NPAGES], I32)
        qcol, pcol = sb("qcol", [128, 1]), sb("pcol", [128, 1])
        jrow = sb("jrow", [128, 32])
        idxq32_f = sb("idxq32_f", [128, NS * NH, 32])
        idxq32 = sb("idxq32", [128, NS * NH * 32], I32)
        wflat = sb("wflat", [128, 128])
        M1, M2 = sb("M1", [LS, NSELS]), sb("M2", [LS, NSELS])
        pairs = sb("pairs", [128, NH, NBLK], BF16)
        Fs = sb("Fs", [NBLK, NPAGES * 128], BF16)
        caus = sb("caus", [LS, 32], BF16)
        wm0 = sb("wm0", [128, 32], BF16)
        gsum = sb("gsum", [16, LS])
        zb = sb("zb", [128, 512], BF16)
        cq = [sb("cq%d" % i, [128, 32, 256]) for i in range(2)]
        kvbs = sb("kvbs", [128, NH, 256])
        kvbK = sb("kvbK", [128, NH, 128], BF16)
        KcTs = sb("KcTs", [64, 2, NCMP], BF16)
        VcS = sb("VcS", [128, NH, 2, 65 + NBLK], BF16)
        qs = sb("qs", [64, 8, LS], BF16)
        KnT = sb("KnT", [64, 3, 2, LS], BF16)
        Vn = sb("Vn", [LS, 3, 2, 2, 65], BF16)
        gns = sb("gns", [16, 3, 2])
        Es = sb("Es", [128, NH, 2, 16], BF16)
        impg = sb("impg", [16, NBLK])
        sc = sb("sc", [LS, NSELS])
        wk_ = sb("wk", [LS, NSELS])
        m8a, m8b, thr = sb("m8a", [LS, 8]), sb("m8b", [LS, 8]), sb("thr", [LS, 1])
        selm = sb("selm", [LS, NBLK])
        selx = sb("selx", [NBLK, 2, 4, LS], BF16)
        pg = [sb("pg%d" % i, [128, 256]) for i in range(8)]
        pgb = [sb("pgb%d" % i, [128, 2, 2, 65], BF16) for i in range(3)]
        KpT = [sb("KpT%d" % i, [64, 2, 128], BF16) for i in range(2)]
        Pp = [sb("Pp%d" % i, [128, 32], BF16) for i in range(2)]
        ONs = sb("ONs", [16, 2, 64])
        ONb = sb("ONb", [16, 2, 64], BF16)
        rr, rg = sb("rr", [16, 1]), sb("rg", [16, 1])
        pS = ps.enter_context(nc.psum_tensor("ns_pS", [128, 512], F32))
        pO = ps.enter_context(nc.psum_tensor("ns_pO", [128, 512], F32))
        pI = ps.enter_context(nc.psum_tensor("ns_pI", [128, 512], F32))
        pT = [ps.enter_context(nc.psum_tensor("ns_pT%d" % i, [128, 1024], BF16)) for i in range(2)]
        pS2 = [ps.enter_context(nc.psum_tensor("ns_pS2%d" % i, [128, 512], F32)) for i in range(2)]
        pA = ps.enter_context(nc.psum_tensor("ns_pA", [128, 512], F32))
        names = ("pt idx const w cq kvbs kvbK KcTs VcS qs KnT Vn gns Es impg sc m8 sel selx ONs rr pS pO pI pA zb").split()
        T = {n: Tok(n) for n in names}
        Tpg, Tpgb = [Tok() for _ in range(8)], [Tok() for _ in range(3)]
        Tcqs = [Tok(), Tok()]
        TKpT, TPp, TpT, TpS2 = ([Tok(), Tok()] for _ in range(4))
        kb.memset(zb[:], 0.0, W=[T["zb"]])
        with nc.allow_non_contiguous_dma(reason="page table transpose (tiny)"):
            kb.dma("sp", ptT[:, :, :], c.pt4.rearrange("b (h p) -> p b h", p=128), W=[T["pt"]])
        kb.dma("sp", ptb[:, :], c.page_table.rearrange("b n -> (b n)").partition_broadcast(128), W=[T["pt"]])
        for nm, t_, src in (("qcol", qcol, c.ns_qcol), ("pcol", pcol, c.ns_pcol), ("M1", M1, c.ns_M1), ("M2", M2, c.ns_M2), ("gsum", gsum, c.ns_gsum)):
            kb.dma("sp", t_[:], src, W=[T["const"]])
        for t_, src in ((pairs, c.ns_pairs), (Fs, c.ns_Fs), (caus, c.ns_caus), (wm0, c.ns_wm0)):
            kb.dma("pool", t_[:], src, W=[T["const"]])
        kb.dma("sp", wflat[:, :], c.w_cmp.rearrange("s j h -> (s j h)").partition_broadcast(128), W=[T["w"]])
        kb.copy(idxq_f[:], ptT[:], R=[T["pt"]], W=[T["idx"]])
        kb.ts(idxq_f[:], idxq_f[:], 4.0, qcol[:, 0:1], ALU.mult, ALU.add, R=[T["idx"], T["const"]], W=[T["idx"]])
        kb.copy(idxq[:], idxq_f[:], R=[T["idx"]], W=[T["idx"]])
        kb.ts(idxq_f[:], idxq_f[:], 32.0, None, ALU.mult, R=[T["idx"]], W=[T["idx"]])
        kb.dma("sp", jrow[:, :], c.ns_jrow.partition_broadcast(128), W=[T["const"]])
        kb.tt(idxq32_f[:, :, :], idxq_f[:].rearrange("p b h -> p (b h)").unsqueeze(2).broadcast_to([128, NS * NH, 32]),
              jrow[:, :].unsqueeze(1).broadcast_to([128, NS * NH, 32]), ALU.add, R=[T["idx"], T["const"]], W=[T["idx"]])
        kb.copy(idxq32[:, :], idxq32_f[:, :, :].rearrange("p a j -> p (a j)"), R=[T["idx"]], W=[T["idx"]])
        kb.copy(idxr_f[:], ptb[:], R=[T["pt"]], W=[T["idx"]])
        kb.ts(idxr_f[:], idxr_f[:], 128.0, pcol[:, 0:1], ALU.mult, ALU.add, R=[T["idx"], T["const"]], W=[T["idx"]])
        kb.copy(idxr[:], idxr_f[:], R=[T["idx"]], W=[T["idx"]])
        for i in range(3):
            kb.memset(pgb[i][:, :, :, 64:65], 1.0, W=[Tpgb[i]])
        kb.memset(VcS[:, :, :, 64:65], 1.0, W=[T["VcS"]])
        kb.memset(Vn[:, :, :, :, 64:65], 1.0, W=[T["Vn"]])
        for hk in range(2):
            kb.copy(VcS[:, :, hk, 65:65 + NBLK], pairs[:, :, :], R=[T["const"]], W=[T["VcS"]])
        cmp_v = c.cache_cmp.rearrange("(n q) f -> n q f", q=32)
        wv = wflat[:, :].rearrange("p (s j h) -> p j s h", s=2, h=2)
        st = {"pg": 0, "k": 0, "g8": 0, "rg": 0}

        def page_dma(dst, cache, b, lp, W):
            col = b * NPAGES + lp
            return kb.dma_fn("pool", lambda e: e.indirect_dma_start(
                out=dst, out_offset=None, in_=cache,
                in_offset=bass.IndirectOffsetOnAxis(ap=idxr[:, col:col + 1], axis=0)), R=[T["idx"]], W=W)

        maskWs = sb("maskWs", [128, 252])
        wcol = sb("wcol", [128, 2, 2])
        Wc = sb("Wc", [128, 4, 252])
        kb.dma("sp", maskWs[:], c.ns_maskWs, W=[T["w"]])
        wsrc = c.w_cmp.rearrange("s j h -> j s h")
        with nc.allow_non_contiguous_dma(reason="tiny weight gather"):
            for q4 in range(4):
                kb.dma("sp", wcol[32 * q4:32 * q4 + 32, :, :], wsrc, W=[T["w"]])
        for s_ in range(2):
            for hk in range(2):
                kb.ts(Wc[:, s_ * 2 + hk, :], maskWs[:, :], wcol[:, s_, hk:hk + 1], None, ALU.mult, R=[T["w"]], W=[T["w"]])
        pK = pS

        def tile_pipeline(src_tile, Tsrc, nrows, mask_rhs, mask_lhsT, Rmask, hk_list=(0, 1)):
            i3 = st["pg"] % 3
            i2 = st["k"] % 2
            st["pg"] += 1
            st["k"] += 1
            pb, Tpb = pgb[i3], Tpgb[i3]
            kb.copy(pb[:nrows, :, :, 0:64], src_tile, R=[Tsrc], W=[Tpb])
            for hk in range(2):
                kb.tr(pT[i2][:64, hk * 128:hk * 128 + nrows], pb[:nrows, 0, hk, 0:64], c.ident_b[:nrows, :nrows],
                      R=[Tpb, c.Tconst], W=[TpT[i2]])
            kb.copy(KpT[i2][:, :, :nrows], pT[i2][:64, 0:256].rearrange("p (h n) -> p h n", h=2)[:, :, :nrows],
                    R=[TpT[i2]], W=[TKpT[i2]], eng="act")
            p2, Tp2 = pS2[i2], TpS2[i2]
            if mask_rhs is not None:
                kb.mm(p2[:nrows, 0:32], mask_lhsT, mask_rhs, True, False, R=Rmask, W=[Tp2], sig=False)
            for hk in range(2):
                first = (mask_rhs is None)
                kb.mm(p2[:nrows, hk * 16:(hk + 1) * 16], KpT[i2][:, hk, :nrows], qs[:, hk * 4:(hk + 1) * 4, :], first, True,
                      R=[TKpT[i2], T["qs"]], W=[Tp2], sig=(hk == 1))
            kb.act(Pp[i2][:nrows, :], p2[:nrows, 0:32], AF.Exp, R=[Tp2], W=[TPp[i2]], scale=0.125)
            for hk in range(2):
                kb.mm(pA[:16, hk * 65:(hk + 1) * 65], Pp[i2][:nrows, hk * 16:(hk + 1) * 16], pb[:nrows, 1, hk, :], False, True,
                      R=[TPp[i2], Tpb], W=[T["pA"]], sig=(hk == 1))

        def finish_branch(br, first):
            for hk in range(2):
                ap_ = pA[:16, hk * 65:(hk + 1) * 65]
                kb.ts(rr[:, :], ap_[:, 64:65], 1e-30, None, ALU.max, R=[T["pA"]], W=[T["rr"]])
                kb.op("dve", lambda e: e.reciprocal(out=rr[:, :], in_=rr[:, :]), [T["rr"]], [T["rr"]])
                kb.tt(rg[:, :], rr[:, :], gns[:, br, hk:hk + 1], ALU.mult, R=[T["rr"], T["gns"]], W=[T["rr"]])
                if first:
                    kb.ts(ONs[:, hk, :], ap_[:, 0:64], rg[:, 0:1], None, ALU.mult, R=[T["pA"], T["rr"]], W=[T["ONs"]])
                else:
                    kb.stt(ONs[:, hk, :], ap_[:, 0:64], rg[:, 0:1], ONs[:, hk, :], ALU.mult, ALU.add, R=[T["pA"], T["rr"], T["ONs"]], W=[T["ONs"]])

        for b in range(NS):
            tg = TP + b * LS
            kb.dma("sp", qs[:, :, :], c.qT_d.rearrange("(h d) t -> d h t", d=64)[:, :, tg:tg + LS], W=[T["qs"]])
            for i3 in range(3):
                kb.dma("sp", KnT[:, i3, :, :], c.kT_d[i3].rearrange("(h d) t -> d h t", d=64)[:, :, tg:tg + LS], W=[T["KnT"]])
            kb.dma("pool", Vn[:, :, :, :, 0:64], c.kvg_d[tg:tg + LS, 0:768].rearrange("l (b s h d) -> l b s h d", b=3, s=2, h=2),
                   W=[T["Vn"]])
            for g in range(4):
                src = c.kvg_d[tg:tg + LS, 768 + g:768 + g + 21:4].rearrange("l (b h) -> l b h", h=2)
                with nc.allow_non_contiguous_dma(reason="tiny gate gather"):
                    kb.dma("sp", gns[4 * g:4 * g + 4, :, :], src, W=[T["gns"]])
            kb.act(gns[:, :, :], gns[:, :, :], AF.Exp, R=[T["gns"]], W=[T["gns"]], scale=-1.0)
            kb.ts(gns[:, :, :], gns[:, :, :], 1.0, None, ALU.add, R=[T["gns"]], W=[T["gns"]])
            kb.op("dve", lambda e: e.reciprocal(out=gns[:, :, :], in_=gns[:, :, :]), [T["gns"]], [T["gns"]])
            kb.mm(pK[:, :], zb[:, 0:128], zb[:, :], True, False, R=[T["zb"]], W=[T["pS"]], sig=False)
            for half in range(NH):
                for lpl in range(32):
                    lp = half * 32 + lpl
                    i3 = st["g8"] % 8
                    st["g8"] += 1
                    t_, Tt_ = pg[i3], Tpg[i3]
                    page_dma(t_[:, 0:256], c.cache_cmp, b, lp, [Tt_])
                    for cc in range(4):
                        kb.mm(pK[:, half * 256 + cc * 64:half * 256 + (cc + 1) * 64], Wc[:, cc, 124 - 4 * lpl:124 - 4 * lpl + 128],
                              t_[:, cc * 64:(cc + 1) * 64], False, True, R=[T["w"], Tt_], W=[T["pS"]],
                              sig=(cc == 3))
            kb.copy(kvbs[:, :, :], pK[:, 0:NH * 256].rearrange("p (a c) -> p a c", a=NH), R=[T["pS"]], W=[T["kvbs"]])
            kb.copy(kvbK[:, :, :], kvbs[:, :, 0:128], R=[T["kvbs"]], W=[T["kvbK"]])
            for hk in range(2):
                kb.copy(VcS[:, :, hk, 0:64], kvbs[:, :, 128 + hk * 64:128 + (hk + 1) * 64], R=[T["kvbs"]], W=[T["VcS"]])
            for half in range(NH):
                for hk in range(2):
                    kb.tr(pT[0][:64, (half * 2 + hk) * 128:(half * 2 + hk + 1) * 128], kvbK[:, half, hk * 64:(hk + 1) * 64], c.ident_b[:, :],
                          R=[T["kvbK"], c.Tconst], W=[TpT[0]])
            kb.copy(KcTs[:, :, :].rearrange("p h (a n) -> p a h n", a=NH),
                    pT[0][:64, 0:NH * 256].rearrange("p (a h n) -> p a h n", a=NH, h=2), R=[TpT[0]], W=[T["KcTs"]])
            for half in range(NH):
                for hk in range(2):
                    kb.mm(pI[:, 384 + (half * 2 + hk) * 16:384 + (half * 2 + hk + 1) * 16], KcTs[:, hk, half * 128:(half + 1) * 128],
                          qs[:, hk * 4:(hk + 1) * 4, :], True, True, R=[T["KcTs"], T["qs"]], W=[T["pI"]], sig=(half == NH - 1 and hk == 1))
            kb.act(Es[:, :, :, :], pI[:, 384:384 + NH * 32].rearrange("p (a h x) -> p a h x", a=NH, h=2), AF.Exp, R=[T["pI"]], W=[T["Es"]], scale=0.125)
            for hk in range(2):
                for half in range(NH):
                    kb.mm(pO[:16, hk * 256:hk * 256 + 65 + NBLK], Es[:, half, hk, :], VcS[:, half, hk, :], half == 0, half == NH - 1,
                          R=[T["Es"], T["VcS"]], W=[T["pO"]], sig=(half == NH - 1))
                ap_ = pO[:16, hk * 256:hk * 256 + 65 + NBLK]
                kb.ts(rr[:, :], ap_[:, 64:65], 1e-30, None, ALU.max, R=[T["pO"]], W=[T["rr"]])
                kb.op("dve", lambda e: e.reciprocal(out=rr[:, :], in_=rr[:, :]), [T["rr"]], [T["rr"]])
                kb.tt(rg[:, :], rr[:, :], gns[:, 0, hk:hk + 1], ALU.mult, R=[T["rr"], T["gns"]], W=[T["rr"]])
                kb.ts(ONs[:, hk, :], ap_[:, 0:64], rg[:, 0:1], None, ALU.mult, R=[T["pO"], T["rr"]], W=[T["ONs"]])
                kb.ts(impg[:, :], ap_[:, 65:65 + NBLK], rr[:, 0:1], None, ALU.mult, R=[T["pO"], T["rr"]], W=[T["impg"]])
                if b == 0:
                    dbg_dump(kb, c, ONs[:, hk, :], [T["ONs"]], 16, 64)
                    dbg_dump(kb, c, impg[:, :], [T["impg"]], 16, NBLK)
                kb.mm(pI[:LS, 0:NBLK], gsum[:, :], impg[:, :], True, True, R=[T["const"], T["impg"]], W=[T["pI"]])
                kb.tt(sc[:, 0:NBLK], pI[:LS, 0:NBLK], M1[:, 0:NBLK], ALU.mult, R=[T["pI"], T["const"]], W=[T["sc"]])
                kb.tt(sc[:, 0:NBLK], sc[:, 0:NBLK], M2[:, 0:NBLK], ALU.add, R=[T["sc"], T["const"]], W=[T["sc"]])
                kb.copy(sc[:, NBLK:NSELS], M2[:, NBLK:NSELS], R=[T["const"]], W=[T["sc"]])
                kb.op("dve", lambda e: e.max(out=m8a[:, :], in_=sc[:, :]), [T["sc"]], [T["m8"]])
                kb.op("dve", lambda e: e.match_replace(out=wk_[:, :], in_to_replace=m8a[:, :], in_values=sc[:, :], imm_value=-2.0),
                      [T["sc"], T["m8"]], [T["sel"]])
                kb.op("dve", lambda e: e.max(out=m8b[:, :], in_=wk_[:, :]), [T["sel"]], [T["m8"]])
                kb.op("dve", lambda e: e.tensor_reduce(out=thr[:, :], in_=m8b[:, :], axis=AX.X, op=ALU.min), [T["m8"]], [T["m8"]])
                kb.ts(selm[:, :], sc[:, 0:NBLK], thr[:, 0:1], None, ALU.is_ge, R=[T["sc"], T["m8"]], W=[T["sel"]])
                kb.ts(selm[:, :], selm[:, :], -1.0, -NEGM, ALU.add, ALU.mult, R=[T["sel"]], W=[T["sel"]])
                if b == 0:
                    dbg_dump(kb, c, selm[:, :], [T["sel"]], LS, NBLK)
                kb.tr(pI[:NBLK, 256:256 + LS], selm[:, :], c.ident_f[:LS, :LS], R=[T["sel"], c.Tconst], W=[T["pI"]])
                kb.copy(selx[:, hk, :, :], pI[:NBLK, 256:256 + LS].unsqueeze(1).broadcast_to([NBLK, 4, LS]), R=[T["pI"]], W=[T["selx"]])
            for br in (1, 2):
                kb.mm(pA[:, :], zb[:, 0:128], zb[:, :], True, False, R=[T["zb"]], W=[T["pA"]], sig=False)
                if br == 1:
                    cache = c.cache_sel
                    for lp in range(NPAGES if not cfg.get("NS_SKIP_SEL") else 0):
                        i3 = st["g8"] % 8
                        st["g8"] += 1
                        t_, Tt_ = pg[i3], Tpg[i3]
                        page_dma(t_[:, 0:256], cache, b, lp, [Tt_])
                        tile_pipeline(t_[:, 0:256].rearrange("p (s h d) -> p s h d", s=2, h=2), Tt_, 128,
                                      selx[:, :, :, :].rearrange("p a g l -> p (a g l)"), Fs[:, lp * 128:(lp + 1) * 128], [T["selx"], T["const"]])
                else:
                    for kt in range(WL // 128):
                        i3 = st["g8"] % 8
                        st["g8"] += 1
                        t_, Tt_ = pg[i3], Tpg[i3]
                        kb.dma("sp", t_[:, 0:256], c.cache_win[b, kt * 128:(kt + 1) * 128, :], W=[Tt_])
                        if kt == 0:
                            tile_pipeline(t_[:, 0:256].rearrange("p (s h d) -> p s h d", s=2, h=2), Tt_, 128,
                                          wm0[:, :], c.ident_b[:, :], [T["const"], c.Tconst])
                        else:
                            tile_pipeline(t_[:, 0:256].rearrange("p (s h d) -> p s h d", s=2, h=2), Tt_, 128, None, None, [])
                p2, Tp2 = pS2[0], TpS2[0]
                kb.mm(p2[:LS, 0:32], c.ident_b[:LS, :LS], caus[:, :], True, False, R=[c.Tconst, T["const"]], W=[Tp2], sig=False)
                for hk in range(2):
                    kb.mm(p2[:LS, hk * 16:(hk + 1) * 16], KnT[:, br, hk, :], qs[:, hk * 4:(hk + 1) * 4, :], False, True,
                          R=[T["KnT"], T["qs"]], W=[Tp2], sig=(hk == 1))
                kb.act(Pp[0][:LS, :], p2[:LS, 0:32], AF.Exp, R=[Tp2], W=[TPp[0]], scale=0.125)
                for hk in range(2):
                    kb.mm(pA[:16, hk * 65:(hk + 1) * 65], Pp[0][:LS, hk * 16:(hk + 1) * 16], Vn[:, br, 1, hk, :], False, True,
                          R=[TPp[0], T["Vn"]], W=[T["pA"]], sig=(hk == 1))
                finish_branch(br, False)
                if b == 0:
                    dbg_dump(kb, c, ONs[:, :, :].rearrange("p a d -> p (a d)"), [T["ONs"]], 16, 128)
            kb.copy(ONb[:, :, :], ONs[:, :, :], R=[T["ONs"]], W=[T["ONs"]])
            for hk in range(2):
                for g in range(4):
                    h = hk * 4 + g
                    kb.dma("sp", c.on_d[tg:tg + LS, h * 64:(h + 1) * 64], ONb[4 * g:4 * g + 4, hk, :], R=[T["ONs"]])
        kb.barrier()


def phase_nsa_zero(kb, c, start=0):
    nc = c.nc
    with contextlib.ExitStack() as ps:
        zt = ps.enter_context(nc.sbuf_tensor("nz_z", [128, 512], BF16))
        Tz = Tok()
        kb.memset(zt[:], 0.0, W=[Tz])
        for r0 in range(start, c.T, 128):
            nr = min(128, c.T - r0)
            kb.dma("sp", c.on_d[r0:r0 + nr, :], zt[:nr, :], R=[Tz])
        kb.barrier()

def build(cfg, stages=("mod", "ffn1", "win", "gdn", "nsap", "nsas", "mix", "ffn2")):
    nc = bass.Bass("TRN2", target_bir_lowering=False)
    c = Ctx()
    c.nc, c.cfg = nc, cfg
    cfg.setdefault("WIN", 512)
    NP, SEQ, NS = cfg["NP"], cfg["SEQ"], cfg["NS"]
    TP, TS = NP * SEQ, NS * LS
    T = TP + TS
    NR = NP + TS
    c.TP, c.TS, c.T = TP, TS, T
    WL = cfg["WIN"]
    PW = min(WL, SEQ)

    def din(name, shape, dt=F32):
        return nc.dram_tensor(name, list(shape), dt, kind="ExternalInput").ap()

    def dout(name, shape, dt=F32):
        return nc.dram_tensor(name, list(shape), dt, kind="ExternalOutput").ap()

    def dscr(name, shape, dt=F32):
        return nc.dram_tensor(name, list(shape), dt, kind="Internal").ap()

    c.xp = din("xp", [TP, D])
    c.xs = din("xs", [TS, D])
    c.c_rows = din("c_rows", [NR, D])
    c.ident_d = din("ident", [128, 128])
    c.ln_g = din("ln_g", [3, D])
    c.ln_b = din("ln_b", [3, D])
    c.w_ada = din("w_ada", [D, 9 * D])
    c.b_ada = din("b_ada", [1, 9 * D])
    c.w_ff1_gu = din("w_ff1_gu", [D, 2 * DFF])
    c.w_ff1_dn = din("w_ff1_dn", [DFF, D])
    c.w_ff2_gu = din("w_ff2_gu", [D, 2 * DFF])
    c.w_ff2_dn = din("w_ff2_dn", [DFF, D])
    c.w_in = din("w_in", [D, DIN])
    c.state_gdn = din("state_gdn", [NS, 8, 64, 64])
    c.conv_buf = din("conv_buf", [NS, 3, 1536])
    c.cache_win = din("cache_win", [NS, WL, 256])
    c.conv_w = din("conv_w", [4, 1536])
    c.w_cmp = din("w_cmp", [2, 32, 2])
    nsel, ncmp = SEQ // 64, SEQ // 32
    c.nsa_M1 = din("nsa_M1", [SEQ, nsel])
    c.nsa_M2 = din("nsa_M2", [SEQ, nsel])
    c.nsa_cmask = din("nsa_cmask", [ncmp, SEQ])
    c.nsa_F = din("nsa_F", [nsel, SEQ])
    c.nsa_Wm = din("nsa_Wm", [128, 8, 512])
    c.nsa_pair = din("nsa_pair", [ncmp, nsel])
    c.nsa_maskW = din("nsa_maskW", [128, 124])
    NPAGES, NPHYS = cfg["NPAGES"], cfg["NPHYS"]
    c.cache_cmp = din("cache_cmp", [NPHYS * 128, 256])
    c.cache_sel = din("cache_sel", [NPHYS * 128, 256])
    c.page_table = din("page_table", [NS, NPAGES], I32)
    c.pt4 = din("pt4", [NS, NPAGES * 4], I32)
    for k_, v_ in nsa_s_consts(NPAGES).items():
        setattr(c, k_, din(k_, list(v_.shape)))
    c.w_br_gdn = din("w_br_gdn", [512, D])
    c.w_br_nsa = din("w_br_nsa", [512, D])
    c.w_out = din("w_out", [D, D])
    c.a_log = din("a_log", [1, 8])
    c.dt_bias = din("dt_bias", [1, 8])
    c.norm_w = din("norm_w", [1, 64])
    c.blk1_d = din("blk1", [128, 128])
    c.gconst_d = din("gconst", [64, 5, 64])
    c.valid_d = din("valid", [64, 1])
    c.yp = dout("y_prompt", [TP, D])
    c.ys = dout("y_sample", [TS, D])
    c.p_gdn = dout("p_gdn", [NP, 8, 64, 64])
    c.p_conv = dout("p_conv", [NP, 3, 1536])
    c.p_cmp = dout("p_cmp", [TP, 256])
    c.p_sel = dout("p_sel", [TP, 256])
    c.p_win = dout("p_win", [NP * PW, 256])
    c.s_gdn = dout("s_gdn", [NS, 8, 64, 64])
    c.s_conv = dout("s_conv", [NS, 3, 1536])
    c.s_cmp = dout("s_cmp", [TS, 256])
    c.s_sel = dout("s_sel", [TS, 256])
    c.s_win = dout("s_win", [NS, WL, 256])
    c.mod_d = dscr("mod_d", [NR, 9 * D])
    c.x1 = {"p": dscr("x1p", [TP, D]), "s": dscr("x1s", [TS, D])}
    c.x2 = {"p": dscr("x2p", [TP, D]), "s": dscr("x2s", [TS, D])}
    c.qkvT_d = dscr("qkvT_d", [1536, T])
    c.qT_d = dscr("qT_d", [512, T], BF16)
    c.kT_d = dscr("kT_d", [3, 128, T], BF16)
    c.mgT_d = dscr("mgT_d", [2048, T], BF16)
    c.zab_d = dscr("zab_d", [T, 528])
    c.kvg_d = dscr("kvg_d", [T, 792])
    c.qkvn_d = dscr("qkvn_d", [1536, T])
    c.kvtok_d = dscr("kvtok_d", [T, 1024])
    dbg = dout if cfg.get("DBG") else dscr
    c.dbg_d = dbg("dbg_d", [16, 128, 512])
    c.dbg_n = 0
    c.og_d = dbg("og_d", [T, 512], BF16)
    c.on_d = dbg("on_d", [T, 512], BF16)

    with contextlib.ExitStack() as es:
        kb = KB(nc, es)
        c.kb = kb
        c.ident_f = es.enter_context(nc.sbuf_tensor("ident_f", [128, 128], F32))
        c.ident_b = es.enter_context(nc.sbuf_tensor("ident_b", [128, 128], BF16))
        c.Tconst = Tok()
        kb.dma("sp", c.ident_f[:], c.ident_d, W=[c.Tconst])
        kb.copy(c.ident_b[:], c.ident_f[:], R=[c.Tconst], W=[c.Tconst])
        if "mod" in stages:
            phase_mod(kb, c)
        if "ffn1" in stages:
            phase_ffn(kb, c, "f1", {"p": c.xp, "s": c.xs}, c.x1, c.w_ff1_gu, c.w_ff1_dn, 0)
        if "win" in stages:
            phase_win(kb, c)
        if "gdn" in stages:
            phase_gdn_a(kb, c)
            phase_gdn_b(kb, c)
        if "nsap" in stages:
            phase_nsa_prompt(kb, c)
        if "nsas" in stages:
            phase_nsa_sample(kb, c)
        if "nsa0" in stages:
            phase_nsa_zero(kb, c)
        if "nsa0s" in stages:
            phase_nsa_zero(kb, c, c.TP)
        if "mix" in stages:
            phase_mix(kb, c)
        if "ffn2" in stages:
            phase_ffn(kb, c, "f2", c.x2, {"p": c.yp, "s": c.ys}, c.w_ff2_gu, c.w_ff2_dn, 2)
        kb.finish()
        c.nops = kb.nops
    return nc, c


def make_in_maps(cfg, inputs, ncores):
    NP, SEQ, NS = cfg["NP"], cfg["SEQ"], cfg["NS"]
    f = lambda a: np.ascontiguousarray(np.asarray(a))
    maps = []
    ident = np.eye(128, dtype=np.float32)
    blk1 = np.kron(np.eye(2, dtype=np.float32), np.ones((64, 64), np.float32))
    ii = np.arange(64)
    gconst = np.stack([
        (ii[:, None] <= ii[None, :]).astype(np.float32),
        np.ones((64, 64), np.float32),
        np.where(ii[None, :] >= ii[:, None], 0.0, -30000.0).astype(np.float32),
        np.where(ii[None, :] >= ii[:, None], 30000.0, 0.0).astype(np.float32),
        np.eye(64, dtype=np.float32)], axis=1)
    valid = (ii < LS).astype(np.float32).reshape(64, 1)
    nconst = nsa_consts(SEQ)
    nconst.update(nsa_s_consts(cfg["NPAGES"]))
    cc_all = f(inputs["cache_cmp_kv"][0]).reshape(-1, 256)
    cs_all = f(inputs["cache_sel_kv"][0]).reshape(-1, 256)
    for i in range(ncores):
        ps, ss = slice(i * NP, (i + 1) * NP), slice(i * NS, (i + 1) * NS)
        m = {
            "xp": f(inputs["x_prompt"][ps]).reshape(NP * SEQ, D),
            "xs": f(inputs["x_sample"][ss]).reshape(NS * LS, D),
            "c_rows": np.concatenate([f(inputs["c_prompt"][ps]), np.repeat(f(inputs["c_sample"][ss]), LS, axis=0)], 0),
            "ident": ident,
            "ln_g": f(inputs["ln_g"][0]), "ln_b": f(inputs["ln_b"][0]),
            "w_ada": f(inputs["w_ada"][0]), "b_ada": f(inputs["b_ada"]).reshape(1, 9 * D),
            "w_ff1_gu": f(inputs["w_ff1_gu"][0]), "w_ff1_dn": f(inputs["w_ff1_dn"][0]),
            "w_ff2_gu": f(inputs["w_ff2_gu"][0]), "w_ff2_dn": f(inputs["w_ff2_dn"][0]),
            "w_in": f(inputs["w_in"][0]),
            "state_gdn": f(inputs["state_gdn"][0][ss]),
            "conv_buf": f(inputs["state_gdn_conv"][0][ss]),
            "cache_win": f(inputs["cache_win_kv"][0][ss]).reshape(NS, -1, 256),
            "conv_w": f(inputs["gdn_conv_w"][0]), "a_log": f(inputs["gdn_a_log"]).reshape(1, 8),
            "dt_bias": f(inputs["gdn_dt_bias"]).reshape(1, 8), "norm_w": f(inputs["gdn_norm_w"]).reshape(1, 64),
            "blk1": blk1, "gconst": gconst, "valid": valid, "w_cmp": f(inputs["nsa_w_cmp"][0]),
            "cache_cmp": cc_all, "cache_sel": cs_all,
            "page_table": f(inputs["page_table"][ss]).astype(np.int32),
            "pt4": np.repeat(f(inputs["page_table"][ss]).astype(np.int32), 4, axis=1),
            "w_br_gdn": f(inputs["w_br_gdn"][0]), "w_br_nsa": f(inputs["w_br_nsa"][0]), "w_out": f(inputs["w_out"][0]),
        }
        m.update(nconst)
        maps.append(m)
    return maps


def gather_outputs(cfg, results, ncores):
    NP, SEQ, NS = cfg["NP"], cfg["SEQ"], cfg["NS"]
    PW = min(cfg["WIN"], SEQ)
    cat = lambda k: np.concatenate([np.asarray(r[k]) for r in results], 0)
    B, Bs = NP * ncores, NS * ncores
    return (
        cat("y_prompt").reshape(B, SEQ, D),
        cat("y_sample").reshape(Bs, LS, D),
        cat("p_gdn").reshape(1, B, 8, 64, 64),
        cat("p_conv").reshape(1, B, 3, 1536),
        cat("p_cmp").reshape(1, B, SEQ, 2, 2, 64),
        cat("p_sel").reshape(1, B, SEQ, 2, 2, 64),
        cat("p_win").reshape(1, B, PW, 2, 2, 64),
        cat("s_gdn").reshape(1, Bs, 8, 64, 64),
        cat("s_conv").reshape(1, Bs, 3, 1536),
        cat("s_cmp").reshape(1, Bs, LS, 2, 2, 64),
        cat("s_sel").reshape(1, Bs, LS, 2, 2, 64),
        cat("s_win").reshape(1, Bs, cfg["WIN"], 2, 2, 64),
    )


_CACHE = {}


def run(cfg, inputs, ncores, stages=None):
    key = (tuple(sorted(cfg.items())), ncores, stages)
    if key not in _CACHE:
        _CACHE[key] = build(dict(cfg), stages) if stages else build(dict(cfg))
    nc, c = _CACHE[key]
    res = run_bass_kernel_spmd(nc, make_in_maps(c.cfg, inputs, ncores), core_ids=list(range(ncores)))
    _CACHE["last"] = res.results
    return gather_outputs(c.cfg, res.results, ncores)


def kernel(**inputs):
    return run(FULL_CFG, inputs, NCORES)
```
